# Optimizing a Trainium2 kernel written in Bass

```python
import math
import jax, jax.numpy as jnp
from jax import lax
import numpy as np

D_MODEL = 1024
BATCH = 4
SEQ = 4096
DEPTH = 2

N_EVEN = (DEPTH + 1) // 2
N_ODD = DEPTH // 2

RET_HEADS = 4
RET_DK = 128
RET_DV = 256
RET_CHUNK = 128
ROPE_BASE = 10000.0
RET_QK = RET_HEADS * RET_DK
RET_V = RET_HEADS * RET_DV
SSM_HEADS = 16
SSM_HEADDIM = 64
SSM_DINNER = SSM_HEADS * SSM_HEADDIM
SSM_STATE = 128
SSM_GROUPS = 2
SSM_CONV = 4
SSM_CHUNK = 128
SSM_XBC = SSM_DINNER + 2 * SSM_GROUPS * SSM_STATE
EVEN_SPLITS = [RET_QK, RET_QK, RET_V, RET_V, SSM_DINNER, SSM_XBC, SSM_HEADS]
EVEN_IN = sum(EVEN_SPLITS)
EVEN_MIX = RET_V + SSM_DINNER
CONF_DIM = D_MODEL // 2
CONF_KERNEL = 31
S5_DIM = D_MODEL // 2
S5_GROUP = 16
S5_GROUPS = S5_DIM // S5_GROUP
S5_STATE = 64
ODD_IN = 2 * CONF_DIM + S5_DIM
ODD_MIX = CONF_DIM + S5_DIM
D_FF = 2816
FFN_CONV = 3
EPS = 1e-6

kernel_name = "hybrid_retention_ssd_conformer_s5_trunk"

F32 = jnp.float32


def rms_norm(x, g, eps=EPS):
    xf = x.astype(F32)
    y = xf * lax.rsqrt(jnp.mean(xf * xf, axis=-1, keepdims=True) + eps)
    return (y * g.astype(F32)).astype(x.dtype)


def layer_norm(x, g, b, eps=EPS):
    xf = x.astype(F32)
    mu = jnp.mean(xf, axis=-1, keepdims=True)
    xc = xf - mu
    var = jnp.mean(xc * xc, axis=-1, keepdims=True)
    return (xc * lax.rsqrt(var + eps) * g.astype(F32) + b.astype(F32)).astype(x.dtype)


def causal_dwconv(x, w, b):
    k = w.shape[0]
    y = lax.conv_general_dilated(
        x, w[:, None, :].astype(x.dtype), window_strides=(1,),
        padding=((k - 1, 0),), dimension_numbers=("NWC", "WIO", "NWC"),
        feature_group_count=x.shape[-1])
    return y + b.astype(x.dtype)


def split_cols(a, sizes):
    return jnp.split(a, np.cumsum(sizes)[:-1].tolist(), axis=-1)


def rotary(x, pos):
    d = x.shape[-1]
    inv = ROPE_BASE ** (-jnp.arange(0, d, 2, dtype=F32) / d)
    ang = pos.astype(F32)[:, None] * inv[None, :]
    cos = jnp.cos(ang)[None, :, None, :]
    sin = jnp.sin(ang)[None, :, None, :]
    x1, x2 = x[..., : d // 2], x[..., d // 2:]
    return jnp.concatenate([x1 * cos - x2 * sin, x1 * sin + x2 * cos], axis=-1)


def retention_chunkwise(q, k, v):
    b, l, h, dk = q.shape
    dv = v.shape[-1]
    c = RET_CHUNK
    nc = l // c
    log_g = jnp.log1p(-(2.0 ** (-5.0 - jnp.arange(h, dtype=F32))))
    idx = jnp.arange(c, dtype=F32)
    diff = idx[:, None] - idx[None, :]
    intra = jnp.where(diff[None] >= 0,
                      jnp.exp(jnp.maximum(diff, 0.0)[None] * log_g[:, None, None]), 0.0)
    q = q.reshape(b, nc, c, h, dk)
    k = k.reshape(b, nc, c, h, dk) * (dk ** -0.5)
    v = v.reshape(b, nc, c, h, dv)
    s = jnp.einsum("bcihd,bcjhd->bchij", q, k) * intra
    inner = jnp.einsum("bchij,bcjhe->bcihe", s, v)
    zeta = jnp.exp((c - 1 - idx)[:, None] * log_g[None, :])
    kv = jnp.einsum("bcjhd,bcjhe->bchde", k * zeta[None, None, :, :, None], v)
    chunk_decay = jnp.exp(c * log_g)[None, :, None, None]

    def step(state, kv_c):
        return state * chunk_decay + kv_c, state

    _, prev = lax.scan(step, jnp.zeros_like(kv[:, 0]), jnp.moveaxis(kv, 1, 0))
    prev = jnp.moveaxis(prev, 0, 1)
    xi = jnp.exp((idx + 1)[:, None] * log_g[None, :])
    cross = jnp.einsum("bcihd,bchde->bcihe", q, prev) * xi[None, None, :, :, None]
    return (inner + cross).reshape(b, l, h, dv)


def head_group_norm(y, eps=EPS):
    mu = jnp.mean(y, axis=-1, keepdims=True)
    yc = y - mu
    return yc * lax.rsqrt(jnp.mean(yc * yc, axis=-1, keepdims=True) + eps)


def ssd_chunked(xh, dt, a_neg, bm, cm):
    b, l, h, p = xh.shape
    g, n = bm.shape[-2], bm.shape[-1]
    hg = h // g
    c = SSM_CHUNK
    nc = l // c
    X = (xh * dt[..., None]).reshape(b, nc, c, g, hg, p)
    acs = jnp.cumsum((dt * a_neg).reshape(b, nc, c, g, hg), axis=2)
    Bc = bm.reshape(b, nc, c, g, n)
    Cc = cm.reshape(b, nc, c, g, n)
    mask = (jnp.arange(c)[:, None] >= jnp.arange(c)[None, :])[:, :, None, None]
    seg = acs[:, :, :, None] - acs[:, :, None, :]
    lmat = jnp.exp(jnp.where(mask, seg, -jnp.inf))
    cb = jnp.einsum("bcign,bcjgn->bcijg", Cc, Bc)
    y_diag = jnp.einsum("bcijgh,bcjghp->bcighp", cb[..., None] * lmat, X)
    decay = jnp.exp(acs[:, :, -1:] - acs)
    states = jnp.einsum("bcjgn,bcjghp->bcghpn", Bc, X * decay[..., None])
    chunk_decay = jnp.exp(acs[:, :, -1])

    def step(state, inp):
        st, cd = inp
        return state * cd[..., None, None] + st, state

    _, prev = lax.scan(step, jnp.zeros_like(states[:, 0]),
                       (jnp.moveaxis(states, 1, 0), jnp.moveaxis(chunk_decay, 1, 0)))
    prev = jnp.moveaxis(prev, 0, 1)
    y_off = jnp.einsum("bcign,bcghpn->bcighp", Cc, prev) * jnp.exp(acs)[..., None]
    return (y_diag + y_off).reshape(b, l, h, p)


def even_mixer(h, w_in, conv_w, conv_b, dt_bias, a_log, d_skip, ssm_norm_w, w_out):
    b, l, _ = h.shape
    proj = h @ w_in
    q, k, v, g, z, xbc, dtr = split_cols(proj, EVEN_SPLITS)
    pos = jnp.arange(l)
    q = rotary(q.reshape(b, l, RET_HEADS, RET_DK).astype(F32), pos)
    k = rotary(k.reshape(b, l, RET_HEADS, RET_DK).astype(F32), pos)
    v = v.reshape(b, l, RET_HEADS, RET_DV).astype(F32)
    r = head_group_norm(retention_chunkwise(q, k, v)).reshape(b, l, RET_V)
    y_ret = jax.nn.silu(g.astype(F32)) * r
    xbc = jax.nn.silu(causal_dwconv(xbc, conv_w, conv_b))
    xs, bm, cm = split_cols(xbc, [SSM_DINNER, SSM_GROUPS * SSM_STATE, SSM_GROUPS * SSM_STATE])
    xs = xs.reshape(b, l, SSM_HEADS, SSM_HEADDIM).astype(F32)
    dt = jax.nn.softplus(dtr.astype(F32) + dt_bias.astype(F32))
    a_neg = -jnp.exp(a_log.astype(F32))
    y = ssd_chunked(xs, dt, a_neg,
                    bm.reshape(b, l, SSM_GROUPS, SSM_STATE).astype(F32),
                    cm.reshape(b, l, SSM_GROUPS, SSM_STATE).astype(F32))
    y = y + d_skip.astype(F32)[:, None] * xs
    y = y.reshape(b, l, SSM_DINNER) * jax.nn.silu(z.astype(F32))
    y = y.reshape(b, l, SSM_GROUPS, SSM_DINNER // SSM_GROUPS)
    y = y * lax.rsqrt(jnp.mean(y * y, axis=-1, keepdims=True) + EPS)
    y_ssm = y.reshape(b, l, SSM_DINNER) * ssm_norm_w.astype(F32)
    mix = jnp.concatenate([y_ret, y_ssm], axis=-1).astype(h.dtype)
    return mix @ w_out


def s5_ssm(u, a_re, a_im, b_re, b_im, c_re, c_im, d_skip, log_step):
    bsz, l, _ = u.shape
    uf = u.astype(F32)
    ug = jnp.moveaxis(uf.reshape(bsz, l, S5_GROUPS, S5_GROUP), 1, 0)
    step = jnp.exp(log_step.astype(F32))[:, None]
    lr, li = a_re.astype(F32), a_im.astype(F32)
    mag = jnp.exp(lr * step)
    ab_re = mag * jnp.cos(li * step)
    ab_im = mag * jnp.sin(li * step)
    den = lr * lr + li * li
    f_re = ((ab_re - 1.0) * lr + ab_im * li) / den
    f_im = (ab_im * lr - (ab_re - 1.0) * li) / den
    br, bi = b_re.astype(F32), b_im.astype(F32)
    bb_re = f_re[..., None] * br - f_im[..., None] * bi
    bb_im = f_re[..., None] * bi + f_im[..., None] * br
    bu_re = jnp.einsum("lbgc,gnc->lbgn", ug, bb_re)
    bu_im = jnp.einsum("lbgc,gnc->lbgn", ug, bb_im)
    a_re_seq = jnp.broadcast_to(ab_re[None, None], (l, 1) + ab_re.shape)
    a_im_seq = jnp.broadcast_to(ab_im[None, None], (l, 1) + ab_im.shape)

    def combine(e1, e2):
        a1r, a1i, b1r, b1i = e1
        a2r, a2i, b2r, b2i = e2
        return (a2r * a1r - a2i * a1i,
                a2r * a1i + a2i * a1r,
                a2r * b1r - a2i * b1i + b2r,
                a2r * b1i + a2i * b1r + b2i)

    _, _, xr, xi = lax.associative_scan(combine, (a_re_seq, a_im_seq, bu_re, bu_im), axis=0)
    y = (jnp.einsum("lbgn,gcn->lbgc", xr, c_re.astype(F32))
         - jnp.einsum("lbgn,gcn->lbgc", xi, c_im.astype(F32)))
    y = jnp.moveaxis(y, 0, 1).reshape(bsz, l, S5_DIM)
    return y + d_skip.astype(F32) * uf


def odd_mixer(h, w_in, conf_dw_w, conf_dw_b, conf_ln_g, conf_ln_b, s5_a_re, s5_a_im,
              s5_b_re, s5_b_im, s5_c_re, s5_c_im, s5_d, s5_log_step, s5_glu_w, w_out):
    proj = h @ w_in
    ca, cg, u = split_cols(proj, [CONF_DIM, CONF_DIM, S5_DIM])
    c = ca * jax.nn.sigmoid(cg)
    c = causal_dwconv(c, conf_dw_w, conf_dw_b)
    c = jax.nn.silu(layer_norm(c, conf_ln_g, conf_ln_b)).astype(F32)
    s = jax.nn.gelu(s5_ssm(u, s5_a_re, s5_a_im, s5_b_re, s5_b_im, s5_c_re, s5_c_im, s5_d, s5_log_step))
    s = s * jax.nn.sigmoid(s @ s5_glu_w.astype(F32))
    mix = jnp.concatenate([c, s], axis=-1).astype(h.dtype)
    return mix @ w_out


def conv_ffn(h, w_up, dw_w, dw_b, w_down):
    a = causal_dwconv(h @ w_up, dw_w, dw_b)
    gate, up = jnp.split(a, 2, axis=-1)
    return (jax.nn.silu(gate) * up) @ w_down


def setup_inputs(seed: int = 0) -> dict:
    key = jax.random.key(seed)
    ks = iter(jax.random.split(key, 48))

    def nrm(shape, scale):
        return jax.random.normal(next(ks), shape, F32) * scale

    def gain(shape):
        return 1.0 + 0.02 * jax.random.normal(next(ks), shape, F32)

    x = jax.random.normal(next(ks), (BATCH, SEQ, D_MODEL), F32)
    mix_norm = gain((DEPTH, D_MODEL))
    e_w_in = nrm((N_EVEN, D_MODEL, EVEN_IN), D_MODEL ** -0.5)
    e_conv_w = nrm((N_EVEN, SSM_CONV, SSM_XBC), SSM_CONV ** -0.5)
    e_conv_b = nrm((N_EVEN, SSM_XBC), 0.02)
    dt0 = jnp.exp(jax.random.uniform(next(ks), (N_EVEN, SSM_HEADS), F32,
                                     math.log(1e-3), math.log(1e-1)))
    e_dt_bias = dt0 + jnp.log(-jnp.expm1(-dt0))
    e_a_log = jnp.log(jax.random.uniform(next(ks), (N_EVEN, SSM_HEADS), F32, 1.0, 16.0))
    e_d = gain((N_EVEN, SSM_HEADS))
    e_ssm_norm = gain((N_EVEN, SSM_DINNER))
    e_w_out = nrm((N_EVEN, EVEN_MIX, D_MODEL), EVEN_MIX ** -0.5)
    o_w_in = nrm((N_ODD, D_MODEL, ODD_IN), D_MODEL ** -0.5)
    o_dw_w = nrm((N_ODD, CONF_KERNEL, CONF_DIM), CONF_KERNEL ** -0.5)
    o_dw_b = nrm((N_ODD, CONF_DIM), 0.02)
    o_ln_g = gain((N_ODD, CONF_DIM))
    o_ln_b = nrm((N_ODD, CONF_DIM), 0.02)
    o_a_re = -0.5 + nrm((N_ODD, S5_GROUPS, S5_STATE), 0.01)
    o_a_im = (math.pi * jnp.arange(S5_STATE, dtype=F32))[None, None, :] + nrm((N_ODD, S5_GROUPS, S5_STATE), 0.01)
    o_b_re = nrm((N_ODD, S5_GROUPS, S5_STATE, S5_GROUP), S5_GROUP ** -0.5)
    o_b_im = nrm((N_ODD, S5_GROUPS, S5_STATE, S5_GROUP), S5_GROUP ** -0.5)
    o_c_re = nrm((N_ODD, S5_GROUPS, S5_GROUP, S5_STATE), S5_STATE ** -0.5)
    o_c_im = nrm((N_ODD, S5_GROUPS, S5_GROUP, S5_STATE), S5_STATE ** -0.5)
    o_d = nrm((N_ODD, S5_DIM), 1.0)
    o_log_step = jax.random.uniform(next(ks), (N_ODD, S5_GROUPS), F32, math.log(1e-3), math.log(1e-1))
    o_glu_w = nrm((N_ODD, S5_DIM, S5_DIM), S5_DIM ** -0.5)
    o_w_out = nrm((N_ODD, ODD_MIX, D_MODEL), ODD_MIX ** -0.5)
    ffn_norm = gain((DEPTH, D_MODEL))
    ffn_w_up = nrm((DEPTH, D_MODEL, 2 * D_FF), D_MODEL ** -0.5)
    ffn_dw_w = nrm((DEPTH, FFN_CONV, 2 * D_FF), FFN_CONV ** -0.5)
    ffn_dw_b = nrm((DEPTH, 2 * D_FF), 0.02)
    ffn_w_down = nrm((DEPTH, D_FF, D_MODEL), D_FF ** -0.5)
    final_norm = gain((D_MODEL,))
    return {
        "x": x, "mix_norm": mix_norm,
        "e_w_in": e_w_in, "e_conv_w": e_conv_w, "e_conv_b": e_conv_b,
        "e_dt_bias": e_dt_bias, "e_a_log": e_a_log, "e_d": e_d,
        "e_ssm_norm": e_ssm_norm, "e_w_out": e_w_out,
        "o_w_in": o_w_in, "o_dw_w": o_dw_w, "o_dw_b": o_dw_b,
        "o_ln_g": o_ln_g, "o_ln_b": o_ln_b, "o_a_re": o_a_re, "o_a_im": o_a_im,
        "o_b_re": o_b_re, "o_b_im": o_b_im, "o_c_re": o_c_re, "o_c_im": o_c_im,
        "o_d": o_d, "o_log_step": o_log_step, "o_glu_w": o_glu_w, "o_w_out": o_w_out,
        "ffn_norm": ffn_norm, "ffn_w_up": ffn_w_up, "ffn_dw_w": ffn_dw_w,
        "ffn_dw_b": ffn_dw_b, "ffn_w_down": ffn_w_down, "final_norm": final_norm,
    }


def reference(x, mix_norm, e_w_in, e_conv_w, e_conv_b, e_dt_bias, e_a_log, e_d,
              e_ssm_norm, e_w_out, o_w_in, o_dw_w, o_dw_b, o_ln_g, o_ln_b, o_a_re,
              o_a_im, o_b_re, o_b_im, o_c_re, o_c_im, o_d, o_log_step, o_glu_w,
              o_w_out, ffn_norm, ffn_w_up, ffn_dw_w, ffn_dw_b, ffn_w_down, final_norm):
    for i in range(DEPTH):
        j = i // 2
        hn = rms_norm(x, mix_norm[i])
        if i % 2 == 0:
            m = even_mixer(hn, e_w_in[j], e_conv_w[j], e_conv_b[j], e_dt_bias[j],
                           e_a_log[j], e_d[j], e_ssm_norm[j], e_w_out[j])
        else:
            m = odd_mixer(hn, o_w_in[j], o_dw_w[j], o_dw_b[j], o_ln_g[j], o_ln_b[j],
                          o_a_re[j], o_a_im[j], o_b_re[j], o_b_im[j], o_c_re[j],
                          o_c_im[j], o_d[j], o_log_step[j], o_glu_w[j], o_w_out[j])
        x = x + m.astype(x.dtype)
        f = conv_ffn(rms_norm(x, ffn_norm[i]), ffn_w_up[i], ffn_dw_w[i], ffn_dw_b[i], ffn_w_down[i])
        x = x + f.astype(x.dtype)
    return rms_norm(x, final_norm)
```

```python
import math
from contextlib import ExitStack
import numpy as np
import concourse.bass as bass
import concourse.mybir as mybir
from concourse.bass_utils import run_bass_kernel_spmd

F32 = mybir.dt.float32
BF16 = mybir.dt.bfloat16
ALU = mybir.AluOpType
AF = mybir.ActivationFunctionType

D = 1024
SEQ = 4096
NB = 512
NCH = NB // 128
DFF = 2816
EPS = 1e-6
SLOT = 4096
NSLOT = 4
import os
SKIP = set(os.environ.get('MK_SKIP', '').split(','))
SSD_STOP = int(os.environ.get('MK_SSD_STOP', '99'))


class Sem:
    def __init__(self, h, name):
        self.h = h
        self.name = name
        self.count = 0


class Res:
    __slots__ = ("name", "w", "r", "dsem", "excl")

    def __init__(self, name, excl=False):
        self.name = name
        self.excl = excl
        self.w = {}
        self.r = {}
        self.dsem = None


class Eng:
    def __init__(self, name, eng, sem):
        self.name = name
        self.eng = eng
        self.sem = sem
        self.known = {}


class Ctx:
    def __init__(self, nc, stack):
        self.nc = nc
        self.stack = stack
        self.nsem = 0
        self.PE = self._eng("pe", nc.tensor)
        self.ACT = self._eng("act", nc.scalar)
        self.DVE = self._eng("dve", nc.vector)
        self.POOL = self._eng("pool", nc.gpsimd)
        self.SP = self._eng("sp", nc.sync)
        self.compute = [self.PE, self.ACT, self.DVE, self.POOL]
        self.dsems = []

    def new_sem(self, name):
        self.nsem += 1
        h = self.stack.enter_context(self.nc.semaphore(name))
        return Sem(h, name)

    def _eng(self, name, eng):
        return Eng(name, eng, self.new_sem("s_" + name))

    def _wait(self, E, need):
        for s, (v, snap) in need.items():
            if E.known.get(s, 0) >= v:
                continue
            E.eng.wait_ge(s.h, v)
            E.known[s] = v
            for s2, v2 in snap.items():
                if E.known.get(s2, 0) < v2:
                    E.known[s2] = v2

    def _collect(self, E, reads, writes):
        need = {}

        def req(s, ev):
            if s not in need or need[s][0] < ev[0]:
                need[s] = ev

        for r in reads:
            for s, ev in r.w.items():
                req(s, ev)
            if r.excl:
                for s, ev in r.r.items():
                    if s is not E.sem:
                        req(s, ev)
        for w in writes:
            for s, ev in w.w.items():
                if s is E.sem:
                    continue
                req(s, ev)
            for s, ev in w.r.items():
                if s is E.sem:
                    continue
                req(s, ev)
        return need

    def op(self, E, fn, r=(), w=()):
        self._wait(E, self._collect(E, r, w))
        ins = fn(E.eng)
        E.sem.count += 1
        ins.then_inc(E.sem.h, 1)
        ev = (E.sem.count, dict(E.known))
        for x in r:
            x.r[E.sem] = ev
        for x in w:
            x.w[E.sem] = ev
            x.r = {}
        return ins

    def dma(self, E, out, in_, r=(), w=(), sem_res=None, **kw):
        self._wait(E, self._collect(E, r, w))
        if sem_res.dsem is None:
            sem_res.dsem = self.new_sem("d_" + sem_res.name)
            self.dsems.append(sem_res.dsem)
        ds = sem_res.dsem
        ins = E.eng.dma_start(out=out, in_=in_, **kw)
        ds.count += 16
        ins.then_inc(ds.h, 16)
        ev = (ds.count, dict(E.known))
        for x in r:
            x.r[ds] = ev
        for x in w:
            x.w[ds] = ev
            x.r = {}
        return ins

    def barrier(self, engines=None, with_dma=False):
        engines = engines or self.compute
        for E in engines:
            need = {}
            for E2 in engines:
                if E2 is not E and E2.sem.count > 0:
                    need[E2.sem] = (E2.sem.count, dict(E2.known))
            if with_dma:
                for ds in self.dsems:
                    if ds.count > 0:
                        need[ds] = (ds.count, {})
            self._wait(E, need)


def make_consts():
    c = {}
    c["ident_f"] = np.eye(128, dtype=np.float32)
    c["ones_f"] = np.ones((128, 128), dtype=np.float32)
    inv = (10000.0 ** (-np.arange(0, 128, 2, dtype=np.float32) / np.float32(128))).astype(np.float32)
    pos = np.arange(SEQ, dtype=np.float32)
    ang = (pos[:, None] * inv[None, :]).astype(np.float32).astype(np.float64)
    c["cosT"] = np.ascontiguousarray(np.cos(ang).reshape(SEQ // 128, 128, 64).transpose(1, 0, 2)).astype(np.float32)
    c["sinT"] = np.ascontiguousarray(np.sin(ang).reshape(SEQ // 128, 128, 64).transpose(1, 0, 2)).astype(np.float32)
    gam = 1.0 - 2.0 ** (-5.0 - np.arange(4, dtype=np.float64))
    idx = np.arange(128, dtype=np.float64)
    diff = idx[None, :] - idx[:, None]
    dm = np.where(diff[None] >= 0, gam[:, None, None] ** np.maximum(diff, 0)[None], 0.0) * (128 ** -0.5)
    c["dmaskT"] = np.ascontiguousarray(dm.transpose(1, 0, 2)).reshape(128, 512).astype(np.float32)
    c["xi"] = (gam[None, :] ** (idx[:, None] + 1)).astype(np.float32)
    c["zeta"] = ((gam[None, :] ** (127 - idx[:, None])) * (128 ** -0.5)).astype(np.float32)
    tri = (idx[:, None] <= idx[None, :]).astype(np.float32)
    c["tri"] = tri
    c["su"] = (1.0 - tri).astype(np.float32)
    sj = np.arange(128) // 16
    c["tzmask"] = (sj[None, :] >= sj[:, None]).astype(np.float32)
    return c


RET_GDEC = [float((1.0 - 2.0 ** (-5.0 - h)) ** 128) for h in range(4)]


class Builder:
    def __init__(self, nblk=SEQ // NB, stages=("even", "ffn0", "odd", "ffn1", "final"), dbg=()):
        self.nblk = nblk
        self.stages = stages
        self.dbg = dbg
        self.ntok = nblk * NB
        self.nc = bass.Bass("TRN2", target_bir_lowering=False)
        self.stack = ExitStack()
        self.cx = Ctx(self.nc, self.stack)
        self.units = []
        self.unit_off = 0
        self.stream_order = []
        self.bank_i = 0
        self.bankb_i = 0

    def din(self, name, shape, dt=F32):
        return self.nc.dram_tensor(name, list(shape), dt, kind="ExternalInput").ap()

    def declare_io(self):
        nc = self.nc
        self.x = self.din("x", [self.ntok, D])
        self.y = nc.dram_tensor("y", [self.ntok, D], F32, kind="ExternalOutput").ap()
        self.mix_norm = self.din("mix_norm", [2, D])
        self.ffn_norm = self.din("ffn_norm", [2, D])
        self.final_norm = self.din("final_norm", [D])
        self.ffn_w_up = self.din("ffn_w_up", [2, D, 2 * DFF])
        self.ffn_dw_w = self.din("ffn_dw_w", [2, 3, 2 * DFF])
        self.ffn_dw_b = self.din("ffn_dw_b", [2, 2 * DFF])
        self.ffn_w_down = self.din("ffn_w_down", [2, DFF, D])
        self.e_w_in = self.din("e_w_in", [1, D, 5648])
        self.e_conv_w = self.din("e_conv_w", [1, 4, 1536])
        self.e_conv_b = self.din("e_conv_b", [1, 1536])
        self.e_dt_bias = self.din("e_dt_bias", [1, 16])
        self.e_a_log = self.din("e_a_log", [1, 16])
        self.e_d = self.din("e_d", [1, 16])
        self.e_ssm_norm = self.din("e_ssm_norm", [1, 1024])
        self.e_w_out = self.din("e_w_out", [1, 2048, 1024])
        self.o_w_in = self.din("o_w_in", [1, D, 1536])
        self.o_dw_w = self.din("o_dw_w", [1, 31, 512])
        self.o_dw_b = self.din("o_dw_b", [1, 512])
        self.o_ln_g = self.din("o_ln_g", [1, 512])
        self.o_ln_b = self.din("o_ln_b", [1, 512])
        self.o_a_re = self.din("o_a_re", [1, 32, 64])
        self.o_a_im = self.din("o_a_im", [1, 32, 64])
        self.o_b_re = self.din("o_b_re", [1, 32, 64, 16])
        self.o_b_im = self.din("o_b_im", [1, 32, 64, 16])
        self.o_c_re = self.din("o_c_re", [1, 32, 16, 64])
        self.o_c_im = self.din("o_c_im", [1, 32, 16, 64])
        self.o_d = self.din("o_d", [1, 512])
        self.o_log_step = self.din("o_log_step", [1, 32])
        self.o_glu_w = self.din("o_glu_w", [1, 512, 512])
        self.o_w_out = self.din("o_w_out", [1, D, D])
        self.consts = {}
        for k, v in make_consts().items():
            self.consts[k] = self.din("c_" + k, v.shape)

    def sb(self, name, shape, dt=F32, stack=None):
        self.name_i = getattr(self, "name_i", 0) + 1
        t = (stack or self.stack).enter_context(self.nc.sbuf_tensor(f"{name}_{self.name_i}", list(shape), dt))
        return t, Res(name)

    def ps(self, name, shape, dt=F32):
        t = self.stack.enter_context(self.nc.psum_tensor(name, list(shape), dt))
        return t, Res(name, excl=True)

    def bank(self):
        i = self.bank_i
        self.bank_i = (i + 1) % len(self.banks)
        return self.banks[i]

    def bankb(self):
        i = self.bankb_i
        self.bankb_i = (i + 1) % len(self.banksb)
        return self.banksb[i]

    def add_unit(self, name, pieces):
        L = sum(kt * w for (_, kt, _, w) in pieces)
        assert L <= SLOT, (name, L)
        u = dict(name=name, L=L, off=self.unit_off, pieces=pieces)
        self.unit_off += 128 * L
        self.units.append(u)
        return u

    def declare_units(self):
        self.U = {}
        for l in range(2):
            wup = self.ffn_w_up[l]
            for j in range(22):
                self.U[("up", l, j)] = self.add_unit(f"up{l}_{j}", [(wup, 8, j * 128, 128), (wup, 8, (22 + j) * 128, 128)])
            wdn = self.ffn_w_down[l]
            for ft in range(8):
                self.U[("dn", l, ft)] = self.add_unit(f"dn{l}_{ft}", [(wdn, 22, ft * 128, 128)])

        w = self.e_w_in[0]
        for i, nm in enumerate(["eq", "ek", "ev0", "ev1", "eg0", "eg1", "ez0", "ez1"]):
            self.U[nm] = self.add_unit(nm, [(w, 8, 512 * i, 512)])
        self.U["edt"] = self.add_unit("edt", [(w, 8, 5632, 16)])
        for i in range(6):
            self.U[("exbc", i)] = self.add_unit(f"exbc{i}", [(w, 8, 4096 + (2 * i) * 128, 128), (w, 8, 4096 + (2 * i + 1) * 128, 128)])
        for ft in range(8):
            self.U[("eout", ft)] = self.add_unit(f"eout{ft}", [(self.e_w_out[0], 16, ft * 128, 128)])

        w = self.o_w_in[0]
        for i in range(4):
            self.U[("oin", i)] = self.add_unit(f"oin{i}", [(w, 8, i * 128, 128), (w, 8, 512 + i * 128, 128)])
        self.U["ou"] = self.add_unit("ou", [(w, 8, 1024, 512)])
        self.U["oglu"] = self.add_unit("oglu", [(self.o_glu_w[0], 4, ot * 128, 128) for ot in range(4)])
        for i in range(4):
            self.U[("oout", i)] = self.add_unit(f"oout{i}", [(self.o_w_out[0], 8, (2 * i) * 128, 128), (self.o_w_out[0], 8, (2 * i + 1) * 128, 128)])
        self.n_cast_units = len(self.units)
        for nm in [("s5bc", 0), ("s5bc", 1), ("s5cc", 0), ("s5cc", 1), "s5tz"]:
            self.U[nm] = dict(name=str(nm), L=4096, off=self.unit_off, pieces=[])
            self.unit_off += 128 * 4096

    def plan_stream(self):
        order = []
        for b in range(self.nblk):
            if "even" in self.stages:
                order += [self.U[nm] for nm in ["eq", "ek", "ev0", "ev1", "eg0", "eg1", "ez0", "ez1", "edt"]]
                order += [self.U[("exbc", i)] for i in range(6)]
                order += [self.U[("eout", ft)] for ft in range(8)]
            for l in range(2):
                if l == 1 and "odd" in self.stages:
                    order += [self.U["ou"], self.U[("s5bc", 0)], self.U[("s5bc", 1)]]
                    order += [self.U[("oin", i)] for i in range(4)]
                    order += [self.U["s5tz"], self.U[("s5cc", 0)], self.U[("s5cc", 1)], self.U["oglu"]]
                    order += [self.U[("oout", i)] for i in range(4)]
                if f"ffn{l}" in self.stages:
                    order += [self.U[("up", l, j)] for j in range(22)]
                    order += [self.U[("dn", l, ft)] for ft in range(8)]
        self.stream_order = order
        self.stream_next_load = 0
        self.stream_next_use = 0

    def stream_get(self, hold_prev=0):
        cx = self.cx
        i = self.stream_next_use
        self.stream_next_use += 1
        lim = min(i - hold_prev + NSLOT - 1, len(self.stream_order) - 1)
        while self.stream_next_load <= lim:
            k = self.stream_next_load
            u = self.stream_order[k]
            st, sr = self.slots[k % NSLOT]
            src = self.wscr[u["off"]: u["off"] + 128 * u["L"]].rearrange("(p l) -> p l", p=128)
            cx.dma(cx.SP, st[:, 0:u["L"]], src, w=[sr], sem_res=sr)
            self.stream_next_load += 1
        st, sr = self.slots[i % NSLOT]
        assert self.stream_order[i] is not None
        return st, sr, self.stream_order[i]

    def prologue(self):
        cx, nc = self.cx, self.nc
        with ExitStack() as st:
            NS = 3
            stg32 = [self.sb(f"stg32_{i}", [128, SLOT], F32, st) for i in range(NS)]
            stg16 = [self.sb(f"stg16_{i}", [128, SLOT], BF16, st) for i in range(NS)]
            cast_engs = [cx.ACT, cx.DVE, cx.POOL]
            for i, u in enumerate(self.units[:self.n_cast_units]):
                t32, r32 = stg32[i % NS]
                t16, r16 = stg16[i % NS]
                off = 0
                for (src, kt, c0, w) in u["pieces"]:
                    dst = t32[:, off: off + kt * w].rearrange("p (k w) -> p k w", k=kt)
                    s = src.rearrange("(k p) n -> p k n", p=128)[:, :, c0:c0 + w]
                    cx.dma(cx.SP, dst, s, w=[r32], sem_res=r32)
                    off += kt * w
                L = u["L"]
                E = cast_engs[i % 3]
                if E is cx.ACT:
                    cx.op(E, lambda e: e.copy(out=t16[:, 0:L], in_=t32[:, 0:L]), r=[r32], w=[r16])
                else:
                    cx.op(E, lambda e: e.tensor_copy(out=t16[:, 0:L], in_=t32[:, 0:L]), r=[r32], w=[r16])
                dst = self.wscr[u["off"]: u["off"] + 128 * L].rearrange("(p l) -> p l", p=128)
                cx.dma(cx.ACT if False else cx.SP, dst, t16[:, 0:L], r=[r16], w=[self.wscr_res], sem_res=r16)
            cx.barrier(engines=cx.compute + [cx.SP], with_dma=True)

    def setup(self):
        cx, nc = self.cx, self.nc
        self.xT, self.xT_r = self.sb("xT", [128, 8, NB], F32)
        self.hnT, self.hnT_r = self.sb("hnT", [128, 8, NB], BF16)
        self.ident_f, self.ident_f_r = self.sb("ident_f", [128, 128], F32)
        self.ones_f, self.ones_f_r = self.sb("ones_f", [128, 128], F32)
        self.gains, self.gains_r = self.sb("gains", [128, 5, 8], F32)
        self.slots = [self.sb(f"wslot{i}", [128, SLOT], BF16) for i in range(NSLOT)]
        self.banks = [self.ps(f"bank{i}", [128, 512], F32) for i in range(6)]
        self.banksb = [self.ps(f"bankb{i}", [128, 1024], BF16) for i in range(2)]
        self.ffn_halo, self.ffn_halo_r = self.sb("ffn_halo", [128, 2, 44, 2], F32)
        self.ffn_cw, self.ffn_cw_r = self.sb("ffn_cw", [128, 2, 44, 3], F32)
        self.ffn_cb, self.ffn_cb_r = self.sb("ffn_cb", [128, 2, 44], F32)
        self.eps_t, self.eps_r = self.sb("eps_t", [128, 1], F32)
        nc.allow_non_contiguous_dma(reason="small parameter loads")
        SP = cx.SP
        cx.dma(SP, self.ident_f[:], self.consts["ident_f"], w=[self.ident_f_r], sem_res=self.ident_f_r)
        cx.dma(SP, self.ones_f[:], self.consts["ones_f"], w=[self.ones_f_r], sem_res=self.ones_f_r)
        gsrc = [self.mix_norm[0], self.ffn_norm[0], self.mix_norm[1], self.ffn_norm[1], self.final_norm]
        for i, g in enumerate(gsrc):
            cx.dma(SP, self.gains[:, i, :], g.rearrange("(t p) -> p t", p=128), w=[self.gains_r], sem_res=self.gains_r, allow_slow_non_contiguous=True)
        for l in range(2):
            for k in range(3):
                cx.dma(SP, self.ffn_cw[:, l, :, k], self.ffn_dw_w[l, k].rearrange("(t p) -> p t", p=128),
                       w=[self.ffn_cw_r], sem_res=self.ffn_cw_r, allow_slow_non_contiguous=True)
            cx.dma(SP, self.ffn_cb[:, l, :], self.ffn_dw_b[l].rearrange("(t p) -> p t", p=128),
                   w=[self.ffn_cb_r], sem_res=self.ffn_cb_r, allow_slow_non_contiguous=True)
        cx.op(cx.DVE, lambda e: e.memset(self.ffn_halo[:], 0.0), w=[self.ffn_halo_r])
        cx.op(cx.DVE, lambda e: e.memset(self.eps_t[:], EPS), w=[self.eps_r])
        self.one_t, _ = self.sb("one_t", [128, 1], F32)
        cx.op(cx.DVE, lambda e: e.memset(self.one_t[:], 1.0), w=[self.eps_r])

    def load_block(self, b):
        cx = self.cx
        with ExitStack() as st:
            xin, xin_r = self.sb("xin", [128, NCH, D], F32, st)
            src = self.x[b * NB:(b + 1) * NB, :].rearrange("(c p) d -> p c d", p=128)
            cx.dma(cx.SP, xin[:], src, w=[xin_r], sem_res=self.xT_r)
            for ft in range(8):
                bk, bk_r = self.bank()
                for c in range(NCH):
                    cx.op(cx.PE, lambda e: e.transpose(bk[:, c * 128:(c + 1) * 128], xin[:, c, ft * 128:(ft + 1) * 128], self.ident_f[:]),
                          r=[xin_r, self.ident_f_r], w=[bk_r])
                if ft % 2 == 0:
                    cx.op(cx.ACT, lambda e: e.copy(out=self.xT[:, ft, :], in_=bk[:]), r=[bk_r], w=[self.xT_r])
                else:
                    cx.op(cx.DVE, lambda e: e.tensor_copy(out=self.xT[:, ft, :], in_=bk[:]), r=[bk_r], w=[self.xT_r])
            cx.barrier()

    def store_block(self, b, srcT, srcT_r):
        cx = self.cx
        with ExitStack() as st:
            yo, yo_r = self.sb("yo", [128, NCH, D], F32, st)
            k = 0
            for c in range(NCH):
                for half in range(2):
                    bk, bk_r = self.bank()
                    for f4 in range(4):
                        ft = half * 4 + f4
                        cx.op(cx.PE, lambda e: e.transpose(bk[:, f4 * 128:(f4 + 1) * 128], srcT[:, ft, c * 128:(c + 1) * 128], self.ident_f[:]),
                              r=[srcT_r, self.ident_f_r], w=[bk_r])
                    if k % 2 == 0:
                        cx.op(cx.ACT, lambda e: e.copy(out=yo[:, c, half * 512:(half + 1) * 512], in_=bk[:]), r=[bk_r], w=[yo_r])
                    else:
                        cx.op(cx.DVE, lambda e: e.tensor_copy(out=yo[:, c, half * 512:(half + 1) * 512], in_=bk[:]), r=[bk_r], w=[yo_r])
                    k += 1
            dst = self.y[b * NB:(b + 1) * NB, :].rearrange("(c p) d -> p c d", p=128)
            cx.dma(cx.SP, dst, yo[:], r=[yo_r], w=[self.y_res], sem_res=self.y_res)
            cx.barrier(engines=cx.compute + [cx.SP], with_dma=True)

    def rmsnorm(self, gi, outT, outT_r, out_f32=False):
        cx = self.cx
        with ExitStack() as st:
            sq = [self.sb(f"sq{i}", [128, NB], F32, st) for i in range(2)]
            rstd, rstd_r = self.sb("rstd", [128, NB], F32, st)
            bk, bk_r = self.bank()
            for ft in range(8):
                s, s_r = sq[ft % 2]
                cx.op(cx.ACT, lambda e: e.activation(out=s[:], in_=self.xT[:, ft, :], func=AF.Square), r=[self.xT_r], w=[s_r])
                cx.op(cx.PE, lambda e: e.matmul(bk[:], lhsT=self.ones_f[:], rhs=s[:], start=(ft == 0), stop=(ft == 7)),
                      r=[s_r, self.ones_f_r], w=[bk_r])
            cx.op(cx.ACT, lambda e: e.activation(out=rstd[:], in_=bk[:], func=AF.Sqrt, bias=self.eps_t[:, 0:1], scale=1.0 / D),
                  r=[bk_r, self.eps_r], w=[rstd_r])
            cx.op(cx.DVE, lambda e: e.reciprocal(out=rstd[:], in_=rstd[:]), r=[rstd_r], w=[rstd_r])
            for ft in range(8):
                E = cx.DVE
                cx.op(E, lambda e: e.scalar_tensor_tensor(out=outT[:, ft, :], in0=self.xT[:, ft, :], scalar=self.gains[:, gi, ft:ft + 1],
                                                          in1=rstd[:], op0=ALU.mult, op1=ALU.mult),
                      r=[self.xT_r, self.gains_r, rstd_r], w=[outT_r])
            cx.barrier()

    def ffn(self, l, b):
        cx = self.cx
        self.rmsnorm(1 + 2 * l, self.hnT, self.hnT_r)
        with ExitStack() as st:
            hT, hT_r = self.sb("hT", [128, 22, NB], BF16, st)
            raws = [self.sb(f"raw{i}", [128, NB + 2], F32, st) for i in range(4)]
            accs = [self.sb(f"acc{i}", [128, NB], F32, st) for i in range(4)]
            sg = [self.sb(f"sgate{i}", [128, NB], F32, st) for i in range(2)]
            for j in range(22):
                slot, slot_r, u = self.stream_get()
                assert u is self.U[("up", l, j)]
                res = []
                for gi in range(2):
                    tile = j + 22 * gi
                    wv = slot[:, gi * 1024:(gi + 1) * 1024].rearrange("p (k w) -> p k w", k=8)
                    bk, bk_r = self.bank()
                    for kt in range(8):
                        cx.op(cx.PE, lambda e: e.matmul(bk[:], lhsT=wv[:, kt, :], rhs=self.hnT[:, kt, :], start=(kt == 0), stop=(kt == 7)),
                              r=[slot_r, self.hnT_r], w=[bk_r])
                    raw, raw_r = raws[(2 * j + gi) % 4]
                    acc, acc_r = accs[(2 * j + gi) % 4]
                    cw = self.ffn_cw[:, l, tile, :]
                    cx.op(cx.ACT, lambda e: e.copy(out=raw[:, 2:NB + 2], in_=bk[:]), r=[bk_r], w=[raw_r])
                    cx.op(cx.POOL, lambda e: e.tensor_copy(out=raw[:, 0:2], in_=self.ffn_halo[:, l, tile, :]), r=[self.ffn_halo_r], w=[raw_r])
                    cx.op(cx.ACT, lambda e: e.activation(out=acc[:], in_=bk[:], func=AF.Identity, bias=self.ffn_cb[:, l, tile:tile + 1],
                                                         scale=cw[:, 2:3]),
                          r=[bk_r, self.ffn_cw_r, self.ffn_cb_r], w=[acc_r])
                    E = cx.DVE
                    cx.op(E, lambda e: e.scalar_tensor_tensor(out=acc[:], in0=raw[:, 1:NB + 1], scalar=cw[:, 1:2], in1=acc[:],
                                                              op0=ALU.mult, op1=ALU.add),
                          r=[raw_r, acc_r, self.ffn_cw_r], w=[acc_r])
                    cx.op(E, lambda e: e.scalar_tensor_tensor(out=acc[:], in0=raw[:, 0:NB], scalar=cw[:, 0:1], in1=acc[:],
                                                              op0=ALU.mult, op1=ALU.add),
                          r=[raw_r, acc_r, self.ffn_cw_r], w=[acc_r])
                    cx.op(cx.POOL, lambda e: e.tensor_copy(out=self.ffn_halo[:, l, tile, :], in_=raw[:, NB:NB + 2]), r=[raw_r], w=[self.ffn_halo_r])
                    res.append((acc, acc_r))
                s, s_r = sg[j % 2]
                cx.op(cx.ACT, lambda e: e.activation(out=s[:], in_=res[0][0][:], func=AF.Silu), r=[res[0][1]], w=[s_r])
                cx.op(cx.DVE, lambda e: e.tensor_tensor(out=hT[:, j, :], in0=s[:], in1=res[1][0][:], op=ALU.mult),
                      r=[s_r, res[1][1]], w=[hT_r])
            for ft in range(8):
                slot, slot_r, u = self.stream_get()
                assert u is self.U[("dn", l, ft)]
                wv = slot[:, 0:22 * 128].rearrange("p (k w) -> p k w", k=22)
                bk, bk_r = self.bank()
                for kt in range(22):
                    cx.op(cx.PE, lambda e: e.matmul(bk[:], lhsT=wv[:, kt, :], rhs=hT[:, kt, :], start=(kt == 0), stop=(kt == 21)),
                          r=[slot_r, hT_r], w=[bk_r])
                cx.op(cx.DVE, lambda e: e.tensor_tensor(out=self.xT[:, ft, :], in0=self.xT[:, ft, :], in1=bk[:], op=ALU.add),
                      r=[bk_r, self.xT_r], w=[self.xT_r])
            cx.barrier()

    def bc_mid(self, ap2, n):
        a = ap2.shape[1]
        return ap2.unsqueeze(2).to_broadcast([128, a, n])

    def bc_h(self, ap2, h):
        n = ap2.shape[1]
        return ap2.unsqueeze(1).to_broadcast([128, h, n])

    def setup_even(self):
        cx = self.cx
        SP = cx.SP
        self.retS, self.retS_r = self.sb("retS", [128, 4, 256], F32)
        self.retSb, self.retSb_r = self.sb("retSb", [128, 4, 256], BF16)
        self.ssS, self.ssS_r = self.sb("ssS", [128, 2, 512], F32)
        self.ssSb, self.ssSb_r = self.sb("ssSb", [128, 2, 512], BF16)
        self.exh, self.exh_r = self.sb("exh", [128, 12, 3], F32)
        self.exw, self.exw_r = self.sb("exw", [128, 12, 4], F32)
        self.exb, self.exb_r = self.sb("exb", [128, 12], F32)
        self.dmaskT, self.dmaskT_r = self.sb("dmaskT", [128, 512], F32)
        self.xi, self.xi_r = self.sb("xi", [128, 4], F32)
        self.zeta, self.zeta_r = self.sb("zeta", [128, 4], F32)
        self.tri, self.tri_r = self.sb("tri", [128, 128], F32)
        self.su, self.su_r = self.sb("su", [128, 128], F32)
        self.ident_b, self.ident_b_r = self.sb("ident_b", [128, 128], BF16)
        self.dtb, self.dtb_r = self.sb("dtb", [128, 16], F32)
        self.aneg, self.aneg_r = self.sb("aneg", [128, 16], F32)
        self.dsk, self.dsk_r = self.sb("dsk", [128, 16], F32)
        self.ssmn, self.ssmn_r = self.sb("ssmn", [128, 8], F32)
        for t, r, nm in [(self.dmaskT, self.dmaskT_r, "dmaskT"), (self.xi, self.xi_r, "xi"), (self.zeta, self.zeta_r, "zeta"),
                         (self.tri, self.tri_r, "tri"), (self.su, self.su_r, "su")]:
            cx.dma(SP, t[:], self.consts[nm], w=[r], sem_res=r)
        cx.dma(SP, self.dtb[:], self.e_dt_bias[0].partition_broadcast(128), w=[self.dtb_r], sem_res=self.dtb_r, allow_slow_non_contiguous=True)
        cx.dma(SP, self.aneg[:], self.e_a_log[0].partition_broadcast(128), w=[self.aneg_r], sem_res=self.aneg_r, allow_slow_non_contiguous=True)
        cx.dma(SP, self.dsk[:], self.e_d[0].partition_broadcast(128), w=[self.dsk_r], sem_res=self.dsk_r, allow_slow_non_contiguous=True)
        cx.dma(SP, self.ssmn[:], self.e_ssm_norm[0].rearrange("(t p) -> p t", p=128), w=[self.ssmn_r], sem_res=self.ssmn_r,
               allow_slow_non_contiguous=True)
        for k in range(4):
            cx.dma(SP, self.exw[:, :, k], self.e_conv_w[0, k].rearrange("(t p) -> p t", p=128), w=[self.exw_r], sem_res=self.exw_r,
                   allow_slow_non_contiguous=True)
        cx.dma(SP, self.exb[:], self.e_conv_b[0].rearrange("(t p) -> p t", p=128), w=[self.exb_r], sem_res=self.exb_r,
               allow_slow_non_contiguous=True)
        cx.op(cx.ACT, lambda e: e.activation(out=self.aneg[:], in_=self.aneg[:], func=AF.Exp), r=[self.aneg_r], w=[self.aneg_r])
        cx.op(cx.DVE, lambda e: e.tensor_scalar(out=self.aneg[:], in0=self.aneg[:], scalar1=-1.0, scalar2=None, op0=ALU.mult),
              r=[self.aneg_r], w=[self.aneg_r])
        cx.op(cx.DVE, lambda e: e.tensor_copy(out=self.ident_b[:], in_=self.ident_f[:]), r=[self.ident_f_r], w=[self.ident_b_r])
        cx.op(cx.DVE, lambda e: e.memset(self.retS[:], 0.0), w=[self.retS_r])
        cx.op(cx.DVE, lambda e: e.memset(self.retSb[:], 0.0), w=[self.retSb_r])
        cx.op(cx.POOL, lambda e: e.memset(self.ssS[:], 0.0), w=[self.ssS_r])
        cx.op(cx.POOL, lambda e: e.memset(self.ssSb[:], 0.0), w=[self.ssSb_r])
        cx.op(cx.POOL, lambda e: e.memset(self.exh[:], 0.0), w=[self.exh_r])

    def dump(self, name, src_ap, src_r, dst_ap):
        cx = self.cx
        r = self.dbg_res.setdefault(name, Res("dbg_" + name))
        cx.dma(cx.SP, dst_ap, src_ap, r=[src_r], w=[r], sem_res=r)

    def even(self, b):
        cx = self.cx
        PE, ACT, DVE, POOL, SP = cx.PE, cx.ACT, cx.DVE, cx.POOL, cx.SP
        self.rmsnorm(0, self.hnT, self.hnT_r)
        hnT, hnT_r = self.hnT, self.hnT_r
        with ExitStack() as st:
            B = {}
            for nm, shp, dt in [("qr", [128, NCH, 512], BF16), ("qx", [128, NCH, 512], BF16), ("kr", [128, NCH, 512], BF16),
                                ("kz", [128, NCH, 512], BF16), ("v", [128, NCH, 1024], BF16), ("sg", [128, NCH, 1024], BF16),
                                ("sz", [128, NCH, 1024], BF16), ("xcT", [128, 8, NB], BF16), ("bcT", [128, 4, NB], BF16),
                                ("dt", [128, NCH, 16], F32), ("dta", [128, NCH, 16], F32), ("mixT", [128, 16, NB], BF16),
                                ("cos", [128, NCH, 64], F32), ("sin", [128, NCH, 64], F32)]:
                B[nm] = self.sb(nm, shp, dt, st)
            self.EB = B
            cos, cos_r = B["cos"]
            sin, sin_r = B["sin"]
            cx.dma(SP, cos[:], self.consts["cosT"][:, b * NCH:(b + 1) * NCH, :], w=[cos_r], sem_res=self.xi_r)
            cx.dma(SP, sin[:], self.consts["sinT"][:, b * NCH:(b + 1) * NCH, :], w=[sin_r], sem_res=self.xi_r)
            rot_t = [self.sb(f"rot{i}", [128, 4, 64], F32, st) for i in range(4)]
            qraw = [self.sb(f"qraw{i}", [128, 512], F32, st) for i in range(2)]
            k_ev = 0
            for nm in ["eq", "ek", "ev0", "ev1", "eg0", "eg1", "ez0", "ez1"]:
                slot, slot_r, u = self.stream_get()
                assert u is self.U[nm]
                wv = slot[:, 0:4096].rearrange("p (k w) -> p k w", k=8)
                for c in range(NCH):
                    bk, bk_r = self.bank()
                    for kt in range(8):
                        cx.op(PE, lambda e: e.matmul(bk[:], lhsT=hnT[:, kt, c * 128:(c + 1) * 128], rhs=wv[:, kt, :], start=(kt == 0), stop=(kt == 7)),
                              r=[slot_r, hnT_r], w=[bk_r])
                    if nm in ("eq", "ek"):
                        raw, raw_r = qraw[k_ev % 2]
                        k_ev += 1
                        cx.op(ACT, lambda e: e.copy(out=raw[:], in_=bk[:]), r=[bk_r], w=[raw_r])
                        r4 = raw[:].rearrange("p (h t d) -> p h t d", h=4, t=2)
                        x1, x2 = r4[:, :, 0, :], r4[:, :, 1, :]
                        cb = self.bc_h(cos[:, c, :], 4)
                        sbb = self.bc_h(sin[:, c, :], 4)
                        (t1, t1r), (t2, t2r), (t3, t3r), (t4, t4r) = rot_t
                        cx.op(POOL, lambda e: e.tensor_tensor(out=t1[:], in0=x1, in1=cb, op=ALU.mult), r=[raw_r, cos_r], w=[t1r])
                        cx.op(POOL, lambda e: e.tensor_tensor(out=t2[:], in0=x2, in1=sbb, op=ALU.mult), r=[raw_r, sin_r], w=[t2r])
                        cx.op(POOL, lambda e: e.tensor_tensor(out=t3[:], in0=x1, in1=sbb, op=ALU.mult), r=[raw_r, sin_r], w=[t3r])
                        cx.op(POOL, lambda e: e.tensor_tensor(out=t4[:], in0=x2, in1=cb, op=ALU.mult), r=[raw_r, cos_r], w=[t4r])
                        dst, dst_r = B["qr"] if nm == "eq" else B["kr"]
                        d4 = dst[:, c, :].rearrange("p (h t d) -> p h t d", h=4, t=2)
                        cx.op(DVE, lambda e: e.tensor_tensor(out=d4[:, :, 0, :], in0=t1[:], in1=t2[:], op=ALU.subtract), r=[t1r, t2r], w=[dst_r])
                        cx.op(DVE, lambda e: e.tensor_tensor(out=d4[:, :, 1, :], in0=t3[:], in1=t4[:], op=ALU.add), r=[t3r, t4r], w=[dst_r])
                        d2, d2_r = B["qx"] if nm == "eq" else B["kz"]
                        tab, tab_r = (self.xi, self.xi_r) if nm == "eq" else (self.zeta, self.zeta_r)
                        cx.op(DVE, lambda e: e.tensor_tensor(out=d2[:, c, :].rearrange("p (h d) -> p h d", h=4),
                                                             in0=dst[:, c, :].rearrange("p (h d) -> p h d", h=4),
                                                             in1=self.bc_mid(tab[:], 128), op=ALU.mult),
                              r=[dst_r, tab_r], w=[d2_r])
                    else:
                        half = int(nm[-1])
                        key = {"v": "v", "g": "sg", "z": "sz"}[nm[1]]
                        dst, dst_r = B[key]
                        if key == "v":
                            cx.op(ACT, lambda e: e.copy(out=dst[:, c, half * 512:(half + 1) * 512], in_=bk[:]), r=[bk_r], w=[dst_r])
                        else:
                            cx.op(ACT, lambda e: e.activation(out=dst[:, c, half * 512:(half + 1) * 512], in_=bk[:], func=AF.Silu), r=[bk_r], w=[dst_r])
            slot, slot_r, u = self.stream_get()
            assert u is self.U["edt"]
            wv = slot[:, 0:128].rearrange("p (k w) -> p k w", k=8)
            dtv, dtv_r = self.sb("dtv", [128, 16], F32, st)
            dt, dt_r = B["dt"]
            dta, dta_r = B["dta"]
            for c in range(NCH):
                bk, bk_r = self.bank()
                for kt in range(8):
                    cx.op(PE, lambda e: e.matmul(bk[:, 0:16], lhsT=hnT[:, kt, c * 128:(c + 1) * 128], rhs=wv[:, kt, :], start=(kt == 0), stop=(kt == 7)),
                          r=[slot_r, hnT_r], w=[bk_r])
                cx.op(DVE, lambda e: e.tensor_tensor(out=dtv[:], in0=bk[:, 0:16], in1=self.dtb[:], op=ALU.add), r=[bk_r, self.dtb_r], w=[dtv_r])
                cx.op(ACT, lambda e: e.activation(out=dtv[:], in_=dtv[:], func=AF.Exp), r=[dtv_r], w=[dtv_r])
                cx.op(ACT, lambda e: e.activation(out=dt[:, c, :], in_=dtv[:], func=AF.Ln, bias=self.one_t[:, 0:1]), r=[dtv_r, self.eps_r], w=[dt_r])
                cx.op(DVE, lambda e: e.tensor_tensor(out=dta[:, c, :], in0=dt[:, c, :], in1=self.aneg[:], op=ALU.mult), r=[dt_r, self.aneg_r], w=[dta_r])
            raws = [self.sb(f"xraw{i}", [128, NB + 3], F32, st) for i in range(2)]
            accs = [self.sb(f"xacc{i}", [128, NB], F32, st) for i in range(2)]
            xcT, xcT_r = B["xcT"]
            bcT, bcT_r = B["bcT"]
            for i in range(6):
                slot, slot_r, u = self.stream_get()
                assert u is self.U[("exbc", i)]
                for gi in range(2):
                    tile = 2 * i + gi
                    wv = slot[:, gi * 1024:(gi + 1) * 1024].rearrange("p (k w) -> p k w", k=8)
                    bk, bk_r = self.bank()
                    for kt in range(8):
                        cx.op(PE, lambda e: e.matmul(bk[:], lhsT=wv[:, kt, :], rhs=hnT[:, kt, :], start=(kt == 0), stop=(kt == 7)),
                              r=[slot_r, hnT_r], w=[bk_r])
                    raw, raw_r = raws[tile % 2]
                    acc, acc_r = accs[tile % 2]
                    cw = self.exw[:, tile, :]
                    cx.op(ACT, lambda e: e.copy(out=raw[:, 3:NB + 3], in_=bk[:]), r=[bk_r], w=[raw_r])
                    cx.op(POOL, lambda e: e.tensor_copy(out=raw[:, 0:3], in_=self.exh[:, tile, :]), r=[self.exh_r], w=[raw_r])
                    cx.op(ACT, lambda e: e.activation(out=acc[:], in_=bk[:], func=AF.Identity, bias=self.exb[:, tile:tile + 1], scale=cw[:, 3:4]),
                          r=[bk_r, self.exw_r, self.exb_r], w=[acc_r])
                    for k in (2, 1, 0):
                        cx.op(DVE, lambda e: e.scalar_tensor_tensor(out=acc[:], in0=raw[:, k:k + NB], scalar=cw[:, k:k + 1], in1=acc[:],
                                                                    op0=ALU.mult, op1=ALU.add),
                              r=[raw_r, acc_r, self.exw_r], w=[acc_r])
                    cx.op(POOL, lambda e: e.tensor_copy(out=self.exh[:, tile, :], in_=raw[:, NB:NB + 3]), r=[raw_r], w=[self.exh_r])
                    if tile < 8:
                        cx.op(ACT, lambda e: e.activation(out=xcT[:, tile, :], in_=acc[:], func=AF.Silu), r=[acc_r], w=[xcT_r])
                    else:
                        cx.op(ACT, lambda e: e.activation(out=bcT[:, tile - 8, :], in_=acc[:], func=AF.Silu), r=[acc_r], w=[bcT_r])
            with ExitStack() as st2:
                T = {}
                for nm, shp, dtp in [("qT", [128, 512], BF16), ("qxT", [128, 512], BF16), ("kT", [128, 512], BF16), ("smT", [128, 512], BF16),
                                     ("st6", [128, 4, 6], F32), ("mv", [128, 4, 2], F32), ("rstd4", [128, 4], F32),
                                     ("ytmp", [128, 1024], F32), ("yretb", [128, 1024], BF16),
                                     ("rhsM0", [128, 512], F32), ("rhsM1", [128, 512], F32), ("Lg0", [128, 512], F32), ("Lg1", [128, 512], F32),
                                     ("cbm", [128, 2, 128], F32), ("MT", [128, 16, 128], BF16), ("dec", [128, 16], F32),
                                     ("xs", [128, 1024], BF16), ("Xdt", [128, 1024], BF16), ("Xdec", [128, 1024], BF16), ("Btok", [128, 256], BF16),
                                     ("ea", [128, 32], F32), ("t2", [128, 1024], F32), ("yzb", [128, 1024], BF16),
                                     ("st6g", [128, 2, 6], F32), ("mvg", [128, 2, 2], F32), ("rs2", [128, 2], F32)]:
                    T[nm] = self.sb(nm, shp, dtp, st2)
                for c in range(NCH):
                    if "chunk" not in SKIP:
                        self.even_chunk(b, c, B, T)
                cx.barrier()
            mixT, mixT_r = B["mixT"]
            for ft in range(8):
                slot, slot_r, u = self.stream_get()
                assert u is self.U[("eout", ft)]
                wv = slot[:, 0:2048].rearrange("p (k w) -> p k w", k=16)
                bk, bk_r = self.bank()
                for kt in range(16):
                    cx.op(PE, lambda e: e.matmul(bk[:], lhsT=wv[:, kt, :], rhs=mixT[:, kt, :], start=(kt == 0), stop=(kt == 15)),
                          r=[slot_r, mixT_r], w=[bk_r])
                cx.op(DVE, lambda e: e.tensor_tensor(out=self.xT[:, ft, :], in0=self.xT[:, ft, :], in1=bk[:], op=ALU.add),
                      r=[bk_r, self.xT_r], w=[self.xT_r])
            cx.barrier()

    def even_chunk(self, b, c, B, T):
        cx = self.cx
        PE, ACT, DVE, POOL, SP = cx.PE, cx.ACT, cx.DVE, cx.POOL, cx.SP
        cs = slice(c * 128, (c + 1) * 128)
        ib, ib_r = self.ident_b, self.ident_b_r
        tok0 = b * NB + c * 128
        if "ret" not in SKIP:
            self.even_chunk_ret(b, c, B, T)
        if "ssd" not in SKIP:
            self.even_chunk_ssd(b, c, B, T)

    def even_chunk_ret(self, b, c, B, T):
        cx = self.cx
        PE, ACT, DVE, POOL, SP = cx.PE, cx.ACT, cx.DVE, cx.POOL, cx.SP
        cs = slice(c * 128, (c + 1) * 128)
        ib, ib_r = self.ident_b, self.ident_b_r
        tok0 = b * NB + c * 128
        for i, (src, dstn) in enumerate([("qr", "qT"), ("qx", "qxT"), ("kr", "kT")]):
            s_t, s_r = B[src]
            d_t, d_r = T[dstn]
            pb, pb_r = self.bankb()
            for h in range(4):
                cx.op(PE, lambda e: e.transpose(pb[:, h * 128:(h + 1) * 128], s_t[:, c, h * 128:(h + 1) * 128], ib[:]), r=[s_r, ib_r], w=[pb_r])
            if i % 2 == 0:
                cx.op(ACT, lambda e: e.copy(out=d_t[:], in_=pb[:, 0:512]), r=[pb_r], w=[d_r])
            else:
                cx.op(DVE, lambda e: e.tensor_copy(out=d_t[:], in_=pb[:, 0:512]), r=[pb_r], w=[d_r])
        qT, qT_r = T["qT"]
        qxT, qxT_r = T["qxT"]
        kT, kT_r = T["kT"]
        smT, smT_r = T["smT"]
        v, v_r = B["v"]
        bk, bk_r = self.bank()
        for h in range(4):
            hs = slice(h * 128, (h + 1) * 128)
            cx.op(PE, lambda e: e.matmul(bk[:, hs], lhsT=kT[:, hs], rhs=qT[:, hs], start=True, stop=True), r=[kT_r, qT_r], w=[bk_r])
        cx.op(DVE, lambda e: e.tensor_tensor(out=smT[:], in0=bk[:], in1=self.dmaskT[:], op=ALU.mult), r=[bk_r, self.dmaskT_r], w=[smT_r])
        bo = [self.bank(), self.bank()]
        for h in range(4):
            hs = slice(h * 128, (h + 1) * 128)
            o_t, o_r = bo[h // 2]
            osl = slice((h % 2) * 256, (h % 2 + 1) * 256)
            cx.op(PE, lambda e: e.matmul(o_t[:, osl], lhsT=smT[:, hs], rhs=v[:, c, h * 256:(h + 1) * 256], start=True, stop=False),
                  r=[smT_r, v_r], w=[o_r])
            cx.op(PE, lambda e: e.matmul(o_t[:, osl], lhsT=qxT[:, hs], rhs=self.retSb[:, h, :], start=False, stop=True),
                  r=[qxT_r, self.retSb_r], w=[o_r])
        kz, kz_r = B["kz"]
        bkv = [self.bank(), self.bank()]
        for h in range(4):
            k_t, k_r = bkv[h // 2]
            osl = slice((h % 2) * 256, (h % 2 + 1) * 256)
            cx.op(PE, lambda e: e.matmul(k_t[:, osl], lhsT=kz[:, c, h * 128:(h + 1) * 128], rhs=v[:, c, h * 256:(h + 1) * 256], start=True, stop=True),
                  r=[kz_r, v_r], w=[k_r])
        for h in range(4):
            k_t, k_r = bkv[h // 2]
            osl = slice((h % 2) * 256, (h % 2 + 1) * 256)
            cx.op(DVE, lambda e: e.scalar_tensor_tensor(out=self.retS[:, h, :], in0=self.retS[:, h, :], scalar=RET_GDEC[h], in1=k_t[:, osl],
                                                        op0=ALU.mult, op1=ALU.add),
                  r=[k_r, self.retS_r], w=[self.retS_r])
        cx.op(ACT, lambda e: e.copy(out=self.retSb[:], in_=self.retS[:]), r=[self.retS_r], w=[self.retSb_r])
        st6, st6_r = T["st6"]
        mv, mv_r = T["mv"]
        rstd4, rstd4_r = T["rstd4"]
        ytmp, ytmp_r = T["ytmp"]
        yretb, yretb_r = T["yretb"]
        sg, sg_r = B["sg"]
        for h in range(4):
            o_t, o_r = bo[h // 2]
            osl = slice((h % 2) * 256, (h % 2 + 1) * 256)
            cx.op(DVE, lambda e: e.bn_stats(out=st6[:, h, :], in_=o_t[:, osl]), r=[o_r], w=[st6_r])
        for h in range(4):
            cx.op(DVE, lambda e: e.bn_aggr(out=mv[:, h, :], in_=st6[:, h, :]), r=[st6_r], w=[mv_r])
        cx.op(ACT, lambda e: e.activation(out=rstd4[:], in_=mv[:, :, 1], func=AF.Sqrt, bias=self.eps_t[:, 0:1]), r=[mv_r, self.eps_r], w=[rstd4_r])
        cx.op(DVE, lambda e: e.reciprocal(out=rstd4[:], in_=rstd4[:]), r=[rstd4_r], w=[rstd4_r])
        for h in range(4):
            o_t, o_r = bo[h // 2]
            osl = slice((h % 2) * 256, (h % 2 + 1) * 256)
            cx.op(DVE, lambda e: e.tensor_scalar(out=ytmp[:, h * 256:(h + 1) * 256], in0=o_t[:, osl], scalar1=mv[:, h, 0:1], scalar2=rstd4[:, h:h + 1],
                                                 op0=ALU.subtract, op1=ALU.mult),
                  r=[o_r, mv_r, rstd4_r], w=[ytmp_r])
        if "ret_n" in self.dbg:
            self.dump("ret_n", ytmp[:], ytmp_r, self.dbg_out["ret_n"][tok0:tok0 + 128, :])
        cx.op(POOL, lambda e: e.tensor_tensor(out=yretb[:], in0=ytmp[:], in1=sg[:, c, :], op=ALU.mult), r=[ytmp_r, sg_r], w=[yretb_r])
        mixT, mixT_r = B["mixT"]
        pb, pb_r = self.bankb()
        for t in range(8):
            cx.op(PE, lambda e: e.transpose(pb[:, t * 128:(t + 1) * 128], yretb[:, t * 128:(t + 1) * 128], ib[:]), r=[yretb_r, ib_r], w=[pb_r])
        cx.op(ACT, lambda e: e.copy(out=mixT[:, 0:8, cs], in_=pb[:].rearrange("p (t n) -> p t n", t=8)), r=[pb_r], w=[mixT_r])

    def even_chunk_ssd(self, b, c, B, T):
        cx = self.cx
        PE, ACT, DVE, POOL, SP = cx.PE, cx.ACT, cx.DVE, cx.POOL, cx.SP
        cs = slice(c * 128, (c + 1) * 128)
        ib, ib_r = self.ident_b, self.ident_b_r
        tok0 = b * NB + c * 128
        mixT, mixT_r = B["mixT"]
        ytmp, ytmp_r = T["ytmp"]
        dt, dt_r = B["dt"]
        dta, dta_r = B["dta"]
        bcT, bcT_r = B["bcT"]
        xcT, xcT_r = B["xcT"]
        sz, sz_r = B["sz"]
        cbm, cbm_r = T["cbm"]
        MT, MT_r = T["MT"]
        dec, dec_r = T["dec"]
        bk, bk_r = self.bank()
        for g in range(2):
            cx.op(PE, lambda e: e.matmul(bk[:, g * 128:(g + 1) * 128], lhsT=bcT[:, g, cs], rhs=bcT[:, 2 + g, cs], start=True, stop=True),
                  r=[bcT_r], w=[bk_r])
        cx.op(DVE, lambda e: e.tensor_tensor(out=cbm[:], in0=bk[:, 0:256].rearrange("p (g n) -> p g n", g=2), in1=self.bc_h(self.tri[:], 2), op=ALU.mult),
              r=[bk_r, self.tri_r], w=[cbm_r])
        if SSD_STOP <= 1:
            return
        for hg in range(4):
            rhsM, rhsM_r = T[f"rhsM{hg % 2}"]
            Lg, Lg_r = T[f"Lg{hg % 2}"]
            g = hg // 2
            cx.op(POOL, lambda e: e.tensor_tensor(out=rhsM[:].rearrange("p (h n) -> p h n", h=4), in0=self.bc_h(self.tri[:], 4),
                                                  in1=self.bc_mid(dta[:, c, hg * 4:(hg + 1) * 4], 128), op=ALU.mult),
                  r=[self.tri_r, dta_r], w=[rhsM_r])
            bk, bk_r = self.bank()
            cx.op(PE, lambda e: e.matmul(bk[:], lhsT=self.su[:], rhs=rhsM[:], start=True, stop=True), r=[self.su_r, rhsM_r], w=[bk_r])
            cx.op(ACT, lambda e: e.activation(out=Lg[:], in_=bk[:], func=AF.Exp), r=[bk_r], w=[Lg_r])
            L3 = Lg[:].rearrange("p (h n) -> p h n", h=4)
            cx.op(DVE, lambda e: e.tensor_tensor(out=MT[:, hg * 4:(hg + 1) * 4, :], in0=L3, in1=self.bc_h(cbm[:, g, :], 4), op=ALU.mult),
                  r=[Lg_r, cbm_r], w=[MT_r])
            cx.op(POOL, lambda e: e.tensor_copy(out=dec[:, hg * 4:(hg + 1) * 4], in_=L3[:, :, 127]), r=[Lg_r], w=[dec_r])
        if SSD_STOP <= 2:
            return
        xs, xs_r = T["xs"]
        Xdt, Xdt_r = T["Xdt"]
        Xdec, Xdec_r = T["Xdec"]
        Btok, Btok_r = T["Btok"]
        pb, pb_r = self.bankb()
        for t in range(8):
            cx.op(PE, lambda e: e.transpose(pb[:, t * 128:(t + 1) * 128], xcT[:, t, cs], ib[:]), r=[xcT_r, ib_r], w=[pb_r])
        cx.op(ACT, lambda e: e.copy(out=xs[:], in_=pb[:]), r=[pb_r], w=[xs_r])
        cx.op(DVE, lambda e: e.tensor_tensor(out=Xdt[:].rearrange("p (h d) -> p h d", h=16), in0=pb[:].rearrange("p (h d) -> p h d", h=16),
                                             in1=self.bc_mid(dt[:, c, :], 64), op=ALU.mult),
              r=[pb_r, dt_r], w=[Xdt_r])
        cx.op(POOL, lambda e: e.tensor_tensor(out=Xdec[:].rearrange("p (h d) -> p h d", h=16), in0=Xdt[:].rearrange("p (h d) -> p h d", h=16),
                                              in1=self.bc_mid(dec[:], 64), op=ALU.mult),
              r=[Xdt_r, dec_r], w=[Xdec_r])
        pb2, pb2_r = self.bankb()
        for g in range(2):
            cx.op(PE, lambda e: e.transpose(pb2[:, g * 128:(g + 1) * 128], bcT[:, g, cs], ib[:]), r=[bcT_r, ib_r], w=[pb2_r])
        cx.op(ACT, lambda e: e.copy(out=Btok[:], in_=pb2[:, 0:256]), r=[pb2_r], w=[Btok_r])
        if SSD_STOP <= 3:
            return
        ea, ea_r = T["ea"]
        bk, bk_r = self.bank()
        cx.op(PE, lambda e: e.matmul(bk[:, 0:16], lhsT=self.tri[:], rhs=dta[:, c, :], start=True, stop=True), r=[self.tri_r, dta_r], w=[bk_r])
        cx.op(PE, lambda e: e.matmul(bk[:, 16:32], lhsT=self.ones_f[:], rhs=dta[:, c, :], start=True, stop=True), r=[self.ones_f_r, dta_r], w=[bk_r])
        cx.op(ACT, lambda e: e.activation(out=ea[:], in_=bk[:, 0:32], func=AF.Exp), r=[bk_r], w=[ea_r])
        if SSD_STOP <= 4:
            return
        bd = [self.bank(), self.bank()]
        for h in range(16):
            d_t, d_r = bd[h // 8]
            cx.op(PE, lambda e: e.matmul(d_t[:, (h % 8) * 64:(h % 8 + 1) * 64], lhsT=MT[:, h, :], rhs=Xdt[:, h * 64:(h + 1) * 64], start=True, stop=True),
                  r=[MT_r, Xdt_r], w=[d_r])
        bf = [self.bank(), self.bank()]
        for g in range(2):
            f_t, f_r = bf[g]
            cx.op(PE, lambda e: e.matmul(f_t[:], lhsT=bcT[:, 2 + g, cs], rhs=self.ssSb[:, g, :], start=True, stop=True), r=[bcT_r, self.ssSb_r], w=[f_r])
        ty, ty_r = T["ytmp"]
        t2, t2_r = T["t2"]
        for g in range(2):
            gs = slice(g * 512, (g + 1) * 512)
            f_t, f_r = bf[g]
            d_t, d_r = bd[g]
            cx.op(DVE, lambda e: e.tensor_tensor(out=ty[:, gs].rearrange("p (h d) -> p h d", h=8), in0=f_t[:].rearrange("p (h d) -> p h d", h=8),
                                                 in1=self.bc_mid(ea[:, g * 8:(g + 1) * 8], 64), op=ALU.mult),
                  r=[f_r, ea_r], w=[ty_r])
            cx.op(DVE, lambda e: e.tensor_tensor(out=ty[:, gs], in0=ty[:, gs], in1=d_t[:], op=ALU.add), r=[d_r, ty_r], w=[ty_r])
        if SSD_STOP <= 5:
            return
        if "ssd_y" in self.dbg:
            self.dump("ssd_y", ty[:], ty_r, self.dbg_out["ssd_y"][tok0:tok0 + 128, :])
        cx.op(POOL, lambda e: e.tensor_tensor(out=t2[:].rearrange("p (h d) -> p h d", h=16), in0=xs[:].rearrange("p (h d) -> p h d", h=16),
                                              in1=self.bc_mid(self.dsk[:], 64), op=ALU.mult),
              r=[xs_r, self.dsk_r], w=[t2_r])
        cx.op(POOL, lambda e: e.tensor_tensor(out=t2[:], in0=t2[:], in1=ty[:], op=ALU.add), r=[t2_r, ty_r], w=[t2_r])
        cx.op(POOL, lambda e: e.tensor_tensor(out=t2[:], in0=t2[:], in1=sz[:, c, :], op=ALU.mult), r=[t2_r, sz_r], w=[t2_r])
        st6g, st6g_r = T["st6g"]
        mvg, mvg_r = T["mvg"]
        rs2, rs2_r = T["rs2"]
        yzb, yzb_r = T["yzb"]
        for g in range(2):
            cx.op(DVE, lambda e: e.bn_stats(out=st6g[:, g, :], in_=t2[:, g * 512:(g + 1) * 512]), r=[t2_r], w=[st6g_r])
        for g in range(2):
            cx.op(DVE, lambda e: e.bn_aggr(out=mvg[:, g, :], in_=st6g[:, g, :]), r=[st6g_r], w=[mvg_r])
        cx.op(DVE, lambda e: e.tensor_tensor(out=rs2[:], in0=mvg[:, :, 0], in1=mvg[:, :, 0], op=ALU.mult), r=[mvg_r], w=[rs2_r])
        cx.op(DVE, lambda e: e.tensor_tensor(out=rs2[:], in0=rs2[:], in1=mvg[:, :, 1], op=ALU.add), r=[mvg_r, rs2_r], w=[rs2_r])
        cx.op(ACT, lambda e: e.activation(out=rs2[:], in_=rs2[:], func=AF.Sqrt, bias=self.eps_t[:, 0:1]), r=[rs2_r, self.eps_r], w=[rs2_r])
        cx.op(DVE, lambda e: e.reciprocal(out=rs2[:], in_=rs2[:]), r=[rs2_r], w=[rs2_r])
        for g in range(2):
            cx.op(POOL, lambda e: e.tensor_scalar(out=yzb[:, g * 512:(g + 1) * 512], in0=t2[:, g * 512:(g + 1) * 512], scalar1=rs2[:, g:g + 1], scalar2=None,
                                                  op0=ALU.mult),
                  r=[t2_r, rs2_r], w=[yzb_r])
        pb, pb_r = self.bankb()
        for t in range(8):
            cx.op(PE, lambda e: e.transpose(pb[:, t * 128:(t + 1) * 128], yzb[:, t * 128:(t + 1) * 128], ib[:]), r=[yzb_r, ib_r], w=[pb_r])
        for t in range(8):
            cx.op(ACT, lambda e: e.activation(out=mixT[:, 8 + t, cs], in_=pb[:, t * 128:(t + 1) * 128], func=AF.Identity, scale=self.ssmn[:, t:t + 1]),
                  r=[pb_r, self.ssmn_r], w=[mixT_r])
        if SSD_STOP <= 6:
            return
        cdec = ea[:, 16:32]
        bs = [self.bank(), self.bank()]
        for g in range(2):
            s_t, s_r = bs[g]
            cx.op(PE, lambda e: e.matmul(s_t[:], lhsT=Btok[:, g * 128:(g + 1) * 128], rhs=Xdec[:, g * 512:(g + 1) * 512], start=True, stop=True),
                  r=[Btok_r, Xdec_r], w=[s_r])
        for g in range(2):
            s_t, s_r = bs[g]
            cx.op(DVE, lambda e: e.tensor_tensor(out=self.ssS[:, g, :].rearrange("p (h d) -> p h d", h=8),
                                                 in0=self.ssS[:, g, :].rearrange("p (h d) -> p h d", h=8),
                                                 in1=self.bc_mid(cdec[:, g * 8:(g + 1) * 8], 64), op=ALU.mult),
                  r=[self.ssS_r, ea_r], w=[self.ssS_r])
            cx.op(DVE, lambda e: e.tensor_tensor(out=self.ssS[:, g, :], in0=self.ssS[:, g, :], in1=s_t[:], op=ALU.add),
                  r=[s_r, self.ssS_r], w=[self.ssS_r])
        cx.op(ACT, lambda e: e.copy(out=self.ssSb[:], in_=self.ssS[:]), r=[self.ssS_r], w=[self.ssSb_r])


    def setup_odd(self):
        cx = self.cx
        PE, ACT, DVE, POOL, SP = cx.PE, cx.ACT, cx.DVE, cx.POOL, cx.SP
        if not hasattr(self, "ident_b"):
            self.ident_b, self.ident_b_r = self.sb("ident_b", [128, 128], BF16)
            cx.op(DVE, lambda e: e.tensor_copy(out=self.ident_b[:], in_=self.ident_f[:]), r=[self.ident_f_r], w=[self.ident_b_r])
        self.chalo, self.chalo_r = self.sb("chalo", [128, 4, 30], F32)
        self.ocw, self.ocw_r = self.sb("ocw", [128, 4, 31], F32)
        self.ocb, self.ocb_r = self.sb("ocb", [128, 4], F32)
        self.olg, self.olg_r = self.sb("olg", [128, 4], F32)
        self.olb, self.olb_r = self.sb("olb", [128, 4], F32)
        self.A8r, self.A8r_r = self.sb("A8r", [128, 2, 16], F32)
        self.A8i, self.A8i_r = self.sb("A8i", [128, 2, 16], F32)
        self.Xs, self.Xs_r = self.sb("Xs", [128, 2, 16], F32)
        cx.op(DVE, lambda e: e.memset(self.chalo[:], 0.0), w=[self.chalo_r])
        cx.op(DVE, lambda e: e.memset(self.Xs[:], 0.0), w=[self.Xs_r])
        for k in range(31):
            cx.dma(SP, self.ocw[:, :, k], self.o_dw_w[0, k].rearrange("(t p) -> p t", p=128), w=[self.ocw_r], sem_res=self.ocw_r,
                   allow_slow_non_contiguous=True)
        for t, r, src in [(self.ocb, self.ocb_r, self.o_dw_b), (self.olg, self.olg_r, self.o_ln_g), (self.olb, self.olb_r, self.o_ln_b)]:
            cx.dma(SP, t[:], src[0].rearrange("(t p) -> p t", p=128), w=[r], sem_res=r, allow_slow_non_contiguous=True)
        with ExitStack() as st:
            def tl(nm, shp, dt=F32):
                return self.sb("s5_" + nm, shp, dt, st)
            lr, lr_r = tl("lr", [128, 16])
            li, li_r = tl("li", [128, 16])
            stp, stp_r = tl("stp", [128, 16])
            Bre, Bre_r = tl("Bre", [128, 16, 16])
            Bim, Bim_r = tl("Bim", [128, 16, 16])
            Cre, Cre_r = tl("Cre", [128, 16, 16])
            Cim, Cim_r = tl("Cim", [128, 16, 16])
            Dbc, Dbc_r = tl("Dbc", [128, 32, 16])
            tzm, tzm_r = tl("tzm", [128, 128])
            cx.dma(SP, tzm[:], self.consts["tzmask"], w=[tzm_r], sem_res=tzm_r)
            cx.dma(SP, lr[:], self.o_a_re[0].rearrange("g n -> (g n)").rearrange("(p q) -> q p", q=128), w=[lr_r], sem_res=lr_r,
                   allow_slow_non_contiguous=True)
            cx.dma(SP, li[:], self.o_a_im[0].rearrange("g n -> (g n)").rearrange("(p q) -> q p", q=128), w=[li_r], sem_res=li_r,
                   allow_slow_non_contiguous=True)
            ls2 = self.o_log_step[0].rearrange("(p gh) -> gh p", gh=2)
            for gh in range(2):
                cx.dma(SP, stp[gh * 64:(gh + 1) * 64, :], ls2[gh].partition_broadcast(64), w=[stp_r], sem_res=stp_r, allow_slow_non_contiguous=True)
            cx.dma(SP, Bre[:], self.o_b_re[0].rearrange("g n c -> (g n) c").rearrange("(p q) c -> q p c", q=128), w=[Bre_r], sem_res=Bre_r,
                   allow_slow_non_contiguous=True)
            cx.dma(SP, Bim[:], self.o_b_im[0].rearrange("g n c -> (g n) c").rearrange("(p q) c -> q p c", q=128), w=[Bim_r], sem_res=Bim_r,
                   allow_slow_non_contiguous=True)
            for (dst, dst_r, src) in [(Cre, Cre_r, self.o_c_re), (Cim, Cim_r, self.o_c_im)]:
                for g in range(32):
                    p, gh = g // 2, g % 2
                    cx.dma(SP, dst[gh * 64:(gh + 1) * 64, p, :], src[0, g].rearrange("co n -> n co"), w=[dst_r], sem_res=dst_r,
                           allow_slow_non_contiguous=True)
            cx.dma(SP, Dbc[:].rearrange("p g c -> p (g c)"), self.o_d[0].partition_broadcast(128), w=[Dbc_r], sem_res=Dbc_r, allow_slow_non_contiguous=True)
            cx.op(ACT, lambda e: e.activation(out=stp[:], in_=stp[:], func=AF.Exp), r=[stp_r], w=[stp_r])
            lrs, lrs_r = tl("lrs", [128, 16])
            lis, lis_r = tl("lis", [128, 16])
            cx.op(DVE, lambda e: e.tensor_tensor(out=lrs[:], in0=lr[:], in1=stp[:], op=ALU.mult), r=[lr_r, stp_r], w=[lrs_r])
            cx.op(DVE, lambda e: e.tensor_tensor(out=lis[:], in0=li[:], in1=stp[:], op=ALU.mult), r=[li_r, stp_r], w=[lis_r])
            NM = 17
            mag, mag_r = tl("mag", [128, 16, NM])
            ang, ang_r = tl("ang", [128, 16, NM])
            Pre, Pre_r = tl("Pre", [128, 16, NM])
            Pim, Pim_r = tl("Pim", [128, 16, NM])
            tmpa, tmpa_r = tl("tmpa", [128, 16, NM])
            for mi in range(NM):
                m = float(mi - 8)
                cx.op(ACT, lambda e: e.activation(out=mag[:, :, mi], in_=lrs[:], func=AF.Exp, scale=m), r=[lrs_r], w=[mag_r])
                cx.op(DVE, lambda e: e.tensor_scalar(out=ang[:, :, mi], in0=lis[:], scalar1=m, scalar2=None, op0=ALU.mult), r=[lis_r], w=[ang_r])
            TWO_PI = 2.0 * math.pi
            MAGIC = 12582912.0

            def sin_of(dst, dst_r, shift):
                cx.op(DVE, lambda e: e.tensor_scalar(out=tmpa[:], in0=ang[:], scalar1=shift, scalar2=1.0 / TWO_PI, op0=ALU.add, op1=ALU.mult),
                      r=[ang_r], w=[tmpa_r])
                cx.op(DVE, lambda e: e.tensor_scalar(out=dst[:], in0=tmpa[:], scalar1=MAGIC, scalar2=None, op0=ALU.add), r=[tmpa_r], w=[dst_r])
                cx.op(DVE, lambda e: e.tensor_scalar(out=dst[:], in0=dst[:], scalar1=-MAGIC, scalar2=None, op0=ALU.add), r=[dst_r], w=[dst_r])
                cx.op(DVE, lambda e: e.tensor_tensor(out=tmpa[:], in0=tmpa[:], in1=dst[:], op=ALU.subtract), r=[tmpa_r, dst_r], w=[tmpa_r])
                cx.op(DVE, lambda e: e.tensor_scalar(out=tmpa[:], in0=tmpa[:], scalar1=TWO_PI, scalar2=math.pi, op0=ALU.mult, op1=ALU.min),
                      r=[tmpa_r], w=[tmpa_r])
                cx.op(DVE, lambda e: e.tensor_scalar(out=tmpa[:], in0=tmpa[:], scalar1=-math.pi, scalar2=None, op0=ALU.max), r=[tmpa_r], w=[tmpa_r])
                cx.op(ACT, lambda e: e.activation(out=dst[:], in_=tmpa[:], func=AF.Sin), r=[tmpa_r], w=[dst_r])

            sin_of(Pim, Pim_r, 0.0)
            sin_of(Pre, Pre_r, math.pi / 2.0)
            cx.op(DVE, lambda e: e.tensor_tensor(out=Pre[:], in0=Pre[:], in1=mag[:], op=ALU.mult), r=[Pre_r, mag_r], w=[Pre_r])
            cx.op(DVE, lambda e: e.tensor_tensor(out=Pim[:], in0=Pim[:], in1=mag[:], op=ALU.mult), r=[Pim_r, mag_r], w=[Pim_r])

            def P(which, m):
                return (Pre if which == "re" else Pim)[:, :, m + 8]
            cx.op(DVE, lambda e: e.tensor_copy(out=self.A8r[:, 0, :], in_=P("re", 8)), r=[Pre_r], w=[self.A8r_r])
            cx.op(DVE, lambda e: e.tensor_copy(out=self.A8r[:, 1, :], in_=P("re", 8)), r=[Pre_r], w=[self.A8r_r])
            cx.op(DVE, lambda e: e.tensor_scalar(out=self.A8i[:, 0, :], in0=P("im", 8), scalar1=-1.0, scalar2=None, op0=ALU.mult), r=[Pim_r], w=[self.A8i_r])
            cx.op(DVE, lambda e: e.tensor_copy(out=self.A8i[:, 1, :], in_=P("im", 8)), r=[Pim_r], w=[self.A8i_r])
            fre, fre_r = tl("fre", [128, 16])
            fim, fim_r = tl("fim", [128, 16])
            den, den_r = tl("den", [128, 16])
            am1, am1_r = tl("am1", [128, 16])
            t16, t16_r = tl("t16", [128, 16])
            cx.op(DVE, lambda e: e.tensor_tensor(out=den[:], in0=lr[:], in1=lr[:], op=ALU.mult), r=[lr_r], w=[den_r])
            cx.op(DVE, lambda e: e.tensor_tensor(out=t16[:], in0=li[:], in1=li[:], op=ALU.mult), r=[li_r], w=[t16_r])
            cx.op(DVE, lambda e: e.tensor_tensor(out=den[:], in0=den[:], in1=t16[:], op=ALU.add), r=[den_r, t16_r], w=[den_r])
            cx.op(DVE, lambda e: e.reciprocal(out=den[:], in_=den[:]), r=[den_r], w=[den_r])
            cx.op(DVE, lambda e: e.tensor_scalar(out=am1[:], in0=P("re", 1), scalar1=-1.0, scalar2=None, op0=ALU.add), r=[Pre_r], w=[am1_r])
            cx.op(DVE, lambda e: e.tensor_tensor(out=fre[:], in0=am1[:], in1=lr[:], op=ALU.mult), r=[am1_r, lr_r], w=[fre_r])
            cx.op(DVE, lambda e: e.tensor_tensor(out=t16[:], in0=P("im", 1), in1=li[:], op=ALU.mult), r=[Pim_r, li_r], w=[t16_r])
            cx.op(DVE, lambda e: e.tensor_tensor(out=fre[:], in0=fre[:], in1=t16[:], op=ALU.add), r=[fre_r, t16_r], w=[fre_r])
            cx.op(DVE, lambda e: e.tensor_tensor(out=fre[:], in0=fre[:], in1=den[:], op=ALU.mult), r=[fre_r, den_r], w=[fre_r])
            cx.op(DVE, lambda e: e.tensor_tensor(out=fim[:], in0=P("im", 1), in1=lr[:], op=ALU.mult), r=[Pim_r, lr_r], w=[fim_r])
            cx.op(DVE, lambda e: e.tensor_tensor(out=t16[:], in0=am1[:], in1=li[:], op=ALU.mult), r=[am1_r, li_r], w=[t16_r])
            cx.op(DVE, lambda e: e.tensor_tensor(out=fim[:], in0=fim[:], in1=t16[:], op=ALU.subtract), r=[fim_r, t16_r], w=[fim_r])
            cx.op(DVE, lambda e: e.tensor_tensor(out=fim[:], in0=fim[:], in1=den[:], op=ALU.mult), r=[fim_r, den_r], w=[fim_r])

            def cmul(dre, dre_r, dim_, dim_r, are, aim, a_rs, bre, bim, b_rs, tA, tA_r, neg_im=False):
                cx.op(DVE, lambda e: e.tensor_tensor(out=dre, in0=are, in1=bre, op=ALU.mult), r=a_rs + b_rs, w=[dre_r])
                cx.op(DVE, lambda e: e.tensor_tensor(out=tA, in0=aim, in1=bim, op=ALU.mult), r=a_rs + b_rs, w=[tA_r])
                cx.op(DVE, lambda e: e.tensor_tensor(out=dre, in0=dre, in1=tA, op=ALU.subtract), r=[dre_r, tA_r], w=[dre_r])
                cx.op(DVE, lambda e: e.tensor_tensor(out=dim_, in0=are, in1=bim, op=ALU.mult), r=a_rs + b_rs, w=[dim_r])
                cx.op(DVE, lambda e: e.tensor_tensor(out=tA, in0=aim, in1=bre, op=ALU.mult), r=a_rs + b_rs, w=[tA_r])
                cx.op(DVE, lambda e: e.tensor_tensor(out=dim_, in0=dim_, in1=tA, op=ALU.add), r=[dim_r, tA_r], w=[dim_r])

            def bc16(ap2):
                return ap2.unsqueeze(2).to_broadcast([128, 16, 16])
            Bbr, Bbr_r = tl("Bbr", [128, 16, 16])
            Bbi, Bbi_r = tl("Bbi", [128, 16, 16])
            t3, t3_r = tl("t3", [128, 16, 16])
            cmul(Bbr[:], Bbr_r, Bbi[:], Bbi_r, bc16(fre[:]), bc16(fim[:]), [fre_r, fim_r], Bre[:], Bim[:], [Bre_r, Bim_r], t3[:], t3_r)
            BcTr, BcTr_r = tl("BcTr", [128, 16, 8, 16])
            BcTi, BcTi_r = tl("BcTi", [128, 16, 8, 16])
            BpTr, BpTr_r = tl("BpTr", [128, 16, 8, 16])
            BpTi, BpTi_r = tl("BpTi", [128, 16, 8, 16])
            CAr, CAr_r = tl("CAr", [128, 16, 8, 16])
            CAi, CAi_r = tl("CAi", [128, 16, 8, 16])
            for s in range(8):
                cmul(BcTr[:, :, s, :], BcTr_r, BcTi[:, :, s, :], BcTi_r, bc16(P("re", 7 - s)), bc16(P("im", 7 - s)), [Pre_r, Pim_r],
                     Bbr[:], Bbi[:], [Bbr_r, Bbi_r], t3[:], t3_r)
                cmul(BpTr[:, :, s, :], BpTr_r, BpTi[:, :, s, :], BpTi_r, bc16(P("re", -1 - s)), bc16(P("im", -1 - s)), [Pre_r, Pim_r],
                     Bbr[:], Bbi[:], [Bbr_r, Bbi_r], t3[:], t3_r)
                cmul(CAr[:, :, s, :], CAr_r, CAi[:, :, s, :], CAi_r, bc16(P("re", s + 1)), bc16(P("im", s + 1)), [Pre_r, Pim_r],
                     Cre[:], Cim[:], [Cre_r, Cim_r], t3[:], t3_r)
            CcU, CcU_r = tl("CcU", [128, 16, 2, 2, 128], BF16)
            cx.op(POOL, lambda e: e.memset(CcU[:], 0.0), w=[CcU_r])
            for gh in range(2):
                ps_ = slice(gh * 64, (gh + 1) * 64)
                cx.op(DVE, lambda e: e.tensor_copy(out=CcU[ps_, :, gh, 0, :], in_=CAr[ps_].rearrange("q p j c -> q p (j c)")), r=[CAr_r], w=[CcU_r])
                cx.op(DVE, lambda e: e.tensor_scalar(out=CcU[ps_, :, gh, 1, :], in0=CAi[ps_].rearrange("q p j c -> q p (j c)"), scalar1=-1.0, scalar2=None,
                                                     op0=ALU.mult), r=[CAi_r], w=[CcU_r])
            ccflat = CcU[:].rearrange("q p a b n -> q (p a b n)")
            for i in range(2):
                u = self.U[("s5cc", i)]
                dst = self.wscr[u["off"]: u["off"] + 128 * 4096].rearrange("(p l) -> p l", p=128)
                cx.dma(SP, dst, ccflat[:, i * 4096:(i + 1) * 4096], r=[CcU_r], w=[self.wscr_res], sem_res=CcU_r)
            BcU, BcU_r = tl("BcU", [128, 16, 2, 2, 128], BF16)
            cx.op(POOL, lambda e: e.memset(BcU[:], 0.0), w=[BcU_r])
            for p in range(16):
                for ri, (src, src_r) in enumerate([(BcTr, BcTr_r), (BcTi, BcTi_r)]):
                    bk, bk_r = self.bank()
                    cx.op(PE, lambda e: e.transpose(bk[:, 0:128], src[:, p, :, :].rearrange("q s c -> q (s c)"), self.ident_f[:]),
                          r=[src_r, self.ident_f_r], w=[bk_r])
                    for gh in range(2):
                        cx.op(ACT if gh == 0 else DVE, lambda e: (e.copy if gh == 0 else e.tensor_copy)(out=BcU[:, p, gh, ri, gh * 64:(gh + 1) * 64],
                                                                                                 in_=bk[:, gh * 64:(gh + 1) * 64]),
                              r=[bk_r], w=[BcU_r])
            bcflat = BcU[:].rearrange("q p a b n -> q (p a b n)")
            for i in range(2):
                u = self.U[("s5bc", i)]
                dst = self.wscr[u["off"]: u["off"] + 128 * 4096].rearrange("(p l) -> p l", p=128)
                cx.dma(SP, dst, bcflat[:, i * 4096:(i + 1) * 4096], r=[BcU_r], w=[self.wscr_res], sem_res=BcU_r)
            TzU, TzU_r = tl("TzU", [128, 32, 128], BF16)
            lm = [tl(f"lm{i}", [128, 128]) for i in range(2)]
            tzt, tzt_r = tl("tzt", [128, 128])
            for g in range(32):
                p, gh = g // 2, g % 2
                ps_ = slice(gh * 64, (gh + 1) * 64)
                (l0, l0_r), (l1, l1_r) = lm
                cx.op(POOL, lambda e: e.memset(l0[:], 0.0), w=[l0_r])
                cx.op(POOL, lambda e: e.memset(l1[:], 0.0), w=[l1_r])
                cx.op(POOL, lambda e: e.tensor_copy(out=l0[ps_, :], in_=BpTr[ps_, p, :, :].rearrange("q s c -> q (s c)")), r=[BpTr_r], w=[l0_r])
                cx.op(POOL, lambda e: e.tensor_scalar(out=l1[ps_, :], in0=BpTi[ps_, p, :, :].rearrange("q s c -> q (s c)"), scalar1=-1.0, scalar2=None,
                                                      op0=ALU.mult), r=[BpTi_r], w=[l1_r])
                bk, bk_r = self.bank()
                cx.op(PE, lambda e: e.matmul(bk[:, 0:128], lhsT=l0[:], rhs=CAr[:, p, :, :].rearrange("q j c -> q (j c)"), start=True, stop=False),
                      r=[l0_r, CAr_r], w=[bk_r])
                cx.op(PE, lambda e: e.matmul(bk[:, 0:128], lhsT=l1[:], rhs=CAi[:, p, :, :].rearrange("q j c -> q (j c)"), start=False, stop=True),
                      r=[l1_r, CAi_r], w=[bk_r])
                cx.op(DVE, lambda e: e.tensor_tensor(out=tzt[:], in0=bk[:, 0:128], in1=tzm[:], op=ALU.mult), r=[bk_r, tzm_r], w=[tzt_r])
                cx.op(DVE, lambda e: e.tensor_tensor(out=TzU[:, g, :].rearrange("q (j c) -> q j c", j=8),
                                                     in0=self.ident_f[:].rearrange("q (j c) -> q j c", j=8),
                                                     in1=Dbc[:, g, :].unsqueeze(1).to_broadcast([128, 8, 16]), op=ALU.mult),
                      r=[self.ident_f_r, Dbc_r], w=[TzU_r])
                cx.op(DVE, lambda e: e.tensor_tensor(out=TzU[:, g, :], in0=TzU[:, g, :], in1=tzt[:], op=ALU.add), r=[tzt_r, TzU_r], w=[TzU_r])
            u = self.U["s5tz"]
            dst = self.wscr[u["off"]: u["off"] + 128 * 4096].rearrange("(p l) -> p l", p=128)
            cx.dma(SP, dst, TzU[:].rearrange("q g n -> q (g n)"), r=[TzU_r], w=[self.wscr_res], sem_res=TzU_r)
            if "s5setup" in self.dbg:
                nc = self.nc
                o1 = nc.dram_tensor("dbg_tz", [128, 4096], BF16, kind="ExternalOutput").ap()
                o2 = nc.dram_tensor("dbg_bc", [128, 8192], BF16, kind="ExternalOutput").ap()
                o3 = nc.dram_tensor("dbg_cc", [128, 8192], BF16, kind="ExternalOutput").ap()
                o4 = nc.dram_tensor("dbg_pre", [128, 16 * 17], F32, kind="ExternalOutput").ap()
                o5 = nc.dram_tensor("dbg_pim", [128, 16 * 17], F32, kind="ExternalOutput").ap()
                o6 = nc.dram_tensor("dbg_bbr", [128, 256], F32, kind="ExternalOutput").ap()
                self.dump("tz", TzU[:].rearrange("q g n -> q (g n)"), TzU_r, o1)
                self.dump("bc", bcflat, BcU_r, o2)
                self.dump("cc", ccflat, CcU_r, o3)
                self.dump("pre", Pre[:].rearrange("q p m -> q (p m)"), Pre_r, o4)
                self.dump("pim", Pim[:].rearrange("q p m -> q (p m)"), Pim_r, o5)
                self.dump("bbr", Bbr[:].rearrange("q p m -> q (p m)"), Bbr_r, o6)
            cx.barrier(engines=cx.compute + [cx.SP], with_dma=True)

    def odd(self, b):
        cx = self.cx
        PE, ACT, DVE, POOL, SP = cx.PE, cx.ACT, cx.DVE, cx.POOL, cx.SP
        self.rmsnorm(2, self.hnT, self.hnT_r)
        hnT, hnT_r = self.hnT, self.hnT_r
        NC8 = NB // 8
        with ExitStack() as st:
            def tl(nm, shp, dt=F32):
                return self.sb("o_" + nm, shp, dt, st)
            mixT, mixT_r = tl("mixT", [128, 8, NB], BF16)
            cbuf, cbuf_r = tl("cbuf", [128, 4, NB + 30])
            cacc, cacc_r = tl("cacc", [128, 4, NB])
            sig = [tl(f"sig{i}", [128, NB]) for i in range(2)]
            uS, uS_r = tl("uS", [64, 32, 128], BF16)
            U, U_r = tl("U", [128, 32, NC8], BF16)
            V, V_r = tl("V", [128, 2, 16, NC8])
            Xst, Xst_r = tl("Xst", [128, 2, 16, NC8], BF16)
            slot, slot_r, u = self.stream_get()
            assert u is self.U["ou"]
            wv = slot[:, 0:4096].rearrange("p (k w) -> p k w", k=8)
            for s in range(8):
                bk, bk_r = self.bank()
                for kt in range(8):
                    lh = hnT[:, kt, :].rearrange("p (c s) -> p s c", s=8)[:, s, :]
                    cx.op(PE, lambda e: e.matmul(bk[0:64, :], lhsT=lh, rhs=wv[:, kt, :], start=(kt == 0), stop=(kt == 7)), r=[slot_r, hnT_r], w=[bk_r])
                src = bk[0:64, :].rearrange("p (g c) -> p g c", g=32)
                if s % 2 == 0:
                    cx.op(ACT, lambda e: e.copy(out=uS[:, :, s * 16:(s + 1) * 16], in_=src), r=[bk_r], w=[uS_r])
                else:
                    cx.op(DVE, lambda e: e.tensor_copy(out=uS[:, :, s * 16:(s + 1) * 16], in_=src), r=[bk_r], w=[uS_r])
            for g8 in range(4):
                pb, pb_r = self.bankb()
                for gi in range(8):
                    g = g8 * 8 + gi
                    cx.op(PE, lambda e: e.transpose(pb[:, gi * 64:(gi + 1) * 64], uS[:, g, :], self.ident_b[0:64, 0:64]), r=[uS_r, self.ident_b_r], w=[pb_r])
                if g8 % 2 == 0:
                    cx.op(ACT, lambda e: e.copy(out=U[:, g8 * 8:(g8 + 1) * 8, :].rearrange("p g c -> p (g c)"), in_=pb[:, 0:512]), r=[pb_r], w=[U_r])
                else:
                    cx.op(DVE, lambda e: e.tensor_copy(out=U[:, g8 * 8:(g8 + 1) * 8, :].rearrange("p g c -> p (g c)"), in_=pb[:, 0:512]), r=[pb_r], w=[U_r])
            bcs = []
            for i in range(2):
                slot, slot_r, u = self.stream_get(hold_prev=i)
                assert u is self.U[("s5bc", i)]
                bcs.append((slot[:, 0:4096].rearrange("q (p a b n) -> q p a b n", p=8, a=2, b=2), slot_r))
            for ri in range(2):
                for ph in range(2):
                    bk, bk_r = self.bank()
                    for pi in range(8):
                        p = ph * 8 + pi
                        bcv, bc_r = bcs[p // 8]
                        for gh in range(2):
                            cx.op(PE, lambda e: e.matmul(bk[:, pi * 64:(pi + 1) * 64], lhsT=bcv[:, p % 8, gh, ri, :], rhs=U[:, 2 * p + gh, :],
                                                         start=(gh == 0), stop=(gh == 1)), r=[bc_r, U_r], w=[bk_r])
                    cx.op(ACT, lambda e: e.copy(out=V[:, ri, ph * 8:(ph + 1) * 8, :].rearrange("q p c -> q (p c)"), in_=bk[:]), r=[bk_r], w=[V_r])
            T1, T1_r = tl("T1", [128, 2, 16])
            T2, T2_r = tl("T2", [128, 2, 16])
            Xs, Xs_r = self.Xs, self.Xs_r
            for c in range(NC8):
                cx.op(POOL, lambda e: e.tensor_copy(out=Xst[:, :, :, c], in_=Xs[:]), r=[Xs_r], w=[Xst_r])
                cx.op(POOL, lambda e: e.tensor_tensor(out=T1[:], in0=Xs[:], in1=self.A8r[:], op=ALU.mult), r=[Xs_r, self.A8r_r], w=[T1_r])
                cx.op(POOL, lambda e: e.tensor_tensor(out=T2[:, 0, :], in0=Xs[:, 1, :], in1=self.A8i[:, 0, :], op=ALU.mult), r=[Xs_r, self.A8i_r], w=[T2_r])
                cx.op(POOL, lambda e: e.tensor_tensor(out=T2[:, 1, :], in0=Xs[:, 0, :], in1=self.A8i[:, 1, :], op=ALU.mult), r=[Xs_r, self.A8i_r], w=[T2_r])
                cx.op(POOL, lambda e: e.tensor_tensor(out=T1[:], in0=T1[:], in1=T2[:], op=ALU.add), r=[T1_r, T2_r], w=[T1_r])
                cx.op(POOL, lambda e: e.tensor_tensor(out=Xs[:], in0=T1[:], in1=V[:, :, :, c], op=ALU.add), r=[T1_r, V_r], w=[Xs_r])
            for i in range(4):
                slot, slot_r, u = self.stream_get()
                assert u is self.U[("oin", i)]
                bks = []
                for gi in range(2):
                    wv = slot[:, gi * 1024:(gi + 1) * 1024].rearrange("p (k w) -> p k w", k=8)
                    bk, bk_r = self.bank()
                    for kt in range(8):
                        cx.op(PE, lambda e: e.matmul(bk[:], lhsT=wv[:, kt, :], rhs=hnT[:, kt, :], start=(kt == 0), stop=(kt == 7)), r=[slot_r, hnT_r], w=[bk_r])
                    bks.append((bk, bk_r))
                sg_t, sg_r = sig[i % 2]
                cx.op(ACT, lambda e: e.activation(out=sg_t[:], in_=bks[1][0][:], func=AF.Sigmoid), r=[bks[1][1]], w=[sg_r])
                cx.op(DVE, lambda e: e.tensor_tensor(out=cbuf[:, i, 30:NB + 30], in0=bks[0][0][:], in1=sg_t[:], op=ALU.mult), r=[bks[0][1], sg_r], w=[cbuf_r])
            cx.op(DVE, lambda e: e.tensor_copy(out=cbuf[:, :, 0:30], in_=self.chalo[:]), r=[self.chalo_r], w=[cbuf_r])
            cx.op(DVE, lambda e: e.tensor_copy(out=self.chalo[:], in_=cbuf[:, :, NB:NB + 30]), r=[cbuf_r], w=[self.chalo_r])
            for i in range(4):
                cx.op(DVE, lambda e: e.tensor_scalar(out=cacc[:, i, :], in0=cbuf[:, i, 30:NB + 30], scalar1=self.ocw[:, i, 30:31], scalar2=self.ocb[:, i:i + 1],
                                                     op0=ALU.mult, op1=ALU.add), r=[cbuf_r, self.ocw_r, self.ocb_r], w=[cacc_r])
                for k in range(30):
                    cx.op(DVE, lambda e: e.scalar_tensor_tensor(out=cacc[:, i, :], in0=cbuf[:, i, k:k + NB], scalar=self.ocw[:, i, k:k + 1], in1=cacc[:, i, :],
                                                                op0=ALU.mult, op1=ALU.add), r=[cbuf_r, cacc_r, self.ocw_r], w=[cacc_r])
            sq = [tl(f"sq{i}", [128, NB]) for i in range(2)]
            b1, b1_r = self.bank()
            b2, b2_r = self.bank()
            for i in range(4):
                cx.op(PE, lambda e: e.matmul(b1[:], lhsT=self.ones_f[:], rhs=cacc[:, i, :], start=(i == 0), stop=(i == 3)), r=[self.ones_f_r, cacc_r], w=[b1_r])
            for i in range(4):
                s_t, s_r = sq[i % 2]
                cx.op(ACT, lambda e: e.activation(out=s_t[:], in_=cacc[:, i, :], func=AF.Square), r=[cacc_r], w=[s_r])
                cx.op(PE, lambda e: e.matmul(b2[:], lhsT=self.ones_f[:], rhs=s_t[:], start=(i == 0), stop=(i == 3)), r=[self.ones_f_r, s_r], w=[b2_r])
            mean, mean_r = tl("mean", [128, NB])
            var, var_r = tl("var", [128, NB])
            cx.op(DVE, lambda e: e.tensor_scalar(out=mean[:], in0=b1[:], scalar1=1.0 / 512, scalar2=None, op0=ALU.mult), r=[b1_r], w=[mean_r])
            cx.op(DVE, lambda e: e.tensor_tensor(out=var[:], in0=mean[:], in1=mean[:], op=ALU.mult), r=[mean_r], w=[var_r])
            cx.op(DVE, lambda e: e.scalar_tensor_tensor(out=var[:], in0=b2[:], scalar=1.0 / 512, in1=var[:], op0=ALU.mult, op1=ALU.subtract),
                  r=[b2_r, var_r], w=[var_r])
            cx.op(ACT, lambda e: e.activation(out=var[:], in_=var[:], func=AF.Sqrt, bias=self.eps_t[:, 0:1]), r=[var_r, self.eps_r], w=[var_r])
            cx.op(DVE, lambda e: e.reciprocal(out=var[:], in_=var[:]), r=[var_r], w=[var_r])
            for i in range(4):
                cx.op(DVE, lambda e: e.tensor_tensor(out=cacc[:, i, :], in0=cacc[:, i, :], in1=mean[:], op=ALU.subtract), r=[cacc_r, mean_r], w=[cacc_r])
                cx.op(DVE, lambda e: e.tensor_tensor(out=cacc[:, i, :], in0=cacc[:, i, :], in1=var[:], op=ALU.mult), r=[cacc_r, var_r], w=[cacc_r])
                cx.op(ACT, lambda e: e.activation(out=mixT[:, i, :], in_=cacc[:, i, :], func=AF.Silu, bias=self.olb[:, i:i + 1], scale=self.olg[:, i:i + 1]),
                      r=[cacc_r, self.olg_r, self.olb_r], w=[mixT_r])
            slot, tz_r, u = self.stream_get()
            assert u is self.U["s5tz"]
            tzv = slot[:, 0:4096].rearrange("q (g n) -> q g n", g=32)
            ccs = []
            for i in range(2):
                slot, slot_r, u = self.stream_get(hold_prev=i + 1)
                assert u is self.U[("s5cc", i)]
                ccs.append((slot[:, 0:4096].rearrange("q (p a b n) -> q p a b n", p=8, a=2, b=2), slot_r))
            stok, stok_r = tl("stok", [64, 8, 512])
            for g4 in range(8):
                bk, bk_r = self.bank()
                for gi in range(4):
                    g = g4 * 4 + gi
                    p, gh = g // 2, g % 2
                    ccv, cc_r = ccs[p // 8]
                    reg = bk[0:64, gi * 128:(gi + 1) * 128]
                    cx.op(PE, lambda e: e.matmul(reg, lhsT=U[:, g, :], rhs=tzv[:, g, :], start=True, stop=False), r=[U_r, tz_r], w=[bk_r])
                    cx.op(PE, lambda e: e.matmul(reg, lhsT=Xst[:, 0, p, :], rhs=ccv[:, p % 8, gh, 0, :], start=False, stop=False), r=[Xst_r, cc_r], w=[bk_r])
                    cx.op(PE, lambda e: e.matmul(reg, lhsT=Xst[:, 1, p, :], rhs=ccv[:, p % 8, gh, 1, :], start=False, stop=True), r=[Xst_r, cc_r], w=[bk_r])
                src = bk[0:64, :].rearrange("p (g j c) -> p j g c", g=4, j=8)
                dst = stok[:, :, g4 * 64:(g4 + 1) * 64].rearrange("p j (g c) -> p j g c", g=4)
                if g4 % 2 == 0:
                    cx.op(ACT, lambda e: e.copy(out=dst, in_=src), r=[bk_r], w=[stok_r])
                else:
                    cx.op(DVE, lambda e: e.tensor_copy(out=dst, in_=src), r=[bk_r], w=[stok_r])
            syT, syT_r = tl("syT", [128, 4, NB])
            for t in range(4):
                bk, bk_r = self.bank()
                for j in range(8):
                    cx.op(PE, lambda e: e.transpose(bk[:, j * 64:(j + 1) * 64], stok[:, j, t * 128:(t + 1) * 128], self.ident_f[0:64, 0:64]),
                          r=[stok_r, self.ident_f_r], w=[bk_r])
                src = bk[:].rearrange("p (j c) -> p j c", j=8)
                dst = syT[:, t, :].rearrange("p (c j) -> p j c", j=8)
                if t % 2 == 0:
                    cx.op(ACT, lambda e: e.copy(out=dst, in_=src), r=[bk_r], w=[syT_r])
                else:
                    cx.op(DVE, lambda e: e.tensor_copy(out=dst, in_=src), r=[bk_r], w=[syT_r])
            if "s5y" in self.dbg:
                for t in range(4):
                    self.dump("s5y", syT[:, t, :], syT_r, self.dbg_out["s5y"][t * 128:(t + 1) * 128, b * NB:(b + 1) * NB])
            gl, gl_r = tl("gl", [128, 4, NB])
            glb, glb_r = tl("glb", [128, 4, NB], BF16)
            gt = [tl(f"gt{i}", [128, NB]) for i in range(2)]
            K2 = 2.0 * math.sqrt(2.0 / math.pi)
            for t in range(4):
                g_t, g_r = gt[t % 2]
                cx.op(DVE, lambda e: e.tensor_tensor(out=g_t[:], in0=syT[:, t, :], in1=syT[:, t, :], op=ALU.mult), r=[syT_r], w=[g_r])
                cx.op(DVE, lambda e: e.tensor_scalar(out=g_t[:], in0=g_t[:], scalar1=0.044715, scalar2=1.0, op0=ALU.mult, op1=ALU.add), r=[g_r], w=[g_r])
                cx.op(DVE, lambda e: e.tensor_tensor(out=g_t[:], in0=g_t[:], in1=syT[:, t, :], op=ALU.mult), r=[g_r, syT_r], w=[g_r])
                cx.op(ACT, lambda e: e.activation(out=g_t[:], in_=g_t[:], func=AF.Sigmoid, scale=K2), r=[g_r], w=[g_r])
                cx.op(DVE, lambda e: e.tensor_tensor(out=gl[:, t, :], in0=g_t[:], in1=syT[:, t, :], op=ALU.mult), r=[g_r, syT_r], w=[gl_r])
                cx.op(POOL, lambda e: e.tensor_copy(out=glb[:, t, :], in_=gl[:, t, :]), r=[gl_r], w=[glb_r])
            slot, slot_r, u = self.stream_get()
            assert u is self.U["oglu"]
            gv = slot[:, 0:2048].rearrange("p (o k w) -> p o k w", o=4, k=4)
            for ot in range(4):
                bk, bk_r = self.bank()
                for kt in range(4):
                    cx.op(PE, lambda e: e.matmul(bk[:], lhsT=gv[:, ot, kt, :], rhs=glb[:, kt, :], start=(kt == 0), stop=(kt == 3)), r=[slot_r, glb_r], w=[bk_r])
                g_t, g_r = gt[ot % 2]
                cx.op(ACT, lambda e: e.activation(out=g_t[:], in_=bk[:], func=AF.Sigmoid), r=[bk_r], w=[g_r])
                cx.op(DVE, lambda e: e.tensor_tensor(out=mixT[:, 4 + ot, :], in0=g_t[:], in1=gl[:, ot, :], op=ALU.mult), r=[g_r, gl_r], w=[mixT_r])
            for i in range(4):
                slot, slot_r, u = self.stream_get()
                assert u is self.U[("oout", i)]
                for gi in range(2):
                    ft = 2 * i + gi
                    wv = slot[:, gi * 1024:(gi + 1) * 1024].rearrange("p (k w) -> p k w", k=8)
                    bk, bk_r = self.bank()
                    for kt in range(8):
                        cx.op(PE, lambda e: e.matmul(bk[:], lhsT=wv[:, kt, :], rhs=mixT[:, kt, :], start=(kt == 0), stop=(kt == 7)), r=[slot_r, mixT_r], w=[bk_r])
                    cx.op(DVE, lambda e: e.tensor_tensor(out=self.xT[:, ft, :], in0=self.xT[:, ft, :], in1=bk[:], op=ALU.add), r=[bk_r, self.xT_r], w=[self.xT_r])
            cx.barrier()


    def final(self, b):
        cx = self.cx
        with ExitStack() as st:
            oT, oT_r = self.sb("oT", [128, 8, NB], F32, st)
            self.rmsnorm(4, oT, oT_r)
            self.store_block(b, oT, oT_r)

    def build(self):
        nc, cx = self.nc, self.cx
        self.declare_io()
        self.declare_units()
        self.wscr = nc.dram_tensor("wscr", [self.unit_off], BF16, kind="Internal").ap()
        self.wscr_res = Res("wscr")
        self.dbg_res = {}
        self.dbg_out = {}
        for nm in self.dbg:
            if nm == "s5setup":
                continue
            self.dbg_out[nm] = nc.dram_tensor("dbg_" + nm, [self.ntok, 1024], F32, kind="ExternalOutput").ap()
        self.y_res = Res("y")
        self.prologue()
        self.setup()
        if "even" in self.stages:
            self.setup_even()
        if "odd" in self.stages:
            self.setup_odd()
        cx.barrier(engines=cx.compute + [cx.SP], with_dma=True)
        self.plan_stream()
        for b in range(self.nblk):
            self.load_block(b)
            if "even" in self.stages:
                self.even(b)
            if "ffn0" in self.stages:
                self.ffn(0, b)
            if "odd" in self.stages:
                self.odd(b)
            if "ffn1" in self.stages:
                self.ffn(1, b)
            if "final" in self.stages:
                self.final(b)
            else:
                self.store_block(b, self.xT, self.xT_r)
        cx.barrier(engines=cx.compute + [cx.SP], with_dma=True)
        return nc


INPUT_NAMES = ["mix_norm", "ffn_norm", "final_norm", "ffn_w_up", "ffn_dw_w", "ffn_dw_b", "ffn_w_down",
               "e_w_in", "e_conv_w", "e_conv_b", "e_dt_bias", "e_a_log", "e_d", "e_ssm_norm", "e_w_out",
               "o_w_in", "o_dw_w", "o_dw_b", "o_ln_g", "o_ln_b", "o_a_re", "o_a_im", "o_b_re", "o_b_im", "o_c_re", "o_c_im",
               "o_d", "o_log_step", "o_glu_w", "o_w_out"]


def kernel(**inputs):
    bld = Builder()
    nc = bld.build()
    consts = make_consts()
    in_maps = []
    for core in range(8):
        bidx = core % 4
        m = {"x": np.ascontiguousarray(inputs["x"][bidx])}
        for k in INPUT_NAMES:
            m[k] = np.ascontiguousarray(inputs[k])
        for k, v in consts.items():
            m["c_" + k] = v
        in_maps.append(m)
    res = run_bass_kernel_spmd(nc, in_maps, core_ids=list(range(8)))
    out = np.stack([res.results[i]["y"] for i in range(4)], axis=0)
    return out.astype(np.float32)
```

```python
import math
from contextlib import ExitStack
import numpy as np
import concourse.bass as bass
import concourse.mybir as mybir
from concourse.bass_utils import run_bass_kernel_spmd

F32 = mybir.dt.float32
BF16 = mybir.dt.bfloat16
ALU = mybir.AluOpType
AF = mybir.ActivationFunctionType

D = 1024
SEQ = 4096
NB = 512
NCH = NB // 128
DFF = 2816
EPS = 1e-6
SLOT = 4096
NSLOT = 4
import os
SKIP = set(os.environ.get('MK_SKIP', '').split(','))
SSD_STOP = int(os.environ.get('MK_SSD_STOP', '99'))


class Sem:
    def __init__(self, h, name):
        self.h = h
        self.name = name
        self.count = 0


class Res:
    __slots__ = ("name", "w", "r", "dsem", "excl")

    def __init__(self, name, excl=False):
        self.name = name
        self.excl = excl
        self.w = {}
        self.r = {}
        self.dsem = None


class Eng:
    def __init__(self, name, eng, sem):
        self.name = name
        self.eng = eng
        self.sem = sem
        self.known = {}


class Ctx:
    def __init__(self, nc, stack):
        self.nc = nc
        self.stack = stack
        self.nsem = 0
        self.PE = self._eng("pe", nc.tensor)
        self.ACT = self._eng("act", nc.scalar)
        self.DVE = self._eng("dve", nc.vector)
        self.POOL = self._eng("pool", nc.gpsimd)
        self.SP = self._eng("sp", nc.sync)
        self.compute = [self.PE, self.ACT, self.DVE, self.POOL]
        self.dsems = []

    def new_sem(self, name):
        self.nsem += 1
        h = self.stack.enter_context(self.nc.semaphore(name))
        return Sem(h, name)

    def _eng(self, name, eng):
        return Eng(name, eng, self.new_sem("s_" + name))

    def _wait(self, E, need):
        for s, (v, snap) in need.items():
            if E.known.get(s, 0) >= v:
                continue
            E.eng.wait_ge(s.h, v)
            E.known[s] = v
            for s2, v2 in snap.items():
                if E.known.get(s2, 0) < v2:
                    E.known[s2] = v2

    def _collect(self, E, reads, writes):
        need = {}

        def req(s, ev):
            if s not in need or need[s][0] < ev[0]:
                need[s] = ev

        for r in reads:
            for s, ev in r.w.items():
                req(s, ev)
            if r.excl:
                for s, ev in r.r.items():
                    if s is not E.sem:
                        req(s, ev)
        for w in writes:
            for s, ev in w.w.items():
                if s is E.sem:
                    continue
                req(s, ev)
            for s, ev in w.r.items():
                if s is E.sem:
                    continue
                req(s, ev)
        return need

    def op(self, E, fn, r=(), w=()):
        self._wait(E, self._collect(E, r, w))
        ins = fn(E.eng)
        E.sem.count += 1
        ins.then_inc(E.sem.h, 1)
        ev = (E.sem.count, dict(E.known))
        for x in r:
            x.r[E.sem] = ev
        for x in w:
            x.w[E.sem] = ev
            x.r = {}
        return ins

    def dma(self, E, out, in_, r=(), w=(), sem_res=None, **kw):
        self._wait(E, self._collect(E, r, w))
        if sem_res.dsem is None:
            sem_res.dsem = self.new_sem("d_" + sem_res.name)
            self.dsems.append(sem_res.dsem)
        ds = sem_res.dsem
        ins = E.eng.dma_start(out=out, in_=in_, **kw)
        ds.count += 16
        ins.then_inc(ds.h, 16)
        ev = (ds.count, dict(E.known))
        for x in r:
            x.r[ds] = ev
        for x in w:
            x.w[ds] = ev
            x.r = {}
        return ins

    def barrier(self, engines=None, with_dma=False):
        engines = engines or self.compute
        for E in engines:
            need = {}
            for E2 in engines:
                if E2 is not E and E2.sem.count > 0:
                    need[E2.sem] = (E2.sem.count, dict(E2.known))
            if with_dma:
                for ds in self.dsems:
                    if ds.count > 0:
                        need[ds] = (ds.count, {})
            self._wait(E, need)


def make_consts():
    c = {}
    c["ident_f"] = np.eye(128, dtype=np.float32)
    c["ones_f"] = np.ones((128, 128), dtype=np.float32)
    inv = (10000.0 ** (-np.arange(0, 128, 2, dtype=np.float32) / np.float32(128))).astype(np.float32)
    pos = np.arange(SEQ, dtype=np.float32)
    ang = (pos[:, None] * inv[None, :]).astype(np.float32).astype(np.float64)
    c["cosT"] = np.ascontiguousarray(np.cos(ang).reshape(SEQ // 128, 128, 64).transpose(1, 0, 2)).astype(np.float32)
    c["sinT"] = np.ascontiguousarray(np.sin(ang).reshape(SEQ // 128, 128, 64).transpose(1, 0, 2)).astype(np.float32)
    gam = 1.0 - 2.0 ** (-5.0 - np.arange(4, dtype=np.float64))
    idx = np.arange(128, dtype=np.float64)
    diff = idx[None, :] - idx[:, None]
    dm = np.where(diff[None] >= 0, gam[:, None, None] ** np.maximum(diff, 0)[None], 0.0) * (128 ** -0.5)
    c["dmaskT"] = np.ascontiguousarray(dm.transpose(1, 0, 2)).reshape(128, 512).astype(np.float32)
    c["xi"] = (gam[None, :] ** (idx[:, None] + 1)).astype(np.float32)
    c["zeta"] = ((gam[None, :] ** (127 - idx[:, None])) * (128 ** -0.5)).astype(np.float32)
    tri = (idx[:, None] <= idx[None, :]).astype(np.float32)
    c["tri"] = tri
    c["su"] = (1.0 - tri).astype(np.float32)
    sj = np.arange(128) // 16
    c["tzmask"] = (sj[None, :] >= sj[:, None]).astype(np.float32)
    return c


RET_GDEC = [float((1.0 - 2.0 ** (-5.0 - h)) ** 128) for h in range(4)]


class Builder:
    def __init__(self, nblk=SEQ // NB, stages=("even", "ffn0", "odd", "ffn1", "final"), dbg=()):
        self.nblk = nblk
        self.stages = stages
        self.dbg = dbg
        self.ntok = nblk * NB
        self.nc = bass.Bass("TRN2", target_bir_lowering=False)
        self.stack = ExitStack()
        self.cx = Ctx(self.nc, self.stack)
        self.units = []
        self.unit_off = 0
        self.stream_order = []
        self.bank_i = 0
        self.bankb_i = 0

    def din(self, name, shape, dt=F32):
        return self.nc.dram_tensor(name, list(shape), dt, kind="ExternalInput").ap()

    def declare_io(self):
        nc = self.nc
        self.x = self.din("x", [self.ntok, D])
        self.y = nc.dram_tensor("y", [self.ntok, D], F32, kind="ExternalOutput").ap()
        self.mix_norm = self.din("mix_norm", [2, D])
        self.ffn_norm = self.din("ffn_norm", [2, D])
        self.final_norm = self.din("final_norm", [D])
        self.ffn_w_up = self.din("ffn_w_up", [2, D, 2 * DFF])
        self.ffn_dw_w = self.din("ffn_dw_w", [2, 3, 2 * DFF])
        self.ffn_dw_b = self.din("ffn_dw_b", [2, 2 * DFF])
        self.ffn_w_down = self.din("ffn_w_down", [2, DFF, D])
        self.e_w_in = self.din("e_w_in", [1, D, 5648])
        self.e_conv_w = self.din("e_conv_w", [1, 4, 1536])
        self.e_conv_b = self.din("e_conv_b", [1, 1536])
        self.e_dt_bias = self.din("e_dt_bias", [1, 16])
        self.e_a_log = self.din("e_a_log", [1, 16])
        self.e_d = self.din("e_d", [1, 16])
        self.e_ssm_norm = self.din("e_ssm_norm", [1, 1024])
        self.e_w_out = self.din("e_w_out", [1, 2048, 1024])
        self.o_w_in = self.din("o_w_in", [1, D, 1536])
        self.o_dw_w = self.din("o_dw_w", [1, 31, 512])
        self.o_dw_b = self.din("o_dw_b", [1, 512])
        self.o_ln_g = self.din("o_ln_g", [1, 512])
        self.o_ln_b = self.din("o_ln_b", [1, 512])
        self.o_a_re = self.din("o_a_re", [1, 32, 64])
        self.o_a_im = self.din("o_a_im", [1, 32, 64])
        self.o_b_re = self.din("o_b_re", [1, 32, 64, 16])
        self.o_b_im = self.din("o_b_im", [1, 32, 64, 16])
        self.o_c_re = self.din("o_c_re", [1, 32, 16, 64])
        self.o_c_im = self.din("o_c_im", [1, 32, 16, 64])
        self.o_d = self.din("o_d", [1, 512])
        self.o_log_step = self.din("o_log_step", [1, 32])
        self.o_glu_w = self.din("o_glu_w", [1, 512, 512])
        self.o_w_out = self.din("o_w_out", [1, D, D])
        self.consts = {}
        for k, v in make_consts().items():
            self.consts[k] = self.din("c_" + k, v.shape)

    def sb(self, name, shape, dt=F32, stack=None):
        self.name_i = getattr(self, "name_i", 0) + 1
        t = (stack or self.stack).enter_context(self.nc.sbuf_tensor(f"{name}_{self.name_i}", list(shape), dt))
        return t, Res(name)

    def ps(self, name, shape, dt=F32):
        t = self.stack.enter_context(self.nc.psum_tensor(name, list(shape), dt))
        return t, Res(name, excl=True)

    def bank(self):
        i = self.bank_i
        self.bank_i = (i + 1) % len(self.banks)
        return self.banks[i]

    def bankb(self):
        i = self.bankb_i
        self.bankb_i = (i + 1) % len(self.banksb)
        return self.banksb[i]

    def add_unit(self, name, pieces):
        L = sum(kt * w for (_, kt, _, w) in pieces)
        assert L <= SLOT, (name, L)
        u = dict(name=name, L=L, off=self.unit_off, pieces=pieces)
        self.unit_off += 128 * L
        self.units.append(u)
        return u

    def declare_units(self):
        self.U = {}
        for l in range(2):
            wup = self.ffn_w_up[l]
            for j in range(22):
                self.U[("up", l, j)] = self.add_unit(f"up{l}_{j}", [(wup, 8, j * 128, 128), (wup, 8, (22 + j) * 128, 128)])
            wdn = self.ffn_w_down[l]
            for ft in range(8):
                self.U[("dn", l, ft)] = self.add_unit(f"dn{l}_{ft}", [(wdn, 22, ft * 128, 128)])

        w = self.e_w_in[0]
        for i, nm in enumerate(["eq", "ek", "ev0", "ev1", "eg0", "eg1", "ez0", "ez1"]):
            self.U[nm] = self.add_unit(nm, [(w, 8, 512 * i, 512)])
        self.U["edt"] = self.add_unit("edt", [(w, 8, 5632, 16)])
        for i in range(6):
            self.U[("exbc", i)] = self.add_unit(f"exbc{i}", [(w, 8, 4096 + (2 * i) * 128, 128), (w, 8, 4096 + (2 * i + 1) * 128, 128)])
        for ft in range(8):
            self.U[("eout", ft)] = self.add_unit(f"eout{ft}", [(self.e_w_out[0], 16, ft * 128, 128)])

        w = self.o_w_in[0]
        for i in range(4):
            self.U[("oin", i)] = self.add_unit(f"oin{i}", [(w, 8, i * 128, 128), (w, 8, 512 + i * 128, 128)])
        self.U["ou"] = self.add_unit("ou", [(w, 8, 1024, 512)])
        self.U["oglu"] = self.add_unit("oglu", [(self.o_glu_w[0], 4, ot * 128, 128) for ot in range(4)])
        for i in range(4):
            self.U[("oout", i)] = self.add_unit(f"oout{i}", [(self.o_w_out[0], 8, (2 * i) * 128, 128), (self.o_w_out[0], 8, (2 * i + 1) * 128, 128)])
        self.n_cast_units = len(self.units)
        for nm in [("s5bc", 0), ("s5bc", 1), ("s5cc", 0), ("s5cc", 1), "s5tz"]:
            self.U[nm] = dict(name=str(nm), L=4096, off=self.unit_off, pieces=[])
            self.unit_off += 128 * 4096

    def plan_stream(self):
        order = []
        for b in range(self.nblk):
            if "even" in self.stages:
                order += [self.U[nm] for nm in ["eq", "ek", "ev0", "ev1", "eg0", "eg1", "ez0", "ez1", "edt"]]
                order += [self.U[("exbc", i)] for i in range(6)]
                order += [self.U[("eout", ft)] for ft in range(8)]
            for l in range(2):
                if l == 1 and "odd" in self.stages:
                    order += [self.U["ou"], self.U[("s5bc", 0)], self.U[("s5bc", 1)]]
                    order += [self.U[("oin", i)] for i in range(4)]
                    order += [self.U["s5tz"], self.U[("s5cc", 0)], self.U[("s5cc", 1)], self.U["oglu"]]
                    order += [self.U[("oout", i)] for i in range(4)]
                if f"ffn{l}" in self.stages:
                    order += [self.U[("up", l, j)] for j in range(22)]
                    order += [self.U[("dn", l, ft)] for ft in range(8)]
        self.stream_order = order
        self.stream_next_load = 0
        self.stream_next_use = 0

    def stream_get(self, hold_prev=0):
        cx = self.cx
        i = self.stream_next_use
        self.stream_next_use += 1
        lim = min(i - hold_prev + NSLOT - 1, len(self.stream_order) - 1)
        while self.stream_next_load <= lim:
            k = self.stream_next_load
            u = self.stream_order[k]
            st, sr = self.slots[k % NSLOT]
            src = self.wscr[u["off"]: u["off"] + 128 * u["L"]].rearrange("(p l) -> p l", p=128)
            cx.dma(cx.SP, st[:, 0:u["L"]], src, w=[sr], sem_res=sr)
            self.stream_next_load += 1
        st, sr = self.slots[i % NSLOT]
        assert self.stream_order[i] is not None
        return st, sr, self.stream_order[i]

    def prologue(self):
        cx, nc = self.cx, self.nc
        with ExitStack() as st:
            NS = 3
            stg32 = [self.sb(f"stg32_{i}", [128, SLOT], F32, st) for i in range(NS)]
            stg16 = [self.sb(f"stg16_{i}", [128, SLOT], BF16, st) for i in range(NS)]
            cast_engs = [cx.ACT, cx.DVE, cx.POOL]
            for i, u in enumerate(self.units[:self.n_cast_units]):
                t32, r32 = stg32[i % NS]
                t16, r16 = stg16[i % NS]
                off = 0
                for (src, kt, c0, w) in u["pieces"]:
                    dst = t32[:, off: off + kt * w].rearrange("p (k w) -> p k w", k=kt)
                    s = src.rearrange("(k p) n -> p k n", p=128)[:, :, c0:c0 + w]
                    cx.dma(cx.SP, dst, s, w=[r32], sem_res=r32)
                    off += kt * w
                L = u["L"]
                E = cast_engs[i % 3]
                if E is cx.ACT:
                    cx.op(E, lambda e: e.copy(out=t16[:, 0:L], in_=t32[:, 0:L]), r=[r32], w=[r16])
                else:
                    cx.op(E, lambda e: e.tensor_copy(out=t16[:, 0:L], in_=t32[:, 0:L]), r=[r32], w=[r16])
                dst = self.wscr[u["off"]: u["off"] + 128 * L].rearrange("(p l) -> p l", p=128)
                cx.dma(cx.ACT if False else cx.SP, dst, t16[:, 0:L], r=[r16], w=[self.wscr_res], sem_res=r16)
            cx.barrier(engines=cx.compute + [cx.SP], with_dma=True)

    def setup(self):
        cx, nc = self.cx, self.nc
        self.xT, self.xT_r = self.sb("xT", [128, 8, NB], F32)
        self.hnT, self.hnT_r = self.sb("hnT", [128, 8, NB], BF16)
        self.ident_f, self.ident_f_r = self.sb("ident_f", [128, 128], F32)
        self.ones_f, self.ones_f_r = self.sb("ones_f", [128, 128], F32)
        self.gains, self.gains_r = self.sb("gains", [128, 5, 8], F32)
        self.slots = [self.sb(f"wslot{i}", [128, SLOT], BF16) for i in range(NSLOT)]
        self.banks = [self.ps(f"bank{i}", [128, 512], F32) for i in range(6)]
        self.banksb = [self.ps(f"bankb{i}", [128, 1024], BF16) for i in range(2)]
        self.ffn_halo, self.ffn_halo_r = self.sb("ffn_halo", [128, 2, 44, 2], F32)
        self.ffn_cw, self.ffn_cw_r = self.sb("ffn_cw", [128, 2, 44, 3], F32)
        self.ffn_cb, self.ffn_cb_r = self.sb("ffn_cb", [128, 2, 44], F32)
        self.eps_t, self.eps_r = self.sb("eps_t", [128, 1], F32)
        nc.allow_non_contiguous_dma(reason="small parameter loads")
        SP = cx.SP
        cx.dma(SP, self.ident_f[:], self.consts["ident_f"], w=[self.ident_f_r], sem_res=self.ident_f_r)
        cx.dma(SP, self.ones_f[:], self.consts["ones_f"], w=[self.ones_f_r], sem_res=self.ones_f_r)
        gsrc = [self.mix_norm[0], self.ffn_norm[0], self.mix_norm[1], self.ffn_norm[1], self.final_norm]
        for i, g in enumerate(gsrc):
            cx.dma(SP, self.gains[:, i, :], g.rearrange("(t p) -> p t", p=128), w=[self.gains_r], sem_res=self.gains_r, allow_slow_non_contiguous=True)
        for l in range(2):
            for k in range(3):
                cx.dma(SP, self.ffn_cw[:, l, :, k], self.ffn_dw_w[l, k].rearrange("(t p) -> p t", p=128),
                       w=[self.ffn_cw_r], sem_res=self.ffn_cw_r, allow_slow_non_contiguous=True)
            cx.dma(SP, self.ffn_cb[:, l, :], self.ffn_dw_b[l].rearrange("(t p) -> p t", p=128),
                   w=[self.ffn_cb_r], sem_res=self.ffn_cb_r, allow_slow_non_contiguous=True)
        cx.op(cx.DVE, lambda e: e.memset(self.ffn_halo[:], 0.0), w=[self.ffn_halo_r])
        cx.op(cx.DVE, lambda e: e.memset(self.eps_t[:], EPS), w=[self.eps_r])
        self.one_t, _ = self.sb("one_t", [128, 1], F32)
        cx.op(cx.DVE, lambda e: e.memset(self.one_t[:], 1.0), w=[self.eps_r])

    def load_block(self, b):
        cx = self.cx
        with ExitStack() as st:
            xin, xin_r = self.sb("xin", [128, NCH, D], F32, st)
            src = self.x[b * NB:(b + 1) * NB, :].rearrange("(c p) d -> p c d", p=128)
            cx.dma(cx.SP, xin[:], src, w=[xin_r], sem_res=self.xT_r)
            for ft in range(8):
                bk, bk_r = self.bank()
                for c in range(NCH):
                    cx.op(cx.PE, lambda e: e.transpose(bk[:, c * 128:(c + 1) * 128], xin[:, c, ft * 128:(ft + 1) * 128], self.ident_f[:]),
                          r=[xin_r, self.ident_f_r], w=[bk_r])
                if ft % 2 == 0:
                    cx.op(cx.ACT, lambda e: e.copy(out=self.xT[:, ft, :], in_=bk[:]), r=[bk_r], w=[self.xT_r])
                else:
                    cx.op(cx.DVE, lambda e: e.tensor_copy(out=self.xT[:, ft, :], in_=bk[:]), r=[bk_r], w=[self.xT_r])
            cx.barrier()

    def store_block(self, b, srcT, srcT_r):
        cx = self.cx
        with ExitStack() as st:
            yo, yo_r = self.sb("yo", [128, NCH, D], F32, st)
            k = 0
            for c in range(NCH):
                for half in range(2):
                    bk, bk_r = self.bank()
                    for f4 in range(4):
                        ft = half * 4 + f4
                        cx.op(cx.PE, lambda e: e.transpose(bk[:, f4 * 128:(f4 + 1) * 128], srcT[:, ft, c * 128:(c + 1) * 128], self.ident_f[:]),
                              r=[srcT_r, self.ident_f_r], w=[bk_r])
                    if k % 2 == 0:
                        cx.op(cx.ACT, lambda e: e.copy(out=yo[:, c, half * 512:(half + 1) * 512], in_=bk[:]), r=[bk_r], w=[yo_r])
                    else:
                        cx.op(cx.DVE, lambda e: e.tensor_copy(out=yo[:, c, half * 512:(half + 1) * 512], in_=bk[:]), r=[bk_r], w=[yo_r])
                    k += 1
            dst = self.y[b * NB:(b + 1) * NB, :].rearrange("(c p) d -> p c d", p=128)
            cx.dma(cx.SP, dst, yo[:], r=[yo_r], w=[self.y_res], sem_res=self.y_res)
            cx.barrier(engines=cx.compute + [cx.SP], with_dma=True)

    def rmsnorm(self, gi, outT, outT_r, out_f32=False):
        cx = self.cx
        with ExitStack() as st:
            sq = [self.sb(f"sq{i}", [128, NB], F32, st) for i in range(2)]
            rstd, rstd_r = self.sb("rstd", [128, NB], F32, st)
            bk, bk_r = self.bank()
            for ft in range(8):
                s, s_r = sq[ft % 2]
                cx.op(cx.ACT, lambda e: e.activation(out=s[:], in_=self.xT[:, ft, :], func=AF.Square), r=[self.xT_r], w=[s_r])
                cx.op(cx.PE, lambda e: e.matmul(bk[:], lhsT=self.ones_f[:], rhs=s[:], start=(ft == 0), stop=(ft == 7)),
                      r=[s_r, self.ones_f_r], w=[bk_r])
            cx.op(cx.ACT, lambda e: e.activation(out=rstd[:], in_=bk[:], func=AF.Sqrt, bias=self.eps_t[:, 0:1], scale=1.0 / D),
                  r=[bk_r, self.eps_r], w=[rstd_r])
            cx.op(cx.DVE, lambda e: e.reciprocal(out=rstd[:], in_=rstd[:]), r=[rstd_r], w=[rstd_r])
            for ft in range(8):
                E = cx.DVE
                cx.op(E, lambda e: e.scalar_tensor_tensor(out=outT[:, ft, :], in0=self.xT[:, ft, :], scalar=self.gains[:, gi, ft:ft + 1],
                                                          in1=rstd[:], op0=ALU.mult, op1=ALU.mult),
                      r=[self.xT_r, self.gains_r, rstd_r], w=[outT_r])
            cx.barrier()

    def ffn(self, l, b):
        cx = self.cx
        self.rmsnorm(1 + 2 * l, self.hnT, self.hnT_r)
        with ExitStack() as st:
            hT, hT_r = self.sb("hT", [128, 22, NB], BF16, st)
            raws = [self.sb(f"raw{i}", [128, NB + 2], F32, st) for i in range(4)]
            accs = [self.sb(f"acc{i}", [128, NB], F32, st) for i in range(4)]
            sg = [self.sb(f"sgate{i}", [128, NB], F32, st) for i in range(2)]
            for j in range(22):
                slot, slot_r, u = self.stream_get()
                assert u is self.U[("up", l, j)]
                res = []
                for gi in range(2):
                    tile = j + 22 * gi
                    wv = slot[:, gi * 1024:(gi + 1) * 1024].rearrange("p (k w) -> p k w", k=8)
                    bk, bk_r = self.bank()
                    for kt in range(8):
                        cx.op(cx.PE, lambda e: e.matmul(bk[:], lhsT=wv[:, kt, :], rhs=self.hnT[:, kt, :], start=(kt == 0), stop=(kt == 7)),
                              r=[slot_r, self.hnT_r], w=[bk_r])
                    raw, raw_r = raws[(2 * j + gi) % 4]
                    acc, acc_r = accs[(2 * j + gi) % 4]
                    cw = self.ffn_cw[:, l, tile, :]
                    cx.op(cx.ACT, lambda e: e.copy(out=raw[:, 2:NB + 2], in_=bk[:]), r=[bk_r], w=[raw_r])
                    cx.op(cx.POOL, lambda e: e.tensor_copy(out=raw[:, 0:2], in_=self.ffn_halo[:, l, tile, :]), r=[self.ffn_halo_r], w=[raw_r])
                    cx.op(cx.ACT, lambda e: e.activation(out=acc[:], in_=bk[:], func=AF.Identity, bias=self.ffn_cb[:, l, tile:tile + 1],
                                                         scale=cw[:, 2:3]),
                          r=[bk_r, self.ffn_cw_r, self.ffn_cb_r], w=[acc_r])
                    E = cx.DVE
                    cx.op(E, lambda e: e.scalar_tensor_tensor(out=acc[:], in0=raw[:, 1:NB + 1], scalar=cw[:, 1:2], in1=acc[:],
                                                              op0=ALU.mult, op1=ALU.add),
                          r=[raw_r, acc_r, self.ffn_cw_r], w=[acc_r])
                    cx.op(E, lambda e: e.scalar_tensor_tensor(out=acc[:], in0=raw[:, 0:NB], scalar=cw[:, 0:1], in1=acc[:],
                                                              op0=ALU.mult, op1=ALU.add),
                          r=[raw_r, acc_r, self.ffn_cw_r], w=[acc_r])
                    cx.op(cx.POOL, lambda e: e.tensor_copy(out=self.ffn_halo[:, l, tile, :], in_=raw[:, NB:NB + 2]), r=[raw_r], w=[self.ffn_halo_r])
                    res.append((acc, acc_r))
                s, s_r = sg[j % 2]
                cx.op(cx.ACT, lambda e: e.activation(out=s[:], in_=res[0][0][:], func=AF.Silu), r=[res[0][1]], w=[s_r])
                cx.op(cx.DVE, lambda e: e.tensor_tensor(out=hT[:, j, :], in0=s[:], in1=res[1][0][:], op=ALU.mult),
                      r=[s_r, res[1][1]], w=[hT_r])
            for ft in range(8):
                slot, slot_r, u = self.stream_get()
                assert u is self.U[("dn", l, ft)]
                wv = slot[:, 0:22 * 128].rearrange("p (k w) -> p k w", k=22)
                bk, bk_r = self.bank()
                for kt in range(22):
                    cx.op(cx.PE, lambda e: e.matmul(bk[:], lhsT=wv[:, kt, :], rhs=hT[:, kt, :], start=(kt == 0), stop=(kt == 21)),
                          r=[slot_r, hT_r], w=[bk_r])
                cx.op(cx.DVE, lambda e: e.tensor_tensor(out=self.xT[:, ft, :], in0=self.xT[:, ft, :], in1=bk[:], op=ALU.add),
                      r=[bk_r, self.xT_r], w=[self.xT_r])
            cx.barrier()

    def bc_mid(self, ap2, n):
        a = ap2.shape[1]
        return ap2.unsqueeze(2).to_broadcast([128, a, n])

    def bc_h(self, ap2, h):
        n = ap2.shape[1]
        return ap2.unsqueeze(1).to_broadcast([128, h, n])

    def setup_even(self):
        cx = self.cx
        SP = cx.SP
        self.retS, self.retS_r = self.sb("retS", [128, 4, 256], F32)
        self.retSb, self.retSb_r = self.sb("retSb", [128, 4, 256], BF16)
        self.ssS, self.ssS_r = self.sb("ssS", [128, 2, 512], F32)
        self.ssSb, self.ssSb_r = self.sb("ssSb", [128, 2, 512], BF16)
        self.exh, self.exh_r = self.sb("exh", [128, 12, 3], F32)
        self.exw, self.exw_r = self.sb("exw", [128, 12, 4], F32)
        self.exb, self.exb_r = self.sb("exb", [128, 12], F32)
        self.dmaskT, self.dmaskT_r = self.sb("dmaskT", [128, 512], F32)
        self.xi, self.xi_r = self.sb("xi", [128, 4], F32)
        self.zeta, self.zeta_r = self.sb("zeta", [128, 4], F32)
        self.tri, self.tri_r = self.sb("tri", [128, 128], F32)
        self.su, self.su_r = self.sb("su", [128, 128], F32)
        self.ident_b, self.ident_b_r = self.sb("ident_b", [128, 128], BF16)
        self.dtb, self.dtb_r = self.sb("dtb", [128, 16], F32)
        self.aneg, self.aneg_r = self.sb("aneg", [128, 16], F32)
        self.dsk, self.dsk_r = self.sb("dsk", [128, 16], F32)
        self.ssmn, self.ssmn_r = self.sb("ssmn", [128, 8], F32)
        for t, r, nm in [(self.dmaskT, self.dmaskT_r, "dmaskT"), (self.xi, self.xi_r, "xi"), (self.zeta, self.zeta_r, "zeta"),
                         (self.tri, self.tri_r, "tri"), (self.su, self.su_r, "su")]:
            cx.dma(SP, t[:], self.consts[nm], w=[r], sem_res=r)
        cx.dma(SP, self.dtb[:], self.e_dt_bias[0].partition_broadcast(128), w=[self.dtb_r], sem_res=self.dtb_r, allow_slow_non_contiguous=True)
        cx.dma(SP, self.aneg[:], self.e_a_log[0].partition_broadcast(128), w=[self.aneg_r], sem_res=self.aneg_r, allow_slow_non_contiguous=True)
        cx.dma(SP, self.dsk[:], self.e_d[0].partition_broadcast(128), w=[self.dsk_r], sem_res=self.dsk_r, allow_slow_non_contiguous=True)
        cx.dma(SP, self.ssmn[:], self.e_ssm_norm[0].rearrange("(t p) -> p t", p=128), w=[self.ssmn_r], sem_res=self.ssmn_r,
               allow_slow_non_contiguous=True)
        for k in range(4):
            cx.dma(SP, self.exw[:, :, k], self.e_conv_w[0, k].rearrange("(t p) -> p t", p=128), w=[self.exw_r], sem_res=self.exw_r,
                   allow_slow_non_contiguous=True)
        cx.dma(SP, self.exb[:], self.e_conv_b[0].rearrange("(t p) -> p t", p=128), w=[self.exb_r], sem_res=self.exb_r,
               allow_slow_non_contiguous=True)
        cx.op(cx.ACT, lambda e: e.activation(out=self.aneg[:], in_=self.aneg[:], func=AF.Exp), r=[self.aneg_r], w=[self.aneg_r])
        cx.op(cx.DVE, lambda e: e.tensor_scalar(out=self.aneg[:], in0=self.aneg[:], scalar1=-1.0, scalar2=None, op0=ALU.mult),
              r=[self.aneg_r], w=[self.aneg_r])
        cx.op(cx.DVE, lambda e: e.tensor_copy(out=self.ident_b[:], in_=self.ident_f[:]), r=[self.ident_f_r], w=[self.ident_b_r])
        cx.op(cx.DVE, lambda e: e.memset(self.retS[:], 0.0), w=[self.retS_r])
        cx.op(cx.DVE, lambda e: e.memset(self.retSb[:], 0.0), w=[self.retSb_r])
        cx.op(cx.POOL, lambda e: e.memset(self.ssS[:], 0.0), w=[self.ssS_r])
        cx.op(cx.POOL, lambda e: e.memset(self.ssSb[:], 0.0), w=[self.ssSb_r])
        cx.op(cx.POOL, lambda e: e.memset(self.exh[:], 0.0), w=[self.exh_r])

    def dump(self, name, src_ap, src_r, dst_ap):
        cx = self.cx
        r = self.dbg_res.setdefault(name, Res("dbg_" + name))
        cx.dma(cx.SP, dst_ap, src_ap, r=[src_r], w=[r], sem_res=r)

    def even(self, b):
        cx = self.cx
        PE, ACT, DVE, POOL, SP = cx.PE, cx.ACT, cx.DVE, cx.POOL, cx.SP
        self.rmsnorm(0, self.hnT, self.hnT_r)
        hnT, hnT_r = self.hnT, self.hnT_r
        with ExitStack() as st:
            B = {}
            for nm, shp, dt in [("qr", [128, NCH, 512], BF16), ("qx", [128, NCH, 512], BF16), ("kr", [128, NCH, 512], BF16),
                                ("kz", [128, NCH, 512], BF16), ("v", [128, NCH, 1024], BF16), ("sg", [128, NCH, 1024], BF16),
                                ("sz", [128, NCH, 1024], BF16), ("xcT", [128, 8, NB], BF16), ("bcT", [128, 4, NB], BF16),
                                ("dt", [128, NCH, 16], F32), ("dta", [128, NCH, 16], F32), ("mixT", [128, 16, NB], BF16),
                                ("cos", [128, NCH, 64], F32), ("sin", [128, NCH, 64], F32)]:
                B[nm] = self.sb(nm, shp, dt, st)
            self.EB = B
            cos, cos_r = B["cos"]
            sin, sin_r = B["sin"]
            cx.dma(SP, cos[:], self.consts["cosT"][:, b * NCH:(b + 1) * NCH, :], w=[cos_r], sem_res=self.xi_r)
            cx.dma(SP, sin[:], self.consts["sinT"][:, b * NCH:(b + 1) * NCH, :], w=[sin_r], sem_res=self.xi_r)
            rot_t = [self.sb(f"rot{i}", [128, 4, 64], F32, st) for i in range(4)]
            qraw = [self.sb(f"qraw{i}", [128, 512], F32, st) for i in range(2)]
            k_ev = 0
            for nm in ["eq", "ek", "ev0", "ev1", "eg0", "eg1", "ez0", "ez1"]:
                slot, slot_r, u = self.stream_get()
                assert u is self.U[nm]
                wv = slot[:, 0:4096].rearrange("p (k w) -> p k w", k=8)
                for c in range(NCH):
                    bk, bk_r = self.bank()
                    for kt in range(8):
                        cx.op(PE, lambda e: e.matmul(bk[:], lhsT=hnT[:, kt, c * 128:(c + 1) * 128], rhs=wv[:, kt, :], start=(kt == 0), stop=(kt == 7)),
                              r=[slot_r, hnT_r], w=[bk_r])
                    if nm in ("eq", "ek"):
                        raw, raw_r = qraw[k_ev % 2]
                        k_ev += 1
                        cx.op(ACT, lambda e: e.copy(out=raw[:], in_=bk[:]), r=[bk_r], w=[raw_r])
                        r4 = raw[:].rearrange("p (h t d) -> p h t d", h=4, t=2)
                        x1, x2 = r4[:, :, 0, :], r4[:, :, 1, :]
                        cb = self.bc_h(cos[:, c, :], 4)
                        sbb = self.bc_h(sin[:, c, :], 4)
                        (t1, t1r), (t2, t2r), (t3, t3r), (t4, t4r) = rot_t
                        cx.op(POOL, lambda e: e.tensor_tensor(out=t1[:], in0=x1, in1=cb, op=ALU.mult), r=[raw_r, cos_r], w=[t1r])
                        cx.op(POOL, lambda e: e.tensor_tensor(out=t2[:], in0=x2, in1=sbb, op=ALU.mult), r=[raw_r, sin_r], w=[t2r])
                        cx.op(POOL, lambda e: e.tensor_tensor(out=t3[:], in0=x1, in1=sbb, op=ALU.mult), r=[raw_r, sin_r], w=[t3r])
                        cx.op(POOL, lambda e: e.tensor_tensor(out=t4[:], in0=x2, in1=cb, op=ALU.mult), r=[raw_r, cos_r], w=[t4r])
                        dst, dst_r = B["qr"] if nm == "eq" else B["kr"]
                        d4 = dst[:, c, :].rearrange("p (h t d) -> p h t d", h=4, t=2)
                        cx.op(DVE, lambda e: e.tensor_tensor(out=d4[:, :, 0, :], in0=t1[:], in1=t2[:], op=ALU.subtract), r=[t1r, t2r], w=[dst_r])
                        cx.op(DVE, lambda e: e.tensor_tensor(out=d4[:, :, 1, :], in0=t3[:], in1=t4[:], op=ALU.add), r=[t3r, t4r], w=[dst_r])
                        d2, d2_r = B["qx"] if nm == "eq" else B["kz"]
                        tab, tab_r = (self.xi, self.xi_r) if nm == "eq" else (self.zeta, self.zeta_r)
                        cx.op(DVE, lambda e: e.tensor_tensor(out=d2[:, c, :].rearrange("p (h d) -> p h d", h=4),
                                                             in0=dst[:, c, :].rearrange("p (h d) -> p h d", h=4),
                                                             in1=self.bc_mid(tab[:], 128), op=ALU.mult),
                              r=[dst_r, tab_r], w=[d2_r])
                    else:
                        half = int(nm[-1])
                        key = {"v": "v", "g": "sg", "z": "sz"}[nm[1]]
                        dst, dst_r = B[key]
                        if key == "v":
                            cx.op(ACT, lambda e: e.copy(out=dst[:, c, half * 512:(half + 1) * 512], in_=bk[:]), r=[bk_r], w=[dst_r])
                        else:
                            cx.op(ACT, lambda e: e.activation(out=dst[:, c, half * 512:(half + 1) * 512], in_=bk[:], func=AF.Silu), r=[bk_r], w=[dst_r])
            slot, slot_r, u = self.stream_get()
            assert u is self.U["edt"]
            wv = slot[:, 0:128].rearrange("p (k w) -> p k w", k=8)
            dtv, dtv_r = self.sb("dtv", [128, 16], F32, st)
            dt, dt_r = B["dt"]
            dta, dta_r = B["dta"]
            for c in range(NCH):
                bk, bk_r = self.bank()
                for kt in range(8):
                    cx.op(PE, lambda e: e.matmul(bk[:, 0:16], lhsT=hnT[:, kt, c * 128:(c + 1) * 128], rhs=wv[:, kt, :], start=(kt == 0), stop=(kt == 7)),
                          r=[slot_r, hnT_r], w=[bk_r])
                cx.op(DVE, lambda e: e.tensor_tensor(out=dtv[:], in0=bk[:, 0:16], in1=self.dtb[:], op=ALU.add), r=[bk_r, self.dtb_r], w=[dtv_r])
                cx.op(ACT, lambda e: e.activation(out=dtv[:], in_=dtv[:], func=AF.Exp), r=[dtv_r], w=[dtv_r])
                cx.op(ACT, lambda e: e.activation(out=dt[:, c, :], in_=dtv[:], func=AF.Ln, bias=self.one_t[:, 0:1]), r=[dtv_r, self.eps_r], w=[dt_r])
                cx.op(DVE, lambda e: e.tensor_tensor(out=dta[:, c, :], in0=dt[:, c, :], in1=self.aneg[:], op=ALU.mult), r=[dt_r, self.aneg_r], w=[dta_r])
            raws = [self.sb(f"xraw{i}", [128, NB + 3], F32, st) for i in range(2)]
            accs = [self.sb(f"xacc{i}", [128, NB], F32, st) for i in range(2)]
            xcT, xcT_r = B["xcT"]
            bcT, bcT_r = B["bcT"]
            for i in range(6):
                slot, slot_r, u = self.stream_get()
                assert u is self.U[("exbc", i)]
                for gi in range(2):
                    tile = 2 * i + gi
                    wv = slot[:, gi * 1024:(gi + 1) * 1024].rearrange("p (k w) -> p k w", k=8)
                    bk, bk_r = self.bank()
                    for kt in range(8):
                        cx.op(PE, lambda e: e.matmul(bk[:], lhsT=wv[:, kt, :], rhs=hnT[:, kt, :], start=(kt == 0), stop=(kt == 7)),
                              r=[slot_r, hnT_r], w=[bk_r])
                    raw, raw_r = raws[tile % 2]
                    acc, acc_r = accs[tile % 2]
                    cw = self.exw[:, tile, :]
                    cx.op(ACT, lambda e: e.copy(out=raw[:, 3:NB + 3], in_=bk[:]), r=[bk_r], w=[raw_r])
                    cx.op(POOL, lambda e: e.tensor_copy(out=raw[:, 0:3], in_=self.exh[:, tile, :]), r=[self.exh_r], w=[raw_r])
                    cx.op(ACT, lambda e: e.activation(out=acc[:], in_=bk[:], func=AF.Identity, bias=self.exb[:, tile:tile + 1], scale=cw[:, 3:4]),
                          r=[bk_r, self.exw_r, self.exb_r], w=[acc_r])
                    for k in (2, 1, 0):
                        cx.op(DVE, lambda e: e.scalar_tensor_tensor(out=acc[:], in0=raw[:, k:k + NB], scalar=cw[:, k:k + 1], in1=acc[:],
                                                                    op0=ALU.mult, op1=ALU.add),
                              r=[raw_r, acc_r, self.exw_r], w=[acc_r])
                    cx.op(POOL, lambda e: e.tensor_copy(out=self.exh[:, tile, :], in_=raw[:, NB:NB + 3]), r=[raw_r], w=[self.exh_r])
                    if tile < 8:
                        cx.op(ACT, lambda e: e.activation(out=xcT[:, tile, :], in_=acc[:], func=AF.Silu), r=[acc_r], w=[xcT_r])
                    else:
                        cx.op(ACT, lambda e: e.activation(out=bcT[:, tile - 8, :], in_=acc[:], func=AF.Silu), r=[acc_r], w=[bcT_r])
            with ExitStack() as st2:
                T = {}
                for nm, shp, dtp in [("qT", [128, 512], BF16), ("qxT", [128, 512], BF16), ("kT", [128, 512], BF16), ("smT", [128, 512], BF16),
                                     ("st6", [128, 4, 6], F32), ("mv", [128, 4, 2], F32), ("rstd4", [128, 4], F32),
                                     ("ytmp", [128, 1024], F32), ("yretb", [128, 1024], BF16),
                                     ("rhsM0", [128, 512], F32), ("rhsM1", [128, 512], F32), ("Lg0", [128, 512], F32), ("Lg1", [128, 512], F32),
                                     ("cbm", [128, 2, 128], F32), ("MT", [128, 16, 128], BF16), ("dec", [128, 16], F32),
                                     ("xs", [128, 1024], BF16), ("Xdt", [128, 1024], BF16), ("Xdec", [128, 1024], BF16), ("Btok", [128, 256], BF16),
                                     ("ea", [128, 32], F32), ("t2", [128, 1024], F32), ("yzb", [128, 1024], BF16),
                                     ("st6g", [128, 2, 6], F32), ("mvg", [128, 2, 2], F32), ("rs2", [128, 2], F32)]:
                    T[nm] = self.sb(nm, shp, dtp, st2)
                for c in range(NCH):
                    if "chunk" not in SKIP:
                        self.even_chunk(b, c, B, T)
                cx.barrier()
            mixT, mixT_r = B["mixT"]
            for ft in range(8):
                slot, slot_r, u = self.stream_get()
                assert u is self.U[("eout", ft)]
                wv = slot[:, 0:2048].rearrange("p (k w) -> p k w", k=16)
                bk, bk_r = self.bank()
                for kt in range(16):
                    cx.op(PE, lambda e: e.matmul(bk[:], lhsT=wv[:, kt, :], rhs=mixT[:, kt, :], start=(kt == 0), stop=(kt == 15)),
                          r=[slot_r, mixT_r], w=[bk_r])
                cx.op(DVE, lambda e: e.tensor_tensor(out=self.xT[:, ft, :], in0=self.xT[:, ft, :], in1=bk[:], op=ALU.add),
                      r=[bk_r, self.xT_r], w=[self.xT_r])
            cx.barrier()

    def even_chunk(self, b, c, B, T):
        cx = self.cx
        PE, ACT, DVE, POOL, SP = cx.PE, cx.ACT, cx.DVE, cx.POOL, cx.SP
        cs = slice(c * 128, (c + 1) * 128)
        ib, ib_r = self.ident_b, self.ident_b_r
        tok0 = b * NB + c * 128
        if "ret" not in SKIP:
            self.even_chunk_ret(b, c, B, T)
        if "ssd" not in SKIP:
            self.even_chunk_ssd(b, c, B, T)

    def even_chunk_ret(self, b, c, B, T):
        cx = self.cx
        PE, ACT, DVE, POOL, SP = cx.PE, cx.ACT, cx.DVE, cx.POOL, cx.SP
        cs = slice(c * 128, (c + 1) * 128)
        ib, ib_r = self.ident_b, self.ident_b_r
        tok0 = b * NB + c * 128
        for i, (src, dstn) in enumerate([("qr", "qT"), ("qx", "qxT"), ("kr", "kT")]):
            s_t, s_r = B[src]
            d_t, d_r = T[dstn]
            pb, pb_r = self.bankb()
            for h in range(4):
                cx.op(PE, lambda e: e.transpose(pb[:, h * 128:(h + 1) * 128], s_t[:, c, h * 128:(h + 1) * 128], ib[:]), r=[s_r, ib_r], w=[pb_r])
            if i % 2 == 0:
                cx.op(ACT, lambda e: e.copy(out=d_t[:], in_=pb[:, 0:512]), r=[pb_r], w=[d_r])
            else:
                cx.op(DVE, lambda e: e.tensor_copy(out=d_t[:], in_=pb[:, 0:512]), r=[pb_r], w=[d_r])
        qT, qT_r = T["qT"]
        qxT, qxT_r = T["qxT"]
        kT, kT_r = T["kT"]
        smT, smT_r = T["smT"]
        v, v_r = B["v"]
        bk, bk_r = self.bank()
        for h in range(4):
            hs = slice(h * 128, (h + 1) * 128)
            cx.op(PE, lambda e: e.matmul(bk[:, hs], lhsT=kT[:, hs], rhs=qT[:, hs], start=True, stop=True), r=[kT_r, qT_r], w=[bk_r])
        cx.op(DVE, lambda e: e.tensor_tensor(out=smT[:], in0=bk[:], in1=self.dmaskT[:], op=ALU.mult), r=[bk_r, self.dmaskT_r], w=[smT_r])
        bo = [self.bank(), self.bank()]
        for h in range(4):
            hs = slice(h * 128, (h + 1) * 128)
            o_t, o_r = bo[h // 2]
            osl = slice((h % 2) * 256, (h % 2 + 1) * 256)
            cx.op(PE, lambda e: e.matmul(o_t[:, osl], lhsT=smT[:, hs], rhs=v[:, c, h * 256:(h + 1) * 256], start=True, stop=False),
                  r=[smT_r, v_r], w=[o_r])
            cx.op(PE, lambda e: e.matmul(o_t[:, osl], lhsT=qxT[:, hs], rhs=self.retSb[:, h, :], start=False, stop=True),
                  r=[qxT_r, self.retSb_r], w=[o_r])
        kz, kz_r = B["kz"]
        bkv = [self.bank(), self.bank()]
        for h in range(4):
            k_t, k_r = bkv[h // 2]
            osl = slice((h % 2) * 256, (h % 2 + 1) * 256)
            cx.op(PE, lambda e: e.matmul(k_t[:, osl], lhsT=kz[:, c, h * 128:(h + 1) * 128], rhs=v[:, c, h * 256:(h + 1) * 256], start=True, stop=True),
                  r=[kz_r, v_r], w=[k_r])
        for h in range(4):
            k_t, k_r = bkv[h // 2]
            osl = slice((h % 2) * 256, (h % 2 + 1) * 256)
            cx.op(DVE, lambda e: e.scalar_tensor_tensor(out=self.retS[:, h, :], in0=self.retS[:, h, :], scalar=RET_GDEC[h], in1=k_t[:, osl],
                                                        op0=ALU.mult, op1=ALU.add),
                  r=[k_r, self.retS_r], w=[self.retS_r])
        cx.op(ACT, lambda e: e.copy(out=self.retSb[:], in_=self.retS[:]), r=[self.retS_r], w=[self.retSb_r])
        st6, st6_r = T["st6"]
        mv, mv_r = T["mv"]
        rstd4, rstd4_r = T["rstd4"]
        ytmp, ytmp_r = T["ytmp"]
        yretb, yretb_r = T["yretb"]
        sg, sg_r = B["sg"]
        for h in range(4):
            o_t, o_r = bo[h // 2]
            osl = slice((h % 2) * 256, (h % 2 + 1) * 256)
            cx.op(DVE, lambda e: e.bn_stats(out=st6[:, h, :], in_=o_t[:, osl]), r=[o_r], w=[st6_r])
        for h in range(4):
            cx.op(DVE, lambda e: e.bn_aggr(out=mv[:, h, :], in_=st6[:, h, :]), r=[st6_r], w=[mv_r])
        cx.op(ACT, lambda e: e.activation(out=rstd4[:], in_=mv[:, :, 1], func=AF.Sqrt, bias=self.eps_t[:, 0:1]), r=[mv_r, self.eps_r], w=[rstd4_r])
        cx.op(DVE, lambda e: e.reciprocal(out=rstd4[:], in_=rstd4[:]), r=[rstd4_r], w=[rstd4_r])
        for h in range(4):
            o_t, o_r = bo[h // 2]
            osl = slice((h % 2) * 256, (h % 2 + 1) * 256)
            cx.op(DVE, lambda e: e.tensor_scalar(out=ytmp[:, h * 256:(h + 1) * 256], in0=o_t[:, osl], scalar1=mv[:, h, 0:1], scalar2=rstd4[:, h:h + 1],
                                                 op0=ALU.subtract, op1=ALU.mult),
                  r=[o_r, mv_r, rstd4_r], w=[ytmp_r])
        if "ret_n" in self.dbg:
            self.dump("ret_n", ytmp[:], ytmp_r, self.dbg_out["ret_n"][tok0:tok0 + 128, :])
        cx.op(POOL, lambda e: e.tensor_tensor(out=yretb[:], in0=ytmp[:], in1=sg[:, c, :], op=ALU.mult), r=[ytmp_r, sg_r], w=[yretb_r])
        mixT, mixT_r = B["mixT"]
        pb, pb_r = self.bankb()
        for t in range(8):
            cx.op(PE, lambda e: e.transpose(pb[:, t * 128:(t + 1) * 128], yretb[:, t * 128:(t + 1) * 128], ib[:]), r=[yretb_r, ib_r], w=[pb_r])
        cx.op(ACT, lambda e: e.copy(out=mixT[:, 0:8, cs], in_=pb[:].rearrange("p (t n) -> p t n", t=8)), r=[pb_r], w=[mixT_r])

    def even_chunk_ssd(self, b, c, B, T):
        cx = self.cx
        PE, ACT, DVE, POOL, SP = cx.PE, cx.ACT, cx.DVE, cx.POOL, cx.SP
        cs = slice(c * 128, (c + 1) * 128)
        ib, ib_r = self.ident_b, self.ident_b_r
        tok0 = b * NB + c * 128
        mixT, mixT_r = B["mixT"]
        ytmp, ytmp_r = T["ytmp"]
        dt, dt_r = B["dt"]
        dta, dta_r = B["dta"]
        bcT, bcT_r = B["bcT"]
        xcT, xcT_r = B["xcT"]
        sz, sz_r = B["sz"]
        cbm, cbm_r = T["cbm"]
        MT, MT_r = T["MT"]
        dec, dec_r = T["dec"]
        bk, bk_r = self.bank()
        for g in range(2):
            cx.op(PE, lambda e: e.matmul(bk[:, g * 128:(g + 1) * 128], lhsT=bcT[:, g, cs], rhs=bcT[:, 2 + g, cs], start=True, stop=True),
                  r=[bcT_r], w=[bk_r])
        cx.op(DVE, lambda e: e.tensor_tensor(out=cbm[:], in0=bk[:, 0:256].rearrange("p (g n) -> p g n", g=2), in1=self.bc_h(self.tri[:], 2), op=ALU.mult),
              r=[bk_r, self.tri_r], w=[cbm_r])
        if SSD_STOP <= 1:
            return
        for hg in range(4):
            rhsM, rhsM_r = T[f"rhsM{hg % 2}"]
            Lg, Lg_r = T[f"Lg{hg % 2}"]
            g = hg // 2
            cx.op(POOL, lambda e: e.tensor_tensor(out=rhsM[:].rearrange("p (h n) -> p h n", h=4), in0=self.bc_h(self.tri[:], 4),
                                                  in1=self.bc_mid(dta[:, c, hg * 4:(hg + 1) * 4], 128), op=ALU.mult),
                  r=[self.tri_r, dta_r], w=[rhsM_r])
            bk, bk_r = self.bank()
            cx.op(PE, lambda e: e.matmul(bk[:], lhsT=self.su[:], rhs=rhsM[:], start=True, stop=True), r=[self.su_r, rhsM_r], w=[bk_r])
            cx.op(ACT, lambda e: e.activation(out=Lg[:], in_=bk[:], func=AF.Exp), r=[bk_r], w=[Lg_r])
            L3 = Lg[:].rearrange("p (h n) -> p h n", h=4)
            cx.op(DVE, lambda e: e.tensor_tensor(out=MT[:, hg * 4:(hg + 1) * 4, :], in0=L3, in1=self.bc_h(cbm[:, g, :], 4), op=ALU.mult),
                  r=[Lg_r, cbm_r], w=[MT_r])
            cx.op(POOL, lambda e: e.tensor_copy(out=dec[:, hg * 4:(hg + 1) * 4], in_=L3[:, :, 127]), r=[Lg_r], w=[dec_r])
        if SSD_STOP <= 2:
            return
        xs, xs_r = T["xs"]
        Xdt, Xdt_r = T["Xdt"]
        Xdec, Xdec_r = T["Xdec"]
        Btok, Btok_r = T["Btok"]
        pb, pb_r = self.bankb()
        for t in range(8):
            cx.op(PE, lambda e: e.transpose(pb[:, t * 128:(t + 1) * 128], xcT[:, t, cs], ib[:]), r=[xcT_r, ib_r], w=[pb_r])
        cx.op(ACT, lambda e: e.copy(out=xs[:], in_=pb[:]), r=[pb_r], w=[xs_r])
        cx.op(DVE, lambda e: e.tensor_tensor(out=Xdt[:].rearrange("p (h d) -> p h d", h=16), in0=pb[:].rearrange("p (h d) -> p h d", h=16),
                                             in1=self.bc_mid(dt[:, c, :], 64), op=ALU.mult),
              r=[pb_r, dt_r], w=[Xdt_r])
        cx.op(POOL, lambda e: e.tensor_tensor(out=Xdec[:].rearrange("p (h d) -> p h d", h=16), in0=Xdt[:].rearrange("p (h d) -> p h d", h=16),
                                              in1=self.bc_mid(dec[:], 64), op=ALU.mult),
              r=[Xdt_r, dec_r], w=[Xdec_r])
        pb2, pb2_r = self.bankb()
        for g in range(2):
            cx.op(PE, lambda e: e.transpose(pb2[:, g * 128:(g + 1) * 128], bcT[:, g, cs], ib[:]), r=[bcT_r, ib_r], w=[pb2_r])
        cx.op(ACT, lambda e: e.copy(out=Btok[:], in_=pb2[:, 0:256]), r=[pb2_r], w=[Btok_r])
        if SSD_STOP <= 3:
            return
        ea, ea_r = T["ea"]
        bk, bk_r = self.bank()
        cx.op(PE, lambda e: e.matmul(bk[:, 0:16], lhsT=self.tri[:], rhs=dta[:, c, :], start=True, stop=True), r=[self.tri_r, dta_r], w=[bk_r])
        cx.op(PE, lambda e: e.matmul(bk[:, 16:32], lhsT=self.ones_f[:], rhs=dta[:, c, :], start=True, stop=True), r=[self.ones_f_r, dta_r], w=[bk_r])
        cx.op(ACT, lambda e: e.activation(out=ea[:], in_=bk[:, 0:32], func=AF.Exp), r=[bk_r], w=[ea_r])
        if SSD_STOP <= 4:
            return
        bd = [self.bank(), self.bank()]
        for h in range(16):
            d_t, d_r = bd[h // 8]
            cx.op(PE, lambda e: e.matmul(d_t[:, (h % 8) * 64:(h % 8 + 1) * 64], lhsT=MT[:, h, :], rhs=Xdt[:, h * 64:(h + 1) * 64], start=True, stop=True),
                  r=[MT_r, Xdt_r], w=[d_r])
        bf = [self.bank(), self.bank()]
        for g in range(2):
            f_t, f_r = bf[g]
            cx.op(PE, lambda e: e.matmul(f_t[:], lhsT=bcT[:, 2 + g, cs], rhs=self.ssSb[:, g, :], start=True, stop=True), r=[bcT_r, self.ssSb_r], w=[f_r])
        ty, ty_r = T["ytmp"]
        t2, t2_r = T["t2"]
        for g in range(2):
            gs = slice(g * 512, (g + 1) * 512)
            f_t, f_r = bf[g]
            d_t, d_r = bd[g]
            cx.op(DVE, lambda e: e.tensor_tensor(out=ty[:, gs].rearrange("p (h d) -> p h d", h=8), in0=f_t[:].rearrange("p (h d) -> p h d", h=8),
                                                 in1=self.bc_mid(ea[:, g * 8:(g + 1) * 8], 64), op=ALU.mult),
                  r=[f_r, ea_r], w=[ty_r])
            cx.op(DVE, lambda e: e.tensor_tensor(out=ty[:, gs], in0=ty[:, gs], in1=d_t[:], op=ALU.add), r=[d_r, ty_r], w=[ty_r])
        if SSD_STOP <= 5:
            return
        if "ssd_y" in self.dbg:
            self.dump("ssd_y", ty[:], ty_r, self.dbg_out["ssd_y"][tok0:tok0 + 128, :])
        cx.op(POOL, lambda e: e.tensor_tensor(out=t2[:].rearrange("p (h d) -> p h d", h=16), in0=xs[:].rearrange("p (h d) -> p h d", h=16),
                                              in1=self.bc_mid(self.dsk[:], 64), op=ALU.mult),
              r=[xs_r, self.dsk_r], w=[t2_r])
        cx.op(POOL, lambda e: e.tensor_tensor(out=t2[:], in0=t2[:], in1=ty[:], op=ALU.add), r=[t2_r, ty_r], w=[t2_r])
        cx.op(POOL, lambda e: e.tensor_tensor(out=t2[:], in0=t2[:], in1=sz[:, c, :], op=ALU.mult), r=[t2_r, sz_r], w=[t2_r])
        st6g, st6g_r = T["st6g"]
        mvg, mvg_r = T["mvg"]
        rs2, rs2_r = T["rs2"]
        yzb, yzb_r = T["yzb"]
        for g in range(2):
            cx.op(DVE, lambda e: e.bn_stats(out=st6g[:, g, :], in_=t2[:, g * 512:(g + 1) * 512]), r=[t2_r], w=[st6g_r])
        for g in range(2):
            cx.op(DVE, lambda e: e.bn_aggr(out=mvg[:, g, :], in_=st6g[:, g, :]), r=[st6g_r], w=[mvg_r])
        cx.op(DVE, lambda e: e.tensor_tensor(out=rs2[:], in0=mvg[:, :, 0], in1=mvg[:, :, 0], op=ALU.mult), r=[mvg_r], w=[rs2_r])
        cx.op(DVE, lambda e: e.tensor_tensor(out=rs2[:], in0=rs2[:], in1=mvg[:, :, 1], op=ALU.add), r=[mvg_r, rs2_r], w=[rs2_r])
        cx.op(ACT, lambda e: e.activation(out=rs2[:], in_=rs2[:], func=AF.Sqrt, bias=self.eps_t[:, 0:1]), r=[rs2_r, self.eps_r], w=[rs2_r])
        cx.op(DVE, lambda e: e.reciprocal(out=rs2[:], in_=rs2[:]), r=[rs2_r], w=[rs2_r])
        for g in range(2):
            cx.op(POOL, lambda e: e.tensor_scalar(out=yzb[:, g * 512:(g + 1) * 512], in0=t2[:, g * 512:(g + 1) * 512], scalar1=rs2[:, g:g + 1], scalar2=None,
                                                  op0=ALU.mult),
                  r=[t2_r, rs2_r], w=[yzb_r])
        pb, pb_r = self.bankb()
        for t in range(8):
            cx.op(PE, lambda e: e.transpose(pb[:, t * 128:(t + 1) * 128], yzb[:, t * 128:(t + 1) * 128], ib[:]), r=[yzb_r, ib_r], w=[pb_r])
        for t in range(8):
            cx.op(ACT, lambda e: e.activation(out=mixT[:, 8 + t, cs], in_=pb[:, t * 128:(t + 1) * 128], func=AF.Identity, scale=self.ssmn[:, t:t + 1]),
                  r=[pb_r, self.ssmn_r], w=[mixT_r])
        if SSD_STOP <= 6:
            return
        cdec = ea[:, 16:32]
        bs = [self.bank(), self.bank()]
        for g in range(2):
            s_t, s_r = bs[g]
            cx.op(PE, lambda e: e.matmul(s_t[:], lhsT=Btok[:, g * 128:(g + 1) * 128], rhs=Xdec[:, g * 512:(g + 1) * 512], start=True, stop=True),
                  r=[Btok_r, Xdec_r], w=[s_r])
        for g in range(2):
            s_t, s_r = bs[g]
            cx.op(DVE, lambda e: e.tensor_tensor(out=self.ssS[:, g, :].rearrange("p (h d) -> p h d", h=8),
                                                 in0=self.ssS[:, g, :].rearrange("p (h d) -> p h d", h=8),
                                                 in1=self.bc_mid(cdec[:, g * 8:(g + 1) * 8], 64), op=ALU.mult),
                  r=[self.ssS_r, ea_r], w=[self.ssS_r])
            cx.op(DVE, lambda e: e.tensor_tensor(out=self.ssS[:, g, :], in0=self.ssS[:, g, :], in1=s_t[:], op=ALU.add),
                  r=[s_r, self.ssS_r], w=[self.ssS_r])
        cx.op(ACT, lambda e: e.copy(out=self.ssSb[:], in_=self.ssS[:]), r=[self.ssS_r], w=[self.ssSb_r])


    def setup_odd(self):
        cx = self.cx
        PE, ACT, DVE, POOL, SP = cx.PE, cx.ACT, cx.DVE, cx.POOL, cx.SP
        if not hasattr(self, "ident_b"):
            self.ident_b, self.ident_b_r = self.sb("ident_b", [128, 128], BF16)
            cx.op(DVE, lambda e: e.tensor_copy(out=self.ident_b[:], in_=self.ident_f[:]), r=[self.ident_f_r], w=[self.ident_b_r])
        self.chalo, self.chalo_r = self.sb("chalo", [128, 4, 30], F32)
        self.ocw, self.ocw_r = self.sb("ocw", [128, 4, 31], F32)
        self.ocb, self.ocb_r = self.sb("ocb", [128, 4], F32)
        self.olg, self.olg_r = self.sb("olg", [128, 4], F32)
        self.olb, self.olb_r = self.sb("olb", [128, 4], F32)
        self.A8r, self.A8r_r = self.sb("A8r", [128, 2, 16], F32)
        self.A8i, self.A8i_r = self.sb("A8i", [128, 2, 16], F32)
        self.Xs, self.Xs_r = self.sb("Xs", [128, 2, 16], F32)
        cx.op(DVE, lambda e: e.memset(self.chalo[:], 0.0), w=[self.chalo_r])
        cx.op(DVE, lambda e: e.memset(self.Xs[:], 0.0), w=[self.Xs_r])
        for k in range(31):
            cx.dma(SP, self.ocw[:, :, k], self.o_dw_w[0, k].rearrange("(t p) -> p t", p=128), w=[self.ocw_r], sem_res=self.ocw_r,
                   allow_slow_non_contiguous=True)
        for t, r, src in [(self.ocb, self.ocb_r, self.o_dw_b), (self.olg, self.olg_r, self.o_ln_g), (self.olb, self.olb_r, self.o_ln_b)]:
            cx.dma(SP, t[:], src[0].rearrange("(t p) -> p t", p=128), w=[r], sem_res=r, allow_slow_non_contiguous=True)
        with ExitStack() as st:
            def tl(nm, shp, dt=F32):
                return self.sb("s5_" + nm, shp, dt, st)
            lr, lr_r = tl("lr", [128, 16])
            li, li_r = tl("li", [128, 16])
            stp, stp_r = tl("stp", [128, 16])
            Bre, Bre_r = tl("Bre", [128, 16, 16])
            Bim, Bim_r = tl("Bim", [128, 16, 16])
            Cre, Cre_r = tl("Cre", [128, 16, 16])
            Cim, Cim_r = tl("Cim", [128, 16, 16])
            Dbc, Dbc_r = tl("Dbc", [128, 32, 16])
            tzm, tzm_r = tl("tzm", [128, 128])
            cx.dma(SP, tzm[:], self.consts["tzmask"], w=[tzm_r], sem_res=tzm_r)
            cx.dma(SP, lr[:], self.o_a_re[0].rearrange("g n -> (g n)").rearrange("(p q) -> q p", q=128), w=[lr_r], sem_res=lr_r,
                   allow_slow_non_contiguous=True)
            cx.dma(SP, li[:], self.o_a_im[0].rearrange("g n -> (g n)").rearrange("(p q) -> q p", q=128), w=[li_r], sem_res=li_r,
                   allow_slow_non_contiguous=True)
            ls2 = self.o_log_step[0].rearrange("(p gh) -> gh p", gh=2)
            for gh in range(2):
                cx.dma(SP, stp[gh * 64:(gh + 1) * 64, :], ls2[gh].partition_broadcast(64), w=[stp_r], sem_res=stp_r, allow_slow_non_contiguous=True)
            cx.dma(SP, Bre[:], self.o_b_re[0].rearrange("g n c -> (g n) c").rearrange("(p q) c -> q p c", q=128), w=[Bre_r], sem_res=Bre_r,
                   allow_slow_non_contiguous=True)
            cx.dma(SP, Bim[:], self.o_b_im[0].rearrange("g n c -> (g n) c").rearrange("(p q) c -> q p c", q=128), w=[Bim_r], sem_res=Bim_r,
                   allow_slow_non_contiguous=True)
            for (dst, dst_r, src) in [(Cre, Cre_r, self.o_c_re), (Cim, Cim_r, self.o_c_im)]:
                for g in range(32):
                    p, gh = g // 2, g % 2
                    cx.dma(SP, dst[gh * 64:(gh + 1) * 64, p, :], src[0, g].rearrange("co n -> n co"), w=[dst_r], sem_res=dst_r,
                           allow_slow_non_contiguous=True)
            cx.dma(SP, Dbc[:].rearrange("p g c -> p (g c)"), self.o_d[0].partition_broadcast(128), w=[Dbc_r], sem_res=Dbc_r, allow_slow_non_contiguous=True)
            cx.op(ACT, lambda e: e.activation(out=stp[:], in_=stp[:], func=AF.Exp), r=[stp_r], w=[stp_r])
            lrs, lrs_r = tl("lrs", [128, 16])
            lis, lis_r = tl("lis", [128, 16])
            cx.op(DVE, lambda e: e.tensor_tensor(out=lrs[:], in0=lr[:], in1=stp[:], op=ALU.mult), r=[lr_r, stp_r], w=[lrs_r])
            cx.op(DVE, lambda e: e.tensor_tensor(out=lis[:], in0=li[:], in1=stp[:], op=ALU.mult), r=[li_r, stp_r], w=[lis_r])
            NM = 17
            mag, mag_r = tl("mag", [128, 16, NM])
            ang, ang_r = tl("ang", [128, 16, NM])
            Pre, Pre_r = tl("Pre", [128, 16, NM])
            Pim, Pim_r = tl("Pim", [128, 16, NM])
            tmpa, tmpa_r = tl("tmpa", [128, 16, NM])
            for mi in range(NM):
                m = float(mi - 8)
                cx.op(ACT, lambda e: e.activation(out=mag[:, :, mi], in_=lrs[:], func=AF.Exp, scale=m), r=[lrs_r], w=[mag_r])
                cx.op(DVE, lambda e: e.tensor_scalar(out=ang[:, :, mi], in0=lis[:], scalar1=m, scalar2=None, op0=ALU.mult), r=[lis_r], w=[ang_r])
            TWO_PI = 2.0 * math.pi
            MAGIC = 12582912.0

            def sin_of(dst, dst_r, shift):
                cx.op(DVE, lambda e: e.tensor_scalar(out=tmpa[:], in0=ang[:], scalar1=shift, scalar2=1.0 / TWO_PI, op0=ALU.add, op1=ALU.mult),
                      r=[ang_r], w=[tmpa_r])
                cx.op(DVE, lambda e: e.tensor_scalar(out=dst[:], in0=tmpa[:], scalar1=MAGIC, scalar2=None, op0=ALU.add), r=[tmpa_r], w=[dst_r])
                cx.op(DVE, lambda e: e.tensor_scalar(out=dst[:], in0=dst[:], scalar1=-MAGIC, scalar2=None, op0=ALU.add), r=[dst_r], w=[dst_r])
                cx.op(DVE, lambda e: e.tensor_tensor(out=tmpa[:], in0=tmpa[:], in1=dst[:], op=ALU.subtract), r=[tmpa_r, dst_r], w=[tmpa_r])
                cx.op(DVE, lambda e: e.tensor_scalar(out=tmpa[:], in0=tmpa[:], scalar1=TWO_PI, scalar2=math.pi, op0=ALU.mult, op1=ALU.min),
                      r=[tmpa_r], w=[tmpa_r])
                cx.op(DVE, lambda e: e.tensor_scalar(out=tmpa[:], in0=tmpa[:], scalar1=-math.pi, scalar2=None, op0=ALU.max), r=[tmpa_r], w=[tmpa_r])
                cx.op(ACT, lambda e: e.activation(out=dst[:], in_=tmpa[:], func=AF.Sin), r=[tmpa_r], w=[dst_r])

            sin_of(Pim, Pim_r, 0.0)
            sin_of(Pre, Pre_r, math.pi / 2.0)
            cx.op(DVE, lambda e: e.tensor_tensor(out=Pre[:], in0=Pre[:], in1=mag[:], op=ALU.mult), r=[Pre_r, mag_r], w=[Pre_r])
            cx.op(DVE, lambda e: e.tensor_tensor(out=Pim[:], in0=Pim[:], in1=mag[:], op=ALU.mult), r=[Pim_r, mag_r], w=[Pim_r])

            def P(which, m):
                return (Pre if which == "re" else Pim)[:, :, m + 8]
            cx.op(DVE, lambda e: e.tensor_copy(out=self.A8r[:, 0, :], in_=P("re", 8)), r=[Pre_r], w=[self.A8r_r])
            cx.op(DVE, lambda e: e.tensor_copy(out=self.A8r[:, 1, :], in_=P("re", 8)), r=[Pre_r], w=[self.A8r_r])
            cx.op(DVE, lambda e: e.tensor_scalar(out=self.A8i[:, 0, :], in0=P("im", 8), scalar1=-1.0, scalar2=None, op0=ALU.mult), r=[Pim_r], w=[self.A8i_r])
            cx.op(DVE, lambda e: e.tensor_copy(out=self.A8i[:, 1, :], in_=P("im", 8)), r=[Pim_r], w=[self.A8i_r])
            fre, fre_r = tl("fre", [128, 16])
            fim, fim_r = tl("fim", [128, 16])
            den, den_r = tl("den", [128, 16])
            am1, am1_r = tl("am1", [128, 16])
            t16, t16_r = tl("t16", [128, 16])
            cx.op(DVE, lambda e: e.tensor_tensor(out=den[:], in0=lr[:], in1=lr[:], op=ALU.mult), r=[lr_r], w=[den_r])
            cx.op(DVE, lambda e: e.tensor_tensor(out=t16[:], in0=li[:], in1=li[:], op=ALU.mult), r=[li_r], w=[t16_r])
            cx.op(DVE, lambda e: e.tensor_tensor(out=den[:], in0=den[:], in1=t16[:], op=ALU.add), r=[den_r, t16_r], w=[den_r])
            cx.op(DVE, lambda e: e.reciprocal(out=den[:], in_=den[:]), r=[den_r], w=[den_r])
            cx.op(DVE, lambda e: e.tensor_scalar(out=am1[:], in0=P("re", 1), scalar1=-1.0, scalar2=None, op0=ALU.add), r=[Pre_r], w=[am1_r])
            cx.op(DVE, lambda e: e.tensor_tensor(out=fre[:], in0=am1[:], in1=lr[:], op=ALU.mult), r=[am1_r, lr_r], w=[fre_r])
            cx.op(DVE, lambda e: e.tensor_tensor(out=t16[:], in0=P("im", 1), in1=li[:], op=ALU.mult), r=[Pim_r, li_r], w=[t16_r])
            cx.op(DVE, lambda e: e.tensor_tensor(out=fre[:], in0=fre[:], in1=t16[:], op=ALU.add), r=[fre_r, t16_r], w=[fre_r])
            cx.op(DVE, lambda e: e.tensor_tensor(out=fre[:], in0=fre[:], in1=den[:], op=ALU.mult), r=[fre_r, den_r], w=[fre_r])
            cx.op(DVE, lambda e: e.tensor_tensor(out=fim[:], in0=P("im", 1), in1=lr[:], op=ALU.mult), r=[Pim_r, lr_r], w=[fim_r])
            cx.op(DVE, lambda e: e.tensor_tensor(out=t16[:], in0=am1[:], in1=li[:], op=ALU.mult), r=[am1_r, li_r], w=[t16_r])
            cx.op(DVE, lambda e: e.tensor_tensor(out=fim[:], in0=fim[:], in1=t16[:], op=ALU.subtract), r=[fim_r, t16_r], w=[fim_r])
            cx.op(DVE, lambda e: e.tensor_tensor(out=fim[:], in0=fim[:], in1=den[:], op=ALU.mult), r=[fim_r, den_r], w=[fim_r])

            def cmul(dre, dre_r, dim_, dim_r, are, aim, a_rs, bre, bim, b_rs, tA, tA_r, neg_im=False):
                cx.op(DVE, lambda e: e.tensor_tensor(out=dre, in0=are, in1=bre, op=ALU.mult), r=a_rs + b_rs, w=[dre_r])
                cx.op(DVE, lambda e: e.tensor_tensor(out=tA, in0=aim, in1=bim, op=ALU.mult), r=a_rs + b_rs, w=[tA_r])
                cx.op(DVE, lambda e: e.tensor_tensor(out=dre, in0=dre, in1=tA, op=ALU.subtract), r=[dre_r, tA_r], w=[dre_r])
                cx.op(DVE, lambda e: e.tensor_tensor(out=dim_, in0=are, in1=bim, op=ALU.mult), r=a_rs + b_rs, w=[dim_r])
                cx.op(DVE, lambda e: e.tensor_tensor(out=tA, in0=aim, in1=bre, op=ALU.mult), r=a_rs + b_rs, w=[tA_r])
                cx.op(DVE, lambda e: e.tensor_tensor(out=dim_, in0=dim_, in1=tA, op=ALU.add), r=[dim_r, tA_r], w=[dim_r])

            def bc16(ap2):
                return ap2.unsqueeze(2).to_broadcast([128, 16, 16])
            Bbr, Bbr_r = tl("Bbr", [128, 16, 16])
            Bbi, Bbi_r = tl("Bbi", [128, 16, 16])
            t3, t3_r = tl("t3", [128, 16, 16])
            cmul(Bbr[:], Bbr_r, Bbi[:], Bbi_r, bc16(fre[:]), bc16(fim[:]), [fre_r, fim_r], Bre[:], Bim[:], [Bre_r, Bim_r], t3[:], t3_r)
            BcTr, BcTr_r = tl("BcTr", [128, 16, 8, 16])
            BcTi, BcTi_r = tl("BcTi", [128, 16, 8, 16])
            BpTr, BpTr_r = tl("BpTr", [128, 16, 8, 16])
            BpTi, BpTi_r = tl("BpTi", [128, 16, 8, 16])
            CAr, CAr_r = tl("CAr", [128, 16, 8, 16])
            CAi, CAi_r = tl("CAi", [128, 16, 8, 16])
            for s in range(8):
                cmul(BcTr[:, :, s, :], BcTr_r, BcTi[:, :, s, :], BcTi_r, bc16(P("re", 7 - s)), bc16(P("im", 7 - s)), [Pre_r, Pim_r],
                     Bbr[:], Bbi[:], [Bbr_r, Bbi_r], t3[:], t3_r)
                cmul(BpTr[:, :, s, :], BpTr_r, BpTi[:, :, s, :], BpTi_r, bc16(P("re", -1 - s)), bc16(P("im", -1 - s)), [Pre_r, Pim_r],
                     Bbr[:], Bbi[:], [Bbr_r, Bbi_r], t3[:], t3_r)
                cmul(CAr[:, :, s, :], CAr_r, CAi[:, :, s, :], CAi_r, bc16(P("re", s + 1)), bc16(P("im", s + 1)), [Pre_r, Pim_r],
                     Cre[:], Cim[:], [Cre_r, Cim_r], t3[:], t3_r)
            CcU, CcU_r = tl("CcU", [128, 16, 2, 2, 128], BF16)
            cx.op(POOL, lambda e: e.memset(CcU[:], 0.0), w=[CcU_r])
            for gh in range(2):
                ps_ = slice(gh * 64, (gh + 1) * 64)
                cx.op(DVE, lambda e: e.tensor_copy(out=CcU[ps_, :, gh, 0, :], in_=CAr[ps_].rearrange("q p j c -> q p (j c)")), r=[CAr_r], w=[CcU_r])
                cx.op(DVE, lambda e: e.tensor_scalar(out=CcU[ps_, :, gh, 1, :], in0=CAi[ps_].rearrange("q p j c -> q p (j c)"), scalar1=-1.0, scalar2=None,
                                                     op0=ALU.mult), r=[CAi_r], w=[CcU_r])
            ccflat = CcU[:].rearrange("q p a b n -> q (p a b n)")
            for i in range(2):
                u = self.U[("s5cc", i)]
                dst = self.wscr[u["off"]: u["off"] + 128 * 4096].rearrange("(p l) -> p l", p=128)
                cx.dma(SP, dst, ccflat[:, i * 4096:(i + 1) * 4096], r=[CcU_r], w=[self.wscr_res], sem_res=CcU_r)
            BcU, BcU_r = tl("BcU", [128, 16, 2, 2, 128], BF16)
            cx.op(POOL, lambda e: e.memset(BcU[:], 0.0), w=[BcU_r])
            for p in range(16):
                for ri, (src, src_r) in enumerate([(BcTr, BcTr_r), (BcTi, BcTi_r)]):
                    bk, bk_r = self.bank()
                    cx.op(PE, lambda e: e.transpose(bk[:, 0:128], src[:, p, :, :].rearrange("q s c -> q (s c)"), self.ident_f[:]),
                          r=[src_r, self.ident_f_r], w=[bk_r])
                    for gh in range(2):
                        cx.op(ACT if gh == 0 else DVE, lambda e: (e.copy if gh == 0 else e.tensor_copy)(out=BcU[:, p, gh, ri, gh * 64:(gh + 1) * 64],
                                                                                                 in_=bk[:, gh * 64:(gh + 1) * 64]),
                              r=[bk_r], w=[BcU_r])
            bcflat = BcU[:].rearrange("q p a b n -> q (p a b n)")
            for i in range(2):
                u = self.U[("s5bc", i)]
                dst = self.wscr[u["off"]: u["off"] + 128 * 4096].rearrange("(p l) -> p l", p=128)
                cx.dma(SP, dst, bcflat[:, i * 4096:(i + 1) * 4096], r=[BcU_r], w=[self.wscr_res], sem_res=BcU_r)
            TzU, TzU_r = tl("TzU", [128, 32, 128], BF16)
            lm = [tl(f"lm{i}", [128, 128]) for i in range(2)]
            tzt, tzt_r = tl("tzt", [128, 128])
            for g in range(32):
                p, gh = g // 2, g % 2
                ps_ = slice(gh * 64, (gh + 1) * 64)
                (l0, l0_r), (l1, l1_r) = lm
                cx.op(POOL, lambda e: e.memset(l0[:], 0.0), w=[l0_r])
                cx.op(POOL, lambda e: e.memset(l1[:], 0.0), w=[l1_r])
                cx.op(POOL, lambda e: e.tensor_copy(out=l0[ps_, :], in_=BpTr[ps_, p, :, :].rearrange("q s c -> q (s c)")), r=[BpTr_r], w=[l0_r])
                cx.op(POOL, lambda e: e.tensor_scalar(out=l1[ps_, :], in0=BpTi[ps_, p, :, :].rearrange("q s c -> q (s c)"), scalar1=-1.0, scalar2=None,
                                                      op0=ALU.mult), r=[BpTi_r], w=[l1_r])
                bk, bk_r = self.bank()
                cx.op(PE, lambda e: e.matmul(bk[:, 0:128], lhsT=l0[:], rhs=CAr[:, p, :, :].rearrange("q j c -> q (j c)"), start=True, stop=False),
                      r=[l0_r, CAr_r], w=[bk_r])
                cx.op(PE, lambda e: e.matmul(bk[:, 0:128], lhsT=l1[:], rhs=CAi[:, p, :, :].rearrange("q j c -> q (j c)"), start=False, stop=True),
                      r=[l1_r, CAi_r], w=[bk_r])
                cx.op(DVE, lambda e: e.tensor_tensor(out=tzt[:], in0=bk[:, 0:128], in1=tzm[:], op=ALU.mult), r=[bk_r, tzm_r], w=[tzt_r])
                cx.op(DVE, lambda e: e.tensor_tensor(out=TzU[:, g, :].rearrange("q (j c) -> q j c", j=8),
                                                     in0=self.ident_f[:].rearrange("q (j c) -> q j c", j=8),
                                                     in1=Dbc[:, g, :].unsqueeze(1).to_broadcast([128, 8, 16]), op=ALU.mult),
                      r=[self.ident_f_r, Dbc_r], w=[TzU_r])
                cx.op(DVE, lambda e: e.tensor_tensor(out=TzU[:, g, :], in0=TzU[:, g, :], in1=tzt[:], op=ALU.add), r=[tzt_r, TzU_r], w=[TzU_r])
            u = self.U["s5tz"]
            dst = self.wscr[u["off"]: u["off"] + 128 * 4096].rearrange("(p l) -> p l", p=128)
            cx.dma(SP, dst, TzU[:].rearrange("q g n -> q (g n)"), r=[TzU_r], w=[self.wscr_res], sem_res=TzU_r)
            if "s5setup" in self.dbg:
                nc = self.nc
                o1 = nc.dram_tensor("dbg_tz", [128, 4096], BF16, kind="ExternalOutput").ap()
                o2 = nc.dram_tensor("dbg_bc", [128, 8192], BF16, kind="ExternalOutput").ap()
                o3 = nc.dram_tensor("dbg_cc", [128, 8192], BF16, kind="ExternalOutput").ap()
                o4 = nc.dram_tensor("dbg_pre", [128, 16 * 17], F32, kind="ExternalOutput").ap()
                o5 = nc.dram_tensor("dbg_pim", [128, 16 * 17], F32, kind="ExternalOutput").ap()
                o6 = nc.dram_tensor("dbg_bbr", [128, 256], F32, kind="ExternalOutput").ap()
                self.dump("tz", TzU[:].rearrange("q g n -> q (g n)"), TzU_r, o1)
                self.dump("bc", bcflat, BcU_r, o2)
                self.dump("cc", ccflat, CcU_r, o3)
                self.dump("pre", Pre[:].rearrange("q p m -> q (p m)"), Pre_r, o4)
                self.dump("pim", Pim[:].rearrange("q p m -> q (p m)"), Pim_r, o5)
                self.dump("bbr", Bbr[:].rearrange("q p m -> q (p m)"), Bbr_r, o6)
            cx.barrier(engines=cx.compute + [cx.SP], with_dma=True)

    def odd(self, b):
        cx = self.cx
        PE, ACT, DVE, POOL, SP = cx.PE, cx.ACT, cx.DVE, cx.POOL, cx.SP
        self.rmsnorm(2, self.hnT, self.hnT_r)
        hnT, hnT_r = self.hnT, self.hnT_r
        NC8 = NB // 8
        with ExitStack() as st:
            def tl(nm, shp, dt=F32):
                return self.sb("o_" + nm, shp, dt, st)
            mixT, mixT_r = tl("mixT", [128, 8, NB], BF16)
            cbuf, cbuf_r = tl("cbuf", [128, 4, NB + 30])
            cacc, cacc_r = tl("cacc", [128, 4, NB])
            sig = [tl(f"sig{i}", [128, NB]) for i in range(2)]
            uS, uS_r = tl("uS", [64, 32, 128], BF16)
            U, U_r = tl("U", [128, 32, NC8], BF16)
            V, V_r = tl("V", [128, 2, 16, NC8])
            Xst, Xst_r = tl("Xst", [128, 2, 16, NC8], BF16)
            slot, slot_r, u = self.stream_get()
            assert u is self.U["ou"]
            wv = slot[:, 0:4096].rearrange("p (k w) -> p k w", k=8)
            for s in range(8):
                bk, bk_r = self.bank()
                for kt in range(8):
                    lh = hnT[:, kt, :].rearrange("p (c s) -> p s c", s=8)[:, s, :]
                    cx.op(PE, lambda e: e.matmul(bk[0:64, :], lhsT=lh, rhs=wv[:, kt, :], start=(kt == 0), stop=(kt == 7)), r=[slot_r, hnT_r], w=[bk_r])
                src = bk[0:64, :].rearrange("p (g c) -> p g c", g=32)
                if s % 2 == 0:
                    cx.op(ACT, lambda e: e.copy(out=uS[:, :, s * 16:(s + 1) * 16], in_=src), r=[bk_r], w=[uS_r])
                else:
                    cx.op(DVE, lambda e: e.tensor_copy(out=uS[:, :, s * 16:(s + 1) * 16], in_=src), r=[bk_r], w=[uS_r])
            for g8 in range(4):
                pb, pb_r = self.bankb()
                for gi in range(8):
                    g = g8 * 8 + gi
                    cx.op(PE, lambda e: e.transpose(pb[:, gi * 64:(gi + 1) * 64], uS[:, g, :], self.ident_b[0:64, 0:64]), r=[uS_r, self.ident_b_r], w=[pb_r])
                if g8 % 2 == 0:
                    cx.op(ACT, lambda e: e.copy(out=U[:, g8 * 8:(g8 + 1) * 8, :].rearrange("p g c -> p (g c)"), in_=pb[:, 0:512]), r=[pb_r], w=[U_r])
                else:
                    cx.op(DVE, lambda e: e.tensor_copy(out=U[:, g8 * 8:(g8 + 1) * 8, :].rearrange("p g c -> p (g c)"), in_=pb[:, 0:512]), r=[pb_r], w=[U_r])
            bcs = []
            for i in range(2):
                slot, slot_r, u = self.stream_get(hold_prev=i)
                assert u is self.U[("s5bc", i)]
                bcs.append((slot[:, 0:4096].rearrange("q (p a b n) -> q p a b n", p=8, a=2, b=2), slot_r))
            for ri in range(2):
                for ph in range(2):
                    bk, bk_r = self.bank()
                    for pi in range(8):
                        p = ph * 8 + pi
                        bcv, bc_r = bcs[p // 8]
                        for gh in range(2):
                            cx.op(PE, lambda e: e.matmul(bk[:, pi * 64:(pi + 1) * 64], lhsT=bcv[:, p % 8, gh, ri, :], rhs=U[:, 2 * p + gh, :],
                                                         start=(gh == 0), stop=(gh == 1)), r=[bc_r, U_r], w=[bk_r])
                    cx.op(ACT, lambda e: e.copy(out=V[:, ri, ph * 8:(ph + 1) * 8, :].rearrange("q p c -> q (p c)"), in_=bk[:]), r=[bk_r], w=[V_r])
            T1, T1_r = tl("T1", [128, 2, 16])
            T2, T2_r = tl("T2", [128, 2, 16])
            Xs, Xs_r = self.Xs, self.Xs_r
            for c in range(NC8):
                cx.op(POOL, lambda e: e.tensor_copy(out=Xst[:, :, :, c], in_=Xs[:]), r=[Xs_r], w=[Xst_r])
                cx.op(POOL, lambda e: e.tensor_tensor(out=T1[:], in0=Xs[:], in1=self.A8r[:], op=ALU.mult), r=[Xs_r, self.A8r_r], w=[T1_r])
                cx.op(POOL, lambda e: e.tensor_tensor(out=T2[:, 0, :], in0=Xs[:, 1, :], in1=self.A8i[:, 0, :], op=ALU.mult), r=[Xs_r, self.A8i_r], w=[T2_r])
                cx.op(POOL, lambda e: e.tensor_tensor(out=T2[:, 1, :], in0=Xs[:, 0, :], in1=self.A8i[:, 1, :], op=ALU.mult), r=[Xs_r, self.A8i_r], w=[T2_r])
                cx.op(POOL, lambda e: e.tensor_tensor(out=T1[:], in0=T1[:], in1=T2[:], op=ALU.add), r=[T1_r, T2_r], w=[T1_r])
                cx.op(POOL, lambda e: e.tensor_tensor(out=Xs[:], in0=T1[:], in1=V[:, :, :, c], op=ALU.add), r=[T1_r, V_r], w=[Xs_r])
            for i in range(4):
                slot, slot_r, u = self.stream_get()
                assert u is self.U[("oin", i)]
                bks = []
                for gi in range(2):
                    wv = slot[:, gi * 1024:(gi + 1) * 1024].rearrange("p (k w) -> p k w", k=8)
                    bk, bk_r = self.bank()
                    for kt in range(8):
                        cx.op(PE, lambda e: e.matmul(bk[:], lhsT=wv[:, kt, :], rhs=hnT[:, kt, :], start=(kt == 0), stop=(kt == 7)), r=[slot_r, hnT_r], w=[bk_r])
                    bks.append((bk, bk_r))
                sg_t, sg_r = sig[i % 2]
                cx.op(ACT, lambda e: e.activation(out=sg_t[:], in_=bks[1][0][:], func=AF.Sigmoid), r=[bks[1][1]], w=[sg_r])
                cx.op(DVE, lambda e: e.tensor_tensor(out=cbuf[:, i, 30:NB + 30], in0=bks[0][0][:], in1=sg_t[:], op=ALU.mult), r=[bks[0][1], sg_r], w=[cbuf_r])
            cx.op(DVE, lambda e: e.tensor_copy(out=cbuf[:, :, 0:30], in_=self.chalo[:]), r=[self.chalo_r], w=[cbuf_r])
            cx.op(DVE, lambda e: e.tensor_copy(out=self.chalo[:], in_=cbuf[:, :, NB:NB + 30]), r=[cbuf_r], w=[self.chalo_r])
            for i in range(4):
                cx.op(DVE, lambda e: e.tensor_scalar(out=cacc[:, i, :], in0=cbuf[:, i, 30:NB + 30], scalar1=self.ocw[:, i, 30:31], scalar2=self.ocb[:, i:i + 1],
                                                     op0=ALU.mult, op1=ALU.add), r=[cbuf_r, self.ocw_r, self.ocb_r], w=[cacc_r])
                for k in range(30):
                    cx.op(DVE, lambda e: e.scalar_tensor_tensor(out=cacc[:, i, :], in0=cbuf[:, i, k:k + NB], scalar=self.ocw[:, i, k:k + 1], in1=cacc[:, i, :],
                                                                op0=ALU.mult, op1=ALU.add), r=[cbuf_r, cacc_r, self.ocw_r], w=[cacc_r])
            sq = [tl(f"sq{i}", [128, NB]) for i in range(2)]
            b1, b1_r = self.bank()
            b2, b2_r = self.bank()
            for i in range(4):
                cx.op(PE, lambda e: e.matmul(b1[:], lhsT=self.ones_f[:], rhs=cacc[:, i, :], start=(i == 0), stop=(i == 3)), r=[self.ones_f_r, cacc_r], w=[b1_r])
            for i in range(4):
                s_t, s_r = sq[i % 2]
                cx.op(ACT, lambda e: e.activation(out=s_t[:], in_=cacc[:, i, :], func=AF.Square), r=[cacc_r], w=[s_r])
                cx.op(PE, lambda e: e.matmul(b2[:], lhsT=self.ones_f[:], rhs=s_t[:], start=(i == 0), stop=(i == 3)), r=[self.ones_f_r, s_r], w=[b2_r])
            mean, mean_r = tl("mean", [128, NB])
            var, var_r = tl("var", [128, NB])
            cx.op(DVE, lambda e: e.tensor_scalar(out=mean[:], in0=b1[:], scalar1=1.0 / 512, scalar2=None, op0=ALU.mult), r=[b1_r], w=[mean_r])
            cx.op(DVE, lambda e: e.tensor_tensor(out=var[:], in0=mean[:], in1=mean[:], op=ALU.mult), r=[mean_r], w=[var_r])
            cx.op(DVE, lambda e: e.scalar_tensor_tensor(out=var[:], in0=b2[:], scalar=1.0 / 512, in1=var[:], op0=ALU.mult, op1=ALU.subtract),
                  r=[b2_r, var_r], w=[var_r])
            cx.op(ACT, lambda e: e.activation(out=var[:], in_=var[:], func=AF.Sqrt, bias=self.eps_t[:, 0:1]), r=[var_r, self.eps_r], w=[var_r])
            cx.op(DVE, lambda e: e.reciprocal(out=var[:], in_=var[:]), r=[var_r], w=[var_r])
            for i in range(4):
                cx.op(DVE, lambda e: e.tensor_tensor(out=cacc[:, i, :], in0=cacc[:, i, :], in1=mean[:], op=ALU.subtract), r=[cacc_r, mean_r], w=[cacc_r])
                cx.op(DVE, lambda e: e.tensor_tensor(out=cacc[:, i, :], in0=cacc[:, i, :], in1=var[:], op=ALU.mult), r=[cacc_r, var_r], w=[cacc_r])
                cx.op(ACT, lambda e: e.activation(out=mixT[:, i, :], in_=cacc[:, i, :], func=AF.Silu, bias=self.olb[:, i:i + 1], scale=self.olg[:, i:i + 1]),
                      r=[cacc_r, self.olg_r, self.olb_r], w=[mixT_r])
            slot, tz_r, u = self.stream_get()
            assert u is self.U["s5tz"]
            tzv = slot[:, 0:4096].rearrange("q (g n) -> q g n", g=32)
            ccs = []
            for i in range(2):
                slot, slot_r, u = self.stream_get(hold_prev=i + 1)
                assert u is self.U[("s5cc", i)]
                ccs.append((slot[:, 0:4096].rearrange("q (p a b n) -> q p a b n", p=8, a=2, b=2), slot_r))
            stok, stok_r = tl("stok", [64, 8, 512])
            for g4 in range(8):
                bk, bk_r = self.bank()
                for gi in range(4):
                    g = g4 * 4 + gi
                    p, gh = g // 2, g % 2
                    ccv, cc_r = ccs[p // 8]
                    reg = bk[0:64, gi * 128:(gi + 1) * 128]
                    cx.op(PE, lambda e: e.matmul(reg, lhsT=U[:, g, :], rhs=tzv[:, g, :], start=True, stop=False), r=[U_r, tz_r], w=[bk_r])
                    cx.op(PE, lambda e: e.matmul(reg, lhsT=Xst[:, 0, p, :], rhs=ccv[:, p % 8, gh, 0, :], start=False, stop=False), r=[Xst_r, cc_r], w=[bk_r])
                    cx.op(PE, lambda e: e.matmul(reg, lhsT=Xst[:, 1, p, :], rhs=ccv[:, p % 8, gh, 1, :], start=False, stop=True), r=[Xst_r, cc_r], w=[bk_r])
                src = bk[0:64, :].rearrange("p (g j c) -> p j g c", g=4, j=8)
                dst = stok[:, :, g4 * 64:(g4 + 1) * 64].rearrange("p j (g c) -> p j g c", g=4)
                if g4 % 2 == 0:
                    cx.op(ACT, lambda e: e.copy(out=dst, in_=src), r=[bk_r], w=[stok_r])
                else:
                    cx.op(DVE, lambda e: e.tensor_copy(out=dst, in_=src), r=[bk_r], w=[stok_r])
            syT, syT_r = tl("syT", [128, 4, NB])
            for t in range(4):
                bk, bk_r = self.bank()
                for j in range(8):
                    cx.op(PE, lambda e: e.transpose(bk[:, j * 64:(j + 1) * 64], stok[:, j, t * 128:(t + 1) * 128], self.ident_f[0:64, 0:64]),
                          r=[stok_r, self.ident_f_r], w=[bk_r])
                src = bk[:].rearrange("p (j c) -> p j c", j=8)
                dst = syT[:, t, :].rearrange("p (c j) -> p j c", j=8)
                if t % 2 == 0:
                    cx.op(ACT, lambda e: e.copy(out=dst, in_=src), r=[bk_r], w=[syT_r])
                else:
                    cx.op(DVE, lambda e: e.tensor_copy(out=dst, in_=src), r=[bk_r], w=[syT_r])
            if "s5y" in self.dbg:
                for t in range(4):
                    self.dump("s5y", syT[:, t, :], syT_r, self.dbg_out["s5y"][t * 128:(t + 1) * 128, b * NB:(b + 1) * NB])
            gl, gl_r = tl("gl", [128, 4, NB])
            glb, glb_r = tl("glb", [128, 4, NB], BF16)
            gt = [tl(f"gt{i}", [128, NB]) for i in range(2)]
            K2 = 2.0 * math.sqrt(2.0 / math.pi)
            for t in range(4):
                g_t, g_r = gt[t % 2]
                cx.op(DVE, lambda e: e.tensor_tensor(out=g_t[:], in0=syT[:, t, :], in1=syT[:, t, :], op=ALU.mult), r=[syT_r], w=[g_r])
                cx.op(DVE, lambda e: e.tensor_scalar(out=g_t[:], in0=g_t[:], scalar1=0.044715, scalar2=1.0, op0=ALU.mult, op1=ALU.add), r=[g_r], w=[g_r])
                cx.op(DVE, lambda e: e.tensor_tensor(out=g_t[:], in0=g_t[:], in1=syT[:, t, :], op=ALU.mult), r=[g_r, syT_r], w=[g_r])
                cx.op(ACT, lambda e: e.activation(out=g_t[:], in_=g_t[:], func=AF.Sigmoid, scale=K2), r=[g_r], w=[g_r])
                cx.op(DVE, lambda e: e.tensor_tensor(out=gl[:, t, :], in0=g_t[:], in1=syT[:, t, :], op=ALU.mult), r=[g_r, syT_r], w=[gl_r])
                cx.op(POOL, lambda e: e.tensor_copy(out=glb[:, t, :], in_=gl[:, t, :]), r=[gl_r], w=[glb_r])
            slot, slot_r, u = self.stream_get()
            assert u is self.U["oglu"]
            gv = slot[:, 0:2048].rearrange("p (o k w) -> p o k w", o=4, k=4)
            for ot in range(4):
                bk, bk_r = self.bank()
                for kt in range(4):
                    cx.op(PE, lambda e: e.matmul(bk[:], lhsT=gv[:, ot, kt, :], rhs=glb[:, kt, :], start=(kt == 0), stop=(kt == 3)), r=[slot_r, glb_r], w=[bk_r])
                g_t, g_r = gt[ot % 2]
                cx.op(ACT, lambda e: e.activation(out=g_t[:], in_=bk[:], func=AF.Sigmoid), r=[bk_r], w=[g_r])
                cx.op(DVE, lambda e: e.tensor_tensor(out=mixT[:, 4 + ot, :], in0=g_t[:], in1=gl[:, ot, :], op=ALU.mult), r=[g_r, gl_r], w=[mixT_r])
            for i in range(4):
                slot, slot_r, u = self.stream_get()
                assert u is self.U[("oout", i)]
                for gi in range(2):
                    ft = 2 * i + gi
                    wv = slot[:, gi * 1024:(gi + 1) * 1024].rearrange("p (k w) -> p k w", k=8)
                    bk, bk_r = self.bank()
                    for kt in range(8):
                        cx.op(PE, lambda e: e.matmul(bk[:], lhsT=wv[:, kt, :], rhs=mixT[:, kt, :], start=(kt == 0), stop=(kt == 7)), r=[slot_r, mixT_r], w=[bk_r])
                    cx.op(DVE, lambda e: e.tensor_tensor(out=self.xT[:, ft, :], in0=self.xT[:, ft, :], in1=bk[:], op=ALU.add), r=[bk_r, self.xT_r], w=[self.xT_r])
            cx.barrier()


    def final(self, b):
        cx = self.cx
        with ExitStack() as st:
            oT, oT_r = self.sb("oT", [128, 8, NB], F32, st)
            self.rmsnorm(4, oT, oT_r)
            self.store_block(b, oT, oT_r)

    def build(self):
        nc, cx = self.nc, self.cx
        self.declare_io()
        self.declare_units()
        self.wscr = nc.dram_tensor("wscr", [self.unit_off], BF16, kind="Internal").ap()
        self.wscr_res = Res("wscr")
        self.dbg_res = {}
        self.dbg_out = {}
        for nm in self.dbg:
            if nm == "s5setup":
                continue
            self.dbg_out[nm] = nc.dram_tensor("dbg_" + nm, [self.ntok, 1024], F32, kind="ExternalOutput").ap()
        self.y_res = Res("y")
        self.prologue()
        self.setup()
        if "even" in self.stages:
            self.setup_even()
        if "odd" in self.stages:
            self.setup_odd()
        cx.barrier(engines=cx.compute + [cx.SP], with_dma=True)
        self.plan_stream()
        for b in range(self.nblk):
            self.load_block(b)
            if "even" in self.stages:
                self.even(b)
            if "ffn0" in self.stages:
                self.ffn(0, b)
            if "odd" in self.stages:
                self.odd(b)
            if "ffn1" in self.stages:
                self.ffn(1, b)
            if "final" in self.stages:
                self.final(b)
            else:
                self.store_block(b, self.xT, self.xT_r)
        cx.barrier(engines=cx.compute + [cx.SP], with_dma=True)
        return nc


INPUT_NAMES = ["mix_norm", "ffn_norm", "final_norm", "ffn_w_up", "ffn_dw_w", "ffn_dw_b", "ffn_w_down",
               "e_w_in", "e_conv_w", "e_conv_b", "e_dt_bias", "e_a_log", "e_d", "e_ssm_norm", "e_w_out",
               "o_w_in", "o_dw_w", "o_dw_b", "o_ln_g", "o_ln_b", "o_a_re", "o_a_im", "o_b_re", "o_b_im", "o_c_re", "o_c_im",
               "o_d", "o_log_step", "o_glu_w", "o_w_out"]


def kernel(**inputs):
    bld = Builder()
    nc = bld.build()
    consts = make_consts()
    in_maps = []
    NCORE = 4
    for core in range(NCORE):
        bidx = core % 4
        m = {"x": np.ascontiguousarray(inputs["x"][bidx])}
        for k in INPUT_NAMES:
            m[k] = np.ascontiguousarray(inputs[k])
        for k, v in consts.items():
            m["c_" + k] = v
        in_maps.append(m)
    res = run_bass_kernel_spmd(nc, in_maps, core_ids=list(range(NCORE)))
    out = np.stack([res.results[i]["y"] for i in range(4)], axis=0)
    return out.astype(np.float32)
```

```python
import math
from contextlib import ExitStack
import numpy as np
import concourse.bass as bass
import concourse.mybir as mybir
from concourse.bass_utils import run_bass_kernel_spmd

F32 = mybir.dt.float32
BF16 = mybir.dt.bfloat16
ALU = mybir.AluOpType
AF = mybir.ActivationFunctionType

D = 1024
SEQ = 4096
NB = 512
NCH = NB // 128
DFF = 2816
EPS = 1e-6
SLOT = 4096
NSLOT = 4
import os
SKIP = set(os.environ.get('MK_SKIP', '').split(','))
SSD_STOP = int(os.environ.get('MK_SSD_STOP', '99'))


class Sem:
    def __init__(self, h, name):
        self.h = h
        self.name = name
        self.count = 0


class Res:
    __slots__ = ("name", "w", "r", "dsem", "excl")

    def __init__(self, name, excl=False):
        self.name = name
        self.excl = excl
        self.w = {}
        self.r = {}
        self.dsem = None


class Eng:
    def __init__(self, name, eng, sem):
        self.name = name
        self.eng = eng
        self.sem = sem
        self.known = {}


class Ctx:
    def __init__(self, nc, stack):
        self.nc = nc
        self.stack = stack
        self.nsem = 0
        self.PE = self._eng("pe", nc.tensor)
        self.ACT = self._eng("act", nc.scalar)
        self.DVE = self._eng("dve", nc.vector)
        self.POOL = self._eng("pool", nc.gpsimd)
        self.SP = self._eng("sp", nc.sync)
        self.compute = [self.PE, self.ACT, self.DVE, self.POOL]
        self.dsems = []

    def new_sem(self, name):
        self.nsem += 1
        h = self.stack.enter_context(self.nc.semaphore(name))
        return Sem(h, name)

    def _eng(self, name, eng):
        return Eng(name, eng, self.new_sem("s_" + name))

    def _wait(self, E, need):
        for s, (v, snap) in need.items():
            if E.known.get(s, 0) >= v:
                continue
            E.eng.wait_ge(s.h, v)
            E.known[s] = v
            for s2, v2 in snap.items():
                if E.known.get(s2, 0) < v2:
                    E.known[s2] = v2

    def _collect(self, E, reads, writes):
        need = {}

        def req(s, ev):
            if s not in need or need[s][0] < ev[0]:
                need[s] = ev

        for r in reads:
            for s, ev in r.w.items():
                req(s, ev)
            if r.excl:
                for s, ev in r.r.items():
                    if s is not E.sem:
                        req(s, ev)
        for w in writes:
            for s, ev in w.w.items():
                if s is E.sem:
                    continue
                req(s, ev)
            for s, ev in w.r.items():
                if s is E.sem:
                    continue
                req(s, ev)
        return need

    def op(self, E, fn, r=(), w=()):
        self._wait(E, self._collect(E, r, w))
        ins = fn(E.eng)
        E.sem.count += 1
        ins.then_inc(E.sem.h, 1)
        ev = (E.sem.count, dict(E.known))
        for x in r:
            x.r[E.sem] = ev
        for x in w:
            x.w[E.sem] = ev
            x.r = {}
        return ins

    def dma(self, E, out, in_, r=(), w=(), sem_res=None, **kw):
        self._wait(E, self._collect(E, r, w))
        if sem_res.dsem is None:
            sem_res.dsem = self.new_sem("d_" + sem_res.name)
            self.dsems.append(sem_res.dsem)
        ds = sem_res.dsem
        ins = E.eng.dma_start(out=out, in_=in_, **kw)
        ds.count += 16
        ins.then_inc(ds.h, 16)
        ev = (ds.count, dict(E.known))
        for x in r:
            x.r[ds] = ev
        for x in w:
            x.w[ds] = ev
            x.r = {}
        return ins

    def barrier(self, engines=None, with_dma=False):
        engines = engines or self.compute
        for E in engines:
            need = {}
            for E2 in engines:
                if E2 is not E and E2.sem.count > 0:
                    need[E2.sem] = (E2.sem.count, dict(E2.known))
            if with_dma:
                for ds in self.dsems:
                    if ds.count > 0:
                        need[ds] = (ds.count, {})
            self._wait(E, need)


def make_consts():
    c = {}
    c["ident_f"] = np.eye(128, dtype=np.float32)
    c["ones_f"] = np.ones((128, 128), dtype=np.float32)
    inv = (10000.0 ** (-np.arange(0, 128, 2, dtype=np.float32) / np.float32(128))).astype(np.float32)
    pos = np.arange(SEQ, dtype=np.float32)
    ang = (pos[:, None] * inv[None, :]).astype(np.float32).astype(np.float64)
    c["cosT"] = np.ascontiguousarray(np.cos(ang).reshape(SEQ // 128, 128, 64).transpose(1, 0, 2)).astype(np.float32)
    c["sinT"] = np.ascontiguousarray(np.sin(ang).reshape(SEQ // 128, 128, 64).transpose(1, 0, 2)).astype(np.float32)
    gam = 1.0 - 2.0 ** (-5.0 - np.arange(4, dtype=np.float64))
    idx = np.arange(128, dtype=np.float64)
    diff = idx[None, :] - idx[:, None]
    dm = np.where(diff[None] >= 0, gam[:, None, None] ** np.maximum(diff, 0)[None], 0.0) * (128 ** -0.5)
    c["dmaskT"] = np.ascontiguousarray(dm.transpose(1, 0, 2)).reshape(128, 512).astype(np.float32)
    c["xi"] = (gam[None, :] ** (idx[:, None] + 1)).astype(np.float32)
    c["zeta"] = ((gam[None, :] ** (127 - idx[:, None])) * (128 ** -0.5)).astype(np.float32)
    tri = (idx[:, None] <= idx[None, :]).astype(np.float32)
    c["tri"] = tri
    c["su"] = (1.0 - tri).astype(np.float32)
    sj = np.arange(128) // 16
    c["tzmask"] = (sj[None, :] >= sj[:, None]).astype(np.float32)
    return c


RET_GDEC = [float((1.0 - 2.0 ** (-5.0 - h)) ** 128) for h in range(4)]


class Builder:
    def __init__(self, nblk=SEQ // NB, stages=("even", "ffn0", "odd", "ffn1", "final"), dbg=()):
        self.nblk = nblk
        self.stages = stages
        self.dbg = dbg
        self.ntok = nblk * NB
        self.nc = bass.Bass("TRN2", target_bir_lowering=False)
        self.stack = ExitStack()
        self.cx = Ctx(self.nc, self.stack)
        self.units = []
        self.unit_off = 0
        self.stream_order = []
        self.bank_i = 0
        self.bankb_i = 0

    def din(self, name, shape, dt=F32):
        return self.nc.dram_tensor(name, list(shape), dt, kind="ExternalInput").ap()

    def declare_io(self):
        nc = self.nc
        self.x = self.din("x", [self.ntok, D])
        self.y = nc.dram_tensor("y", [self.ntok, D], F32, kind="ExternalOutput").ap()
        self.mix_norm = self.din("mix_norm", [2, D])
        self.ffn_norm = self.din("ffn_norm", [2, D])
        self.final_norm = self.din("final_norm", [D])
        self.ffn_w_up = self.din("ffn_w_up", [2, D, 2 * DFF])
        self.ffn_dw_w = self.din("ffn_dw_w", [2, 3, 2 * DFF])
        self.ffn_dw_b = self.din("ffn_dw_b", [2, 2 * DFF])
        self.ffn_w_down = self.din("ffn_w_down", [2, DFF, D])
        self.e_w_in = self.din("e_w_in", [1, D, 5648])
        self.e_conv_w = self.din("e_conv_w", [1, 4, 1536])
        self.e_conv_b = self.din("e_conv_b", [1, 1536])
        self.e_dt_bias = self.din("e_dt_bias", [1, 16])
        self.e_a_log = self.din("e_a_log", [1, 16])
        self.e_d = self.din("e_d", [1, 16])
        self.e_ssm_norm = self.din("e_ssm_norm", [1, 1024])
        self.e_w_out = self.din("e_w_out", [1, 2048, 1024])
        self.o_w_in = self.din("o_w_in", [1, D, 1536])
        self.o_dw_w = self.din("o_dw_w", [1, 31, 512])
        self.o_dw_b = self.din("o_dw_b", [1, 512])
        self.o_ln_g = self.din("o_ln_g", [1, 512])
        self.o_ln_b = self.din("o_ln_b", [1, 512])
        self.o_a_re = self.din("o_a_re", [1, 32, 64])
        self.o_a_im = self.din("o_a_im", [1, 32, 64])
        self.o_b_re = self.din("o_b_re", [1, 32, 64, 16])
        self.o_b_im = self.din("o_b_im", [1, 32, 64, 16])
        self.o_c_re = self.din("o_c_re", [1, 32, 16, 64])
        self.o_c_im = self.din("o_c_im", [1, 32, 16, 64])
        self.o_d = self.din("o_d", [1, 512])
        self.o_log_step = self.din("o_log_step", [1, 32])
        self.o_glu_w = self.din("o_glu_w", [1, 512, 512])
        self.o_w_out = self.din("o_w_out", [1, D, D])
        self.consts = {}
        for k, v in make_consts().items():
            self.consts[k] = self.din("c_" + k, v.shape)

    def sb(self, name, shape, dt=F32, stack=None):
        self.name_i = getattr(self, "name_i", 0) + 1
        t = (stack or self.stack).enter_context(self.nc.sbuf_tensor(f"{name}_{self.name_i}", list(shape), dt))
        return t, Res(name)

    def ps(self, name, shape, dt=F32):
        t = self.stack.enter_context(self.nc.psum_tensor(name, list(shape), dt))
        return t, Res(name, excl=True)

    def bank(self):
        i = self.bank_i
        self.bank_i = (i + 1) % len(self.banks)
        return self.banks[i]

    def bankb(self):
        i = self.bankb_i
        self.bankb_i = (i + 1) % len(self.banksb)
        return self.banksb[i]

    def add_unit(self, name, pieces):
        L = sum(kt * w for (_, kt, _, w) in pieces)
        assert L <= SLOT, (name, L)
        u = dict(name=name, L=L, off=self.unit_off, pieces=pieces)
        self.unit_off += 128 * L
        self.units.append(u)
        return u

    def declare_units(self):
        self.U = {}
        for l in range(2):
            wup = self.ffn_w_up[l]
            for j in range(22):
                self.U[("up", l, j)] = self.add_unit(f"up{l}_{j}", [(wup, 8, j * 128, 128), (wup, 8, (22 + j) * 128, 128)])
            wdn = self.ffn_w_down[l]
            for ft in range(8):
                self.U[("dn", l, ft)] = self.add_unit(f"dn{l}_{ft}", [(wdn, 22, ft * 128, 128)])

        w = self.e_w_in[0]
        for i, nm in enumerate(["eq", "ek", "ev0", "ev1", "eg0", "eg1", "ez0", "ez1"]):
            self.U[nm] = self.add_unit(nm, [(w, 8, 512 * i, 512)])
        self.U["edt"] = self.add_unit("edt", [(w, 8, 5632, 16)])
        for i in range(6):
            self.U[("exbc", i)] = self.add_unit(f"exbc{i}", [(w, 8, 4096 + (2 * i) * 128, 128), (w, 8, 4096 + (2 * i + 1) * 128, 128)])
        for ft in range(8):
            self.U[("eout", ft)] = self.add_unit(f"eout{ft}", [(self.e_w_out[0], 16, ft * 128, 128)])

        w = self.o_w_in[0]
        for i in range(4):
            self.U[("oin", i)] = self.add_unit(f"oin{i}", [(w, 8, i * 128, 128), (w, 8, 512 + i * 128, 128)])
        self.U["ou"] = self.add_unit("ou", [(w, 8, 1024, 512)])
        self.U["oglu"] = self.add_unit("oglu", [(self.o_glu_w[0], 4, ot * 128, 128) for ot in range(4)])
        for i in range(4):
            self.U[("oout", i)] = self.add_unit(f"oout{i}", [(self.o_w_out[0], 8, (2 * i) * 128, 128), (self.o_w_out[0], 8, (2 * i + 1) * 128, 128)])
        self.n_cast_units = len(self.units)
        for nm in [("s5bc", 0), ("s5bc", 1), ("s5cc", 0), ("s5cc", 1), "s5tz"]:
            self.U[nm] = dict(name=str(nm), L=4096, off=self.unit_off, pieces=[])
            self.unit_off += 128 * 4096

    def plan_stream(self):
        order = []
        for b in range(self.nblk):
            if "even" in self.stages:
                order += [self.U[("exbc", i)] for i in range(6)]
                order += [self.U[nm] for nm in ["eq", "ek", "ev0", "ev1", "eg0", "eg1", "ez0", "ez1", "edt"]]
                order += [self.U[("eout", ft)] for ft in range(8)]
            for l in range(2):
                if l == 1 and "odd" in self.stages:
                    order += [self.U["ou"], self.U[("s5bc", 0)], self.U[("s5bc", 1)]]
                    order += [self.U[("oin", i)] for i in range(4)]
                    order += [self.U["s5tz"], self.U[("s5cc", 0)], self.U[("s5cc", 1)], self.U["oglu"]]
                    order += [self.U[("oout", i)] for i in range(4)]
                if f"ffn{l}" in self.stages:
                    order += [self.U[("up", l, j)] for j in range(22)]
                    order += [self.U[("dn", l, ft)] for ft in range(8)]
        self.stream_order = order
        self.stream_next_load = 0
        self.stream_next_use = 0

    def stream_get(self, hold_prev=0):
        cx = self.cx
        i = self.stream_next_use
        self.stream_next_use += 1
        lim = min(i - hold_prev + NSLOT - 1, len(self.stream_order) - 1)
        while self.stream_next_load <= lim:
            k = self.stream_next_load
            u = self.stream_order[k]
            st, sr = self.slots[k % NSLOT]
            src = self.wscr[u["off"]: u["off"] + 128 * u["L"]].rearrange("(p l) -> p l", p=128)
            cx.dma(cx.SP, st[:, 0:u["L"]], src, w=[sr], sem_res=sr)
            self.stream_next_load += 1
        st, sr = self.slots[i % NSLOT]
        assert self.stream_order[i] is not None
        return st, sr, self.stream_order[i]

    def prologue(self):
        cx, nc = self.cx, self.nc
        with ExitStack() as st:
            NS = 3
            stg32 = [self.sb(f"stg32_{i}", [128, SLOT], F32, st) for i in range(NS)]
            stg16 = [self.sb(f"stg16_{i}", [128, SLOT], BF16, st) for i in range(NS)]
            cast_engs = [cx.ACT, cx.DVE, cx.POOL]
            for i, u in enumerate(self.units[:self.n_cast_units]):
                t32, r32 = stg32[i % NS]
                t16, r16 = stg16[i % NS]
                off = 0
                for (src, kt, c0, w) in u["pieces"]:
                    dst = t32[:, off: off + kt * w].rearrange("p (k w) -> p k w", k=kt)
                    s = src.rearrange("(k p) n -> p k n", p=128)[:, :, c0:c0 + w]
                    cx.dma(cx.SP, dst, s, w=[r32], sem_res=r32)
                    off += kt * w
                L = u["L"]
                E = cast_engs[i % 3]
                if E is cx.ACT:
                    cx.op(E, lambda e: e.copy(out=t16[:, 0:L], in_=t32[:, 0:L]), r=[r32], w=[r16])
                else:
                    cx.op(E, lambda e: e.tensor_copy(out=t16[:, 0:L], in_=t32[:, 0:L]), r=[r32], w=[r16])
                dst = self.wscr[u["off"]: u["off"] + 128 * L].rearrange("(p l) -> p l", p=128)
                cx.dma(cx.ACT if False else cx.SP, dst, t16[:, 0:L], r=[r16], w=[self.wscr_res], sem_res=r16)
            cx.barrier(engines=cx.compute + [cx.SP], with_dma=True)

    def setup(self):
        cx, nc = self.cx, self.nc
        self.xT, self.xT_r = self.sb("xT", [128, 8, NB], F32)
        self.hnT, self.hnT_r = self.sb("hnT", [128, 8, NB], BF16)
        self.ident_f, self.ident_f_r = self.sb("ident_f", [128, 128], F32)
        self.ones_f, self.ones_f_r = self.sb("ones_f", [128, 128], F32)
        self.gains, self.gains_r = self.sb("gains", [128, 5, 8], F32)
        self.slots = [self.sb(f"wslot{i}", [128, SLOT], BF16) for i in range(NSLOT)]
        self.banks = [self.ps(f"bank{i}", [128, 512], F32) for i in range(6)]
        self.banksb = [self.ps(f"bankb{i}", [128, 1024], BF16) for i in range(2)]
        self.ffn_halo, self.ffn_halo_r = self.sb("ffn_halo", [128, 2, 44, 2], F32)
        self.ffn_cw, self.ffn_cw_r = self.sb("ffn_cw", [128, 2, 44, 3], F32)
        self.ffn_cb, self.ffn_cb_r = self.sb("ffn_cb", [128, 2, 44], F32)
        self.eps_t, self.eps_r = self.sb("eps_t", [128, 1], F32)
        nc.allow_non_contiguous_dma(reason="small parameter loads")
        SP = cx.SP
        cx.dma(SP, self.ident_f[:], self.consts["ident_f"], w=[self.ident_f_r], sem_res=self.ident_f_r)
        cx.dma(SP, self.ones_f[:], self.consts["ones_f"], w=[self.ones_f_r], sem_res=self.ones_f_r)
        gsrc = [self.mix_norm[0], self.ffn_norm[0], self.mix_norm[1], self.ffn_norm[1], self.final_norm]
        for i, g in enumerate(gsrc):
            cx.dma(SP, self.gains[:, i, :], g.rearrange("(t p) -> p t", p=128), w=[self.gains_r], sem_res=self.gains_r, allow_slow_non_contiguous=True)
        for l in range(2):
            for k in range(3):
                cx.dma(SP, self.ffn_cw[:, l, :, k], self.ffn_dw_w[l, k].rearrange("(t p) -> p t", p=128),
                       w=[self.ffn_cw_r], sem_res=self.ffn_cw_r, allow_slow_non_contiguous=True)
            cx.dma(SP, self.ffn_cb[:, l, :], self.ffn_dw_b[l].rearrange("(t p) -> p t", p=128),
                   w=[self.ffn_cb_r], sem_res=self.ffn_cb_r, allow_slow_non_contiguous=True)
        cx.op(cx.DVE, lambda e: e.memset(self.ffn_halo[:], 0.0), w=[self.ffn_halo_r])
        cx.op(cx.DVE, lambda e: e.memset(self.eps_t[:], EPS), w=[self.eps_r])
        self.ones_b, self.ones_b_r = self.sb("ones_b", [128, 128], BF16)
        cx.op(cx.DVE, lambda e: e.memset(self.ones_b[:], 1.0), w=[self.ones_b_r])
        self.one_t, _ = self.sb("one_t", [128, 1], F32)
        cx.op(cx.DVE, lambda e: e.memset(self.one_t[:], 1.0), w=[self.eps_r])

    def load_block(self, b):
        cx = self.cx
        with ExitStack() as st:
            xin, xin_r = self.sb("xin", [128, NCH, D], F32, st)
            src = self.x[b * NB:(b + 1) * NB, :].rearrange("(c p) d -> p c d", p=128)
            cx.dma(cx.SP, xin[:], src, w=[xin_r], sem_res=self.xT_r)
            for ft in range(8):
                bk, bk_r = self.bank()
                for c in range(NCH):
                    cx.op(cx.PE, lambda e: e.transpose(bk[:, c * 128:(c + 1) * 128], xin[:, c, ft * 128:(ft + 1) * 128], self.ident_f[:]),
                          r=[xin_r, self.ident_f_r], w=[bk_r])
                if ft % 2 == 0:
                    cx.op(cx.ACT, lambda e: e.copy(out=self.xT[:, ft, :], in_=bk[:]), r=[bk_r], w=[self.xT_r])
                else:
                    cx.op(cx.DVE, lambda e: e.tensor_copy(out=self.xT[:, ft, :], in_=bk[:]), r=[bk_r], w=[self.xT_r])
            cx.barrier()

    def store_block(self, b, srcT, srcT_r):
        cx = self.cx
        with ExitStack() as st:
            yo, yo_r = self.sb("yo", [128, NCH, D], F32, st)
            k = 0
            for c in range(NCH):
                for half in range(2):
                    bk, bk_r = self.bank()
                    for f4 in range(4):
                        ft = half * 4 + f4
                        cx.op(cx.PE, lambda e: e.transpose(bk[:, f4 * 128:(f4 + 1) * 128], srcT[:, ft, c * 128:(c + 1) * 128], self.ident_f[:]),
                              r=[srcT_r, self.ident_f_r], w=[bk_r])
                    if k % 2 == 0:
                        cx.op(cx.ACT, lambda e: e.copy(out=yo[:, c, half * 512:(half + 1) * 512], in_=bk[:]), r=[bk_r], w=[yo_r])
                    else:
                        cx.op(cx.DVE, lambda e: e.tensor_copy(out=yo[:, c, half * 512:(half + 1) * 512], in_=bk[:]), r=[bk_r], w=[yo_r])
                    k += 1
            dst = self.y[b * NB:(b + 1) * NB, :].rearrange("(c p) d -> p c d", p=128)
            cx.dma(cx.SP, dst, yo[:], r=[yo_r], w=[self.y_res], sem_res=self.y_res)
            cx.barrier(engines=cx.compute + [cx.SP], with_dma=True)

    def rmsnorm(self, gi, outT, outT_r, out_f32=False):
        cx = self.cx
        with ExitStack() as st:
            sq = [self.sb(f"sq{i}", [128, NB], BF16, st) for i in range(4)]
            rstd, rstd_r = self.sb("rstd", [128, NB], F32, st)
            bk, bk_r = self.bank()
            for ft in range(8):
                s, s_r = sq[ft % 4]
                if ft % 2 == 0:
                    cx.op(cx.ACT, lambda e: e.activation(out=s[:], in_=self.xT[:, ft, :], func=AF.Square), r=[self.xT_r], w=[s_r])
                else:
                    cx.op(cx.DVE, lambda e: e.tensor_tensor(out=s[:], in0=self.xT[:, ft, :], in1=self.xT[:, ft, :], op=ALU.mult), r=[self.xT_r], w=[s_r])
                cx.op(cx.PE, lambda e: e.matmul(bk[:], lhsT=self.ones_b[:], rhs=s[:], start=(ft == 0), stop=(ft == 7)),
                      r=[s_r, self.ones_b_r], w=[bk_r])
            cx.op(cx.ACT, lambda e: e.activation(out=rstd[:], in_=bk[:], func=AF.Sqrt, bias=self.eps_t[:, 0:1], scale=1.0 / D),
                  r=[bk_r, self.eps_r], w=[rstd_r])
            cx.op(cx.DVE, lambda e: e.reciprocal(out=rstd[:], in_=rstd[:]), r=[rstd_r], w=[rstd_r])
            for ft in range(8):
                E = cx.DVE
                cx.op(E, lambda e: e.scalar_tensor_tensor(out=outT[:, ft, :], in0=self.xT[:, ft, :], scalar=self.gains[:, gi, ft:ft + 1],
                                                          in1=rstd[:], op0=ALU.mult, op1=ALU.mult),
                      r=[self.xT_r, self.gains_r, rstd_r], w=[outT_r])
            cx.barrier()

    def ffn(self, l, b):
        cx = self.cx
        self.rmsnorm(1 + 2 * l, self.hnT, self.hnT_r)
        with ExitStack() as st:
            hT, hT_r = self.sb("hT", [128, 22, NB], BF16, st)
            raws = [self.sb(f"raw{i}", [128, NB + 2], F32, st) for i in range(4)]
            accs = [self.sb(f"acc{i}", [128, NB], F32, st) for i in range(4)]
            sg = [self.sb(f"sgate{i}", [128, NB], F32, st) for i in range(2)]
            for j in range(22):
                slot, slot_r, u = self.stream_get()
                assert u is self.U[("up", l, j)]
                res = []
                for gi in range(2):
                    tile = j + 22 * gi
                    wv = slot[:, gi * 1024:(gi + 1) * 1024].rearrange("p (k w) -> p k w", k=8)
                    bk, bk_r = self.bank()
                    for kt in range(8):
                        cx.op(cx.PE, lambda e: e.matmul(bk[:], lhsT=wv[:, kt, :], rhs=self.hnT[:, kt, :], start=(kt == 0), stop=(kt == 7)),
                              r=[slot_r, self.hnT_r], w=[bk_r])
                    raw, raw_r = raws[(2 * j + gi) % 4]
                    acc, acc_r = accs[(2 * j + gi) % 4]
                    cw = self.ffn_cw[:, l, tile, :]
                    cx.op(cx.ACT, lambda e: e.copy(out=raw[:, 2:NB + 2], in_=bk[:]), r=[bk_r], w=[raw_r])
                    cx.op(cx.POOL, lambda e: e.tensor_copy(out=raw[:, 0:2], in_=self.ffn_halo[:, l, tile, :]), r=[self.ffn_halo_r], w=[raw_r])
                    cx.op(cx.ACT, lambda e: e.activation(out=acc[:], in_=bk[:], func=AF.Identity, bias=self.ffn_cb[:, l, tile:tile + 1],
                                                         scale=cw[:, 2:3]),
                          r=[bk_r, self.ffn_cw_r, self.ffn_cb_r], w=[acc_r])
                    E = cx.DVE
                    cx.op(E, lambda e: e.scalar_tensor_tensor(out=acc[:], in0=raw[:, 1:NB + 1], scalar=cw[:, 1:2], in1=acc[:],
                                                              op0=ALU.mult, op1=ALU.add),
                          r=[raw_r, acc_r, self.ffn_cw_r], w=[acc_r])
                    cx.op(E, lambda e: e.scalar_tensor_tensor(out=acc[:], in0=raw[:, 0:NB], scalar=cw[:, 0:1], in1=acc[:],
                                                              op0=ALU.mult, op1=ALU.add),
                          r=[raw_r, acc_r, self.ffn_cw_r], w=[acc_r])
                    cx.op(cx.POOL, lambda e: e.tensor_copy(out=self.ffn_halo[:, l, tile, :], in_=raw[:, NB:NB + 2]), r=[raw_r], w=[self.ffn_halo_r])
                    res.append((acc, acc_r))
                s, s_r = sg[j % 2]
                cx.op(cx.ACT, lambda e: e.activation(out=s[:], in_=res[0][0][:], func=AF.Silu), r=[res[0][1]], w=[s_r])
                cx.op(cx.DVE, lambda e: e.tensor_tensor(out=hT[:, j, :], in0=s[:], in1=res[1][0][:], op=ALU.mult),
                      r=[s_r, res[1][1]], w=[hT_r])
            for ft in range(8):
                slot, slot_r, u = self.stream_get()
                assert u is self.U[("dn", l, ft)]
                wv = slot[:, 0:22 * 128].rearrange("p (k w) -> p k w", k=22)
                bk, bk_r = self.bank()
                for kt in range(22):
                    cx.op(cx.PE, lambda e: e.matmul(bk[:], lhsT=wv[:, kt, :], rhs=hT[:, kt, :], start=(kt == 0), stop=(kt == 21)),
                          r=[slot_r, hT_r], w=[bk_r])
                cx.op(cx.DVE, lambda e: e.tensor_tensor(out=self.xT[:, ft, :], in0=self.xT[:, ft, :], in1=bk[:], op=ALU.add),
                      r=[bk_r, self.xT_r], w=[self.xT_r])
            cx.barrier()

    def bc_mid(self, ap2, n):
        a = ap2.shape[1]
        return ap2.unsqueeze(2).to_broadcast([128, a, n])

    def bc_h(self, ap2, h):
        n = ap2.shape[1]
        return ap2.unsqueeze(1).to_broadcast([128, h, n])

    def setup_even(self):
        cx = self.cx
        SP = cx.SP
        self.retS, self.retS_r = self.sb("retS", [128, 4, 256], F32)
        self.retSb, self.retSb_r = self.sb("retSb", [128, 4, 256], BF16)
        self.ssS, self.ssS_r = self.sb("ssS", [128, 2, 512], F32)
        self.ssSb, self.ssSb_r = self.sb("ssSb", [128, 2, 512], BF16)
        self.exh, self.exh_r = self.sb("exh", [128, 12, 3], F32)
        self.exw, self.exw_r = self.sb("exw", [128, 12, 4], F32)
        self.exb, self.exb_r = self.sb("exb", [128, 12], F32)
        self.dmaskT, self.dmaskT_r = self.sb("dmaskT", [128, 512], F32)
        self.xi, self.xi_r = self.sb("xi", [128, 4], F32)
        self.zeta, self.zeta_r = self.sb("zeta", [128, 4], F32)
        self.tri, self.tri_r = self.sb("tri", [128, 128], F32)
        self.su, self.su_r = self.sb("su", [128, 128], F32)
        self.ident_b, self.ident_b_r = self.sb("ident_b", [128, 128], BF16)
        self.dtb, self.dtb_r = self.sb("dtb", [128, 16], F32)
        self.aneg, self.aneg_r = self.sb("aneg", [128, 16], F32)
        self.dsk, self.dsk_r = self.sb("dsk", [128, 16], F32)
        self.ssmn, self.ssmn_r = self.sb("ssmn", [128, 8], F32)
        for t, r, nm in [(self.dmaskT, self.dmaskT_r, "dmaskT"), (self.xi, self.xi_r, "xi"), (self.zeta, self.zeta_r, "zeta"),
                         (self.tri, self.tri_r, "tri"), (self.su, self.su_r, "su")]:
            cx.dma(SP, t[:], self.consts[nm], w=[r], sem_res=r)
        cx.dma(SP, self.dtb[:], self.e_dt_bias[0].partition_broadcast(128), w=[self.dtb_r], sem_res=self.dtb_r, allow_slow_non_contiguous=True)
        cx.dma(SP, self.aneg[:], self.e_a_log[0].partition_broadcast(128), w=[self.aneg_r], sem_res=self.aneg_r, allow_slow_non_contiguous=True)
        cx.dma(SP, self.dsk[:], self.e_d[0].partition_broadcast(128), w=[self.dsk_r], sem_res=self.dsk_r, allow_slow_non_contiguous=True)
        cx.dma(SP, self.ssmn[:], self.e_ssm_norm[0].rearrange("(t p) -> p t", p=128), w=[self.ssmn_r], sem_res=self.ssmn_r,
               allow_slow_non_contiguous=True)
        for k in range(4):
            cx.dma(SP, self.exw[:, :, k], self.e_conv_w[0, k].rearrange("(t p) -> p t", p=128), w=[self.exw_r], sem_res=self.exw_r,
                   allow_slow_non_contiguous=True)
        cx.dma(SP, self.exb[:], self.e_conv_b[0].rearrange("(t p) -> p t", p=128), w=[self.exb_r], sem_res=self.exb_r,
               allow_slow_non_contiguous=True)
        cx.op(cx.ACT, lambda e: e.activation(out=self.aneg[:], in_=self.aneg[:], func=AF.Exp), r=[self.aneg_r], w=[self.aneg_r])
        cx.op(cx.DVE, lambda e: e.tensor_scalar(out=self.aneg[:], in0=self.aneg[:], scalar1=-1.0, scalar2=None, op0=ALU.mult),
              r=[self.aneg_r], w=[self.aneg_r])
        cx.op(cx.DVE, lambda e: e.tensor_copy(out=self.ident_b[:], in_=self.ident_f[:]), r=[self.ident_f_r], w=[self.ident_b_r])
        cx.op(cx.DVE, lambda e: e.memset(self.retS[:], 0.0), w=[self.retS_r])
        cx.op(cx.DVE, lambda e: e.memset(self.retSb[:], 0.0), w=[self.retSb_r])
        cx.op(cx.POOL, lambda e: e.memset(self.ssS[:], 0.0), w=[self.ssS_r])
        cx.op(cx.POOL, lambda e: e.memset(self.ssSb[:], 0.0), w=[self.ssSb_r])
        cx.op(cx.POOL, lambda e: e.memset(self.exh[:], 0.0), w=[self.exh_r])

    def dump(self, name, src_ap, src_r, dst_ap):
        cx = self.cx
        r = self.dbg_res.setdefault(name, Res("dbg_" + name))
        cx.dma(cx.SP, dst_ap, src_ap, r=[src_r], w=[r], sem_res=r)

    def even(self, b):
        cx = self.cx
        PE, ACT, DVE, POOL, SP = cx.PE, cx.ACT, cx.DVE, cx.POOL, cx.SP
        self.rmsnorm(0, self.hnT, self.hnT_r)
        hnT, hnT_r = self.hnT, self.hnT_r
        with ExitStack() as st:
            B = {}
            for nm, shp, dt in [("qr", [128, NCH, 512], BF16), ("qx", [128, NCH, 512], BF16), ("kr", [128, NCH, 512], BF16),
                                ("kz", [128, NCH, 512], BF16), ("v", [128, NCH, 1024], BF16), ("sg", [128, NCH, 1024], BF16),
                                ("sz", [128, NCH, 1024], BF16), ("xcT", [128, 8, NB], BF16), ("bcT", [128, 4, NB], BF16),
                                ("dt", [128, NCH, 16], F32), ("dta", [128, NCH, 16], F32), ("mixT", [128, 16, NB], BF16),
                                ("cos", [128, NCH, 64], F32), ("sin", [128, NCH, 64], F32)]:
                B[nm] = self.sb(nm, shp, dt, st)
            self.EB = B
            cos, cos_r = B["cos"]
            sin, sin_r = B["sin"]
            cx.dma(SP, cos[:], self.consts["cosT"][:, b * NCH:(b + 1) * NCH, :], w=[cos_r], sem_res=self.xi_r)
            cx.dma(SP, sin[:], self.consts["sinT"][:, b * NCH:(b + 1) * NCH, :], w=[sin_r], sem_res=self.xi_r)
            rot_t = [self.sb(f"rot{i}", [128, 4, 64], F32, st) for i in range(4)]
            qraw = [self.sb(f"qraw{i}", [128, 512], F32, st) for i in range(2)]
            raws = [self.sb(f"xraw{i}", [128, NB + 3], F32, st) for i in range(2)]
            accs = [self.sb(f"xacc{i}", [128, NB], F32, st) for i in range(2)]
            xcT, xcT_r = B["xcT"]
            bcT, bcT_r = B["bcT"]
            for i in range(6):
                slot, slot_r, u = self.stream_get()
                assert u is self.U[("exbc", i)]
                for gi in range(2):
                    tile = 2 * i + gi
                    wv = slot[:, gi * 1024:(gi + 1) * 1024].rearrange("p (k w) -> p k w", k=8)
                    bk, bk_r = self.bank()
                    for kt in range(8):
                        cx.op(PE, lambda e: e.matmul(bk[:], lhsT=wv[:, kt, :], rhs=hnT[:, kt, :], start=(kt == 0), stop=(kt == 7)),
                              r=[slot_r, hnT_r], w=[bk_r])
                    raw, raw_r = raws[tile % 2]
                    acc, acc_r = accs[tile % 2]
                    cw = self.exw[:, tile, :]
                    cx.op(ACT, lambda e: e.copy(out=raw[:, 3:NB + 3], in_=bk[:]), r=[bk_r], w=[raw_r])
                    cx.op(POOL, lambda e: e.tensor_copy(out=raw[:, 0:3], in_=self.exh[:, tile, :]), r=[self.exh_r], w=[raw_r])
                    cx.op(ACT, lambda e: e.activation(out=acc[:], in_=bk[:], func=AF.Identity, bias=self.exb[:, tile:tile + 1], scale=cw[:, 3:4]),
                          r=[bk_r, self.exw_r, self.exb_r], w=[acc_r])
                    for k in (2, 1, 0):
                        cx.op(DVE, lambda e: e.scalar_tensor_tensor(out=acc[:], in0=raw[:, k:k + NB], scalar=cw[:, k:k + 1], in1=acc[:],
                                                                    op0=ALU.mult, op1=ALU.add),
                              r=[raw_r, acc_r, self.exw_r], w=[acc_r])
                    cx.op(POOL, lambda e: e.tensor_copy(out=self.exh[:, tile, :], in_=raw[:, NB:NB + 3]), r=[raw_r], w=[self.exh_r])
                    if tile < 8:
                        cx.op(ACT, lambda e: e.activation(out=xcT[:, tile, :], in_=acc[:], func=AF.Silu), r=[acc_r], w=[xcT_r])
                    else:
                        cx.op(ACT, lambda e: e.activation(out=bcT[:, tile - 8, :], in_=acc[:], func=AF.Silu), r=[acc_r], w=[bcT_r])
            k_ev = 0
            for nm in ["eq", "ek", "ev0", "ev1", "eg0", "eg1", "ez0", "ez1"]:
                slot, slot_r, u = self.stream_get()
                assert u is self.U[nm]
                wv = slot[:, 0:4096].rearrange("p (k w) -> p k w", k=8)
                for c in range(NCH):
                    bk, bk_r = self.bank()
                    for kt in range(8):
                        cx.op(PE, lambda e: e.matmul(bk[:], lhsT=hnT[:, kt, c * 128:(c + 1) * 128], rhs=wv[:, kt, :], start=(kt == 0), stop=(kt == 7)),
                              r=[slot_r, hnT_r], w=[bk_r])
                    if nm in ("eq", "ek"):
                        raw, raw_r = qraw[k_ev % 2]
                        k_ev += 1
                        cx.op(ACT, lambda e: e.copy(out=raw[:], in_=bk[:]), r=[bk_r], w=[raw_r])
                        r4 = raw[:].rearrange("p (h t d) -> p h t d", h=4, t=2)
                        x1, x2 = r4[:, :, 0, :], r4[:, :, 1, :]
                        cb = self.bc_h(cos[:, c, :], 4)
                        sbb = self.bc_h(sin[:, c, :], 4)
                        (t1, t1r), (t2, t2r), (t3, t3r), (t4, t4r) = rot_t
                        cx.op(POOL, lambda e: e.tensor_tensor(out=t1[:], in0=x1, in1=cb, op=ALU.mult), r=[raw_r, cos_r], w=[t1r])
                        cx.op(POOL, lambda e: e.tensor_tensor(out=t2[:], in0=x2, in1=sbb, op=ALU.mult), r=[raw_r, sin_r], w=[t2r])
                        cx.op(POOL, lambda e: e.tensor_tensor(out=t3[:], in0=x1, in1=sbb, op=ALU.mult), r=[raw_r, sin_r], w=[t3r])
                        cx.op(POOL, lambda e: e.tensor_tensor(out=t4[:], in0=x2, in1=cb, op=ALU.mult), r=[raw_r, cos_r], w=[t4r])
                        dst, dst_r = B["qr"] if nm == "eq" else B["kr"]
                        d4 = dst[:, c, :].rearrange("p (h t d) -> p h t d", h=4, t=2)
                        cx.op(DVE, lambda e: e.tensor_tensor(out=d4[:, :, 0, :], in0=t1[:], in1=t2[:], op=ALU.subtract), r=[t1r, t2r], w=[dst_r])
                        cx.op(DVE, lambda e: e.tensor_tensor(out=d4[:, :, 1, :], in0=t3[:], in1=t4[:], op=ALU.add), r=[t3r, t4r], w=[dst_r])
                        d2, d2_r = B["qx"] if nm == "eq" else B["kz"]
                        tab, tab_r = (self.xi, self.xi_r) if nm == "eq" else (self.zeta, self.zeta_r)
                        cx.op(DVE, lambda e: e.tensor_tensor(out=d2[:, c, :].rearrange("p (h d) -> p h d", h=4),
                                                             in0=dst[:, c, :].rearrange("p (h d) -> p h d", h=4),
                                                             in1=self.bc_mid(tab[:], 128), op=ALU.mult),
                              r=[dst_r, tab_r], w=[d2_r])
                    else:
                        half = int(nm[-1])
                        key = {"v": "v", "g": "sg", "z": "sz"}[nm[1]]
                        dst, dst_r = B[key]
                        if key == "v":
                            cx.op(ACT, lambda e: e.copy(out=dst[:, c, half * 512:(half + 1) * 512], in_=bk[:]), r=[bk_r], w=[dst_r])
                        else:
                            cx.op(ACT, lambda e: e.activation(out=dst[:, c, half * 512:(half + 1) * 512], in_=bk[:], func=AF.Silu), r=[bk_r], w=[dst_r])
            slot, slot_r, u = self.stream_get()
            assert u is self.U["edt"]
            wv = slot[:, 0:128].rearrange("p (k w) -> p k w", k=8)
            dtv, dtv_r = self.sb("dtv", [128, 16], F32, st)
            dt, dt_r = B["dt"]
            dta, dta_r = B["dta"]
            for c in range(NCH):
                bk, bk_r = self.bank()
                for kt in range(8):
                    cx.op(PE, lambda e: e.matmul(bk[:, 0:16], lhsT=hnT[:, kt, c * 128:(c + 1) * 128], rhs=wv[:, kt, :], start=(kt == 0), stop=(kt == 7)),
                          r=[slot_r, hnT_r], w=[bk_r])
                cx.op(DVE, lambda e: e.tensor_tensor(out=dtv[:], in0=bk[:, 0:16], in1=self.dtb[:], op=ALU.add), r=[bk_r, self.dtb_r], w=[dtv_r])
                cx.op(ACT, lambda e: e.activation(out=dtv[:], in_=dtv[:], func=AF.Exp), r=[dtv_r], w=[dtv_r])
                cx.op(ACT, lambda e: e.activation(out=dt[:, c, :], in_=dtv[:], func=AF.Ln, bias=self.one_t[:, 0:1]), r=[dtv_r, self.eps_r], w=[dt_r])
                cx.op(DVE, lambda e: e.tensor_tensor(out=dta[:, c, :], in0=dt[:, c, :], in1=self.aneg[:], op=ALU.mult), r=[dt_r, self.aneg_r], w=[dta_r])
            with ExitStack() as st2:
                T = {}
                for nm, shp, dtp in [("qT", [128, 512], BF16), ("qxT", [128, 512], BF16), ("kT", [128, 512], BF16), ("smT", [128, 512], BF16),
                                     ("st6", [128, 4, 6], F32), ("mv", [128, 4, 2], F32), ("rstd4", [128, 4], F32),
                                     ("ytmp", [128, 1024], F32), ("yretb", [128, 1024], BF16),
                                     ("rhsM0", [128, 512], F32), ("rhsM1", [128, 512], F32), ("Lg0", [128, 512], F32), ("Lg1", [128, 512], F32),
                                     ("cbm", [128, 2, 128], F32), ("MT", [128, 16, 128], BF16), ("dec", [128, 16], F32),
                                     ("xs", [128, 1024], BF16), ("Xdt", [128, 1024], BF16), ("Xdec", [128, 1024], BF16), ("Btok", [128, 256], BF16),
                                     ("ea", [128, 32], F32), ("ty", [128, 1024], F32), ("t2", [128, 1024], F32), ("yzb", [128, 1024], BF16),
                                     ("st6g", [128, 2, 6], F32), ("mvg", [128, 2, 2], F32), ("rs2", [128, 2], F32)]:
                    T[nm] = self.sb(nm, shp, dtp, st2)
                for c in range(NCH):
                    if "chunk" not in SKIP:
                        self.even_chunk(b, c, B, T)
                cx.barrier()
            mixT, mixT_r = B["mixT"]
            for ft in range(8):
                slot, slot_r, u = self.stream_get()
                assert u is self.U[("eout", ft)]
                wv = slot[:, 0:2048].rearrange("p (k w) -> p k w", k=16)
                bk, bk_r = self.bank()
                for kt in range(16):
                    cx.op(PE, lambda e: e.matmul(bk[:], lhsT=wv[:, kt, :], rhs=mixT[:, kt, :], start=(kt == 0), stop=(kt == 15)),
                          r=[slot_r, mixT_r], w=[bk_r])
                cx.op(DVE, lambda e: e.tensor_tensor(out=self.xT[:, ft, :], in0=self.xT[:, ft, :], in1=bk[:], op=ALU.add),
                      r=[bk_r, self.xT_r], w=[self.xT_r])
            cx.barrier()

    def even_chunk(self, b, c, B, T):
        gens = []
        if "ret" not in SKIP:
            gens.append(self.even_chunk_ret(b, c, B, T))
        if "ssd" not in SKIP:
            gens.append(self.even_chunk_ssd(b, c, B, T))
        while gens:
            for g in list(gens):
                try:
                    next(g)
                except StopIteration:
                    gens.remove(g)

    def rbank(self):
        self.rbank_i = (getattr(self, "rbank_i", 0) + 1) % 3
        return self.banks[self.rbank_i]

    def sbank(self):
        self.sbank_i = (getattr(self, "sbank_i", 0) + 1) % 3
        return self.banks[3 + self.sbank_i]

    def even_chunk_ret(self, b, c, B, T):
        cx = self.cx
        PE, ACT, DVE, POOL, SP = cx.PE, cx.ACT, cx.DVE, cx.POOL, cx.SP
        cs = slice(c * 128, (c + 1) * 128)
        ib, ib_r = self.ident_b, self.ident_b_r
        for i, (src, dstn) in enumerate([("qr", "qT"), ("qx", "qxT"), ("kr", "kT")]):
            s_t, s_r = B[src]
            d_t, d_r = T[dstn]
            pb, pb_r = self.bankb()
            for h in range(4):
                cx.op(PE, lambda e: e.transpose(pb[:, h * 128:(h + 1) * 128], s_t[:, c, h * 128:(h + 1) * 128], ib[:]), r=[s_r, ib_r], w=[pb_r])
            if i % 2 == 0:
                cx.op(ACT, lambda e: e.copy(out=d_t[:], in_=pb[:, 0:512]), r=[pb_r], w=[d_r])
            else:
                cx.op(DVE, lambda e: e.tensor_copy(out=d_t[:], in_=pb[:, 0:512]), r=[pb_r], w=[d_r])
            yield
        qT, qT_r = T["qT"]
        qxT, qxT_r = T["qxT"]
        kT, kT_r = T["kT"]
        smT, smT_r = T["smT"]
        v, v_r = B["v"]
        bk, bk_r = self.rbank()
        for h in range(4):
            hs = slice(h * 128, (h + 1) * 128)
            cx.op(PE, lambda e: e.matmul(bk[:, hs], lhsT=kT[:, hs], rhs=qT[:, hs], start=True, stop=True), r=[kT_r, qT_r], w=[bk_r])
        yield
        cx.op(DVE, lambda e: e.tensor_tensor(out=smT[:], in0=bk[:], in1=self.dmaskT[:], op=ALU.mult), r=[bk_r, self.dmaskT_r], w=[smT_r])
        yield
        bo = [self.rbank(), self.rbank()]
        for h in range(4):
            hs = slice(h * 128, (h + 1) * 128)
            o_t, o_r = bo[h // 2]
            osl = slice((h % 2) * 256, (h % 2 + 1) * 256)
            cx.op(PE, lambda e: e.matmul(o_t[:, osl], lhsT=smT[:, hs], rhs=v[:, c, h * 256:(h + 1) * 256], start=True, stop=False),
                  r=[smT_r, v_r], w=[o_r])
            cx.op(PE, lambda e: e.matmul(o_t[:, osl], lhsT=qxT[:, hs], rhs=self.retSb[:, h, :], start=False, stop=True),
                  r=[qxT_r, self.retSb_r], w=[o_r])
        yield
        st6, st6_r = T["st6"]
        mv, mv_r = T["mv"]
        rstd4, rstd4_r = T["rstd4"]
        ytmp, ytmp_r = T["ytmp"]
        yretb, yretb_r = T["yretb"]
        sg, sg_r = B["sg"]
        for h in range(4):
            o_t, o_r = bo[h // 2]
            osl = slice((h % 2) * 256, (h % 2 + 1) * 256)
            cx.op(DVE, lambda e: e.bn_stats(out=st6[:, h, :], in_=o_t[:, osl]), r=[o_r], w=[st6_r])
        yield
        for h in range(4):
            cx.op(DVE, lambda e: e.bn_aggr(out=mv[:, h, :], in_=st6[:, h, :]), r=[st6_r], w=[mv_r])
        yield
        cx.op(ACT, lambda e: e.activation(out=rstd4[:], in_=mv[:, :, 1], func=AF.Sqrt, bias=self.eps_t[:, 0:1]), r=[mv_r, self.eps_r], w=[rstd4_r])
        yield
        cx.op(DVE, lambda e: e.reciprocal(out=rstd4[:], in_=rstd4[:]), r=[rstd4_r], w=[rstd4_r])
        yield
        for h in range(4):
            o_t, o_r = bo[h // 2]
            osl = slice((h % 2) * 256, (h % 2 + 1) * 256)
            cx.op(DVE, lambda e: e.tensor_scalar(out=ytmp[:, h * 256:(h + 1) * 256], in0=o_t[:, osl], scalar1=mv[:, h, 0:1], scalar2=rstd4[:, h:h + 1],
                                                 op0=ALU.subtract, op1=ALU.mult),
                  r=[o_r, mv_r, rstd4_r], w=[ytmp_r])
        yield
        cx.op(POOL, lambda e: e.tensor_tensor(out=yretb[:], in0=ytmp[:], in1=sg[:, c, :], op=ALU.mult), r=[ytmp_r, sg_r], w=[yretb_r])
        yield
        mixT, mixT_r = B["mixT"]
        pb, pb_r = self.bankb()
        for t in range(8):
            cx.op(PE, lambda e: e.transpose(pb[:, t * 128:(t + 1) * 128], yretb[:, t * 128:(t + 1) * 128], ib[:]), r=[yretb_r, ib_r], w=[pb_r])
        cx.op(ACT, lambda e: e.copy(out=mixT[:, 0:8, cs], in_=pb[:].rearrange("p (t n) -> p t n", t=8)), r=[pb_r], w=[mixT_r])
        yield
        kz, kz_r = B["kz"]
        bkv = [self.rbank(), self.rbank()]
        for h in range(4):
            k_t, k_r = bkv[h // 2]
            osl = slice((h % 2) * 256, (h % 2 + 1) * 256)
            cx.op(PE, lambda e: e.matmul(k_t[:, osl], lhsT=kz[:, c, h * 128:(h + 1) * 128], rhs=v[:, c, h * 256:(h + 1) * 256], start=True, stop=True),
                  r=[kz_r, v_r], w=[k_r])
        yield
        for h in range(4):
            k_t, k_r = bkv[h // 2]
            osl = slice((h % 2) * 256, (h % 2 + 1) * 256)
            cx.op(DVE, lambda e: e.scalar_tensor_tensor(out=self.retS[:, h, :], in0=self.retS[:, h, :], scalar=RET_GDEC[h], in1=k_t[:, osl],
                                                        op0=ALU.mult, op1=ALU.add),
                  r=[k_r, self.retS_r], w=[self.retS_r])
        yield
        cx.op(ACT, lambda e: e.copy(out=self.retSb[:], in_=self.retS[:]), r=[self.retS_r], w=[self.retSb_r])

    def even_chunk_ssd(self, b, c, B, T):
        cx = self.cx
        PE, ACT, DVE, POOL, SP = cx.PE, cx.ACT, cx.DVE, cx.POOL, cx.SP
        cs = slice(c * 128, (c + 1) * 128)
        ib, ib_r = self.ident_b, self.ident_b_r
        mixT, mixT_r = B["mixT"]
        dt, dt_r = B["dt"]
        dta, dta_r = B["dta"]
        bcT, bcT_r = B["bcT"]
        xcT, xcT_r = B["xcT"]
        sz, sz_r = B["sz"]
        cbm, cbm_r = T["cbm"]
        MT, MT_r = T["MT"]
        dec, dec_r = T["dec"]
        bk, bk_r = self.sbank()
        for g in range(2):
            cx.op(PE, lambda e: e.matmul(bk[:, g * 128:(g + 1) * 128], lhsT=bcT[:, g, cs], rhs=bcT[:, 2 + g, cs], start=True, stop=True),
                  r=[bcT_r], w=[bk_r])
        yield
        cx.op(DVE, lambda e: e.tensor_tensor(out=cbm[:], in0=bk[:, 0:256].rearrange("p (g n) -> p g n", g=2), in1=self.bc_h(self.tri[:], 2), op=ALU.mult),
              r=[bk_r, self.tri_r], w=[cbm_r])
        yield
        for hg in range(4):
            rhsM, rhsM_r = T[f"rhsM{hg % 2}"]
            Lg, Lg_r = T[f"Lg{hg % 2}"]
            g = hg // 2
            cx.op(POOL, lambda e: e.tensor_tensor(out=rhsM[:].rearrange("p (h n) -> p h n", h=4), in0=self.bc_h(self.tri[:], 4),
                                                  in1=self.bc_mid(dta[:, c, hg * 4:(hg + 1) * 4], 128), op=ALU.mult),
                  r=[self.tri_r, dta_r], w=[rhsM_r])
            yield
            bk, bk_r = self.sbank()
            cx.op(PE, lambda e: e.matmul(bk[:], lhsT=self.su[:], rhs=rhsM[:], start=True, stop=True), r=[self.su_r, rhsM_r], w=[bk_r])
            yield
            cx.op(ACT, lambda e: e.activation(out=Lg[:], in_=bk[:], func=AF.Exp), r=[bk_r], w=[Lg_r])
            yield
            L3 = Lg[:].rearrange("p (h n) -> p h n", h=4)
            cx.op(DVE, lambda e: e.tensor_tensor(out=MT[:, hg * 4:(hg + 1) * 4, :], in0=L3, in1=self.bc_h(cbm[:, g, :], 4), op=ALU.mult),
                  r=[Lg_r, cbm_r], w=[MT_r])
            cx.op(POOL, lambda e: e.tensor_copy(out=dec[:, hg * 4:(hg + 1) * 4], in_=L3[:, :, 127]), r=[Lg_r], w=[dec_r])
        yield
        xs, xs_r = T["xs"]
        Xdt, Xdt_r = T["Xdt"]
        Xdec, Xdec_r = T["Xdec"]
        Btok, Btok_r = T["Btok"]
        pb, pb_r = self.bankb()
        for t in range(8):
            cx.op(PE, lambda e: e.transpose(pb[:, t * 128:(t + 1) * 128], xcT[:, t, cs], ib[:]), r=[xcT_r, ib_r], w=[pb_r])
        cx.op(ACT, lambda e: e.copy(out=xs[:], in_=pb[:]), r=[pb_r], w=[xs_r])
        cx.op(DVE, lambda e: e.tensor_tensor(out=Xdt[:].rearrange("p (h d) -> p h d", h=16), in0=pb[:].rearrange("p (h d) -> p h d", h=16),
                                             in1=self.bc_mid(dt[:, c, :], 64), op=ALU.mult),
              r=[pb_r, dt_r], w=[Xdt_r])
        yield
        cx.op(POOL, lambda e: e.tensor_tensor(out=Xdec[:].rearrange("p (h d) -> p h d", h=16), in0=Xdt[:].rearrange("p (h d) -> p h d", h=16),
                                              in1=self.bc_mid(dec[:], 64), op=ALU.mult),
              r=[Xdt_r, dec_r], w=[Xdec_r])
        pb2, pb2_r = self.bankb()
        for g in range(2):
            cx.op(PE, lambda e: e.transpose(pb2[:, g * 128:(g + 1) * 128], bcT[:, g, cs], ib[:]), r=[bcT_r, ib_r], w=[pb2_r])
        cx.op(ACT, lambda e: e.copy(out=Btok[:], in_=pb2[:, 0:256]), r=[pb2_r], w=[Btok_r])
        yield
        ea, ea_r = T["ea"]
        bk, bk_r = self.sbank()
        cx.op(PE, lambda e: e.matmul(bk[:, 0:16], lhsT=self.tri[:], rhs=dta[:, c, :], start=True, stop=True), r=[self.tri_r, dta_r], w=[bk_r])
        cx.op(PE, lambda e: e.matmul(bk[:, 16:32], lhsT=self.ones_f[:], rhs=dta[:, c, :], start=True, stop=True), r=[self.ones_f_r, dta_r], w=[bk_r])
        yield
        cx.op(ACT, lambda e: e.activation(out=ea[:], in_=bk[:, 0:32], func=AF.Exp), r=[bk_r], w=[ea_r])
        yield
        ty, ty_r = T["ty"]
        t2, t2_r = T["t2"]
        for g in range(2):
            gs = slice(g * 512, (g + 1) * 512)
            d_t, d_r = self.sbank()
            for hh in range(8):
                h = g * 8 + hh
                cx.op(PE, lambda e: e.matmul(d_t[:, hh * 64:(hh + 1) * 64], lhsT=MT[:, h, :], rhs=Xdt[:, h * 64:(h + 1) * 64], start=True, stop=True),
                      r=[MT_r, Xdt_r], w=[d_r])
            f_t, f_r = self.sbank()
            cx.op(PE, lambda e: e.matmul(f_t[:], lhsT=bcT[:, 2 + g, cs], rhs=self.ssSb[:, g, :], start=True, stop=True), r=[bcT_r, self.ssSb_r], w=[f_r])
            yield
            cx.op(DVE, lambda e: e.tensor_tensor(out=ty[:, gs].rearrange("p (h d) -> p h d", h=8), in0=f_t[:].rearrange("p (h d) -> p h d", h=8),
                                                 in1=self.bc_mid(ea[:, g * 8:(g + 1) * 8], 64), op=ALU.mult),
                  r=[f_r, ea_r], w=[ty_r])
            cx.op(DVE, lambda e: e.tensor_tensor(out=ty[:, gs], in0=ty[:, gs], in1=d_t[:], op=ALU.add), r=[d_r, ty_r], w=[ty_r])
            yield
        cx.op(POOL, lambda e: e.tensor_tensor(out=t2[:].rearrange("p (h d) -> p h d", h=16), in0=xs[:].rearrange("p (h d) -> p h d", h=16),
                                              in1=self.bc_mid(self.dsk[:], 64), op=ALU.mult),
              r=[xs_r, self.dsk_r], w=[t2_r])
        yield
        cx.op(POOL, lambda e: e.tensor_tensor(out=t2[:], in0=t2[:], in1=ty[:], op=ALU.add), r=[t2_r, ty_r], w=[t2_r])
        yield
        cx.op(POOL, lambda e: e.tensor_tensor(out=t2[:], in0=t2[:], in1=sz[:, c, :], op=ALU.mult), r=[t2_r, sz_r], w=[t2_r])
        yield
        st6g, st6g_r = T["st6g"]
        mvg, mvg_r = T["mvg"]
        rs2, rs2_r = T["rs2"]
        yzb, yzb_r = T["yzb"]
        for g in range(2):
            cx.op(DVE, lambda e: e.bn_stats(out=st6g[:, g, :], in_=t2[:, g * 512:(g + 1) * 512]), r=[t2_r], w=[st6g_r])
        yield
        for g in range(2):
            cx.op(DVE, lambda e: e.bn_aggr(out=mvg[:, g, :], in_=st6g[:, g, :]), r=[st6g_r], w=[mvg_r])
        yield
        cx.op(DVE, lambda e: e.tensor_tensor(out=rs2[:], in0=mvg[:, :, 0], in1=mvg[:, :, 0], op=ALU.mult), r=[mvg_r], w=[rs2_r])
        yield
        cx.op(DVE, lambda e: e.tensor_tensor(out=rs2[:], in0=rs2[:], in1=mvg[:, :, 1], op=ALU.add), r=[mvg_r, rs2_r], w=[rs2_r])
        yield
        cx.op(ACT, lambda e: e.activation(out=rs2[:], in_=rs2[:], func=AF.Sqrt, bias=self.eps_t[:, 0:1]), r=[rs2_r, self.eps_r], w=[rs2_r])
        yield
        cx.op(DVE, lambda e: e.reciprocal(out=rs2[:], in_=rs2[:]), r=[rs2_r], w=[rs2_r])
        yield
        for g in range(2):
            cx.op(ACT, lambda e: e.activation(out=yzb[:, g * 512:(g + 1) * 512], in_=t2[:, g * 512:(g + 1) * 512], func=AF.Copy, scale=rs2[:, g:g + 1]),
                  r=[t2_r, rs2_r], w=[yzb_r])
        yield
        pb, pb_r = self.bankb()
        for t in range(8):
            cx.op(PE, lambda e: e.transpose(pb[:, t * 128:(t + 1) * 128], yzb[:, t * 128:(t + 1) * 128], ib[:]), r=[yzb_r, ib_r], w=[pb_r])
        for t in range(8):
            cx.op(ACT, lambda e: e.activation(out=mixT[:, 8 + t, cs], in_=pb[:, t * 128:(t + 1) * 128], func=AF.Identity, scale=self.ssmn[:, t:t + 1]),
                  r=[pb_r, self.ssmn_r], w=[mixT_r])
        yield
        cdec = ea[:, 16:32]
        for g in range(2):
            s_t, s_r = self.sbank()
            cx.op(PE, lambda e: e.matmul(s_t[:], lhsT=Btok[:, g * 128:(g + 1) * 128], rhs=Xdec[:, g * 512:(g + 1) * 512], start=True, stop=True),
                  r=[Btok_r, Xdec_r], w=[s_r])
            yield
            cx.op(DVE, lambda e: e.tensor_tensor(out=self.ssS[:, g, :].rearrange("p (h d) -> p h d", h=8),
                                                 in0=self.ssS[:, g, :].rearrange("p (h d) -> p h d", h=8),
                                                 in1=self.bc_mid(cdec[:, g * 8:(g + 1) * 8], 64), op=ALU.mult),
                  r=[self.ssS_r, ea_r], w=[self.ssS_r])
            cx.op(DVE, lambda e: e.tensor_tensor(out=self.ssS[:, g, :], in0=self.ssS[:, g, :], in1=s_t[:], op=ALU.add),
                  r=[s_r, self.ssS_r], w=[self.ssS_r])
            yield
        cx.op(ACT, lambda e: e.copy(out=self.ssSb[:], in_=self.ssS[:]), r=[self.ssS_r], w=[self.ssSb_r])


    def setup_odd(self):
        cx = self.cx
        PE, ACT, DVE, POOL, SP = cx.PE, cx.ACT, cx.DVE, cx.POOL, cx.SP
        if not hasattr(self, "ident_b"):
            self.ident_b, self.ident_b_r = self.sb("ident_b", [128, 128], BF16)
            cx.op(DVE, lambda e: e.tensor_copy(out=self.ident_b[:], in_=self.ident_f[:]), r=[self.ident_f_r], w=[self.ident_b_r])
        self.chalo, self.chalo_r = self.sb("chalo", [128, 4, 30], F32)
        self.ocw, self.ocw_r = self.sb("ocw", [128, 4, 31], F32)
        self.ocb, self.ocb_r = self.sb("ocb", [128, 4], F32)
        self.olg, self.olg_r = self.sb("olg", [128, 4], F32)
        self.olb, self.olb_r = self.sb("olb", [128, 4], F32)
        self.A8r, self.A8r_r = self.sb("A8r", [128, 2, 16], F32)
        self.A8i, self.A8i_r = self.sb("A8i", [128, 2, 16], F32)
        self.Xs, self.Xs_r = self.sb("Xs", [128, 2, 16], F32)
        cx.op(DVE, lambda e: e.memset(self.chalo[:], 0.0), w=[self.chalo_r])
        cx.op(DVE, lambda e: e.memset(self.Xs[:], 0.0), w=[self.Xs_r])
        for k in range(31):
            cx.dma(SP, self.ocw[:, :, k], self.o_dw_w[0, k].rearrange("(t p) -> p t", p=128), w=[self.ocw_r], sem_res=self.ocw_r,
                   allow_slow_non_contiguous=True)
        for t, r, src in [(self.ocb, self.ocb_r, self.o_dw_b), (self.olg, self.olg_r, self.o_ln_g), (self.olb, self.olb_r, self.o_ln_b)]:
            cx.dma(SP, t[:], src[0].rearrange("(t p) -> p t", p=128), w=[r], sem_res=r, allow_slow_non_contiguous=True)
        with ExitStack() as st:
            def tl(nm, shp, dt=F32):
                return self.sb("s5_" + nm, shp, dt, st)
            lr, lr_r = tl("lr", [128, 16])
            li, li_r = tl("li", [128, 16])
            stp, stp_r = tl("stp", [128, 16])
            Bre, Bre_r = tl("Bre", [128, 16, 16])
            Bim, Bim_r = tl("Bim", [128, 16, 16])
            Cre, Cre_r = tl("Cre", [128, 16, 16])
            Cim, Cim_r = tl("Cim", [128, 16, 16])
            Dbc, Dbc_r = tl("Dbc", [128, 32, 16])
            tzm, tzm_r = tl("tzm", [128, 128])
            cx.dma(SP, tzm[:], self.consts["tzmask"], w=[tzm_r], sem_res=tzm_r)
            cx.dma(SP, lr[:], self.o_a_re[0].rearrange("g n -> (g n)").rearrange("(p q) -> q p", q=128), w=[lr_r], sem_res=lr_r,
                   allow_slow_non_contiguous=True)
            cx.dma(SP, li[:], self.o_a_im[0].rearrange("g n -> (g n)").rearrange("(p q) -> q p", q=128), w=[li_r], sem_res=li_r,
                   allow_slow_non_contiguous=True)
            ls2 = self.o_log_step[0].rearrange("(p gh) -> gh p", gh=2)
            for gh in range(2):
                cx.dma(SP, stp[gh * 64:(gh + 1) * 64, :], ls2[gh].partition_broadcast(64), w=[stp_r], sem_res=stp_r, allow_slow_non_contiguous=True)
            cx.dma(SP, Bre[:], self.o_b_re[0].rearrange("g n c -> (g n) c").rearrange("(p q) c -> q p c", q=128), w=[Bre_r], sem_res=Bre_r,
                   allow_slow_non_contiguous=True)
            cx.dma(SP, Bim[:], self.o_b_im[0].rearrange("g n c -> (g n) c").rearrange("(p q) c -> q p c", q=128), w=[Bim_r], sem_res=Bim_r,
                   allow_slow_non_contiguous=True)
            for (dst, dst_r, src) in [(Cre, Cre_r, self.o_c_re), (Cim, Cim_r, self.o_c_im)]:
                for g in range(32):
                    p, gh = g // 2, g % 2
                    cx.dma(SP, dst[gh * 64:(gh + 1) * 64, p, :], src[0, g].rearrange("co n -> n co"), w=[dst_r], sem_res=dst_r,
                           allow_slow_non_contiguous=True)
            cx.dma(SP, Dbc[:].rearrange("p g c -> p (g c)"), self.o_d[0].partition_broadcast(128), w=[Dbc_r], sem_res=Dbc_r, allow_slow_non_contiguous=True)
            cx.op(ACT, lambda e: e.activation(out=stp[:], in_=stp[:], func=AF.Exp), r=[stp_r], w=[stp_r])
            lrs, lrs_r = tl("lrs", [128, 16])
            lis, lis_r = tl("lis", [128, 16])
            cx.op(DVE, lambda e: e.tensor_tensor(out=lrs[:], in0=lr[:], in1=stp[:], op=ALU.mult), r=[lr_r, stp_r], w=[lrs_r])
            cx.op(DVE, lambda e: e.tensor_tensor(out=lis[:], in0=li[:], in1=stp[:], op=ALU.mult), r=[li_r, stp_r], w=[lis_r])
            NM = 17
            mag, mag_r = tl("mag", [128, 16, NM])
            ang, ang_r = tl("ang", [128, 16, NM])
            Pre, Pre_r = tl("Pre", [128, 16, NM])
            Pim, Pim_r = tl("Pim", [128, 16, NM])
            tmpa, tmpa_r = tl("tmpa", [128, 16, NM])
            for mi in range(NM):
                m = float(mi - 8)
                cx.op(ACT, lambda e: e.activation(out=mag[:, :, mi], in_=lrs[:], func=AF.Exp, scale=m), r=[lrs_r], w=[mag_r])
                cx.op(DVE, lambda e: e.tensor_scalar(out=ang[:, :, mi], in0=lis[:], scalar1=m, scalar2=None, op0=ALU.mult), r=[lis_r], w=[ang_r])
            TWO_PI = 2.0 * math.pi
            MAGIC = 12582912.0

            def sin_of(dst, dst_r, shift):
                cx.op(DVE, lambda e: e.tensor_scalar(out=tmpa[:], in0=ang[:], scalar1=shift, scalar2=1.0 / TWO_PI, op0=ALU.add, op1=ALU.mult),
                      r=[ang_r], w=[tmpa_r])
                cx.op(DVE, lambda e: e.tensor_scalar(out=dst[:], in0=tmpa[:], scalar1=MAGIC, scalar2=None, op0=ALU.add), r=[tmpa_r], w=[dst_r])
                cx.op(DVE, lambda e: e.tensor_scalar(out=dst[:], in0=dst[:], scalar1=-MAGIC, scalar2=None, op0=ALU.add), r=[dst_r], w=[dst_r])
                cx.op(DVE, lambda e: e.tensor_tensor(out=tmpa[:], in0=tmpa[:], in1=dst[:], op=ALU.subtract), r=[tmpa_r, dst_r], w=[tmpa_r])
                cx.op(DVE, lambda e: e.tensor_scalar(out=tmpa[:], in0=tmpa[:], scalar1=TWO_PI, scalar2=math.pi, op0=ALU.mult, op1=ALU.min),
                      r=[tmpa_r], w=[tmpa_r])
                cx.op(DVE, lambda e: e.tensor_scalar(out=tmpa[:], in0=tmpa[:], scalar1=-math.pi, scalar2=None, op0=ALU.max), r=[tmpa_r], w=[tmpa_r])
                cx.op(ACT, lambda e: e.activation(out=dst[:], in_=tmpa[:], func=AF.Sin), r=[tmpa_r], w=[dst_r])

            sin_of(Pim, Pim_r, 0.0)
            sin_of(Pre, Pre_r, math.pi / 2.0)
            cx.op(DVE, lambda e: e.tensor_tensor(out=Pre[:], in0=Pre[:], in1=mag[:], op=ALU.mult), r=[Pre_r, mag_r], w=[Pre_r])
            cx.op(DVE, lambda e: e.tensor_tensor(out=Pim[:], in0=Pim[:], in1=mag[:], op=ALU.mult), r=[Pim_r, mag_r], w=[Pim_r])

            def P(which, m):
                return (Pre if which == "re" else Pim)[:, :, m + 8]
            cx.op(DVE, lambda e: e.tensor_copy(out=self.A8r[:, 0, :], in_=P("re", 8)), r=[Pre_r], w=[self.A8r_r])
            cx.op(DVE, lambda e: e.tensor_copy(out=self.A8r[:, 1, :], in_=P("re", 8)), r=[Pre_r], w=[self.A8r_r])
            cx.op(DVE, lambda e: e.tensor_scalar(out=self.A8i[:, 0, :], in0=P("im", 8), scalar1=-1.0, scalar2=None, op0=ALU.mult), r=[Pim_r], w=[self.A8i_r])
            cx.op(DVE, lambda e: e.tensor_copy(out=self.A8i[:, 1, :], in_=P("im", 8)), r=[Pim_r], w=[self.A8i_r])
            fre, fre_r = tl("fre", [128, 16])
            fim, fim_r = tl("fim", [128, 16])
            den, den_r = tl("den", [128, 16])
            am1, am1_r = tl("am1", [128, 16])
            t16, t16_r = tl("t16", [128, 16])
            cx.op(DVE, lambda e: e.tensor_tensor(out=den[:], in0=lr[:], in1=lr[:], op=ALU.mult), r=[lr_r], w=[den_r])
            cx.op(DVE, lambda e: e.tensor_tensor(out=t16[:], in0=li[:], in1=li[:], op=ALU.mult), r=[li_r], w=[t16_r])
            cx.op(DVE, lambda e: e.tensor_tensor(out=den[:], in0=den[:], in1=t16[:], op=ALU.add), r=[den_r, t16_r], w=[den_r])
            cx.op(DVE, lambda e: e.reciprocal(out=den[:], in_=den[:]), r=[den_r], w=[den_r])
            cx.op(DVE, lambda e: e.tensor_scalar(out=am1[:], in0=P("re", 1), scalar1=-1.0, scalar2=None, op0=ALU.add), r=[Pre_r], w=[am1_r])
            cx.op(DVE, lambda e: e.tensor_tensor(out=fre[:], in0=am1[:], in1=lr[:], op=ALU.mult), r=[am1_r, lr_r], w=[fre_r])
            cx.op(DVE, lambda e: e.tensor_tensor(out=t16[:], in0=P("im", 1), in1=li[:], op=ALU.mult), r=[Pim_r, li_r], w=[t16_r])
            cx.op(DVE, lambda e: e.tensor_tensor(out=fre[:], in0=fre[:], in1=t16[:], op=ALU.add), r=[fre_r, t16_r], w=[fre_r])
            cx.op(DVE, lambda e: e.tensor_tensor(out=fre[:], in0=fre[:], in1=den[:], op=ALU.mult), r=[fre_r, den_r], w=[fre_r])
            cx.op(DVE, lambda e: e.tensor_tensor(out=fim[:], in0=P("im", 1), in1=lr[:], op=ALU.mult), r=[Pim_r, lr_r], w=[fim_r])
            cx.op(DVE, lambda e: e.tensor_tensor(out=t16[:], in0=am1[:], in1=li[:], op=ALU.mult), r=[am1_r, li_r], w=[t16_r])
            cx.op(DVE, lambda e: e.tensor_tensor(out=fim[:], in0=fim[:], in1=t16[:], op=ALU.subtract), r=[fim_r, t16_r], w=[fim_r])
            cx.op(DVE, lambda e: e.tensor_tensor(out=fim[:], in0=fim[:], in1=den[:], op=ALU.mult), r=[fim_r, den_r], w=[fim_r])

            def cmul(dre, dre_r, dim_, dim_r, are, aim, a_rs, bre, bim, b_rs, tA, tA_r, neg_im=False):
                cx.op(DVE, lambda e: e.tensor_tensor(out=dre, in0=are, in1=bre, op=ALU.mult), r=a_rs + b_rs, w=[dre_r])
                cx.op(DVE, lambda e: e.tensor_tensor(out=tA, in0=aim, in1=bim, op=ALU.mult), r=a_rs + b_rs, w=[tA_r])
                cx.op(DVE, lambda e: e.tensor_tensor(out=dre, in0=dre, in1=tA, op=ALU.subtract), r=[dre_r, tA_r], w=[dre_r])
                cx.op(DVE, lambda e: e.tensor_tensor(out=dim_, in0=are, in1=bim, op=ALU.mult), r=a_rs + b_rs, w=[dim_r])
                cx.op(DVE, lambda e: e.tensor_tensor(out=tA, in0=aim, in1=bre, op=ALU.mult), r=a_rs + b_rs, w=[tA_r])
                cx.op(DVE, lambda e: e.tensor_tensor(out=dim_, in0=dim_, in1=tA, op=ALU.add), r=[dim_r, tA_r], w=[dim_r])

            def bc16(ap2):
                return ap2.unsqueeze(2).to_broadcast([128, 16, 16])
            Bbr, Bbr_r = tl("Bbr", [128, 16, 16])
            Bbi, Bbi_r = tl("Bbi", [128, 16, 16])
            t3, t3_r = tl("t3", [128, 16, 16])
            cmul(Bbr[:], Bbr_r, Bbi[:], Bbi_r, bc16(fre[:]), bc16(fim[:]), [fre_r, fim_r], Bre[:], Bim[:], [Bre_r, Bim_r], t3[:], t3_r)
            BcTr, BcTr_r = tl("BcTr", [128, 16, 8, 16])
            BcTi, BcTi_r = tl("BcTi", [128, 16, 8, 16])
            BpTr, BpTr_r = tl("BpTr", [128, 16, 8, 16])
            BpTi, BpTi_r = tl("BpTi", [128, 16, 8, 16])
            CAr, CAr_r = tl("CAr", [128, 16, 8, 16])
            CAi, CAi_r = tl("CAi", [128, 16, 8, 16])
            for s in range(8):
                cmul(BcTr[:, :, s, :], BcTr_r, BcTi[:, :, s, :], BcTi_r, bc16(P("re", 7 - s)), bc16(P("im", 7 - s)), [Pre_r, Pim_r],
                     Bbr[:], Bbi[:], [Bbr_r, Bbi_r], t3[:], t3_r)
                cmul(BpTr[:, :, s, :], BpTr_r, BpTi[:, :, s, :], BpTi_r, bc16(P("re", -1 - s)), bc16(P("im", -1 - s)), [Pre_r, Pim_r],
                     Bbr[:], Bbi[:], [Bbr_r, Bbi_r], t3[:], t3_r)
                cmul(CAr[:, :, s, :], CAr_r, CAi[:, :, s, :], CAi_r, bc16(P("re", s + 1)), bc16(P("im", s + 1)), [Pre_r, Pim_r],
                     Cre[:], Cim[:], [Cre_r, Cim_r], t3[:], t3_r)
            CcU, CcU_r = tl("CcU", [128, 16, 2, 2, 128], BF16)
            cx.op(POOL, lambda e: e.memset(CcU[:], 0.0), w=[CcU_r])
            for gh in range(2):
                ps_ = slice(gh * 64, (gh + 1) * 64)
                cx.op(DVE, lambda e: e.tensor_copy(out=CcU[ps_, :, gh, 0, :], in_=CAr[ps_].rearrange("q p j c -> q p (j c)")), r=[CAr_r], w=[CcU_r])
                cx.op(DVE, lambda e: e.tensor_scalar(out=CcU[ps_, :, gh, 1, :], in0=CAi[ps_].rearrange("q p j c -> q p (j c)"), scalar1=-1.0, scalar2=None,
                                                     op0=ALU.mult), r=[CAi_r], w=[CcU_r])
            ccflat = CcU[:].rearrange("q p a b n -> q (p a b n)")
            for i in range(2):
                u = self.U[("s5cc", i)]
                dst = self.wscr[u["off"]: u["off"] + 128 * 4096].rearrange("(p l) -> p l", p=128)
                cx.dma(SP, dst, ccflat[:, i * 4096:(i + 1) * 4096], r=[CcU_r], w=[self.wscr_res], sem_res=CcU_r)
            BcU, BcU_r = tl("BcU", [128, 16, 2, 2, 128], BF16)
            cx.op(POOL, lambda e: e.memset(BcU[:], 0.0), w=[BcU_r])
            for p in range(16):
                for ri, (src, src_r) in enumerate([(BcTr, BcTr_r), (BcTi, BcTi_r)]):
                    bk, bk_r = self.bank()
                    cx.op(PE, lambda e: e.transpose(bk[:, 0:128], src[:, p, :, :].rearrange("q s c -> q (s c)"), self.ident_f[:]),
                          r=[src_r, self.ident_f_r], w=[bk_r])
                    for gh in range(2):
                        cx.op(ACT if gh == 0 else DVE, lambda e: (e.copy if gh == 0 else e.tensor_copy)(out=BcU[:, p, gh, ri, gh * 64:(gh + 1) * 64],
                                                                                                 in_=bk[:, gh * 64:(gh + 1) * 64]),
                              r=[bk_r], w=[BcU_r])
            bcflat = BcU[:].rearrange("q p a b n -> q (p a b n)")
            for i in range(2):
                u = self.U[("s5bc", i)]
                dst = self.wscr[u["off"]: u["off"] + 128 * 4096].rearrange("(p l) -> p l", p=128)
                cx.dma(SP, dst, bcflat[:, i * 4096:(i + 1) * 4096], r=[BcU_r], w=[self.wscr_res], sem_res=BcU_r)
            TzU, TzU_r = tl("TzU", [128, 32, 128], BF16)
            lm = [tl(f"lm{i}", [128, 128]) for i in range(2)]
            tzt, tzt_r = tl("tzt", [128, 128])
            for g in range(32):
                p, gh = g // 2, g % 2
                ps_ = slice(gh * 64, (gh + 1) * 64)
                (l0, l0_r), (l1, l1_r) = lm
                cx.op(POOL, lambda e: e.memset(l0[:], 0.0), w=[l0_r])
                cx.op(POOL, lambda e: e.memset(l1[:], 0.0), w=[l1_r])
                cx.op(POOL, lambda e: e.tensor_copy(out=l0[ps_, :], in_=BpTr[ps_, p, :, :].rearrange("q s c -> q (s c)")), r=[BpTr_r], w=[l0_r])
                cx.op(POOL, lambda e: e.tensor_scalar(out=l1[ps_, :], in0=BpTi[ps_, p, :, :].rearrange("q s c -> q (s c)"), scalar1=-1.0, scalar2=None,
                                                      op0=ALU.mult), r=[BpTi_r], w=[l1_r])
                bk, bk_r = self.bank()
                cx.op(PE, lambda e: e.matmul(bk[:, 0:128], lhsT=l0[:], rhs=CAr[:, p, :, :].rearrange("q j c -> q (j c)"), start=True, stop=False),
                      r=[l0_r, CAr_r], w=[bk_r])
                cx.op(PE, lambda e: e.matmul(bk[:, 0:128], lhsT=l1[:], rhs=CAi[:, p, :, :].rearrange("q j c -> q (j c)"), start=False, stop=True),
                      r=[l1_r, CAi_r], w=[bk_r])
                cx.op(DVE, lambda e: e.tensor_tensor(out=tzt[:], in0=bk[:, 0:128], in1=tzm[:], op=ALU.mult), r=[bk_r, tzm_r], w=[tzt_r])
                cx.op(DVE, lambda e: e.tensor_tensor(out=TzU[:, g, :].rearrange("q (j c) -> q j c", j=8),
                                                     in0=self.ident_f[:].rearrange("q (j c) -> q j c", j=8),
                                                     in1=Dbc[:, g, :].unsqueeze(1).to_broadcast([128, 8, 16]), op=ALU.mult),
                      r=[self.ident_f_r, Dbc_r], w=[TzU_r])
                cx.op(DVE, lambda e: e.tensor_tensor(out=TzU[:, g, :], in0=TzU[:, g, :], in1=tzt[:], op=ALU.add), r=[tzt_r, TzU_r], w=[TzU_r])
            u = self.U["s5tz"]
            dst = self.wscr[u["off"]: u["off"] + 128 * 4096].rearrange("(p l) -> p l", p=128)
            cx.dma(SP, dst, TzU[:].rearrange("q g n -> q (g n)"), r=[TzU_r], w=[self.wscr_res], sem_res=TzU_r)
            if "s5setup" in self.dbg:
                nc = self.nc
                o1 = nc.dram_tensor("dbg_tz", [128, 4096], BF16, kind="ExternalOutput").ap()
                o2 = nc.dram_tensor("dbg_bc", [128, 8192], BF16, kind="ExternalOutput").ap()
                o3 = nc.dram_tensor("dbg_cc", [128, 8192], BF16, kind="ExternalOutput").ap()
                o4 = nc.dram_tensor("dbg_pre", [128, 16 * 17], F32, kind="ExternalOutput").ap()
                o5 = nc.dram_tensor("dbg_pim", [128, 16 * 17], F32, kind="ExternalOutput").ap()
                o6 = nc.dram_tensor("dbg_bbr", [128, 256], F32, kind="ExternalOutput").ap()
                self.dump("tz", TzU[:].rearrange("q g n -> q (g n)"), TzU_r, o1)
                self.dump("bc", bcflat, BcU_r, o2)
                self.dump("cc", ccflat, CcU_r, o3)
                self.dump("pre", Pre[:].rearrange("q p m -> q (p m)"), Pre_r, o4)
                self.dump("pim", Pim[:].rearrange("q p m -> q (p m)"), Pim_r, o5)
                self.dump("bbr", Bbr[:].rearrange("q p m -> q (p m)"), Bbr_r, o6)
            cx.barrier(engines=cx.compute + [cx.SP], with_dma=True)

    def odd(self, b):
        cx = self.cx
        PE, ACT, DVE, POOL, SP = cx.PE, cx.ACT, cx.DVE, cx.POOL, cx.SP
        self.rmsnorm(2, self.hnT, self.hnT_r)
        hnT, hnT_r = self.hnT, self.hnT_r
        NC8 = NB // 8
        with ExitStack() as st:
            def tl(nm, shp, dt=F32):
                return self.sb("o_" + nm, shp, dt, st)
            mixT, mixT_r = tl("mixT", [128, 8, NB], BF16)
            cbuf, cbuf_r = tl("cbuf", [128, 4, NB + 30])
            cacc, cacc_r = tl("cacc", [128, 4, NB])
            sig = [tl(f"sig{i}", [128, NB]) for i in range(2)]
            uS, uS_r = tl("uS", [64, 32, 128], BF16)
            U, U_r = tl("U", [128, 32, NC8], BF16)
            V, V_r = tl("V", [128, 2, 16, NC8])
            Xst, Xst_r = tl("Xst", [128, 2, 16, NC8], BF16)
            slot, slot_r, u = self.stream_get()
            assert u is self.U["ou"]
            wv = slot[:, 0:4096].rearrange("p (k w) -> p k w", k=8)
            for s in range(8):
                bk, bk_r = self.bank()
                for kt in range(8):
                    lh = hnT[:, kt, :].rearrange("p (c s) -> p s c", s=8)[:, s, :]
                    cx.op(PE, lambda e: e.matmul(bk[0:64, :], lhsT=lh, rhs=wv[:, kt, :], start=(kt == 0), stop=(kt == 7)), r=[slot_r, hnT_r], w=[bk_r])
                src = bk[0:64, :].rearrange("p (g c) -> p g c", g=32)
                if s % 2 == 0:
                    cx.op(ACT, lambda e: e.copy(out=uS[:, :, s * 16:(s + 1) * 16], in_=src), r=[bk_r], w=[uS_r])
                else:
                    cx.op(DVE, lambda e: e.tensor_copy(out=uS[:, :, s * 16:(s + 1) * 16], in_=src), r=[bk_r], w=[uS_r])
            for g8 in range(4):
                pb, pb_r = self.bankb()
                for gi in range(8):
                    g = g8 * 8 + gi
                    cx.op(PE, lambda e: e.transpose(pb[:, gi * 64:(gi + 1) * 64], uS[:, g, :], self.ident_b[0:64, 0:64]), r=[uS_r, self.ident_b_r], w=[pb_r])
                if g8 % 2 == 0:
                    cx.op(ACT, lambda e: e.copy(out=U[:, g8 * 8:(g8 + 1) * 8, :].rearrange("p g c -> p (g c)"), in_=pb[:, 0:512]), r=[pb_r], w=[U_r])
                else:
                    cx.op(DVE, lambda e: e.tensor_copy(out=U[:, g8 * 8:(g8 + 1) * 8, :].rearrange("p g c -> p (g c)"), in_=pb[:, 0:512]), r=[pb_r], w=[U_r])
            bcs = []
            for i in range(2):
                slot, slot_r, u = self.stream_get(hold_prev=i)
                assert u is self.U[("s5bc", i)]
                bcs.append((slot[:, 0:4096].rearrange("q (p a b n) -> q p a b n", p=8, a=2, b=2), slot_r))
            for ri in range(2):
                for ph in range(2):
                    bk, bk_r = self.bank()
                    for pi in range(8):
                        p = ph * 8 + pi
                        bcv, bc_r = bcs[p // 8]
                        for gh in range(2):
                            cx.op(PE, lambda e: e.matmul(bk[:, pi * 64:(pi + 1) * 64], lhsT=bcv[:, p % 8, gh, ri, :], rhs=U[:, 2 * p + gh, :],
                                                         start=(gh == 0), stop=(gh == 1)), r=[bc_r, U_r], w=[bk_r])
                    cx.op(ACT, lambda e: e.copy(out=V[:, ri, ph * 8:(ph + 1) * 8, :].rearrange("q p c -> q (p c)"), in_=bk[:]), r=[bk_r], w=[V_r])
            T1, T1_r = tl("T1", [128, 2, 16])
            T2, T2_r = tl("T2", [128, 2, 16])
            Xs, Xs_r = self.Xs, self.Xs_r
            for c in range(NC8):
                cx.op(POOL, lambda e: e.tensor_copy(out=Xst[:, :, :, c], in_=Xs[:]), r=[Xs_r], w=[Xst_r])
                cx.op(POOL, lambda e: e.tensor_tensor(out=T1[:], in0=Xs[:], in1=self.A8r[:], op=ALU.mult), r=[Xs_r, self.A8r_r], w=[T1_r])
                cx.op(POOL, lambda e: e.tensor_tensor(out=T2[:, 0, :], in0=Xs[:, 1, :], in1=self.A8i[:, 0, :], op=ALU.mult), r=[Xs_r, self.A8i_r], w=[T2_r])
                cx.op(POOL, lambda e: e.tensor_tensor(out=T2[:, 1, :], in0=Xs[:, 0, :], in1=self.A8i[:, 1, :], op=ALU.mult), r=[Xs_r, self.A8i_r], w=[T2_r])
                cx.op(POOL, lambda e: e.tensor_tensor(out=T1[:], in0=T1[:], in1=T2[:], op=ALU.add), r=[T1_r, T2_r], w=[T1_r])
                cx.op(POOL, lambda e: e.tensor_tensor(out=Xs[:], in0=T1[:], in1=V[:, :, :, c], op=ALU.add), r=[T1_r, V_r], w=[Xs_r])
            for i in range(4):
                slot, slot_r, u = self.stream_get()
                assert u is self.U[("oin", i)]
                bks = []
                for gi in range(2):
                    wv = slot[:, gi * 1024:(gi + 1) * 1024].rearrange("p (k w) -> p k w", k=8)
                    bk, bk_r = self.bank()
                    for kt in range(8):
                        cx.op(PE, lambda e: e.matmul(bk[:], lhsT=wv[:, kt, :], rhs=hnT[:, kt, :], start=(kt == 0), stop=(kt == 7)), r=[slot_r, hnT_r], w=[bk_r])
                    bks.append((bk, bk_r))
                sg_t, sg_r = sig[i % 2]
                cx.op(ACT, lambda e: e.activation(out=sg_t[:], in_=bks[1][0][:], func=AF.Sigmoid), r=[bks[1][1]], w=[sg_r])
                cx.op(DVE, lambda e: e.tensor_tensor(out=cbuf[:, i, 30:NB + 30], in0=bks[0][0][:], in1=sg_t[:], op=ALU.mult), r=[bks[0][1], sg_r], w=[cbuf_r])
            cx.op(DVE, lambda e: e.tensor_copy(out=cbuf[:, :, 0:30], in_=self.chalo[:]), r=[self.chalo_r], w=[cbuf_r])
            cx.op(DVE, lambda e: e.tensor_copy(out=self.chalo[:], in_=cbuf[:, :, NB:NB + 30]), r=[cbuf_r], w=[self.chalo_r])
            for i in range(4):
                cx.op(DVE, lambda e: e.tensor_scalar(out=cacc[:, i, :], in0=cbuf[:, i, 30:NB + 30], scalar1=self.ocw[:, i, 30:31], scalar2=self.ocb[:, i:i + 1],
                                                     op0=ALU.mult, op1=ALU.add), r=[cbuf_r, self.ocw_r, self.ocb_r], w=[cacc_r])
                for k in range(30):
                    cx.op(DVE, lambda e: e.scalar_tensor_tensor(out=cacc[:, i, :], in0=cbuf[:, i, k:k + NB], scalar=self.ocw[:, i, k:k + 1], in1=cacc[:, i, :],
                                                                op0=ALU.mult, op1=ALU.add), r=[cbuf_r, cacc_r, self.ocw_r], w=[cacc_r])
            sq = [tl(f"sq{i}", [128, NB]) for i in range(2)]
            b1, b1_r = self.bank()
            b2, b2_r = self.bank()
            for i in range(4):
                cx.op(PE, lambda e: e.matmul(b1[:], lhsT=self.ones_f[:], rhs=cacc[:, i, :], start=(i == 0), stop=(i == 3)), r=[self.ones_f_r, cacc_r], w=[b1_r])
            for i in range(4):
                s_t, s_r = sq[i % 2]
                cx.op(ACT, lambda e: e.activation(out=s_t[:], in_=cacc[:, i, :], func=AF.Square), r=[cacc_r], w=[s_r])
                cx.op(PE, lambda e: e.matmul(b2[:], lhsT=self.ones_f[:], rhs=s_t[:], start=(i == 0), stop=(i == 3)), r=[self.ones_f_r, s_r], w=[b2_r])
            mean, mean_r = tl("mean", [128, NB])
            var, var_r = tl("var", [128, NB])
            cx.op(DVE, lambda e: e.tensor_scalar(out=mean[:], in0=b1[:], scalar1=1.0 / 512, scalar2=None, op0=ALU.mult), r=[b1_r], w=[mean_r])
            cx.op(DVE, lambda e: e.tensor_tensor(out=var[:], in0=mean[:], in1=mean[:], op=ALU.mult), r=[mean_r], w=[var_r])
            cx.op(DVE, lambda e: e.scalar_tensor_tensor(out=var[:], in0=b2[:], scalar=1.0 / 512, in1=var[:], op0=ALU.mult, op1=ALU.subtract),
                  r=[b2_r, var_r], w=[var_r])
            cx.op(ACT, lambda e: e.activation(out=var[:], in_=var[:], func=AF.Sqrt, bias=self.eps_t[:, 0:1]), r=[var_r, self.eps_r], w=[var_r])
            cx.op(DVE, lambda e: e.reciprocal(out=var[:], in_=var[:]), r=[var_r], w=[var_r])
            for i in range(4):
                cx.op(DVE, lambda e: e.tensor_tensor(out=cacc[:, i, :], in0=cacc[:, i, :], in1=mean[:], op=ALU.subtract), r=[cacc_r, mean_r], w=[cacc_r])
                cx.op(DVE, lambda e: e.tensor_tensor(out=cacc[:, i, :], in0=cacc[:, i, :], in1=var[:], op=ALU.mult), r=[cacc_r, var_r], w=[cacc_r])
                cx.op(ACT, lambda e: e.activation(out=mixT[:, i, :], in_=cacc[:, i, :], func=AF.Silu, bias=self.olb[:, i:i + 1], scale=self.olg[:, i:i + 1]),
                      r=[cacc_r, self.olg_r, self.olb_r], w=[mixT_r])
            slot, tz_r, u = self.stream_get()
            assert u is self.U["s5tz"]
            tzv = slot[:, 0:4096].rearrange("q (g n) -> q g n", g=32)
            ccs = []
            for i in range(2):
                slot, slot_r, u = self.stream_get(hold_prev=i + 1)
                assert u is self.U[("s5cc", i)]
                ccs.append((slot[:, 0:4096].rearrange("q (p a b n) -> q p a b n", p=8, a=2, b=2), slot_r))
            stok, stok_r = tl("stok", [64, 8, 512])
            for g4 in range(8):
                bk, bk_r = self.bank()
                for gi in range(4):
                    g = g4 * 4 + gi
                    p, gh = g // 2, g % 2
                    ccv, cc_r = ccs[p // 8]
                    reg = bk[0:64, gi * 128:(gi + 1) * 128]
                    cx.op(PE, lambda e: e.matmul(reg, lhsT=U[:, g, :], rhs=tzv[:, g, :], start=True, stop=False), r=[U_r, tz_r], w=[bk_r])
                    cx.op(PE, lambda e: e.matmul(reg, lhsT=Xst[:, 0, p, :], rhs=ccv[:, p % 8, gh, 0, :], start=False, stop=False), r=[Xst_r, cc_r], w=[bk_r])
                    cx.op(PE, lambda e: e.matmul(reg, lhsT=Xst[:, 1, p, :], rhs=ccv[:, p % 8, gh, 1, :], start=False, stop=True), r=[Xst_r, cc_r], w=[bk_r])
                src = bk[0:64, :].rearrange("p (g j c) -> p j g c", g=4, j=8)
                dst = stok[:, :, g4 * 64:(g4 + 1) * 64].rearrange("p j (g c) -> p j g c", g=4)
                if g4 % 2 == 0:
                    cx.op(ACT, lambda e: e.copy(out=dst, in_=src), r=[bk_r], w=[stok_r])
                else:
                    cx.op(DVE, lambda e: e.tensor_copy(out=dst, in_=src), r=[bk_r], w=[stok_r])
            syT, syT_r = tl("syT", [128, 4, NB])
            for t in range(4):
                bk, bk_r = self.bank()
                for j in range(8):
                    cx.op(PE, lambda e: e.transpose(bk[:, j * 64:(j + 1) * 64], stok[:, j, t * 128:(t + 1) * 128], self.ident_f[0:64, 0:64]),
                          r=[stok_r, self.ident_f_r], w=[bk_r])
                src = bk[:].rearrange("p (j c) -> p j c", j=8)
                dst = syT[:, t, :].rearrange("p (c j) -> p j c", j=8)
                if t % 2 == 0:
                    cx.op(ACT, lambda e: e.copy(out=dst, in_=src), r=[bk_r], w=[syT_r])
                else:
                    cx.op(DVE, lambda e: e.tensor_copy(out=dst, in_=src), r=[bk_r], w=[syT_r])
            if "s5y" in self.dbg:
                for t in range(4):
                    self.dump("s5y", syT[:, t, :], syT_r, self.dbg_out["s5y"][t * 128:(t + 1) * 128, b * NB:(b + 1) * NB])
            gl, gl_r = tl("gl", [128, 4, NB])
            glb, glb_r = tl("glb", [128, 4, NB], BF16)
            gt = [tl(f"gt{i}", [128, NB]) for i in range(2)]
            K2 = 2.0 * math.sqrt(2.0 / math.pi)
            for t in range(4):
                g_t, g_r = gt[t % 2]
                cx.op(DVE, lambda e: e.tensor_tensor(out=g_t[:], in0=syT[:, t, :], in1=syT[:, t, :], op=ALU.mult), r=[syT_r], w=[g_r])
                cx.op(DVE, lambda e: e.tensor_scalar(out=g_t[:], in0=g_t[:], scalar1=0.044715, scalar2=1.0, op0=ALU.mult, op1=ALU.add), r=[g_r], w=[g_r])
                cx.op(DVE, lambda e: e.tensor_tensor(out=g_t[:], in0=g_t[:], in1=syT[:, t, :], op=ALU.mult), r=[g_r, syT_r], w=[g_r])
                cx.op(ACT, lambda e: e.activation(out=g_t[:], in_=g_t[:], func=AF.Sigmoid, scale=K2), r=[g_r], w=[g_r])
                cx.op(DVE, lambda e: e.tensor_tensor(out=gl[:, t, :], in0=g_t[:], in1=syT[:, t, :], op=ALU.mult), r=[g_r, syT_r], w=[gl_r])
                cx.op(POOL, lambda e: e.tensor_copy(out=glb[:, t, :], in_=gl[:, t, :]), r=[gl_r], w=[glb_r])
            slot, slot_r, u = self.stream_get()
            assert u is self.U["oglu"]
            gv = slot[:, 0:2048].rearrange("p (o k w) -> p o k w", o=4, k=4)
            for ot in range(4):
                bk, bk_r = self.bank()
                for kt in range(4):
                    cx.op(PE, lambda e: e.matmul(bk[:], lhsT=gv[:, ot, kt, :], rhs=glb[:, kt, :], start=(kt == 0), stop=(kt == 3)), r=[slot_r, glb_r], w=[bk_r])
                g_t, g_r = gt[ot % 2]
                cx.op(ACT, lambda e: e.activation(out=g_t[:], in_=bk[:], func=AF.Sigmoid), r=[bk_r], w=[g_r])
                cx.op(DVE, lambda e: e.tensor_tensor(out=mixT[:, 4 + ot, :], in0=g_t[:], in1=gl[:, ot, :], op=ALU.mult), r=[g_r, gl_r], w=[mixT_r])
            for i in range(4):
                slot, slot_r, u = self.stream_get()
                assert u is self.U[("oout", i)]
                for gi in range(2):
                    ft = 2 * i + gi
                    wv = slot[:, gi * 1024:(gi + 1) * 1024].rearrange("p (k w) -> p k w", k=8)
                    bk, bk_r = self.bank()
                    for kt in range(8):
                        cx.op(PE, lambda e: e.matmul(bk[:], lhsT=wv[:, kt, :], rhs=mixT[:, kt, :], start=(kt == 0), stop=(kt == 7)), r=[slot_r, mixT_r], w=[bk_r])
                    cx.op(DVE, lambda e: e.tensor_tensor(out=self.xT[:, ft, :], in0=self.xT[:, ft, :], in1=bk[:], op=ALU.add), r=[bk_r, self.xT_r], w=[self.xT_r])
            cx.barrier()


    def final(self, b):
        cx = self.cx
        with ExitStack() as st:
            oT, oT_r = self.sb("oT", [128, 8, NB], F32, st)
            self.rmsnorm(4, oT, oT_r)
            self.store_block(b, oT, oT_r)

    def build(self):
        nc, cx = self.nc, self.cx
        self.declare_io()
        self.declare_units()
        self.wscr = nc.dram_tensor("wscr", [self.unit_off], BF16, kind="Internal").ap()
        self.wscr_res = Res("wscr")
        self.dbg_res = {}
        self.dbg_out = {}
        for nm in self.dbg:
            if nm == "s5setup":
                continue
            self.dbg_out[nm] = nc.dram_tensor("dbg_" + nm, [self.ntok, 1024], F32, kind="ExternalOutput").ap()
        self.y_res = Res("y")
        self.prologue()
        self.setup()
        if "even" in self.stages:
            self.setup_even()
        if "odd" in self.stages:
            self.setup_odd()
        cx.barrier(engines=cx.compute + [cx.SP], with_dma=True)
        self.plan_stream()
        for b in range(self.nblk):
            self.load_block(b)
            if "even" in self.stages:
                self.even(b)
            if "ffn0" in self.stages:
                self.ffn(0, b)
            if "odd" in self.stages:
                self.odd(b)
            if "ffn1" in self.stages:
                self.ffn(1, b)
            if "final" in self.stages:
                self.final(b)
            else:
                self.store_block(b, self.xT, self.xT_r)
        cx.barrier(engines=cx.compute + [cx.SP], with_dma=True)
        return nc


INPUT_NAMES = ["mix_norm", "ffn_norm", "final_norm", "ffn_w_up", "ffn_dw_w", "ffn_dw_b", "ffn_w_down",
               "e_w_in", "e_conv_w", "e_conv_b", "e_dt_bias", "e_a_log", "e_d", "e_ssm_norm", "e_w_out",
               "o_w_in", "o_dw_w", "o_dw_b", "o_ln_g", "o_ln_b", "o_a_re", "o_a_im", "o_b_re", "o_b_im", "o_c_re", "o_c_im",
               "o_d", "o_log_step", "o_glu_w", "o_w_out"]


def kernel(**inputs):
    bld = Builder()
    nc = bld.build()
    consts = make_consts()
    in_maps = []
    NCORE = 4
    for core in range(NCORE):
        bidx = core % 4
        m = {"x": np.ascontiguousarray(inputs["x"][bidx])}
        for k in INPUT_NAMES:
            m[k] = np.ascontiguousarray(inputs[k])
        for k, v in consts.items():
            m["c_" + k] = v
        in_maps.append(m)
    res = run_bass_kernel_spmd(nc, in_maps, core_ids=list(range(NCORE)))
    out = np.stack([res.results[i]["y"] for i in range(4)], axis=0)
    return out.astype(np.float32)
```

```python
import math
from contextlib import ExitStack
import numpy as np
import concourse.bass as bass
import concourse.mybir as mybir
from concourse.bass_utils import run_bass_kernel_spmd

F32 = mybir.dt.float32
BF16 = mybir.dt.bfloat16
ALU = mybir.AluOpType
AF = mybir.ActivationFunctionType

D = 1024
SEQ = 4096
NB = 512
NCH = NB // 128
DFF = 2816
EPS = 1e-6
SLOT = 4096
NSLOT = 4
import os
SKIP = set(os.environ.get('MK_SKIP', '').split(','))
SSD_STOP = int(os.environ.get('MK_SSD_STOP', '99'))


class Sem:
    def __init__(self, h, name):
        self.h = h
        self.name = name
        self.count = 0


class Res:
    __slots__ = ("name", "w", "r", "dsem", "excl")

    def __init__(self, name, excl=False):
        self.name = name
        self.excl = excl
        self.w = {}
        self.r = {}
        self.dsem = None


class Eng:
    def __init__(self, name, eng, sem):
        self.name = name
        self.eng = eng
        self.sem = sem
        self.known = {}


class Ctx:
    def __init__(self, nc, stack):
        self.nc = nc
        self.stack = stack
        self.nsem = 0
        self.PE = self._eng("pe", nc.tensor)
        self.ACT = self._eng("act", nc.scalar)
        self.DVE = self._eng("dve", nc.vector)
        self.POOL = self._eng("pool", nc.gpsimd)
        self.SP = self._eng("sp", nc.sync)
        self.compute = [self.PE, self.ACT, self.DVE, self.POOL]
        self.dsems = []

    def new_sem(self, name):
        self.nsem += 1
        h = self.stack.enter_context(self.nc.semaphore(name))
        return Sem(h, name)

    def _eng(self, name, eng):
        return Eng(name, eng, self.new_sem("s_" + name))

    def _wait(self, E, need):
        for s, (v, snap) in need.items():
            if E.known.get(s, 0) >= v:
                continue
            E.eng.wait_ge(s.h, v)
            E.known[s] = v
            for s2, v2 in snap.items():
                if E.known.get(s2, 0) < v2:
                    E.known[s2] = v2

    def _collect(self, E, reads, writes):
        need = {}

        def req(s, ev):
            if s not in need or need[s][0] < ev[0]:
                need[s] = ev

        for r in reads:
            for s, ev in r.w.items():
                req(s, ev)
            if r.excl:
                for s, ev in r.r.items():
                    if s is not E.sem:
                        req(s, ev)
        for w in writes:
            for s, ev in w.w.items():
                if s is E.sem:
                    continue
                req(s, ev)
            for s, ev in w.r.items():
                if s is E.sem:
                    continue
                req(s, ev)
        return need

    hook = None
    hook_n = 0

    def _tick(self):
        if self.hook is not None:
            self.hook_n += 1
            if self.hook_n % 6 == 0:
                h, self.hook = self.hook, None
                h()
                if self.hook is None:
                    self.hook = h

    dry = False

    def op(self, E, fn, r=(), w=()):
        if self.dry:
            return None
        self._tick()
        self._wait(E, self._collect(E, r, w))
        ins = fn(E.eng)
        E.sem.count += 1
        ins.then_inc(E.sem.h, 1)
        ev = (E.sem.count, dict(E.known))
        for x in r:
            x.r[E.sem] = ev
        for x in w:
            x.w[E.sem] = ev
            x.r = {}
        return ins

    def dma(self, E, out, in_, r=(), w=(), sem_res=None, **kw):
        if self.dry:
            return None
        self._tick()
        self._wait(E, self._collect(E, r, w))
        if sem_res.dsem is None:
            sem_res.dsem = self.new_sem("d_" + sem_res.name)
            self.dsems.append(sem_res.dsem)
        ds = sem_res.dsem
        ins = E.eng.dma_start(out=out, in_=in_, **kw)
        ds.count += 16
        ins.then_inc(ds.h, 16)
        ev = (ds.count, dict(E.known))
        for x in r:
            x.r[ds] = ev
        for x in w:
            x.w[ds] = ev
            x.r = {}
        return ins

    def barrier(self, engines=None, with_dma=False):
        if self.dry:
            return
        engines = engines or self.compute
        for E in engines:
            need = {}
            for E2 in engines:
                if E2 is not E and E2.sem.count > 0:
                    need[E2.sem] = (E2.sem.count, dict(E2.known))
            if with_dma:
                for ds in self.dsems:
                    if ds.count > 0:
                        need[ds] = (ds.count, {})
            self._wait(E, need)


def make_consts():
    c = {}
    c["ident_f"] = np.eye(128, dtype=np.float32)
    c["ones_f"] = np.ones((128, 128), dtype=np.float32)
    inv = (10000.0 ** (-np.arange(0, 128, 2, dtype=np.float32) / np.float32(128))).astype(np.float32)
    pos = np.arange(SEQ, dtype=np.float32)
    ang = (pos[:, None] * inv[None, :]).astype(np.float32).astype(np.float64)
    c["cosT"] = np.ascontiguousarray(np.cos(ang).reshape(SEQ // 128, 128, 64).transpose(1, 0, 2)).astype(np.float32)
    c["sinT"] = np.ascontiguousarray(np.sin(ang).reshape(SEQ // 128, 128, 64).transpose(1, 0, 2)).astype(np.float32)
    gam = 1.0 - 2.0 ** (-5.0 - np.arange(4, dtype=np.float64))
    idx = np.arange(128, dtype=np.float64)
    diff = idx[None, :] - idx[:, None]
    dm = np.where(diff[None] >= 0, gam[:, None, None] ** np.maximum(diff, 0)[None], 0.0) * (128 ** -0.5)
    c["dmaskT"] = np.ascontiguousarray(dm.transpose(1, 0, 2)).reshape(128, 512).astype(np.float32)
    c["xi"] = (gam[None, :] ** (idx[:, None] + 1)).astype(np.float32)
    c["zeta"] = ((gam[None, :] ** (127 - idx[:, None])) * (128 ** -0.5)).astype(np.float32)
    tri = (idx[:, None] <= idx[None, :]).astype(np.float32)
    c["tri"] = tri
    c["su"] = (1.0 - tri).astype(np.float32)
    sj = np.arange(128) // 16
    c["tzmask"] = (sj[None, :] >= sj[:, None]).astype(np.float32)
    c["cidx"] = np.tile(np.arange(1, 65, dtype=np.float32)[None, :], (128, 1))
    return c


RET_GDEC = [float((1.0 - 2.0 ** (-5.0 - h)) ** 128) for h in range(4)]


class Builder:
    def __init__(self, nblk=SEQ // NB, stages=("even", "ffn0", "odd", "ffn1", "final"), dbg=()):
        self.nblk = nblk
        self.stages = stages
        self.dbg = dbg
        self.ntok = nblk * NB
        self.nc = bass.Bass("TRN2", target_bir_lowering=False)
        self.stack = ExitStack()
        self.cx = Ctx(self.nc, self.stack)
        self.units = []
        self.unit_off = 0
        self.stream_order = []
        self.bank_i = 0
        self.bankb_i = 0

    def din(self, name, shape, dt=F32):
        return self.nc.dram_tensor(name, list(shape), dt, kind="ExternalInput").ap()

    def declare_io(self):
        nc = self.nc
        self.x = self.din("x", [self.ntok, D])
        self.y = nc.dram_tensor("y", [self.ntok, D], F32, kind="ExternalOutput").ap()
        self.mix_norm = self.din("mix_norm", [2, D])
        self.ffn_norm = self.din("ffn_norm", [2, D])
        self.final_norm = self.din("final_norm", [D])
        self.ffn_w_up = self.din("ffn_w_up", [2, D, 2 * DFF])
        self.ffn_dw_w = self.din("ffn_dw_w", [2, 3, 2 * DFF])
        self.ffn_dw_b = self.din("ffn_dw_b", [2, 2 * DFF])
        self.ffn_w_down = self.din("ffn_w_down", [2, DFF, D])
        self.e_w_in = self.din("e_w_in", [1, D, 5648])
        self.e_conv_w = self.din("e_conv_w", [1, 4, 1536])
        self.e_conv_b = self.din("e_conv_b", [1, 1536])
        self.e_dt_bias = self.din("e_dt_bias", [1, 16])
        self.e_a_log = self.din("e_a_log", [1, 16])
        self.e_d = self.din("e_d", [1, 16])
        self.e_ssm_norm = self.din("e_ssm_norm", [1, 1024])
        self.e_w_out = self.din("e_w_out", [1, 2048, 1024])
        self.o_w_in = self.din("o_w_in", [1, D, 1536])
        self.o_dw_w = self.din("o_dw_w", [1, 31, 512])
        self.o_dw_b = self.din("o_dw_b", [1, 512])
        self.o_ln_g = self.din("o_ln_g", [1, 512])
        self.o_ln_b = self.din("o_ln_b", [1, 512])
        self.o_a_re = self.din("o_a_re", [1, 32, 64])
        self.o_a_im = self.din("o_a_im", [1, 32, 64])
        self.o_b_re = self.din("o_b_re", [1, 32, 64, 16])
        self.o_b_im = self.din("o_b_im", [1, 32, 64, 16])
        self.o_c_re = self.din("o_c_re", [1, 32, 16, 64])
        self.o_c_im = self.din("o_c_im", [1, 32, 16, 64])
        self.o_d = self.din("o_d", [1, 512])
        self.o_log_step = self.din("o_log_step", [1, 32])
        self.o_glu_w = self.din("o_glu_w", [1, 512, 512])
        self.o_w_out = self.din("o_w_out", [1, D, D])
        self.consts = {}
        for k, v in make_consts().items():
            self.consts[k] = self.din("c_" + k, v.shape)

    def sb(self, name, shape, dt=F32, stack=None):
        if stack is None:
            if not hasattr(self, "persist"):
                self.persist = {}
            if name not in self.persist:
                t = self.stack.enter_context(self.nc.sbuf_tensor(f"{name}_p", list(shape), dt))
                self.persist[name] = (t, Res(name))
            return self.persist[name]
        self.name_i = getattr(self, "name_i", 0) + 1
        t = stack.enter_context(self.nc.sbuf_tensor(f"{name}_{self.name_i}", list(shape), dt))
        return t, Res(name)

    def ps(self, name, shape, dt=F32):
        if not hasattr(self, "persist_ps"):
            self.persist_ps = {}
        if name not in self.persist_ps:
            t = self.stack.enter_context(self.nc.psum_tensor(name, list(shape), dt))
            self.persist_ps[name] = (t, Res(name, excl=True))
        return self.persist_ps[name]

    def bank(self):
        i = self.bank_i
        self.bank_i = (i + 1) % len(self.banks)
        return self.banks[i]

    def bankb(self):
        i = self.bankb_i
        self.bankb_i = (i + 1) % len(self.banksb)
        return self.banksb[i]

    def add_unit(self, name, pieces):
        L = sum(kt * w for (_, kt, _, w) in pieces)
        assert L <= SLOT, (name, L)
        u = dict(name=name, L=L, off=self.unit_off, pieces=pieces)
        self.unit_off += 128 * L
        self.units.append(u)
        return u

    def declare_units(self):
        self.U = {}
        for l in range(2):
            wup = self.ffn_w_up[l]
            for j in range(22):
                self.U[("up", l, j)] = self.add_unit(f"up{l}_{j}", [(wup, 8, j * 128, 128), (wup, 8, (22 + j) * 128, 128)])
            wdn = self.ffn_w_down[l]
            for ft in range(8):
                self.U[("dn", l, ft)] = self.add_unit(f"dn{l}_{ft}", [(wdn, 22, ft * 128, 128)])

        w = self.e_w_in[0]
        for i, nm in enumerate(["eq", "ek", "ev0", "ev1", "eg0", "eg1", "ez0", "ez1"]):
            self.U[nm] = self.add_unit(nm, [(w, 8, 512 * i, 512)])
        self.U["edt"] = self.add_unit("edt", [(w, 8, 5632, 16)])
        for i in range(6):
            self.U[("exbc", i)] = self.add_unit(f"exbc{i}", [(w, 8, 4096 + (2 * i) * 128, 128), (w, 8, 4096 + (2 * i + 1) * 128, 128)])
        for ft in range(8):
            self.U[("eout", ft)] = self.add_unit(f"eout{ft}", [(self.e_w_out[0], 16, ft * 128, 128)])

        w = self.o_w_in[0]
        for i in range(4):
            self.U[("oin", i)] = self.add_unit(f"oin{i}", [(w, 8, i * 128, 128), (w, 8, 512 + i * 128, 128)])
        self.U["ou"] = self.add_unit("ou", [(w, 8, 1024, 512)])
        self.U["oglu"] = self.add_unit("oglu", [(self.o_glu_w[0], 4, ot * 128, 128) for ot in range(4)])
        for i in range(4):
            self.U[("oout", i)] = self.add_unit(f"oout{i}", [(self.o_w_out[0], 8, (2 * i) * 128, 128), (self.o_w_out[0], 8, (2 * i + 1) * 128, 128)])
        self.n_cast_units = len(self.units)
        for nm in [("s5bc", 0), ("s5bc", 1), ("s5cc", 0), ("s5cc", 1), "s5tz"] + [("ocv", i) for i in range(4)]:
            self.U[nm] = dict(name=str(nm), L=4096, off=self.unit_off, pieces=[])
            self.unit_off += 128 * 4096

    def plan_stream(self):
        order = []
        for b in range(self.nblk):
            if "even" in self.stages:
                order += [self.U[("exbc", i)] for i in range(6)]
                order += [self.U[nm] for nm in ["eq", "ek", "ev0", "ev1", "eg0", "eg1", "ez0", "ez1", "edt"]]
                order += [self.U[("eout", ft)] for ft in range(8)]
            for l in range(2):
                if l == 1 and "odd" in self.stages:
                    order += [self.U["ou"], self.U[("s5bc", 0)], self.U[("s5bc", 1)]]
                    for i in range(4):
                        order += [self.U[("oin", i)], self.U[("ocv", i)]]
                    order += [self.U["s5tz"], self.U[("s5cc", 0)], self.U[("s5cc", 1)], self.U["oglu"]]
                    order += [self.U[("oout", i)] for i in range(4)]
                if f"ffn{l}" in self.stages:
                    order += [self.U[("up", l, j)] for j in range(22)]
                    order += [self.U[("dn", l, ft)] for ft in range(8)]
        self.stream_order = order
        self.stream_next_load = 0
        self.stream_next_use = 0

    def stream_get(self, hold_prev=0):
        cx = self.cx
        i = self.stream_next_use
        self.stream_next_use += 1
        lim = min(i - hold_prev + NSLOT - 1, len(self.stream_order) - 1)
        while self.stream_next_load <= lim:
            k = self.stream_next_load
            u = self.stream_order[k]
            st, sr = self.slots[k % NSLOT]
            src = self.wscr[u["off"]: u["off"] + 128 * u["L"]].rearrange("(p l) -> p l", p=128)
            cx.dma(cx.SP, st[:, 0:u["L"]], src, w=[sr], sem_res=sr)
            self.stream_next_load += 1
        st, sr = self.slots[i % NSLOT]
        assert self.stream_order[i] is not None
        return st, sr, self.stream_order[i]

    def prologue(self):
        cx, nc = self.cx, self.nc
        with ExitStack() as st:
            NS = 3
            stg32 = [self.sb(f"stg32_{i}", [128, SLOT], F32, st) for i in range(NS)]
            stg16 = [self.sb(f"stg16_{i}", [128, SLOT], BF16, st) for i in range(NS)]
            cast_engs = [cx.ACT, cx.DVE]
            units = self.units[:self.n_cast_units]

            def load(i):
                u = units[i]
                t32, r32 = stg32[i % NS]
                off = 0
                for (src, kt, c0, w) in u["pieces"]:
                    dst = t32[:, off: off + kt * w].rearrange("p (k w) -> p k w", k=kt)
                    s_ = src.rearrange("(k p) n -> p k n", p=128)[:, :, c0:c0 + w]
                    cx.dma(cx.SP, dst, s_, w=[r32], sem_res=r32)
                    off += kt * w

            def cast_store(i):
                u = units[i]
                t32, r32 = stg32[i % NS]
                t16, r16 = stg16[i % NS]
                L = u["L"]
                E = cast_engs[i % len(cast_engs)]
                if E is cx.ACT:
                    cx.op(E, lambda e: e.copy(out=t16[:, 0:L], in_=t32[:, 0:L]), r=[r32], w=[r16])
                else:
                    cx.op(E, lambda e: e.tensor_copy(out=t16[:, 0:L], in_=t32[:, 0:L]), r=[r32], w=[r16])
                dst = self.wscr[u["off"]: u["off"] + 128 * L].rearrange("(p l) -> p l", p=128)
                cx.dma(cx.SP, dst, t16[:, 0:L], r=[r16], sem_res=r16)

            n = len(units)
            for i in range(min(NS - 1, n)):
                load(i)
            for i in range(n):
                if i + NS - 1 < n:
                    load(i + NS - 1)
                cast_store(i)
                yield
            while not getattr(self, "pro_finish", False):
                yield
            cx.barrier(engines=cx.compute + [cx.SP], with_dma=True)

    def setup(self):
        cx, nc = self.cx, self.nc
        self.xT, self.xT_r = self.sb("xT", [128, 8, NB], F32)
        self.hnT, self.hnT_r = self.sb("hnT", [128, 8, NB], BF16)
        self.ident_f, self.ident_f_r = self.sb("ident_f", [128, 128], F32)
        self.ones_f, self.ones_f_r = self.sb("ones_f", [128, 128], F32)
        self.gains, self.gains_r = self.sb("gains", [128, 5, 8], F32)
        self.slots = [self.sb(f"wslot{i}", [128, SLOT], BF16) for i in range(NSLOT)]
        self.banks = [self.ps(f"bank{i}", [128, 512], F32) for i in range(6)]
        self.banksb = [self.ps(f"bankb{i}", [128, 1024], BF16) for i in range(2)]
        self.ffn_halo, self.ffn_halo_r = self.sb("ffn_halo", [128, 2, 44, 2], F32)
        self.ffn_cw, self.ffn_cw_r = self.sb("ffn_cw", [128, 2, 44, 3], F32)
        self.ffn_cb, self.ffn_cb_r = self.sb("ffn_cb", [128, 2, 44], F32)
        self.eps_t, self.eps_r = self.sb("eps_t", [128, 1], F32)
        nc.allow_non_contiguous_dma(reason="small parameter loads")
        SP = cx.SP
        cx.dma(SP, self.ident_f[:], self.consts["ident_f"], w=[self.ident_f_r], sem_res=self.ident_f_r)
        cx.dma(SP, self.ones_f[:], self.consts["ones_f"], w=[self.ones_f_r], sem_res=self.ones_f_r)
        gsrc = [self.mix_norm[0], self.ffn_norm[0], self.mix_norm[1], self.ffn_norm[1], self.final_norm]
        for i, g in enumerate(gsrc):
            cx.dma(SP, self.gains[:, i, :], g.rearrange("(t p) -> p t", p=128), w=[self.gains_r], sem_res=self.gains_r, allow_slow_non_contiguous=True)
        for l in range(2):
            for k in range(3):
                cx.dma(SP, self.ffn_cw[:, l, :, k], self.ffn_dw_w[l, k].rearrange("(t p) -> p t", p=128),
                       w=[self.ffn_cw_r], sem_res=self.ffn_cw_r, allow_slow_non_contiguous=True)
            cx.dma(SP, self.ffn_cb[:, l, :], self.ffn_dw_b[l].rearrange("(t p) -> p t", p=128),
                   w=[self.ffn_cb_r], sem_res=self.ffn_cb_r, allow_slow_non_contiguous=True)
        cx.op(cx.DVE, lambda e: e.memset(self.ffn_halo[:], 0.0), w=[self.ffn_halo_r])
        cx.op(cx.DVE, lambda e: e.memset(self.eps_t[:], EPS), w=[self.eps_r])
        self.ones_b, self.ones_b_r = self.sb("ones_b", [128, 128], BF16)
        cx.op(cx.DVE, lambda e: e.memset(self.ones_b[:], 1.0), w=[self.ones_b_r])
        self.one_t, _ = self.sb("one_t", [128, 1], F32)
        cx.op(cx.DVE, lambda e: e.memset(self.one_t[:], 1.0), w=[self.eps_r])

    def load_block(self, b):
        cx = self.cx
        with ExitStack() as st:
            xin, xin_r = self.sb("xin", [128, NCH, D], F32, st)
            src = self.x[b * NB:(b + 1) * NB, :].rearrange("(c p) d -> p c d", p=128)
            cx.dma(cx.SP, xin[:], src, w=[xin_r], sem_res=self.xT_r)
            for ft in range(8):
                bk, bk_r = self.bank()
                for c in range(NCH):
                    cx.op(cx.PE, lambda e: e.transpose(bk[:, c * 128:(c + 1) * 128], xin[:, c, ft * 128:(ft + 1) * 128], self.ident_f[:]),
                          r=[xin_r, self.ident_f_r], w=[bk_r])
                if ft % 2 == 0:
                    cx.op(cx.ACT, lambda e: e.copy(out=self.xT[:, ft, :], in_=bk[:]), r=[bk_r], w=[self.xT_r])
                else:
                    cx.op(cx.DVE, lambda e: e.tensor_copy(out=self.xT[:, ft, :], in_=bk[:]), r=[bk_r], w=[self.xT_r])
            cx.barrier()

    def store_block(self, b, srcT, srcT_r):
        cx = self.cx
        with ExitStack() as st:
            yo, yo_r = self.sb("yo", [128, NCH, D], F32, st)
            k = 0
            for c in range(NCH):
                for half in range(2):
                    bk, bk_r = self.bank()
                    for f4 in range(4):
                        ft = half * 4 + f4
                        cx.op(cx.PE, lambda e: e.transpose(bk[:, f4 * 128:(f4 + 1) * 128], srcT[:, ft, c * 128:(c + 1) * 128], self.ident_f[:]),
                              r=[srcT_r, self.ident_f_r], w=[bk_r])
                    if k % 2 == 0:
                        cx.op(cx.ACT, lambda e: e.copy(out=yo[:, c, half * 512:(half + 1) * 512], in_=bk[:]), r=[bk_r], w=[yo_r])
                    else:
                        cx.op(cx.DVE, lambda e: e.tensor_copy(out=yo[:, c, half * 512:(half + 1) * 512], in_=bk[:]), r=[bk_r], w=[yo_r])
                    k += 1
            dst = self.y[b * NB:(b + 1) * NB, :].rearrange("(c p) d -> p c d", p=128)
            cx.dma(cx.SP, dst, yo[:], r=[yo_r], w=[self.y_res], sem_res=self.y_res)
            cx.barrier(engines=cx.compute + [cx.SP], with_dma=True)

    def rmsnorm(self, gi, outT, outT_r, out_f32=False):
        cx = self.cx
        with ExitStack() as st:
            sq = [self.sb(f"sq{i}", [128, NB], BF16, st) for i in range(4)]
            rstd, rstd_r = self.sb("rstd", [128, NB], F32, st)
            bk, bk_r = self.bank()
            for ft in range(8):
                s, s_r = sq[ft % 4]
                if ft % 2 == 0:
                    cx.op(cx.ACT, lambda e: e.activation(out=s[:], in_=self.xT[:, ft, :], func=AF.Square), r=[self.xT_r], w=[s_r])
                else:
                    cx.op(cx.DVE, lambda e: e.tensor_tensor(out=s[:], in0=self.xT[:, ft, :], in1=self.xT[:, ft, :], op=ALU.mult), r=[self.xT_r], w=[s_r])
                cx.op(cx.PE, lambda e: e.matmul(bk[:], lhsT=self.ones_b[:], rhs=s[:], start=(ft == 0), stop=(ft == 7)),
                      r=[s_r, self.ones_b_r], w=[bk_r])
            cx.op(cx.ACT, lambda e: e.activation(out=rstd[:], in_=bk[:], func=AF.Sqrt, bias=self.eps_t[:, 0:1], scale=1.0 / D),
                  r=[bk_r, self.eps_r], w=[rstd_r])
            cx.op(cx.DVE, lambda e: e.reciprocal(out=rstd[:], in_=rstd[:]), r=[rstd_r], w=[rstd_r])
            for ft in range(8):
                E = cx.DVE
                cx.op(E, lambda e: e.scalar_tensor_tensor(out=outT[:, ft, :], in0=self.xT[:, ft, :], scalar=self.gains[:, gi, ft:ft + 1],
                                                          in1=rstd[:], op0=ALU.mult, op1=ALU.mult),
                      r=[self.xT_r, self.gains_r, rstd_r], w=[outT_r])
            cx.barrier()

    def ffn(self, l, b):
        cx = self.cx
        self.rmsnorm(1 + 2 * l, self.hnT, self.hnT_r)
        with ExitStack() as st:
            hT, hT_r = self.sb("hT", [128, 22, NB], BF16, st)
            raws = [self.sb(f"raw{i}", [128, NB + 2], F32, st) for i in range(4)]
            accs = [self.sb(f"acc{i}", [128, NB], F32, st) for i in range(4)]
            sg = [self.sb(f"sgate{i}", [128, NB], F32, st) for i in range(2)]
            for j in range(22):
                slot, slot_r, u = self.stream_get()
                assert u is self.U[("up", l, j)]
                res = []
                for gi in range(2):
                    tile = j + 22 * gi
                    wv = slot[:, gi * 1024:(gi + 1) * 1024].rearrange("p (k w) -> p k w", k=8)
                    bk, bk_r = self.bank()
                    for kt in range(8):
                        cx.op(cx.PE, lambda e: e.matmul(bk[:], lhsT=wv[:, kt, :], rhs=self.hnT[:, kt, :], start=(kt == 0), stop=(kt == 7)),
                              r=[slot_r, self.hnT_r], w=[bk_r])
                    raw, raw_r = raws[(2 * j + gi) % 4]
                    acc, acc_r = accs[(2 * j + gi) % 4]
                    cw = self.ffn_cw[:, l, tile, :]
                    cx.op(cx.ACT, lambda e: e.copy(out=raw[:, 2:NB + 2], in_=bk[:]), r=[bk_r], w=[raw_r])
                    cx.op(cx.POOL, lambda e: e.tensor_copy(out=raw[:, 0:2], in_=self.ffn_halo[:, l, tile, :]), r=[self.ffn_halo_r], w=[raw_r])
                    cx.op(cx.ACT, lambda e: e.activation(out=acc[:], in_=bk[:], func=AF.Identity, bias=self.ffn_cb[:, l, tile:tile + 1],
                                                         scale=cw[:, 2:3]),
                          r=[bk_r, self.ffn_cw_r, self.ffn_cb_r], w=[acc_r])
                    E = cx.DVE
                    cx.op(E, lambda e: e.scalar_tensor_tensor(out=acc[:], in0=raw[:, 1:NB + 1], scalar=cw[:, 1:2], in1=acc[:],
                                                              op0=ALU.mult, op1=ALU.add),
                          r=[raw_r, acc_r, self.ffn_cw_r], w=[acc_r])
                    cx.op(E, lambda e: e.scalar_tensor_tensor(out=acc[:], in0=raw[:, 0:NB], scalar=cw[:, 0:1], in1=acc[:],
                                                              op0=ALU.mult, op1=ALU.add),
                          r=[raw_r, acc_r, self.ffn_cw_r], w=[acc_r])
                    cx.op(cx.POOL, lambda e: e.tensor_copy(out=self.ffn_halo[:, l, tile, :], in_=raw[:, NB:NB + 2]), r=[raw_r], w=[self.ffn_halo_r])
                    res.append((acc, acc_r))
                s, s_r = sg[j % 2]
                cx.op(cx.ACT, lambda e: e.activation(out=s[:], in_=res[0][0][:], func=AF.Silu), r=[res[0][1]], w=[s_r])
                cx.op(cx.DVE, lambda e: e.tensor_tensor(out=hT[:, j, :], in0=s[:], in1=res[1][0][:], op=ALU.mult),
                      r=[s_r, res[1][1]], w=[hT_r])
            for ft in range(8):
                slot, slot_r, u = self.stream_get()
                assert u is self.U[("dn", l, ft)]
                wv = slot[:, 0:22 * 128].rearrange("p (k w) -> p k w", k=22)
                bk, bk_r = self.bank()
                for kt in range(22):
                    cx.op(cx.PE, lambda e: e.matmul(bk[:], lhsT=wv[:, kt, :], rhs=hT[:, kt, :], start=(kt == 0), stop=(kt == 21)),
                          r=[slot_r, hT_r], w=[bk_r])
                cx.op(cx.DVE, lambda e: e.tensor_tensor(out=self.xT[:, ft, :], in0=self.xT[:, ft, :], in1=bk[:], op=ALU.add),
                      r=[bk_r, self.xT_r], w=[self.xT_r])
            cx.barrier()

    def bc_mid(self, ap2, n):
        a = ap2.shape[1]
        return ap2.unsqueeze(2).to_broadcast([128, a, n])

    def bc_h(self, ap2, h):
        n = ap2.shape[1]
        return ap2.unsqueeze(1).to_broadcast([128, h, n])

    def setup_even(self):
        cx = self.cx
        SP = cx.SP
        self.retS, self.retS_r = self.sb("retS", [128, 4, 256], F32)
        self.retSb, self.retSb_r = self.sb("retSb", [128, 4, 256], BF16)
        self.ssS, self.ssS_r = self.sb("ssS", [128, 2, 512], F32)
        self.ssSb, self.ssSb_r = self.sb("ssSb", [128, 2, 512], BF16)
        self.exh, self.exh_r = self.sb("exh", [128, 12, 3], F32)
        self.exw, self.exw_r = self.sb("exw", [128, 12, 4], F32)
        self.exb, self.exb_r = self.sb("exb", [128, 12], F32)
        self.dmaskT, self.dmaskT_r = self.sb("dmaskT", [128, 512], F32)
        self.xi, self.xi_r = self.sb("xi", [128, 4], F32)
        self.zeta, self.zeta_r = self.sb("zeta", [128, 4], F32)
        self.tri, self.tri_r = self.sb("tri", [128, 128], F32)
        self.su, self.su_r = self.sb("su", [128, 128], F32)
        self.ident_b, self.ident_b_r = self.sb("ident_b", [128, 128], BF16)
        self.dtb, self.dtb_r = self.sb("dtb", [128, 16], F32)
        self.aneg, self.aneg_r = self.sb("aneg", [128, 16], F32)
        self.dsk, self.dsk_r = self.sb("dsk", [128, 16], F32)
        self.ssmn, self.ssmn_r = self.sb("ssmn", [128, 8], F32)
        for t, r, nm in [(self.dmaskT, self.dmaskT_r, "dmaskT"), (self.xi, self.xi_r, "xi"), (self.zeta, self.zeta_r, "zeta"),
                         (self.tri, self.tri_r, "tri"), (self.su, self.su_r, "su")]:
            cx.dma(SP, t[:], self.consts[nm], w=[r], sem_res=r)
        cx.dma(SP, self.dtb[:], self.e_dt_bias[0].partition_broadcast(128), w=[self.dtb_r], sem_res=self.dtb_r, allow_slow_non_contiguous=True)
        cx.dma(SP, self.aneg[:], self.e_a_log[0].partition_broadcast(128), w=[self.aneg_r], sem_res=self.aneg_r, allow_slow_non_contiguous=True)
        cx.dma(SP, self.dsk[:], self.e_d[0].partition_broadcast(128), w=[self.dsk_r], sem_res=self.dsk_r, allow_slow_non_contiguous=True)
        cx.dma(SP, self.ssmn[:], self.e_ssm_norm[0].rearrange("(t p) -> p t", p=128), w=[self.ssmn_r], sem_res=self.ssmn_r,
               allow_slow_non_contiguous=True)
        for k in range(4):
            cx.dma(SP, self.exw[:, :, k], self.e_conv_w[0, k].rearrange("(t p) -> p t", p=128), w=[self.exw_r], sem_res=self.exw_r,
                   allow_slow_non_contiguous=True)
        cx.dma(SP, self.exb[:], self.e_conv_b[0].rearrange("(t p) -> p t", p=128), w=[self.exb_r], sem_res=self.exb_r,
               allow_slow_non_contiguous=True)
        cx.op(cx.ACT, lambda e: e.activation(out=self.aneg[:], in_=self.aneg[:], func=AF.Exp), r=[self.aneg_r], w=[self.aneg_r])
        cx.op(cx.DVE, lambda e: e.tensor_scalar(out=self.aneg[:], in0=self.aneg[:], scalar1=-1.0, scalar2=None, op0=ALU.mult),
              r=[self.aneg_r], w=[self.aneg_r])
        cx.op(cx.DVE, lambda e: e.tensor_copy(out=self.ident_b[:], in_=self.ident_f[:]), r=[self.ident_f_r], w=[self.ident_b_r])
        cx.op(cx.DVE, lambda e: e.memset(self.retS[:], 0.0), w=[self.retS_r])
        cx.op(cx.DVE, lambda e: e.memset(self.retSb[:], 0.0), w=[self.retSb_r])
        cx.op(cx.DVE, lambda e: e.memset(self.ssS[:], 0.0), w=[self.ssS_r])
        cx.op(cx.DVE, lambda e: e.memset(self.ssSb[:], 0.0), w=[self.ssSb_r])
        cx.op(cx.POOL, lambda e: e.memset(self.exh[:], 0.0), w=[self.exh_r])

    def dump(self, name, src_ap, src_r, dst_ap):
        cx = self.cx
        r = self.dbg_res.setdefault(name, Res("dbg_" + name))
        cx.dma(cx.SP, dst_ap, src_ap, r=[src_r], w=[r], sem_res=r)

    def even(self, b):
        cx = self.cx
        PE, ACT, DVE, POOL, SP = cx.PE, cx.ACT, cx.DVE, cx.POOL, cx.SP
        self.rmsnorm(0, self.hnT, self.hnT_r)
        hnT, hnT_r = self.hnT, self.hnT_r
        with ExitStack() as st:
            B = {}
            for nm, shp, dt in [("qr", [128, NCH, 512], BF16), ("qx", [128, NCH, 512], BF16), ("kr", [128, NCH, 512], BF16),
                                ("kz", [128, NCH, 512], BF16), ("v", [128, NCH, 1024], BF16), ("sg", [128, NCH, 1024], BF16),
                                ("sz", [128, NCH, 1024], BF16), ("xcT", [128, 8, NB], BF16), ("bcT", [128, 4, NB], BF16),
                                ("dt", [128, NCH, 16], F32), ("dta", [128, NCH, 16], F32), ("mixT", [128, 16, NB], BF16),
                                ("cos", [128, NCH, 64], F32), ("sin", [128, NCH, 64], F32)]:
                B[nm] = self.sb(nm, shp, dt, st)
            self.EB = B
            cos, cos_r = B["cos"]
            sin, sin_r = B["sin"]
            cx.dma(SP, cos[:], self.consts["cosT"][:, b * NCH:(b + 1) * NCH, :], w=[cos_r], sem_res=self.xi_r)
            cx.dma(SP, sin[:], self.consts["sinT"][:, b * NCH:(b + 1) * NCH, :], w=[sin_r], sem_res=self.xi_r)
            rot_t = [self.sb(f"rot{i}", [128, 4, 64], F32, st) for i in range(4)]
            qraw = [self.sb(f"qraw{i}", [128, 512], F32, st) for i in range(2)]
            raws = [self.sb(f"xraw{i}", [128, NB + 3], F32, st) for i in range(2)]
            accs = [self.sb(f"xacc{i}", [128, NB], F32, st) for i in range(2)]
            xcT, xcT_r = B["xcT"]
            bcT, bcT_r = B["bcT"]
            for i in range(6):
                slot, slot_r, u = self.stream_get()
                assert u is self.U[("exbc", i)]
                for gi in range(2):
                    tile = 2 * i + gi
                    wv = slot[:, gi * 1024:(gi + 1) * 1024].rearrange("p (k w) -> p k w", k=8)
                    bk, bk_r = self.bank()
                    for kt in range(8):
                        cx.op(PE, lambda e: e.matmul(bk[:], lhsT=wv[:, kt, :], rhs=hnT[:, kt, :], start=(kt == 0), stop=(kt == 7)),
                              r=[slot_r, hnT_r], w=[bk_r])
                    raw, raw_r = raws[tile % 2]
                    acc, acc_r = accs[tile % 2]
                    cw = self.exw[:, tile, :]
                    cx.op(ACT, lambda e: e.copy(out=raw[:, 3:NB + 3], in_=bk[:]), r=[bk_r], w=[raw_r])
                    cx.op(POOL, lambda e: e.tensor_copy(out=raw[:, 0:3], in_=self.exh[:, tile, :]), r=[self.exh_r], w=[raw_r])
                    cx.op(ACT, lambda e: e.activation(out=acc[:], in_=bk[:], func=AF.Identity, bias=self.exb[:, tile:tile + 1], scale=cw[:, 3:4]),
                          r=[bk_r, self.exw_r, self.exb_r], w=[acc_r])
                    for k in (2, 1, 0):
                        cx.op(DVE, lambda e: e.scalar_tensor_tensor(out=acc[:], in0=raw[:, k:k + NB], scalar=cw[:, k:k + 1], in1=acc[:],
                                                                    op0=ALU.mult, op1=ALU.add),
                              r=[raw_r, acc_r, self.exw_r], w=[acc_r])
                    cx.op(POOL, lambda e: e.tensor_copy(out=self.exh[:, tile, :], in_=raw[:, NB:NB + 3]), r=[raw_r], w=[self.exh_r])
                    if tile < 8:
                        cx.op(ACT, lambda e: e.activation(out=xcT[:, tile, :], in_=acc[:], func=AF.Silu), r=[acc_r], w=[xcT_r])
                    else:
                        cx.op(ACT, lambda e: e.activation(out=bcT[:, tile - 8, :], in_=acc[:], func=AF.Silu), r=[acc_r], w=[bcT_r])
            k_ev = 0
            for nm in ["eq", "ek", "ev0", "ev1", "eg0", "eg1", "ez0", "ez1"]:
                slot, slot_r, u = self.stream_get()
                assert u is self.U[nm]
                wv = slot[:, 0:4096].rearrange("p (k w) -> p k w", k=8)
                for c in range(NCH):
                    bk, bk_r = self.bank()
                    for kt in range(8):
                        cx.op(PE, lambda e: e.matmul(bk[:], lhsT=hnT[:, kt, c * 128:(c + 1) * 128], rhs=wv[:, kt, :], start=(kt == 0), stop=(kt == 7)),
                              r=[slot_r, hnT_r], w=[bk_r])
                    if nm in ("eq", "ek"):
                        raw, raw_r = qraw[k_ev % 2]
                        k_ev += 1
                        cx.op(ACT, lambda e: e.copy(out=raw[:], in_=bk[:]), r=[bk_r], w=[raw_r])
                        r4 = raw[:].rearrange("p (h t d) -> p h t d", h=4, t=2)
                        x1, x2 = r4[:, :, 0, :], r4[:, :, 1, :]
                        cb = self.bc_h(cos[:, c, :], 4)
                        sbb = self.bc_h(sin[:, c, :], 4)
                        (t1, t1r), (t2, t2r), (t3, t3r), (t4, t4r) = rot_t
                        cx.op(POOL, lambda e: e.tensor_tensor(out=t1[:], in0=x1, in1=cb, op=ALU.mult), r=[raw_r, cos_r], w=[t1r])
                        cx.op(POOL, lambda e: e.tensor_tensor(out=t2[:], in0=x2, in1=sbb, op=ALU.mult), r=[raw_r, sin_r], w=[t2r])
                        cx.op(POOL, lambda e: e.tensor_tensor(out=t3[:], in0=x1, in1=sbb, op=ALU.mult), r=[raw_r, sin_r], w=[t3r])
                        cx.op(POOL, lambda e: e.tensor_tensor(out=t4[:], in0=x2, in1=cb, op=ALU.mult), r=[raw_r, cos_r], w=[t4r])
                        dst, dst_r = B["qr"] if nm == "eq" else B["kr"]
                        d4 = dst[:, c, :].rearrange("p (h t d) -> p h t d", h=4, t=2)
                        cx.op(DVE, lambda e: e.tensor_tensor(out=d4[:, :, 0, :], in0=t1[:], in1=t2[:], op=ALU.subtract), r=[t1r, t2r], w=[dst_r])
                        cx.op(DVE, lambda e: e.tensor_tensor(out=d4[:, :, 1, :], in0=t3[:], in1=t4[:], op=ALU.add), r=[t3r, t4r], w=[dst_r])
                        d2, d2_r = B["qx"] if nm == "eq" else B["kz"]
                        tab, tab_r = (self.xi, self.xi_r) if nm == "eq" else (self.zeta, self.zeta_r)
                        cx.op(DVE, lambda e: e.tensor_tensor(out=d2[:, c, :].rearrange("p (h d) -> p h d", h=4),
                                                             in0=dst[:, c, :].rearrange("p (h d) -> p h d", h=4),
                                                             in1=self.bc_mid(tab[:], 128), op=ALU.mult),
                              r=[dst_r, tab_r], w=[d2_r])
                    else:
                        half = int(nm[-1])
                        key = {"v": "v", "g": "sg", "z": "sz"}[nm[1]]
                        dst, dst_r = B[key]
                        if key == "v":
                            cx.op(ACT, lambda e: e.copy(out=dst[:, c, half * 512:(half + 1) * 512], in_=bk[:]), r=[bk_r], w=[dst_r])
                        else:
                            cx.op(ACT, lambda e: e.activation(out=dst[:, c, half * 512:(half + 1) * 512], in_=bk[:], func=AF.Silu), r=[bk_r], w=[dst_r])
            slot, slot_r, u = self.stream_get()
            assert u is self.U["edt"]
            wv = slot[:, 0:128].rearrange("p (k w) -> p k w", k=8)
            dtv, dtv_r = self.sb("dtv", [128, 16], F32, st)
            dt, dt_r = B["dt"]
            dta, dta_r = B["dta"]
            for c in range(NCH):
                bk, bk_r = self.bank()
                for kt in range(8):
                    cx.op(PE, lambda e: e.matmul(bk[:, 0:16], lhsT=hnT[:, kt, c * 128:(c + 1) * 128], rhs=wv[:, kt, :], start=(kt == 0), stop=(kt == 7)),
                          r=[slot_r, hnT_r], w=[bk_r])
                cx.op(DVE, lambda e: e.tensor_tensor(out=dtv[:], in0=bk[:, 0:16], in1=self.dtb[:], op=ALU.add), r=[bk_r, self.dtb_r], w=[dtv_r])
                cx.op(ACT, lambda e: e.activation(out=dtv[:], in_=dtv[:], func=AF.Exp), r=[dtv_r], w=[dtv_r])
                cx.op(ACT, lambda e: e.activation(out=dt[:, c, :], in_=dtv[:], func=AF.Ln, bias=self.one_t[:, 0:1]), r=[dtv_r, self.eps_r], w=[dt_r])
                cx.op(DVE, lambda e: e.tensor_tensor(out=dta[:, c, :], in0=dt[:, c, :], in1=self.aneg[:], op=ALU.mult), r=[dt_r, self.aneg_r], w=[dta_r])
            with ExitStack() as st2:
                T = {}
                for nm, shp, dtp in [("qT", [128, 512], BF16), ("qxT", [128, 512], BF16), ("kT", [128, 512], BF16), ("smT", [128, 512], BF16),
                                     ("st6", [128, 4, 6], F32), ("mv", [128, 4, 2], F32), ("rstd4", [128, 4], F32),
                                     ("ytmp", [128, 1024], F32), ("yretb", [128, 1024], BF16),
                                     ("rhsM0", [128, 512], F32), ("rhsM1", [128, 512], F32), ("Lg0", [128, 512], F32), ("Lg1", [128, 512], F32),
                                     ("cbm", [128, 2, 128], F32), ("MT", [128, 16, 128], BF16), ("dec", [128, 16], F32),
                                     ("xs", [128, 1024], BF16), ("Xdt", [128, 1024], BF16), ("Xdec", [128, 1024], BF16), ("Btok", [128, 256], BF16),
                                     ("ea", [128, 32], F32), ("ty", [128, 1024], F32), ("t2", [128, 1024], F32), ("yzb", [128, 1024], BF16),
                                     ("st6g", [128, 2, 6], F32), ("mvg", [128, 2, 2], F32), ("rs2", [128, 2], F32)]:
                    T[nm] = self.sb(nm, shp, dtp, st2)
                for c in range(NCH):
                    if "chunk" not in SKIP:
                        self.even_chunk(b, c, B, T)
                cx.barrier()
            mixT, mixT_r = B["mixT"]
            for ft in range(8):
                slot, slot_r, u = self.stream_get()
                assert u is self.U[("eout", ft)]
                wv = slot[:, 0:2048].rearrange("p (k w) -> p k w", k=16)
                bk, bk_r = self.bank()
                for kt in range(16):
                    cx.op(PE, lambda e: e.matmul(bk[:], lhsT=wv[:, kt, :], rhs=mixT[:, kt, :], start=(kt == 0), stop=(kt == 15)),
                          r=[slot_r, mixT_r], w=[bk_r])
                cx.op(DVE, lambda e: e.tensor_tensor(out=self.xT[:, ft, :], in0=self.xT[:, ft, :], in1=bk[:], op=ALU.add),
                      r=[bk_r, self.xT_r], w=[self.xT_r])
            cx.barrier()

    def even_chunk(self, b, c, B, T):
        gens = []
        if "ret" not in SKIP:
            gens.append(self.even_chunk_ret(b, c, B, T))
        if "ssd" not in SKIP:
            gens.append(self.even_chunk_ssd(b, c, B, T))
        while gens:
            for g in list(gens):
                try:
                    next(g)
                except StopIteration:
                    gens.remove(g)

    def rbank(self):
        self.rbank_i = (getattr(self, "rbank_i", 0) + 1) % 3
        return self.banks[self.rbank_i]

    def sbank(self):
        self.sbank_i = (getattr(self, "sbank_i", 0) + 1) % 3
        return self.banks[3 + self.sbank_i]

    def even_chunk_ret(self, b, c, B, T):
        cx = self.cx
        PE, ACT, DVE, POOL, SP = cx.PE, cx.ACT, cx.DVE, cx.POOL, cx.SP
        cs = slice(c * 128, (c + 1) * 128)
        ib, ib_r = self.ident_b, self.ident_b_r
        for i, (src, dstn) in enumerate([("qr", "qT"), ("qx", "qxT"), ("kr", "kT")]):
            s_t, s_r = B[src]
            d_t, d_r = T[dstn]
            pb, pb_r = self.bankb()
            for h in range(4):
                cx.op(PE, lambda e: e.transpose(pb[:, h * 128:(h + 1) * 128], s_t[:, c, h * 128:(h + 1) * 128], ib[:]), r=[s_r, ib_r], w=[pb_r])
            if i % 2 == 0:
                cx.op(ACT, lambda e: e.copy(out=d_t[:], in_=pb[:, 0:512]), r=[pb_r], w=[d_r])
            else:
                cx.op(DVE, lambda e: e.tensor_copy(out=d_t[:], in_=pb[:, 0:512]), r=[pb_r], w=[d_r])
            yield
        qT, qT_r = T["qT"]
        qxT, qxT_r = T["qxT"]
        kT, kT_r = T["kT"]
        smT, smT_r = T["smT"]
        v, v_r = B["v"]
        bk, bk_r = self.rbank()
        for h in range(4):
            hs = slice(h * 128, (h + 1) * 128)
            cx.op(PE, lambda e: e.matmul(bk[:, hs], lhsT=kT[:, hs], rhs=qT[:, hs], start=True, stop=True), r=[kT_r, qT_r], w=[bk_r])
        yield
        cx.op(DVE, lambda e: e.tensor_tensor(out=smT[:], in0=bk[:], in1=self.dmaskT[:], op=ALU.mult), r=[bk_r, self.dmaskT_r], w=[smT_r])
        yield
        bo = [self.rbank(), self.rbank()]
        for h in range(4):
            hs = slice(h * 128, (h + 1) * 128)
            o_t, o_r = bo[h // 2]
            osl = slice((h % 2) * 256, (h % 2 + 1) * 256)
            cx.op(PE, lambda e: e.matmul(o_t[:, osl], lhsT=smT[:, hs], rhs=v[:, c, h * 256:(h + 1) * 256], start=True, stop=False),
                  r=[smT_r, v_r], w=[o_r])
            cx.op(PE, lambda e: e.matmul(o_t[:, osl], lhsT=qxT[:, hs], rhs=self.retSb[:, h, :], start=False, stop=True),
                  r=[qxT_r, self.retSb_r], w=[o_r])
        yield
        st6, st6_r = T["st6"]
        mv, mv_r = T["mv"]
        rstd4, rstd4_r = T["rstd4"]
        ytmp, ytmp_r = T["ytmp"]
        yretb, yretb_r = T["yretb"]
        sg, sg_r = B["sg"]
        for h in range(4):
            o_t, o_r = bo[h // 2]
            osl = slice((h % 2) * 256, (h % 2 + 1) * 256)
            cx.op(DVE, lambda e: e.bn_stats(out=st6[:, h, :], in_=o_t[:, osl]), r=[o_r], w=[st6_r])
        yield
        for h in range(4):
            cx.op(DVE, lambda e: e.bn_aggr(out=mv[:, h, :], in_=st6[:, h, :]), r=[st6_r], w=[mv_r])
        yield
        cx.op(ACT, lambda e: e.activation(out=rstd4[:], in_=mv[:, :, 1], func=AF.Sqrt, bias=self.eps_t[:, 0:1]), r=[mv_r, self.eps_r], w=[rstd4_r])
        yield
        cx.op(DVE, lambda e: e.reciprocal(out=rstd4[:], in_=rstd4[:]), r=[rstd4_r], w=[rstd4_r])
        yield
        for h in range(4):
            o_t, o_r = bo[h // 2]
            osl = slice((h % 2) * 256, (h % 2 + 1) * 256)
            cx.op(DVE, lambda e: e.tensor_scalar(out=ytmp[:, h * 256:(h + 1) * 256], in0=o_t[:, osl], scalar1=mv[:, h, 0:1], scalar2=rstd4[:, h:h + 1],
                                                 op0=ALU.subtract, op1=ALU.mult),
                  r=[o_r, mv_r, rstd4_r], w=[ytmp_r])
        yield
        cx.op(POOL, lambda e: e.tensor_tensor(out=yretb[:], in0=ytmp[:], in1=sg[:, c, :], op=ALU.mult), r=[ytmp_r, sg_r], w=[yretb_r])
        yield
        mixT, mixT_r = B["mixT"]
        pb, pb_r = self.bankb()
        for t in range(8):
            cx.op(PE, lambda e: e.transpose(pb[:, t * 128:(t + 1) * 128], yretb[:, t * 128:(t + 1) * 128], ib[:]), r=[yretb_r, ib_r], w=[pb_r])
        cx.op(ACT, lambda e: e.copy(out=mixT[:, 0:8, cs], in_=pb[:].rearrange("p (t n) -> p t n", t=8)), r=[pb_r], w=[mixT_r])
        yield
        kz, kz_r = B["kz"]
        bkv = [self.rbank(), self.rbank()]
        for h in range(4):
            k_t, k_r = bkv[h // 2]
            osl = slice((h % 2) * 256, (h % 2 + 1) * 256)
            cx.op(PE, lambda e: e.matmul(k_t[:, osl], lhsT=kz[:, c, h * 128:(h + 1) * 128], rhs=v[:, c, h * 256:(h + 1) * 256], start=True, stop=True),
                  r=[kz_r, v_r], w=[k_r])
        yield
        for h in range(4):
            k_t, k_r = bkv[h // 2]
            osl = slice((h % 2) * 256, (h % 2 + 1) * 256)
            cx.op(DVE, lambda e: e.scalar_tensor_tensor(out=self.retS[:, h, :], in0=self.retS[:, h, :], scalar=RET_GDEC[h], in1=k_t[:, osl],
                                                        op0=ALU.mult, op1=ALU.add),
                  r=[k_r, self.retS_r], w=[self.retS_r])
        yield
        cx.op(ACT, lambda e: e.copy(out=self.retSb[:], in_=self.retS[:]), r=[self.retS_r], w=[self.retSb_r])

    def even_chunk_ssd(self, b, c, B, T):
        cx = self.cx
        PE, ACT, DVE, POOL, SP = cx.PE, cx.ACT, cx.DVE, cx.POOL, cx.SP
        cs = slice(c * 128, (c + 1) * 128)
        ib, ib_r = self.ident_b, self.ident_b_r
        mixT, mixT_r = B["mixT"]
        dt, dt_r = B["dt"]
        dta, dta_r = B["dta"]
        bcT, bcT_r = B["bcT"]
        xcT, xcT_r = B["xcT"]
        sz, sz_r = B["sz"]
        cbm, cbm_r = T["cbm"]
        MT, MT_r = T["MT"]
        dec, dec_r = T["dec"]
        bk, bk_r = self.sbank()
        for g in range(2):
            cx.op(PE, lambda e: e.matmul(bk[:, g * 128:(g + 1) * 128], lhsT=bcT[:, g, cs], rhs=bcT[:, 2 + g, cs], start=True, stop=True),
                  r=[bcT_r], w=[bk_r])
        yield
        cx.op(DVE, lambda e: e.tensor_tensor(out=cbm[:], in0=bk[:, 0:256].rearrange("p (g n) -> p g n", g=2), in1=self.bc_h(self.tri[:], 2), op=ALU.mult),
              r=[bk_r, self.tri_r], w=[cbm_r])
        yield
        for hg in range(4):
            rhsM, rhsM_r = T[f"rhsM{hg % 2}"]
            Lg, Lg_r = T[f"Lg{hg % 2}"]
            g = hg // 2
            cx.op(POOL, lambda e: e.tensor_tensor(out=rhsM[:].rearrange("p (h n) -> p h n", h=4), in0=self.bc_h(self.tri[:], 4),
                                                  in1=self.bc_mid(dta[:, c, hg * 4:(hg + 1) * 4], 128), op=ALU.mult),
                  r=[self.tri_r, dta_r], w=[rhsM_r])
            yield
            bk, bk_r = self.sbank()
            cx.op(PE, lambda e: e.matmul(bk[:], lhsT=self.su[:], rhs=rhsM[:], start=True, stop=True), r=[self.su_r, rhsM_r], w=[bk_r])
            yield
            cx.op(ACT, lambda e: e.activation(out=Lg[:], in_=bk[:], func=AF.Exp), r=[bk_r], w=[Lg_r])
            yield
            L3 = Lg[:].rearrange("p (h n) -> p h n", h=4)
            cx.op(DVE, lambda e: e.tensor_tensor(out=MT[:, hg * 4:(hg + 1) * 4, :], in0=L3, in1=self.bc_h(cbm[:, g, :], 4), op=ALU.mult),
                  r=[Lg_r, cbm_r], w=[MT_r])
            cx.op(POOL, lambda e: e.tensor_copy(out=dec[:, hg * 4:(hg + 1) * 4], in_=L3[:, :, 127]), r=[Lg_r], w=[dec_r])
        yield
        xs, xs_r = T["xs"]
        Xdt, Xdt_r = T["Xdt"]
        Xdec, Xdec_r = T["Xdec"]
        Btok, Btok_r = T["Btok"]
        pb, pb_r = self.bankb()
        for t in range(8):
            cx.op(PE, lambda e: e.transpose(pb[:, t * 128:(t + 1) * 128], xcT[:, t, cs], ib[:]), r=[xcT_r, ib_r], w=[pb_r])
        cx.op(ACT, lambda e: e.copy(out=xs[:], in_=pb[:]), r=[pb_r], w=[xs_r])
        cx.op(DVE, lambda e: e.tensor_tensor(out=Xdt[:].rearrange("p (h d) -> p h d", h=16), in0=pb[:].rearrange("p (h d) -> p h d", h=16),
                                             in1=self.bc_mid(dt[:, c, :], 64), op=ALU.mult),
              r=[pb_r, dt_r], w=[Xdt_r])
        yield
        cx.op(POOL, lambda e: e.tensor_tensor(out=Xdec[:].rearrange("p (h d) -> p h d", h=16), in0=Xdt[:].rearrange("p (h d) -> p h d", h=16),
                                              in1=self.bc_mid(dec[:], 64), op=ALU.mult),
              r=[Xdt_r, dec_r], w=[Xdec_r])
        pb2, pb2_r = self.bankb()
        for g in range(2):
            cx.op(PE, lambda e: e.transpose(pb2[:, g * 128:(g + 1) * 128], bcT[:, g, cs], ib[:]), r=[bcT_r, ib_r], w=[pb2_r])
        cx.op(ACT, lambda e: e.copy(out=Btok[:], in_=pb2[:, 0:256]), r=[pb2_r], w=[Btok_r])
        yield
        ea, ea_r = T["ea"]
        bk, bk_r = self.sbank()
        cx.op(PE, lambda e: e.matmul(bk[:, 0:16], lhsT=self.tri[:], rhs=dta[:, c, :], start=True, stop=True), r=[self.tri_r, dta_r], w=[bk_r])
        cx.op(PE, lambda e: e.matmul(bk[:, 16:32], lhsT=self.ones_f[:], rhs=dta[:, c, :], start=True, stop=True), r=[self.ones_f_r, dta_r], w=[bk_r])
        yield
        cx.op(ACT, lambda e: e.activation(out=ea[:], in_=bk[:, 0:32], func=AF.Exp), r=[bk_r], w=[ea_r])
        yield
        ty, ty_r = T["ty"]
        t2, t2_r = T["t2"]
        for g in range(2):
            gs = slice(g * 512, (g + 1) * 512)
            d_t, d_r = self.sbank()
            for hh in range(8):
                h = g * 8 + hh
                cx.op(PE, lambda e: e.matmul(d_t[:, hh * 64:(hh + 1) * 64], lhsT=MT[:, h, :], rhs=Xdt[:, h * 64:(h + 1) * 64], start=True, stop=True),
                      r=[MT_r, Xdt_r], w=[d_r])
            f_t, f_r = self.sbank()
            cx.op(PE, lambda e: e.matmul(f_t[:], lhsT=bcT[:, 2 + g, cs], rhs=self.ssSb[:, g, :], start=True, stop=True), r=[bcT_r, self.ssSb_r], w=[f_r])
            yield
            cx.op(DVE, lambda e: e.tensor_tensor(out=ty[:, gs].rearrange("p (h d) -> p h d", h=8), in0=f_t[:].rearrange("p (h d) -> p h d", h=8),
                                                 in1=self.bc_mid(ea[:, g * 8:(g + 1) * 8], 64), op=ALU.mult),
                  r=[f_r, ea_r], w=[ty_r])
            cx.op(DVE, lambda e: e.tensor_tensor(out=ty[:, gs], in0=ty[:, gs], in1=d_t[:], op=ALU.add), r=[d_r, ty_r], w=[ty_r])
            yield
        cx.op(POOL, lambda e: e.tensor_tensor(out=t2[:].rearrange("p (h d) -> p h d", h=16), in0=xs[:].rearrange("p (h d) -> p h d", h=16),
                                              in1=self.bc_mid(self.dsk[:], 64), op=ALU.mult),
              r=[xs_r, self.dsk_r], w=[t2_r])
        yield
        cx.op(POOL, lambda e: e.tensor_tensor(out=t2[:], in0=t2[:], in1=ty[:], op=ALU.add), r=[t2_r, ty_r], w=[t2_r])
        yield
        cx.op(POOL, lambda e: e.tensor_tensor(out=t2[:], in0=t2[:], in1=sz[:, c, :], op=ALU.mult), r=[t2_r, sz_r], w=[t2_r])
        yield
        st6g, st6g_r = T["st6g"]
        mvg, mvg_r = T["mvg"]
        rs2, rs2_r = T["rs2"]
        yzb, yzb_r = T["yzb"]
        for g in range(2):
            cx.op(DVE, lambda e: e.bn_stats(out=st6g[:, g, :], in_=t2[:, g * 512:(g + 1) * 512]), r=[t2_r], w=[st6g_r])
        yield
        for g in range(2):
            cx.op(DVE, lambda e: e.bn_aggr(out=mvg[:, g, :], in_=st6g[:, g, :]), r=[st6g_r], w=[mvg_r])
        yield
        cx.op(DVE, lambda e: e.tensor_tensor(out=rs2[:], in0=mvg[:, :, 0], in1=mvg[:, :, 0], op=ALU.mult), r=[mvg_r], w=[rs2_r])
        yield
        cx.op(DVE, lambda e: e.tensor_tensor(out=rs2[:], in0=rs2[:], in1=mvg[:, :, 1], op=ALU.add), r=[mvg_r, rs2_r], w=[rs2_r])
        yield
        cx.op(ACT, lambda e: e.activation(out=rs2[:], in_=rs2[:], func=AF.Sqrt, bias=self.eps_t[:, 0:1]), r=[rs2_r, self.eps_r], w=[rs2_r])
        yield
        cx.op(DVE, lambda e: e.reciprocal(out=rs2[:], in_=rs2[:]), r=[rs2_r], w=[rs2_r])
        yield
        for g in range(2):
            cx.op(ACT, lambda e: e.activation(out=yzb[:, g * 512:(g + 1) * 512], in_=t2[:, g * 512:(g + 1) * 512], func=AF.Copy, scale=rs2[:, g:g + 1]),
                  r=[t2_r, rs2_r], w=[yzb_r])
        yield
        pb, pb_r = self.bankb()
        for t in range(8):
            cx.op(PE, lambda e: e.transpose(pb[:, t * 128:(t + 1) * 128], yzb[:, t * 128:(t + 1) * 128], ib[:]), r=[yzb_r, ib_r], w=[pb_r])
        for t in range(8):
            cx.op(ACT, lambda e: e.activation(out=mixT[:, 8 + t, cs], in_=pb[:, t * 128:(t + 1) * 128], func=AF.Identity, scale=self.ssmn[:, t:t + 1]),
                  r=[pb_r, self.ssmn_r], w=[mixT_r])
        yield
        cdec = ea[:, 16:32]
        for g in range(2):
            s_t, s_r = self.sbank()
            cx.op(PE, lambda e: e.matmul(s_t[:], lhsT=Btok[:, g * 128:(g + 1) * 128], rhs=Xdec[:, g * 512:(g + 1) * 512], start=True, stop=True),
                  r=[Btok_r, Xdec_r], w=[s_r])
            yield
            cx.op(DVE, lambda e: e.tensor_tensor(out=self.ssS[:, g, :].rearrange("p (h d) -> p h d", h=8),
                                                 in0=self.ssS[:, g, :].rearrange("p (h d) -> p h d", h=8),
                                                 in1=self.bc_mid(cdec[:, g * 8:(g + 1) * 8], 64), op=ALU.mult),
                  r=[self.ssS_r, ea_r], w=[self.ssS_r])
            cx.op(DVE, lambda e: e.tensor_tensor(out=self.ssS[:, g, :], in0=self.ssS[:, g, :], in1=s_t[:], op=ALU.add),
                  r=[s_r, self.ssS_r], w=[self.ssS_r])
            yield
        cx.op(ACT, lambda e: e.copy(out=self.ssSb[:], in_=self.ssS[:]), r=[self.ssS_r], w=[self.ssSb_r])


    def setup_odd(self):
        cx = self.cx
        PE, ACT, DVE, POOL, SP = cx.PE, cx.ACT, cx.DVE, cx.POOL, cx.SP
        if "even" not in self.stages:
            self.ident_b, self.ident_b_r = self.sb("ident_b", [128, 128], BF16)
            cx.op(DVE, lambda e: e.tensor_copy(out=self.ident_b[:], in_=self.ident_f[:]), r=[self.ident_f_r], w=[self.ident_b_r])
        self.chalo, self.chalo_r = self.sb("chalo", [128, 4, 30], BF16)
        self.ocw, self.ocw_r = self.sb("ocw", [128, 4, 31], F32)
        self.ocb, self.ocb_r = self.sb("ocb", [128, 4], F32)
        self.olg, self.olg_r = self.sb("olg", [128, 4], F32)
        self.olb, self.olb_r = self.sb("olb", [128, 4], F32)
        self.A8r, self.A8r_r = self.sb("A8r", [128, 2, 16], F32)
        self.A8i, self.A8i_r = self.sb("A8i", [128, 2, 16], F32)
        self.Xs, self.Xs_r = self.sb("Xs", [128, 2, 16], F32)
        self.r8, self.r8_r = self.sb("r8", [128, 2, 16], F32)
        if not hasattr(self, "s5tab"):
            self.s5tab = self.nc.dram_tensor("s5tab", [128, 4096], F32, kind="ExternalOutput" if "dumpw" in SKIP else "Internal").ap()
        cx.op(DVE, lambda e: e.memset(self.chalo[:], 0.0), w=[self.chalo_r])
        cx.op(DVE, lambda e: e.memset(self.Xs[:], 0.0), w=[self.Xs_r])
        for k in range(31):
            cx.dma(SP, self.ocw[:, :, k], self.o_dw_w[0, k].rearrange("(t p) -> p t", p=128), w=[self.ocw_r], sem_res=self.ocw_r,
                   allow_slow_non_contiguous=True)
        for t, r, src in [(self.ocb, self.ocb_r, self.o_dw_b), (self.olg, self.olg_r, self.o_ln_g), (self.olb, self.olb_r, self.o_ln_b)]:
            cx.dma(SP, t[:], src[0].rearrange("(t p) -> p t", p=128), w=[r], sem_res=r, allow_slow_non_contiguous=True)
        with ExitStack() as st0:
            for i in range(4):
                dg, dg_r = self.sb(f"dg{i}", [128, 32, 128], BF16, st0)
                for k in range(31):
                    cx.op(DVE if k % 2 == 0 else POOL, lambda e: e.tensor_scalar(out=dg[:, k, :], in0=self.ident_f[:], scalar1=self.ocw[:, i, k:k + 1], scalar2=None,
                                                                              op0=ALU.mult), r=[self.ident_f_r, self.ocw_r], w=[dg_r])
                cx.op(DVE, lambda e: e.memset(dg[:, 31, :], 0.0), w=[dg_r])
                u = self.U[("ocv", i)]
                dst = self.wscr[u["off"]: u["off"] + 128 * 4096].rearrange("(p l) -> p l", p=128)
                cx.dma(SP, dst, dg[:].rearrange("q k n -> q (k n)"), r=[dg_r], sem_res=dg_r)
            cx.barrier(engines=cx.compute + [cx.SP], with_dma=True)
        with ExitStack() as st:
            def tl(nm, shp, dt=F32):
                return self.sb("s5_" + nm, shp, dt, st)
            lr, lr_r = tl("lr", [128, 16])
            li, li_r = tl("li", [128, 16])
            stp, stp_r = tl("stp", [128, 16])
            Bre, Bre_r = tl("Bre", [128, 16, 16])
            Bim, Bim_r = tl("Bim", [128, 16, 16])
            Cre, Cre_r = tl("Cre", [128, 16, 16])
            Cim, Cim_r = tl("Cim", [128, 16, 16])
            Dbc, Dbc_r = tl("Dbc", [128, 32, 16])
            tzm, tzm_r = tl("tzm", [128, 128])
            cx.dma(SP, tzm[:], self.consts["tzmask"], w=[tzm_r], sem_res=tzm_r)
            cx.dma(SP, lr[:], self.o_a_re[0].rearrange("g n -> (g n)").rearrange("(p q) -> q p", q=128), w=[lr_r], sem_res=lr_r,
                   allow_slow_non_contiguous=True)
            cx.dma(SP, li[:], self.o_a_im[0].rearrange("g n -> (g n)").rearrange("(p q) -> q p", q=128), w=[li_r], sem_res=li_r,
                   allow_slow_non_contiguous=True)
            ls2 = self.o_log_step[0].rearrange("(p gh) -> gh p", gh=2)
            for gh in range(2):
                cx.dma(SP, stp[gh * 64:(gh + 1) * 64, :], ls2[gh].partition_broadcast(64), w=[stp_r], sem_res=stp_r, allow_slow_non_contiguous=True)
            cx.dma(SP, Bre[:], self.o_b_re[0].rearrange("g n c -> (g n) c").rearrange("(p q) c -> q p c", q=128), w=[Bre_r], sem_res=Bre_r,
                   allow_slow_non_contiguous=True)
            cx.dma(SP, Bim[:], self.o_b_im[0].rearrange("g n c -> (g n) c").rearrange("(p q) c -> q p c", q=128), w=[Bim_r], sem_res=Bim_r,
                   allow_slow_non_contiguous=True)
            for (dst, dst_r, src) in [(Cre, Cre_r, self.o_c_re), (Cim, Cim_r, self.o_c_im)]:
                for g in range(32):
                    p, gh = g // 2, g % 2
                    cx.dma(SP, dst[gh * 64:(gh + 1) * 64, p, :], src[0, g].rearrange("co n -> n co"), w=[dst_r], sem_res=dst_r,
                           allow_slow_non_contiguous=True)
            cx.dma(SP, Dbc[:].rearrange("p g c -> p (g c)"), self.o_d[0].partition_broadcast(128), w=[Dbc_r], sem_res=Dbc_r, allow_slow_non_contiguous=True)
            cx.barrier(engines=cx.compute + [cx.SP], with_dma=True)
            cx.hook = getattr(self, "pro_hook", None)
            cx.op(ACT, lambda e: e.activation(out=stp[:], in_=stp[:], func=AF.Exp), r=[stp_r], w=[stp_r])
            lrs, lrs_r = tl("lrs", [128, 16])
            lis, lis_r = tl("lis", [128, 16])
            cx.op(DVE, lambda e: e.tensor_tensor(out=lrs[:], in0=lr[:], in1=stp[:], op=ALU.mult), r=[lr_r, stp_r], w=[lrs_r])
            cx.op(DVE, lambda e: e.tensor_tensor(out=lis[:], in0=li[:], in1=stp[:], op=ALU.mult), r=[li_r, stp_r], w=[lis_r])
            NM = 17
            mag, mag_r = tl("mag", [128, 16, NM])
            ang, ang_r = tl("ang", [128, 16, NM])
            Pre, Pre_r = tl("Pre", [128, 16, NM])
            Pim, Pim_r = tl("Pim", [128, 16, NM])
            tmpa, tmpa_r = tl("tmpa", [128, 16, NM])
            for mi in range(NM):
                m = float(mi - 8)
                cx.op(ACT, lambda e: e.activation(out=mag[:, :, mi], in_=lrs[:], func=AF.Exp, scale=m), r=[lrs_r], w=[mag_r])
                cx.op(DVE, lambda e: e.tensor_scalar(out=ang[:, :, mi], in0=lis[:], scalar1=m, scalar2=None, op0=ALU.mult), r=[lis_r], w=[ang_r])
            TWO_PI = 2.0 * math.pi
            MAGIC = 12582912.0

            def sin_of(dst, dst_r, shift, ang=ang, ang_r=ang_r, tmpa=tmpa, tmpa_r=tmpa_r):
                cx.op(DVE, lambda e: e.tensor_scalar(out=tmpa[:], in0=ang[:], scalar1=shift, scalar2=1.0 / TWO_PI, op0=ALU.add, op1=ALU.mult),
                      r=[ang_r], w=[tmpa_r])
                cx.op(DVE, lambda e: e.tensor_scalar(out=dst[:], in0=tmpa[:], scalar1=MAGIC, scalar2=None, op0=ALU.add), r=[tmpa_r], w=[dst_r])
                cx.op(DVE, lambda e: e.tensor_scalar(out=dst[:], in0=dst[:], scalar1=-MAGIC, scalar2=None, op0=ALU.add), r=[dst_r], w=[dst_r])
                cx.op(DVE, lambda e: e.tensor_tensor(out=tmpa[:], in0=tmpa[:], in1=dst[:], op=ALU.subtract), r=[tmpa_r, dst_r], w=[tmpa_r])
                cx.op(DVE, lambda e: e.tensor_scalar(out=tmpa[:], in0=tmpa[:], scalar1=TWO_PI, scalar2=math.pi, op0=ALU.mult, op1=ALU.min),
                      r=[tmpa_r], w=[tmpa_r])
                cx.op(DVE, lambda e: e.tensor_scalar(out=tmpa[:], in0=tmpa[:], scalar1=-math.pi, scalar2=None, op0=ALU.max), r=[tmpa_r], w=[tmpa_r])
                cx.op(ACT, lambda e: e.activation(out=dst[:], in_=tmpa[:], func=AF.Sin), r=[tmpa_r], w=[dst_r])

            sin_of(Pim, Pim_r, 0.0)
            sin_of(Pre, Pre_r, math.pi / 2.0)
            cx.op(DVE, lambda e: e.tensor_tensor(out=Pre[:], in0=Pre[:], in1=mag[:], op=ALU.mult), r=[Pre_r, mag_r], w=[Pre_r])
            cx.op(DVE, lambda e: e.tensor_tensor(out=Pim[:], in0=Pim[:], in1=mag[:], op=ALU.mult), r=[Pim_r, mag_r], w=[Pim_r])

            def P(which, m):
                return (Pre if which == "re" else Pim)[:, :, m + 8]
            cx.op(DVE, lambda e: e.tensor_copy(out=self.A8r[:, 0, :], in_=P("re", 8)), r=[Pre_r], w=[self.A8r_r])
            cx.op(DVE, lambda e: e.tensor_copy(out=self.A8r[:, 1, :], in_=P("re", 8)), r=[Pre_r], w=[self.A8r_r])
            cx.op(DVE, lambda e: e.tensor_scalar(out=self.A8i[:, 0, :], in0=P("im", 8), scalar1=-1.0, scalar2=None, op0=ALU.mult), r=[Pim_r], w=[self.A8i_r])
            cx.op(DVE, lambda e: e.tensor_copy(out=self.A8i[:, 1, :], in_=P("im", 8)), r=[Pim_r], w=[self.A8i_r])
            with ExitStack() as st_tab:
                def tl2(nm, shp, dt=F32):
                    return self.sb("s5t_" + nm, shp, dt, st_tab)
                ph8, ph8_r = tl2("ph8", [128, 16])
                k8, k8_r = tl2("k8", [128, 16])
                cx.op(DVE, lambda e: e.tensor_scalar(out=ph8[:], in0=lis[:], scalar1=8.0 / TWO_PI, scalar2=None, op0=ALU.mult), r=[lis_r], w=[ph8_r])
                cx.op(DVE, lambda e: e.tensor_scalar(out=k8[:], in0=ph8[:], scalar1=MAGIC, scalar2=None, op0=ALU.add), r=[ph8_r], w=[k8_r])
                cx.op(DVE, lambda e: e.tensor_scalar(out=k8[:], in0=k8[:], scalar1=-MAGIC, scalar2=None, op0=ALU.add), r=[k8_r], w=[k8_r])
                cx.op(DVE, lambda e: e.tensor_tensor(out=ph8[:], in0=ph8[:], in1=k8[:], op=ALU.subtract), r=[ph8_r, k8_r], w=[ph8_r])
                cx.op(DVE, lambda e: e.tensor_scalar(out=ph8[:], in0=ph8[:], scalar1=TWO_PI, scalar2=None, op0=ALU.mult), r=[ph8_r], w=[ph8_r])
                cidx, cidx_r = tl2("cidx", [128, 64])
                cx.dma(SP, cidx[:], self.consts["cidx"], w=[cidx_r], sem_res=cidx_r)
                angm, angm_r = tl2("angm", [128, 16, 64])
                tmpm, tmpm_r = tl2("tmpm", [128, 16, 64])
                stab, stab_r = tl2("stab", [128, 4096])
                cx.op(DVE, lambda e: e.tensor_tensor(out=angm[:], in0=ph8[:].unsqueeze(2).to_broadcast([128, 16, 64]),
                                                     in1=cidx[:].unsqueeze(1).to_broadcast([128, 16, 64]), op=ALU.mult), r=[ph8_r, cidx_r], w=[angm_r])
                Ec = stab[:, 0:1024].rearrange("q (p c) -> q p c", p=16)
                Es = stab[:, 1024:2048].rearrange("q (p c) -> q p c", p=16)
                rt = stab[:, 2048:4096].rearrange("q (a p c) -> q a p c", a=2, p=16)

                class _V:
                    def __init__(self, ap):
                        self.ap = ap

                    def __getitem__(self, k):
                        return self.ap
                sin_of(_V(Es), stab_r, 0.0, ang=angm, ang_r=angm_r, tmpa=tmpm, tmpa_r=tmpm_r)
                sin_of(_V(Ec), stab_r, math.pi / 2.0, ang=angm, ang_r=angm_r, tmpa=tmpm, tmpa_r=tmpm_r)
                for a_ in range(2):
                    cx.op(DVE, lambda e: e.tensor_copy(out=rt[:, a_, :, :], in_=mag[:, :, 16].unsqueeze(2).to_broadcast([128, 16, 64])), r=[mag_r], w=[stab_r])
                cx.op(DVE, lambda e: e.memset(rt[:, :, :, 0], 0.0), w=[stab_r])
                cx.op(DVE, lambda e: e.tensor_copy(out=self.r8[:, 0, :], in_=mag[:, :, 16]), r=[mag_r], w=[self.r8_r])
                cx.op(DVE, lambda e: e.tensor_copy(out=self.r8[:, 1, :], in_=mag[:, :, 16]), r=[mag_r], w=[self.r8_r])
                cx.dma(SP, self.s5tab, stab[:], r=[stab_r], sem_res=stab_r)
                cx.barrier(engines=cx.compute + [cx.SP], with_dma=True)
            fre, fre_r = tl("fre", [128, 16])
            fim, fim_r = tl("fim", [128, 16])
            den, den_r = tl("den", [128, 16])
            am1, am1_r = tl("am1", [128, 16])
            t16, t16_r = tl("t16", [128, 16])
            cx.op(DVE, lambda e: e.tensor_tensor(out=den[:], in0=lr[:], in1=lr[:], op=ALU.mult), r=[lr_r], w=[den_r])
            cx.op(DVE, lambda e: e.tensor_tensor(out=t16[:], in0=li[:], in1=li[:], op=ALU.mult), r=[li_r], w=[t16_r])
            cx.op(DVE, lambda e: e.tensor_tensor(out=den[:], in0=den[:], in1=t16[:], op=ALU.add), r=[den_r, t16_r], w=[den_r])
            cx.op(DVE, lambda e: e.reciprocal(out=den[:], in_=den[:]), r=[den_r], w=[den_r])
            cx.op(DVE, lambda e: e.tensor_scalar(out=am1[:], in0=P("re", 1), scalar1=-1.0, scalar2=None, op0=ALU.add), r=[Pre_r], w=[am1_r])
            cx.op(DVE, lambda e: e.tensor_tensor(out=fre[:], in0=am1[:], in1=lr[:], op=ALU.mult), r=[am1_r, lr_r], w=[fre_r])
            cx.op(DVE, lambda e: e.tensor_tensor(out=t16[:], in0=P("im", 1), in1=li[:], op=ALU.mult), r=[Pim_r, li_r], w=[t16_r])
            cx.op(DVE, lambda e: e.tensor_tensor(out=fre[:], in0=fre[:], in1=t16[:], op=ALU.add), r=[fre_r, t16_r], w=[fre_r])
            cx.op(DVE, lambda e: e.tensor_tensor(out=fre[:], in0=fre[:], in1=den[:], op=ALU.mult), r=[fre_r, den_r], w=[fre_r])
            cx.op(DVE, lambda e: e.tensor_tensor(out=fim[:], in0=P("im", 1), in1=lr[:], op=ALU.mult), r=[Pim_r, lr_r], w=[fim_r])
            cx.op(DVE, lambda e: e.tensor_tensor(out=t16[:], in0=am1[:], in1=li[:], op=ALU.mult), r=[am1_r, li_r], w=[t16_r])
            cx.op(DVE, lambda e: e.tensor_tensor(out=fim[:], in0=fim[:], in1=t16[:], op=ALU.subtract), r=[fim_r, t16_r], w=[fim_r])
            cx.op(DVE, lambda e: e.tensor_tensor(out=fim[:], in0=fim[:], in1=den[:], op=ALU.mult), r=[fim_r, den_r], w=[fim_r])

            def cmul(dre, dre_r, dim_, dim_r, are, aim, a_rs, bre, bim, b_rs, tA, tA_r, neg_im=False):
                cx.op(DVE, lambda e: e.tensor_tensor(out=dre, in0=are, in1=bre, op=ALU.mult), r=a_rs + b_rs, w=[dre_r])
                cx.op(DVE, lambda e: e.tensor_tensor(out=tA, in0=aim, in1=bim, op=ALU.mult), r=a_rs + b_rs, w=[tA_r])
                cx.op(DVE, lambda e: e.tensor_tensor(out=dre, in0=dre, in1=tA, op=ALU.subtract), r=[dre_r, tA_r], w=[dre_r])
                cx.op(DVE, lambda e: e.tensor_tensor(out=dim_, in0=are, in1=bim, op=ALU.mult), r=a_rs + b_rs, w=[dim_r])
                cx.op(DVE, lambda e: e.tensor_tensor(out=tA, in0=aim, in1=bre, op=ALU.mult), r=a_rs + b_rs, w=[tA_r])
                cx.op(DVE, lambda e: e.tensor_tensor(out=dim_, in0=dim_, in1=tA, op=ALU.add), r=[dim_r, tA_r], w=[dim_r])

            def bc16(ap2):
                return ap2.unsqueeze(2).to_broadcast([128, 16, 16])
            Bbr, Bbr_r = tl("Bbr", [128, 16, 16])
            Bbi, Bbi_r = tl("Bbi", [128, 16, 16])
            t3, t3_r = tl("t3", [128, 16, 16])
            cmul(Bbr[:], Bbr_r, Bbi[:], Bbi_r, bc16(fre[:]), bc16(fim[:]), [fre_r, fim_r], Bre[:], Bim[:], [Bre_r, Bim_r], t3[:], t3_r)
            CAr, CAr_r = tl("CAr", [128, 16, 8, 16])
            CAi, CAi_r = tl("CAi", [128, 16, 8, 16])
            for s in range(8):
                cmul(CAr[:, :, s, :], CAr_r, CAi[:, :, s, :], CAi_r, bc16(P("re", s + 1)), bc16(P("im", s + 1)), [Pre_r, Pim_r],
                     Cre[:], Cim[:], [Cre_r, Cim_r], t3[:], t3_r)
            with ExitStack() as sc:
                CcU, CcU_r = self.sb("s5_CcU", [128, 16, 2, 2, 128], BF16, sc)
                for ph in range(4):
                    cx.op(DVE, lambda e: e.memset(CcU[:, ph * 4:(ph + 1) * 4], 0.0), w=[CcU_r])
                for gh in range(2):
                    ps_ = slice(gh * 64, (gh + 1) * 64)
                    cx.op(DVE, lambda e: e.tensor_copy(out=CcU[ps_, :, gh, 0, :], in_=CAr[ps_].rearrange("q p j c -> q p (j c)")), r=[CAr_r], w=[CcU_r])
                    cx.op(DVE, lambda e: e.tensor_scalar(out=CcU[ps_, :, gh, 1, :], in0=CAi[ps_].rearrange("q p j c -> q p (j c)"), scalar1=-1.0, scalar2=None,
                                                         op0=ALU.mult), r=[CAi_r], w=[CcU_r])
                ccflat = CcU[:].rearrange("q p a b n -> q (p a b n)")
                for i in range(2):
                    u = self.U[("s5cc", i)]
                    dst = self.wscr[u["off"]: u["off"] + 128 * 4096].rearrange("(p l) -> p l", p=128)
                    cx.dma(SP, dst, ccflat[:, i * 4096:(i + 1) * 4096], r=[CcU_r], sem_res=CcU_r)
                cx.barrier(engines=cx.compute + [cx.SP], with_dma=True)
            with ExitStack() as sc:
                BcTr, BcTr_r = self.sb("s5_BcTr", [128, 16, 8, 16], F32, sc)
                BcTi, BcTi_r = self.sb("s5_BcTi", [128, 16, 8, 16], F32, sc)
                for s in range(8):
                    cmul(BcTr[:, :, s, :], BcTr_r, BcTi[:, :, s, :], BcTi_r, bc16(P("re", 7 - s)), bc16(P("im", 7 - s)), [Pre_r, Pim_r],
                         Bbr[:], Bbi[:], [Bbr_r, Bbi_r], t3[:], t3_r)
                BcU, BcU_r = self.sb("s5_BcU", [128, 8, 2, 2, 128], BF16, sc)
                for i in range(2):
                    for ph in range(2):
                        cx.op(DVE, lambda e: e.memset(BcU[:, ph * 4:(ph + 1) * 4], 0.0), w=[BcU_r])
                    for pp in range(8):
                        p = i * 8 + pp
                        for ri, (src, src_r) in enumerate([(BcTr, BcTr_r), (BcTi, BcTi_r)]):
                            bk, bk_r = self.bank()
                            cx.op(PE, lambda e: e.transpose(bk[:, 0:128], src[:, p, :, :].rearrange("q s c -> q (s c)"), self.ident_f[:]),
                                  r=[src_r, self.ident_f_r], w=[bk_r])
                            for gh in range(2):
                                cx.op(ACT if gh == 0 else DVE, lambda e: (e.copy if gh == 0 else e.tensor_copy)(out=BcU[:, pp, gh, ri, gh * 64:(gh + 1) * 64],
                                                                                                         in_=bk[:, gh * 64:(gh + 1) * 64]),
                                      r=[bk_r], w=[BcU_r])
                    u = self.U[("s5bc", i)]
                    dst = self.wscr[u["off"]: u["off"] + 128 * 4096].rearrange("(p l) -> p l", p=128)
                    cx.dma(SP, dst, BcU[:].rearrange("q p a b n -> q (p a b n)"), r=[BcU_r], sem_res=BcU_r)
                cx.barrier(engines=cx.compute + [cx.SP], with_dma=True)
            with ExitStack() as sc:
                BpTr, BpTr_r = self.sb("s5_BpTr", [128, 16, 8, 16], F32, sc)
                BpTi, BpTi_r = self.sb("s5_BpTi", [128, 16, 8, 16], F32, sc)
                for s in range(8):
                    cmul(BpTr[:, :, s, :], BpTr_r, BpTi[:, :, s, :], BpTi_r, bc16(P("re", -1 - s)), bc16(P("im", -1 - s)), [Pre_r, Pim_r],
                         Bbr[:], Bbi[:], [Bbr_r, Bbi_r], t3[:], t3_r)
                TzU, TzU_r = self.sb("s5_TzU", [128, 32, 128], BF16, sc)
                lm = [self.sb(f"s5_lm{i}", [128, 128], F32, sc) for i in range(4)]
                tzt, tzt_r = self.sb("s5_tzt", [128, 128], F32, sc)
                for (l_, l_r) in lm:
                    cx.op(POOL, lambda e: e.memset(l_[:], 0.0), w=[l_r])
                for g in range(32):
                    p, gh = g // 2, g % 2
                    ps_ = slice(gh * 64, (gh + 1) * 64)
                    po_ = slice((1 - gh) * 64, (2 - gh) * 64)
                    (l0, l0_r), (l1, l1_r) = lm[(g % 2) * 2], lm[(g % 2) * 2 + 1]
                    cx.op(POOL, lambda e: e.tensor_copy(out=l0[ps_, :], in_=BpTr[ps_, p, :, :].rearrange("q s c -> q (s c)")), r=[BpTr_r], w=[l0_r])
                    cx.op(POOL, lambda e: e.tensor_scalar(out=l1[ps_, :], in0=BpTi[ps_, p, :, :].rearrange("q s c -> q (s c)"), scalar1=-1.0, scalar2=None,
                                                          op0=ALU.mult), r=[BpTi_r], w=[l1_r])
                    bk, bk_r = self.bank()
                    cx.op(PE, lambda e: e.matmul(bk[:, 0:128], lhsT=l0[:], rhs=CAr[:, p, :, :].rearrange("q j c -> q (j c)"), start=True, stop=False),
                          r=[l0_r, CAr_r], w=[bk_r])
                    cx.op(PE, lambda e: e.matmul(bk[:, 0:128], lhsT=l1[:], rhs=CAi[:, p, :, :].rearrange("q j c -> q (j c)"), start=False, stop=True),
                          r=[l1_r, CAi_r], w=[bk_r])
                    cx.op(DVE, lambda e: e.tensor_tensor(out=tzt[:], in0=bk[:, 0:128], in1=tzm[:], op=ALU.mult), r=[bk_r, tzm_r], w=[tzt_r])
                    cx.op(DVE, lambda e: e.tensor_tensor(out=TzU[:, g, :].rearrange("q (j c) -> q j c", j=8),
                                                         in0=self.ident_f[:].rearrange("q (j c) -> q j c", j=8),
                                                         in1=Dbc[:, g, :].unsqueeze(1).to_broadcast([128, 8, 16]), op=ALU.mult),
                          r=[self.ident_f_r, Dbc_r], w=[TzU_r])
                    cx.op(DVE, lambda e: e.tensor_tensor(out=TzU[:, g, :], in0=TzU[:, g, :], in1=tzt[:], op=ALU.add), r=[tzt_r, TzU_r], w=[TzU_r])
                u = self.U["s5tz"]
                dst = self.wscr[u["off"]: u["off"] + 128 * 4096].rearrange("(p l) -> p l", p=128)
                cx.dma(SP, dst, TzU[:].rearrange("q g n -> q (g n)"), r=[TzU_r], sem_res=TzU_r)
                cx.barrier(engines=cx.compute + [cx.SP], with_dma=True)
            cx.barrier(engines=cx.compute + [cx.SP], with_dma=True)

    def odd(self, b):
        cx = self.cx
        PE, ACT, DVE, POOL, SP = cx.PE, cx.ACT, cx.DVE, cx.POOL, cx.SP
        self.rmsnorm(2, self.hnT, self.hnT_r)
        hnT, hnT_r = self.hnT, self.hnT_r
        NC8 = NB // 8
        with ExitStack() as st:
            def tl(nm, shp, dt=F32):
                return self.sb("o_" + nm, shp, dt, st)
            mixT, mixT_r = tl("mixT", [128, 8, NB], BF16)
            cbuf, cbuf_r = tl("cbuf", [128, 4, NB + 30], BF16)
            cacc, cacc_r = tl("cacc", [128, 4, NB])
            sig = [tl(f"sig{i}", [128, NB]) for i in range(2)]
            uS, uS_r = tl("uS", [64, 32, 128], BF16)
            U, U_r = tl("U", [128, 32, NC8], BF16)
            V, V_r = tl("V", [128, 2, 16, NC8])
            Xst, Xst_r = tl("Xst", [128, 2, 16, NC8], BF16)
            slot, slot_r, u = self.stream_get()
            assert u is self.U["ou"]
            wv = slot[:, 0:4096].rearrange("p (k w) -> p k w", k=8)
            for s in range(8):
                bk, bk_r = self.bank()
                for kt in range(8):
                    lh = hnT[:, kt, :].rearrange("p (c s) -> p s c", s=8)[:, s, :]
                    cx.op(PE, lambda e: e.matmul(bk[0:64, :], lhsT=lh, rhs=wv[:, kt, :], start=(kt == 0), stop=(kt == 7)), r=[slot_r, hnT_r], w=[bk_r])
                src = bk[0:64, :].rearrange("p (g c) -> p g c", g=32)
                if s % 2 == 0:
                    cx.op(ACT, lambda e: e.copy(out=uS[:, :, s * 16:(s + 1) * 16], in_=src), r=[bk_r], w=[uS_r])
                else:
                    cx.op(DVE, lambda e: e.tensor_copy(out=uS[:, :, s * 16:(s + 1) * 16], in_=src), r=[bk_r], w=[uS_r])
            for g8 in range(4):
                pb, pb_r = self.bankb()
                for gi in range(8):
                    g = g8 * 8 + gi
                    cx.op(PE, lambda e: e.transpose(pb[:, gi * 64:(gi + 1) * 64], uS[:, g, :], self.ident_b[0:64, 0:64]), r=[uS_r, self.ident_b_r], w=[pb_r])
                if g8 % 2 == 0:
                    cx.op(ACT, lambda e: e.copy(out=U[:, g8 * 8:(g8 + 1) * 8, :].rearrange("p g c -> p (g c)"), in_=pb[:, 0:512]), r=[pb_r], w=[U_r])
                else:
                    cx.op(DVE, lambda e: e.tensor_copy(out=U[:, g8 * 8:(g8 + 1) * 8, :].rearrange("p g c -> p (g c)"), in_=pb[:, 0:512]), r=[pb_r], w=[U_r])
            bcs = []
            for i in range(2):
                slot, slot_r, u = self.stream_get(hold_prev=i)
                assert u is self.U[("s5bc", i)]
                bcs.append((slot[:, 0:4096].rearrange("q (p a b n) -> q p a b n", p=8, a=2, b=2), slot_r))
            for ri in range(2):
                for ph in range(2):
                    bk, bk_r = self.bank()
                    for pi in range(8):
                        p = ph * 8 + pi
                        bcv, bc_r = bcs[p // 8]
                        for gh in range(2):
                            cx.op(PE, lambda e: e.matmul(bk[:, pi * 64:(pi + 1) * 64], lhsT=bcv[:, p % 8, gh, ri, :], rhs=U[:, 2 * p + gh, :],
                                                         start=(gh == 0), stop=(gh == 1)), r=[bc_r, U_r], w=[bk_r])
                    cx.op(ACT, lambda e: e.copy(out=V[:, ri, ph * 8:(ph + 1) * 8, :].rearrange("q p c -> q (p c)"), in_=bk[:]), r=[bk_r], w=[V_r])
            with ExitStack() as st_scan:
                def tls(nm, shp, dt=F32):
                    return self.sb("os_" + nm, shp, dt, st_scan if "noscanscope" not in SKIP else st)
                tab, tab_r = tls("tab", [128, 4096])
                cx.dma(SP, tab[:], self.s5tab, w=[tab_r], sem_res=self.r8_r)
                Ec = tab[:, 0:1024].rearrange("q (p c) -> q p c", p=16)
                Es = tab[:, 1024:2048].rearrange("q (p c) -> q p c", p=16)
                rtf = tab[:, 2048:4096]
                Vt, Vt_r = tls("Vt", [128, 2, 16, NC8])
                Yt, Yt_r = tls("Yt", [128, 2, 16, NC8])
                mA, mA_r = tls("mA", [128, 16, NC8])
                mB, mB_r = tls("mB", [128, 16, NC8])
                Xs, Xs_r = self.Xs, self.Xs_r
                Vr, Vi = V[:, 0, :, :], V[:, 1, :, :]
                cx.op(POOL, lambda e: e.tensor_tensor(out=mA[:], in0=Vr, in1=Ec, op=ALU.mult), r=[V_r, tab_r], w=[mA_r])
                cx.op(DVE, lambda e: e.tensor_tensor(out=mB[:], in0=Vi, in1=Es, op=ALU.mult), r=[V_r, tab_r], w=[mB_r])
                cx.op(DVE, lambda e: e.tensor_tensor(out=Vt[:, 0, :, :], in0=mA[:], in1=mB[:], op=ALU.add), r=[mA_r, mB_r], w=[Vt_r])
                cx.op(POOL, lambda e: e.tensor_tensor(out=mA[:], in0=Vi, in1=Ec, op=ALU.mult), r=[V_r, tab_r], w=[mA_r])
                cx.op(DVE, lambda e: e.tensor_tensor(out=mB[:], in0=Vr, in1=Es, op=ALU.mult), r=[V_r, tab_r], w=[mB_r])
                cx.op(DVE, lambda e: e.tensor_tensor(out=Vt[:, 1, :, :], in0=mA[:], in1=mB[:], op=ALU.subtract), r=[mA_r, mB_r], w=[Vt_r])
                T1, T1_r = tls("T1", [128, 2, 16])
                cx.op(DVE, lambda e: e.tensor_tensor(out=T1[:], in0=Xs[:], in1=self.r8[:], op=ALU.mult), r=[Xs_r, self.r8_r], w=[T1_r])
                cx.op(DVE, lambda e: e.tensor_tensor(out=Vt[:, :, :, 0], in0=Vt[:, :, :, 0], in1=T1[:], op=ALU.add), r=[Vt_r, T1_r], w=[Vt_r])
                cx.op(POOL, lambda e: e.tensor_copy(out=Xst[:, :, :, 0], in_=Xs[:]), r=[Xs_r], w=[Xst_r])
                cx.op(DVE, lambda e: e.tensor_tensor_scan(out=Yt[:].rearrange("q a p c -> q (a p c)"), data0=rtf,
                                                          data1=Vt[:].rearrange("q a p c -> q (a p c)"), initial=0.0, op0=ALU.mult, op1=ALU.add),
                      r=[tab_r, Vt_r], w=[Yt_r])
                Ytr, Yti = Yt[:, 0, :, :], Yt[:, 1, :, :]
                cx.op(POOL, lambda e: e.tensor_tensor(out=mA[:], in0=Ytr, in1=Ec, op=ALU.mult), r=[Yt_r, tab_r], w=[mA_r])
                cx.op(DVE, lambda e: e.tensor_tensor(out=mB[:], in0=Yti, in1=Es, op=ALU.mult), r=[Yt_r, tab_r], w=[mB_r])
                cx.op(DVE, lambda e: e.tensor_tensor(out=V[:, 0, :, :], in0=mA[:], in1=mB[:], op=ALU.subtract), r=[mA_r, mB_r], w=[V_r])
                cx.op(POOL, lambda e: e.tensor_tensor(out=mA[:], in0=Yti, in1=Ec, op=ALU.mult), r=[Yt_r, tab_r], w=[mA_r])
                cx.op(DVE, lambda e: e.tensor_tensor(out=mB[:], in0=Ytr, in1=Es, op=ALU.mult), r=[Yt_r, tab_r], w=[mB_r])
                cx.op(DVE, lambda e: e.tensor_tensor(out=V[:, 1, :, :], in0=mA[:], in1=mB[:], op=ALU.add), r=[mA_r, mB_r], w=[V_r])
                cx.op(ACT, lambda e: e.copy(out=Xst[:, :, :, 1:NC8], in_=V[:, :, :, 0:NC8 - 1]), r=[V_r], w=[Xst_r])
                cx.op(POOL, lambda e: e.tensor_copy(out=Xs[:], in_=V[:, :, :, NC8 - 1]), r=[V_r], w=[Xs_r])
                for i in range(4):
                    slot, slot_r, u = self.stream_get()
                    assert u is self.U[("oin", i)]
                    bks = []
                    for gi in range(2):
                        wv = slot[:, gi * 1024:(gi + 1) * 1024].rearrange("p (k w) -> p k w", k=8)
                        bk, bk_r = self.bank()
                        for kt in range(8):
                            cx.op(PE, lambda e: e.matmul(bk[:], lhsT=wv[:, kt, :], rhs=hnT[:, kt, :], start=(kt == 0), stop=(kt == 7)), r=[slot_r, hnT_r], w=[bk_r])
                        bks.append((bk, bk_r))
                    sg_t, sg_r = sig[i % 2]
                    cx.op(ACT, lambda e: e.activation(out=sg_t[:], in_=bks[1][0][:], func=AF.Sigmoid), r=[bks[1][1]], w=[sg_r])
                    cx.op(DVE, lambda e: e.tensor_tensor(out=cbuf[:, i, 30:NB + 30], in0=bks[0][0][:], in1=sg_t[:], op=ALU.mult), r=[bks[0][1], sg_r], w=[cbuf_r])
                    cx.op(POOL, lambda e: e.tensor_copy(out=cbuf[:, i, 0:30], in_=self.chalo[:, i, :]), r=[self.chalo_r], w=[cbuf_r])
                    cx.op(POOL, lambda e: e.tensor_copy(out=self.chalo[:, i, :], in_=cbuf[:, i, NB:NB + 30]), r=[cbuf_r], w=[self.chalo_r])
                    dslot, dslot_r, u = self.stream_get(hold_prev=0)
                    assert u is self.U[("ocv", i)]
                    dv = dslot[:, 0:4096].rearrange("q (k n) -> q k n", k=32)
                    bk, bk_r = self.bank()
                    for k in range(31):
                        cx.op(PE, lambda e: e.matmul(bk[:], lhsT=dv[:, k, :], rhs=cbuf[:, i, k:k + NB], start=(k == 0), stop=(k == 30)), r=[dslot_r, cbuf_r], w=[bk_r])
                    cx.op(ACT, lambda e: e.activation(out=cacc[:, i, :], in_=bk[:], func=AF.Identity, bias=self.ocb[:, i:i + 1]), r=[bk_r, self.ocb_r], w=[cacc_r])
                cx.barrier()
            sq = [tl(f"sq{i}", [128, NB]) for i in range(2)]
            b1, b1_r = self.bank()
            b2, b2_r = self.bank()
            for i in range(4):
                cx.op(PE, lambda e: e.matmul(b1[:], lhsT=self.ones_f[:], rhs=cacc[:, i, :], start=(i == 0), stop=(i == 3)), r=[self.ones_f_r, cacc_r], w=[b1_r])
            for i in range(4):
                s_t, s_r = sq[i % 2]
                cx.op(ACT, lambda e: e.activation(out=s_t[:], in_=cacc[:, i, :], func=AF.Square), r=[cacc_r], w=[s_r])
                cx.op(PE, lambda e: e.matmul(b2[:], lhsT=self.ones_f[:], rhs=s_t[:], start=(i == 0), stop=(i == 3)), r=[self.ones_f_r, s_r], w=[b2_r])
            mean, mean_r = tl("mean", [128, NB])
            var, var_r = tl("var", [128, NB])
            cx.op(DVE, lambda e: e.tensor_scalar(out=mean[:], in0=b1[:], scalar1=1.0 / 512, scalar2=None, op0=ALU.mult), r=[b1_r], w=[mean_r])
            cx.op(DVE, lambda e: e.tensor_tensor(out=var[:], in0=mean[:], in1=mean[:], op=ALU.mult), r=[mean_r], w=[var_r])
            cx.op(DVE, lambda e: e.scalar_tensor_tensor(out=var[:], in0=b2[:], scalar=1.0 / 512, in1=var[:], op0=ALU.mult, op1=ALU.subtract),
                  r=[b2_r, var_r], w=[var_r])
            cx.op(ACT, lambda e: e.activation(out=var[:], in_=var[:], func=AF.Sqrt, bias=self.eps_t[:, 0:1]), r=[var_r, self.eps_r], w=[var_r])
            cx.op(DVE, lambda e: e.reciprocal(out=var[:], in_=var[:]), r=[var_r], w=[var_r])
            for i in range(4):
                cx.op(DVE, lambda e: e.tensor_tensor(out=cacc[:, i, :], in0=cacc[:, i, :], in1=mean[:], op=ALU.subtract), r=[cacc_r, mean_r], w=[cacc_r])
                cx.op(DVE, lambda e: e.tensor_tensor(out=cacc[:, i, :], in0=cacc[:, i, :], in1=var[:], op=ALU.mult), r=[cacc_r, var_r], w=[cacc_r])
                cx.op(ACT, lambda e: e.activation(out=mixT[:, i, :], in_=cacc[:, i, :], func=AF.Silu, bias=self.olb[:, i:i + 1], scale=self.olg[:, i:i + 1]),
                      r=[cacc_r, self.olg_r, self.olb_r], w=[mixT_r])
            slot, tz_r, u = self.stream_get()
            assert u is self.U["s5tz"]
            tzv = slot[:, 0:4096].rearrange("q (g n) -> q g n", g=32)
            ccs = []
            for i in range(2):
                slot, slot_r, u = self.stream_get(hold_prev=i + 1)
                assert u is self.U[("s5cc", i)]
                ccs.append((slot[:, 0:4096].rearrange("q (p a b n) -> q p a b n", p=8, a=2, b=2), slot_r))
            stok, stok_r = tl("stok", [64, 8, 512])
            for g4 in range(8):
                bk, bk_r = self.bank()
                for gi in range(4):
                    g = g4 * 4 + gi
                    p, gh = g // 2, g % 2
                    ccv, cc_r = ccs[p // 8]
                    reg = bk[0:64, gi * 128:(gi + 1) * 128]
                    cx.op(PE, lambda e: e.matmul(reg, lhsT=U[:, g, :], rhs=tzv[:, g, :], start=True, stop=False), r=[U_r, tz_r], w=[bk_r])
                    cx.op(PE, lambda e: e.matmul(reg, lhsT=Xst[:, 0, p, :], rhs=ccv[:, p % 8, gh, 0, :], start=False, stop=False), r=[Xst_r, cc_r], w=[bk_r])
                    cx.op(PE, lambda e: e.matmul(reg, lhsT=Xst[:, 1, p, :], rhs=ccv[:, p % 8, gh, 1, :], start=False, stop=True), r=[Xst_r, cc_r], w=[bk_r])
                src = bk[0:64, :].rearrange("p (g j c) -> p j g c", g=4, j=8)
                dst = stok[:, :, g4 * 64:(g4 + 1) * 64].rearrange("p j (g c) -> p j g c", g=4)
                if g4 % 2 == 0:
                    cx.op(ACT, lambda e: e.copy(out=dst, in_=src), r=[bk_r], w=[stok_r])
                else:
                    cx.op(DVE, lambda e: e.tensor_copy(out=dst, in_=src), r=[bk_r], w=[stok_r])
            syT, syT_r = tl("syT", [128, 4, NB])
            for t in range(4):
                bk, bk_r = self.bank()
                for j in range(8):
                    cx.op(PE, lambda e: e.transpose(bk[:, j * 64:(j + 1) * 64], stok[:, j, t * 128:(t + 1) * 128], self.ident_f[0:64, 0:64]),
                          r=[stok_r, self.ident_f_r], w=[bk_r])
                src = bk[:].rearrange("p (j c) -> p j c", j=8)
                dst = syT[:, t, :].rearrange("p (c j) -> p j c", j=8)
                if t % 2 == 0:
                    cx.op(ACT, lambda e: e.copy(out=dst, in_=src), r=[bk_r], w=[syT_r])
                else:
                    cx.op(DVE, lambda e: e.tensor_copy(out=dst, in_=src), r=[bk_r], w=[syT_r])
            if "s5y" in self.dbg:
                for t in range(4):
                    self.dump("s5y", syT[:, t, :], syT_r, self.dbg_out["s5y"][t * 128:(t + 1) * 128, b * NB:(b + 1) * NB])
            gl, gl_r = tl("gl", [128, 4, NB])
            glb, glb_r = tl("glb", [128, 4, NB], BF16)
            gt = [tl(f"gt{i}", [128, NB]) for i in range(2)]
            K2 = 2.0 * math.sqrt(2.0 / math.pi)
            for t in range(4):
                g_t, g_r = gt[t % 2]
                cx.op(DVE, lambda e: e.tensor_tensor(out=g_t[:], in0=syT[:, t, :], in1=syT[:, t, :], op=ALU.mult), r=[syT_r], w=[g_r])
                cx.op(DVE, lambda e: e.tensor_scalar(out=g_t[:], in0=g_t[:], scalar1=0.044715, scalar2=1.0, op0=ALU.mult, op1=ALU.add), r=[g_r], w=[g_r])
                cx.op(DVE, lambda e: e.tensor_tensor(out=g_t[:], in0=g_t[:], in1=syT[:, t, :], op=ALU.mult), r=[g_r, syT_r], w=[g_r])
                cx.op(ACT, lambda e: e.activation(out=g_t[:], in_=g_t[:], func=AF.Sigmoid, scale=K2), r=[g_r], w=[g_r])
                cx.op(DVE, lambda e: e.tensor_tensor(out=gl[:, t, :], in0=g_t[:], in1=syT[:, t, :], op=ALU.mult), r=[g_r, syT_r], w=[gl_r])
                cx.op(POOL, lambda e: e.tensor_copy(out=glb[:, t, :], in_=gl[:, t, :]), r=[gl_r], w=[glb_r])
            slot, slot_r, u = self.stream_get()
            assert u is self.U["oglu"]
            gv = slot[:, 0:2048].rearrange("p (o k w) -> p o k w", o=4, k=4)
            for ot in range(4):
                bk, bk_r = self.bank()
                for kt in range(4):
                    cx.op(PE, lambda e: e.matmul(bk[:], lhsT=gv[:, ot, kt, :], rhs=glb[:, kt, :], start=(kt == 0), stop=(kt == 3)), r=[slot_r, glb_r], w=[bk_r])
                g_t, g_r = gt[ot % 2]
                cx.op(ACT, lambda e: e.activation(out=g_t[:], in_=bk[:], func=AF.Sigmoid), r=[bk_r], w=[g_r])
                cx.op(DVE, lambda e: e.tensor_tensor(out=mixT[:, 4 + ot, :], in0=g_t[:], in1=gl[:, ot, :], op=ALU.mult), r=[g_r, gl_r], w=[mixT_r])
            for i in range(4):
                slot, slot_r, u = self.stream_get()
                assert u is self.U[("oout", i)]
                for gi in range(2):
                    ft = 2 * i + gi
                    wv = slot[:, gi * 1024:(gi + 1) * 1024].rearrange("p (k w) -> p k w", k=8)
                    bk, bk_r = self.bank()
                    for kt in range(8):
                        cx.op(PE, lambda e: e.matmul(bk[:], lhsT=wv[:, kt, :], rhs=mixT[:, kt, :], start=(kt == 0), stop=(kt == 7)), r=[slot_r, mixT_r], w=[bk_r])
                    cx.op(DVE, lambda e: e.tensor_tensor(out=self.xT[:, ft, :], in0=self.xT[:, ft, :], in1=bk[:], op=ALU.add), r=[bk_r, self.xT_r], w=[self.xT_r])
            cx.barrier()


    def final(self, b):
        cx = self.cx
        with ExitStack() as st:
            oT, oT_r = self.sb("oT", [128, 8, NB], F32, st)
            self.rmsnorm(4, oT, oT_r)
            self.store_block(b, oT, oT_r)

    def build(self):
        nc, cx = self.nc, self.cx
        self.declare_io()
        self.declare_units()
        self.wscr = nc.dram_tensor("wscr", [self.unit_off], BF16, kind="ExternalOutput" if "dumpw" in SKIP else "Internal").ap()
        self.wscr_res = Res("wscr")
        self.dbg_res = {}
        self.dbg_out = {}
        for nm in self.dbg:
            if nm == "s5setup":
                continue
            self.dbg_out[nm] = nc.dram_tensor("dbg_" + nm, [self.ntok, 1024], F32, kind="ExternalOutput").ap()
        self.y_res = Res("y")
        if "nodry" in SKIP:
            self.pro_finish = True
            for _ in self.prologue():
                pass
        if "nodry" not in SKIP:
            cx.dry = True
            self.setup()
            if "even" in self.stages:
                self.setup_even()
            if "odd" in self.stages:
                self.setup_odd()
            cx.dry = False
        pro = self.prologue() if "nodry" not in SKIP else iter(())
        next(pro, None)
        self.pro_hook = (lambda: next(pro, None)) if "hook" in SKIP else None
        self.setup()
        if "even" in self.stages:
            self.setup_even()
        if "odd" in self.stages:
            self.setup_odd()
        cx.hook = None
        self.pro_finish = True
        for _ in pro:
            pass
        cx.barrier(engines=cx.compute + [cx.SP], with_dma=True)
        self.plan_stream()
        for b in range(self.nblk):
            self.load_block(b)
            if "even" in self.stages:
                self.even(b)
            if "ffn0" in self.stages:
                self.ffn(0, b)
            if "odd" in self.stages:
                self.odd(b)
            if "ffn1" in self.stages:
                self.ffn(1, b)
            if "final" in self.stages:
                self.final(b)
            else:
                self.store_block(b, self.xT, self.xT_r)
        cx.barrier(engines=cx.compute + [cx.SP], with_dma=True)
        return nc


INPUT_NAMES = ["mix_norm", "ffn_norm", "final_norm", "ffn_w_up", "ffn_dw_w", "ffn_dw_b", "ffn_w_down",
               "e_w_in", "e_conv_w", "e_conv_b", "e_dt_bias", "e_a_log", "e_d", "e_ssm_norm", "e_w_out",
               "o_w_in", "o_dw_w", "o_dw_b", "o_ln_g", "o_ln_b", "o_a_re", "o_a_im", "o_b_re", "o_b_im", "o_c_re", "o_c_im",
               "o_d", "o_log_step", "o_glu_w", "o_w_out"]


def kernel(**inputs):
    bld = Builder()
    nc = bld.build()
    consts = make_consts()
    in_maps = []
    NCORE = 4
    for core in range(NCORE):
        bidx = core % 4
        m = {"x": np.ascontiguousarray(inputs["x"][bidx])}
        for k in INPUT_NAMES:
            m[k] = np.ascontiguousarray(inputs[k])
        for k, v in consts.items():
            m["c_" + k] = v
        in_maps.append(m)
    res = run_bass_kernel_spmd(nc, in_maps, core_ids=list(range(NCORE)))
    out = np.stack([res.results[i]["y"] for i in range(4)], axis=0)
    return out.astype(np.float32)
```

```python
import math
from contextlib import ExitStack
import numpy as np
import concourse.bass as bass
import concourse.mybir as mybir
from concourse.bass_utils import run_bass_kernel_spmd

F32 = mybir.dt.float32
BF16 = mybir.dt.bfloat16
ALU = mybir.AluOpType
AF = mybir.ActivationFunctionType

D = 1024
SEQ = 4096
NB = 512
NCH = NB // 128
DFF = 2816
EPS = 1e-6
SLOT = 4096
NSLOT = 4
import os
SKIP = set(os.environ.get('MK_SKIP', '').split(','))
SSD_STOP = int(os.environ.get('MK_SSD_STOP', '99'))


class Sem:
    def __init__(self, h, name):
        self.h = h
        self.name = name
        self.count = 0


class Res:
    __slots__ = ("name", "w", "r", "dsem", "excl")

    def __init__(self, name, excl=False):
        self.name = name
        self.excl = excl
        self.w = {}
        self.r = {}
        self.dsem = None


class Eng:
    def __init__(self, name, eng, sem):
        self.name = name
        self.eng = eng
        self.sem = sem
        self.known = {}


class Ctx:
    def __init__(self, nc, stack):
        self.nc = nc
        self.stack = stack
        self.nsem = 0
        self.PE = self._eng("pe", nc.tensor)
        self.ACT = self._eng("act", nc.scalar)
        self.DVE = self._eng("dve", nc.vector)
        self.POOL = self._eng("pool", nc.gpsimd)
        self.SP = self._eng("sp", nc.sync)
        self.compute = [self.PE, self.ACT, self.DVE, self.POOL]
        self.dsems = []

    def new_sem(self, name):
        self.nsem += 1
        h = self.stack.enter_context(self.nc.semaphore(name))
        return Sem(h, name)

    def _eng(self, name, eng):
        return Eng(name, eng, self.new_sem("s_" + name))

    def _wait(self, E, need):
        for s, (v, snap) in need.items():
            if E.known.get(s, 0) >= v:
                continue
            E.eng.wait_ge(s.h, v)
            E.known[s] = v
            for s2, v2 in snap.items():
                if E.known.get(s2, 0) < v2:
                    E.known[s2] = v2

    def _collect(self, E, reads, writes):
        need = {}

        def req(s, ev):
            if s not in need or need[s][0] < ev[0]:
                need[s] = ev

        for r in reads:
            for s, ev in r.w.items():
                req(s, ev)
            if r.excl:
                for s, ev in r.r.items():
                    if s is not E.sem:
                        req(s, ev)
        for w in writes:
            for s, ev in w.w.items():
                if s is E.sem:
                    continue
                req(s, ev)
            for s, ev in w.r.items():
                if s is E.sem:
                    continue
                req(s, ev)
        return need

    hook = None
    hook_n = 0

    def _tick(self):
        if self.hook is not None:
            self.hook_n += 1
            if self.hook_n % 6 == 0:
                h, self.hook = self.hook, None
                h()
                if self.hook is None:
                    self.hook = h

    dry = False

    def op(self, E, fn, r=(), w=()):
        if self.dry:
            return None
        self._tick()
        self._wait(E, self._collect(E, r, w))
        ins = fn(E.eng)
        E.sem.count += 1
        ins.then_inc(E.sem.h, 1)
        ev = (E.sem.count, dict(E.known))
        for x in r:
            x.r[E.sem] = ev
        for x in w:
            x.w[E.sem] = ev
            x.r = {}
        return ins

    def dma(self, E, out, in_, r=(), w=(), sem_res=None, **kw):
        if self.dry:
            return None
        self._tick()
        self._wait(E, self._collect(E, r, w))
        if sem_res.dsem is None:
            sem_res.dsem = self.new_sem("d_" + sem_res.name)
            self.dsems.append(sem_res.dsem)
        ds = sem_res.dsem
        ins = E.eng.dma_start(out=out, in_=in_, **kw)
        ds.count += 16
        ins.then_inc(ds.h, 16)
        ev = (ds.count, dict(E.known))
        for x in r:
            x.r[ds] = ev
        for x in w:
            x.w[ds] = ev
            x.r = {}
        return ins

    def barrier(self, engines=None, with_dma=False):
        if self.dry:
            return
        engines = engines or self.compute
        for E in engines:
            need = {}
            for E2 in engines:
                if E2 is not E and E2.sem.count > 0:
                    need[E2.sem] = (E2.sem.count, dict(E2.known))
            if with_dma:
                for ds in self.dsems:
                    if ds.count > 0:
                        need[ds] = (ds.count, {})
            self._wait(E, need)


def make_consts():
    c = {}
    c["ident_f"] = np.eye(128, dtype=np.float32)
    c["ones_f"] = np.ones((128, 128), dtype=np.float32)
    inv = (10000.0 ** (-np.arange(0, 128, 2, dtype=np.float32) / np.float32(128))).astype(np.float32)
    pos = np.arange(SEQ, dtype=np.float32)
    ang = (pos[:, None] * inv[None, :]).astype(np.float32).astype(np.float64)
    c["cosT"] = np.ascontiguousarray(np.cos(ang).reshape(SEQ // 128, 128, 64).transpose(1, 0, 2)).astype(np.float32)
    c["sinT"] = np.ascontiguousarray(np.sin(ang).reshape(SEQ // 128, 128, 64).transpose(1, 0, 2)).astype(np.float32)
    gam = 1.0 - 2.0 ** (-5.0 - np.arange(4, dtype=np.float64))
    idx = np.arange(128, dtype=np.float64)
    diff = idx[None, :] - idx[:, None]
    dm = np.where(diff[None] >= 0, gam[:, None, None] ** np.maximum(diff, 0)[None], 0.0) * (128 ** -0.5)
    c["dmaskT"] = np.ascontiguousarray(dm.transpose(1, 0, 2)).reshape(128, 512).astype(np.float32)
    c["xi"] = (gam[None, :] ** (idx[:, None] + 1)).astype(np.float32)
    c["zeta"] = ((gam[None, :] ** (127 - idx[:, None])) * (128 ** -0.5)).astype(np.float32)
    tri = (idx[:, None] <= idx[None, :]).astype(np.float32)
    c["tri"] = tri
    c["su"] = (1.0 - tri).astype(np.float32)
    sj = np.arange(128) // 16
    c["tzmask"] = (sj[None, :] >= sj[:, None]).astype(np.float32)
    c["cidx"] = np.tile(np.arange(1, 65, dtype=np.float32)[None, :], (128, 1))
    return c


RET_GDEC = [float((1.0 - 2.0 ** (-5.0 - h)) ** 128) for h in range(4)]


class Builder:
    def __init__(self, nblk=SEQ // NB, stages=("even", "ffn0", "odd", "ffn1", "final"), dbg=()):
        self.nblk = nblk
        self.stages = stages
        self.dbg = dbg
        self.ntok = nblk * NB
        self.nc = bass.Bass("TRN2", target_bir_lowering=False)
        self.stack = ExitStack()
        self.cx = Ctx(self.nc, self.stack)
        self.units = []
        self.unit_off = 0
        self.stream_order = []
        self.bank_i = 0
        self.bankb_i = 0

    def din(self, name, shape, dt=F32):
        return self.nc.dram_tensor(name, list(shape), dt, kind="ExternalInput").ap()

    def declare_io(self):
        nc = self.nc
        self.x = self.din("x", [self.ntok, D])
        self.y = nc.dram_tensor("y", [self.ntok, D], F32, kind="ExternalOutput").ap()
        self.mix_norm = self.din("mix_norm", [2, D])
        self.ffn_norm = self.din("ffn_norm", [2, D])
        self.final_norm = self.din("final_norm", [D])
        self.ffn_w_up = self.din("ffn_w_up", [2, D, 2 * DFF])
        self.ffn_dw_w = self.din("ffn_dw_w", [2, 3, 2 * DFF])
        self.ffn_dw_b = self.din("ffn_dw_b", [2, 2 * DFF])
        self.ffn_w_down = self.din("ffn_w_down", [2, DFF, D])
        self.e_w_in = self.din("e_w_in", [1, D, 5648])
        self.e_conv_w = self.din("e_conv_w", [1, 4, 1536])
        self.e_conv_b = self.din("e_conv_b", [1, 1536])
        self.e_dt_bias = self.din("e_dt_bias", [1, 16])
        self.e_a_log = self.din("e_a_log", [1, 16])
        self.e_d = self.din("e_d", [1, 16])
        self.e_ssm_norm = self.din("e_ssm_norm", [1, 1024])
        self.e_w_out = self.din("e_w_out", [1, 2048, 1024])
        self.o_w_in = self.din("o_w_in", [1, D, 1536])
        self.o_dw_w = self.din("o_dw_w", [1, 31, 512])
        self.o_dw_b = self.din("o_dw_b", [1, 512])
        self.o_ln_g = self.din("o_ln_g", [1, 512])
        self.o_ln_b = self.din("o_ln_b", [1, 512])
        self.o_a_re = self.din("o_a_re", [1, 32, 64])
        self.o_a_im = self.din("o_a_im", [1, 32, 64])
        self.o_b_re = self.din("o_b_re", [1, 32, 64, 16])
        self.o_b_im = self.din("o_b_im", [1, 32, 64, 16])
        self.o_c_re = self.din("o_c_re", [1, 32, 16, 64])
        self.o_c_im = self.din("o_c_im", [1, 32, 16, 64])
        self.o_d = self.din("o_d", [1, 512])
        self.o_log_step = self.din("o_log_step", [1, 32])
        self.o_glu_w = self.din("o_glu_w", [1, 512, 512])
        self.o_w_out = self.din("o_w_out", [1, D, D])
        self.consts = {}
        for k, v in make_consts().items():
            self.consts[k] = self.din("c_" + k, v.shape)

    def sb(self, name, shape, dt=F32, stack=None):
        if stack is None:
            if not hasattr(self, "persist"):
                self.persist = {}
            if name not in self.persist:
                t = self.stack.enter_context(self.nc.sbuf_tensor(f"{name}_p", list(shape), dt))
                self.persist[name] = (t, Res(name))
            return self.persist[name]
        self.name_i = getattr(self, "name_i", 0) + 1
        t = stack.enter_context(self.nc.sbuf_tensor(f"{name}_{self.name_i}", list(shape), dt))
        return t, Res(name)

    def ps(self, name, shape, dt=F32):
        if not hasattr(self, "persist_ps"):
            self.persist_ps = {}
        if name not in self.persist_ps:
            t = self.stack.enter_context(self.nc.psum_tensor(name, list(shape), dt))
            self.persist_ps[name] = (t, Res(name, excl=True))
        return self.persist_ps[name]

    def bank(self):
        i = self.bank_i
        self.bank_i = (i + 1) % len(self.banks)
        return self.banks[i]

    def bankb(self):
        i = self.bankb_i
        self.bankb_i = (i + 1) % len(self.banksb)
        return self.banksb[i]

    def add_unit(self, name, pieces):
        L = sum(kt * w for (_, kt, _, w) in pieces)
        assert L <= SLOT, (name, L)
        u = dict(name=name, L=L, off=self.unit_off, pieces=pieces)
        self.unit_off += 128 * L
        self.units.append(u)
        return u

    def declare_units(self):
        self.U = {}
        for l in range(2):
            wup = self.ffn_w_up[l]
            for j in range(22):
                self.U[("up", l, j)] = self.add_unit(f"up{l}_{j}", [(wup, 8, j * 128, 128), (wup, 8, (22 + j) * 128, 128)])
            wdn = self.ffn_w_down[l]
            for ft in range(8):
                self.U[("dn", l, ft)] = self.add_unit(f"dn{l}_{ft}", [(wdn, 22, ft * 128, 128)])

        w = self.e_w_in[0]
        for i, nm in enumerate(["eq", "ek", "ev0", "ev1", "eg0", "eg1", "ez0", "ez1"]):
            self.U[nm] = self.add_unit(nm, [(w, 8, 512 * i, 512)])
        self.U["edt"] = self.add_unit("edt", [(w, 8, 5632, 16)])
        for i in range(6):
            self.U[("exbc", i)] = self.add_unit(f"exbc{i}", [(w, 8, 4096 + (2 * i) * 128, 128), (w, 8, 4096 + (2 * i + 1) * 128, 128)])
        for ft in range(8):
            self.U[("eout", ft)] = self.add_unit(f"eout{ft}", [(self.e_w_out[0], 16, ft * 128, 128)])

        w = self.o_w_in[0]
        for i in range(4):
            self.U[("oin", i)] = self.add_unit(f"oin{i}", [(w, 8, i * 128, 128), (w, 8, 512 + i * 128, 128)])
        self.U["ou"] = self.add_unit("ou", [(w, 8, 1024, 512)])
        self.U["oglu"] = self.add_unit("oglu", [(self.o_glu_w[0], 4, ot * 128, 128) for ot in range(4)])
        for i in range(4):
            self.U[("oout", i)] = self.add_unit(f"oout{i}", [(self.o_w_out[0], 8, (2 * i) * 128, 128), (self.o_w_out[0], 8, (2 * i + 1) * 128, 128)])
        self.n_cast_units = len(self.units)
        for nm in [("s5bc", 0), ("s5bc", 1), ("s5cc", 0), ("s5cc", 1), "s5tz"] + [("ocv", i) for i in range(4)]:
            self.U[nm] = dict(name=str(nm), L=4096, off=self.unit_off, pieces=[])
            self.unit_off += 128 * 4096

    def plan_stream(self):
        order = []
        for b in range(self.nblk):
            if "even" in self.stages:
                order += [self.U[("exbc", i)] for i in range(6)]
                order += [self.U[nm] for nm in ["eq", "ek", "ev0", "ev1", "eg0", "eg1", "ez0", "ez1", "edt"]]
                order += [self.U[("eout", ft)] for ft in range(8)]
            for l in range(2):
                if l == 1 and "odd" in self.stages:
                    order += [self.U["ou"], self.U[("s5bc", 0)], self.U[("s5bc", 1)]]
                    for i in range(4):
                        order += [self.U[("oin", i)], self.U[("ocv", i)]]
                    order += [self.U["s5tz"], self.U[("s5cc", 0)], self.U[("s5cc", 1)], self.U["oglu"]]
                    order += [self.U[("oout", i)] for i in range(4)]
                if f"ffn{l}" in self.stages:
                    order += [self.U[("up", l, j)] for j in range(22)]
                    order += [self.U[("dn", l, ft)] for ft in range(8)]
        self.stream_order = order
        self.stream_next_load = 0
        self.stream_next_use = 0

    def stream_get(self, hold_prev=0):
        cx = self.cx
        i = self.stream_next_use
        self.stream_next_use += 1
        lim = min(i - hold_prev + NSLOT - 1, len(self.stream_order) - 1)
        while self.stream_next_load <= lim:
            k = self.stream_next_load
            u = self.stream_order[k]
            st, sr = self.slots[k % NSLOT]
            src = self.wscr[u["off"]: u["off"] + 128 * u["L"]].rearrange("(p l) -> p l", p=128)
            cx.dma(cx.SP, st[:, 0:u["L"]], src, w=[sr], sem_res=sr)
            self.stream_next_load += 1
        st, sr = self.slots[i % NSLOT]
        assert self.stream_order[i] is not None
        return st, sr, self.stream_order[i]

    def prologue_swdge(self):
        cx = self.cx
        units = self.units[:self.n_cast_units]
        for u in units:
            off = 0
            for (src, kt, c0, w) in u["pieces"]:
                dst = self.wscr[u["off"]: u["off"] + 128 * u["L"]].rearrange("(p l) -> p l", p=128)[:, off: off + kt * w].rearrange("p (k w) -> p k w", k=kt)
                s_ = src.rearrange("(k p) n -> p k n", p=128)[:, :, c0:c0 + w]
                cx.dma(cx.POOL, dst, s_, sem_res=self.wscr_res)
                off += kt * w
        return
        yield

    def prologue(self):
        if "noswdge" not in SKIP:
            yield from self.prologue_swdge()
            return
        cx, nc = self.cx, self.nc
        with ExitStack() as st:
            NS = 3
            stg32 = [self.sb(f"stg32_{i}", [128, SLOT], F32, st) for i in range(NS)]
            stg16 = [self.sb(f"stg16_{i}", [128, SLOT], BF16, st) for i in range(NS)]
            cast_engs = [cx.ACT, cx.DVE]
            units = self.units[:self.n_cast_units]

            def load(i):
                u = units[i]
                t32, r32 = stg32[i % NS]
                off = 0
                for (src, kt, c0, w) in u["pieces"]:
                    dst = t32[:, off: off + kt * w].rearrange("p (k w) -> p k w", k=kt)
                    s_ = src.rearrange("(k p) n -> p k n", p=128)[:, :, c0:c0 + w]
                    cx.dma(cx.SP, dst, s_, w=[r32], sem_res=r32)
                    off += kt * w

            def cast_store(i):
                u = units[i]
                t32, r32 = stg32[i % NS]
                t16, r16 = stg16[i % NS]
                L = u["L"]
                E = cast_engs[i % len(cast_engs)]
                if E is cx.ACT:
                    cx.op(E, lambda e: e.copy(out=t16[:, 0:L], in_=t32[:, 0:L]), r=[r32], w=[r16])
                else:
                    cx.op(E, lambda e: e.tensor_copy(out=t16[:, 0:L], in_=t32[:, 0:L]), r=[r32], w=[r16])
                dst = self.wscr[u["off"]: u["off"] + 128 * L].rearrange("(p l) -> p l", p=128)
                cx.dma(cx.SP, dst, t16[:, 0:L], r=[r16], sem_res=r16)

            n = len(units)
            for i in range(min(NS - 1, n)):
                load(i)
            for i in range(n):
                if i + NS - 1 < n:
                    load(i + NS - 1)
                cast_store(i)
                yield
            while not getattr(self, "pro_finish", False):
                yield
            cx.barrier(engines=cx.compute + [cx.SP], with_dma=True)

    def setup(self):
        cx, nc = self.cx, self.nc
        self.xT, self.xT_r = self.sb("xT", [128, 8, NB], F32)
        self.hnT, self.hnT_r = self.sb("hnT", [128, 8, NB], BF16)
        self.ident_f, self.ident_f_r = self.sb("ident_f", [128, 128], F32)
        self.ones_f, self.ones_f_r = self.sb("ones_f", [128, 128], F32)
        self.gains, self.gains_r = self.sb("gains", [128, 5, 8], F32)
        self.slots = [self.sb(f"wslot{i}", [128, SLOT], BF16) for i in range(NSLOT)]
        self.banks = [self.ps(f"bank{i}", [128, 512], F32) for i in range(6)]
        self.banksb = [self.ps(f"bankb{i}", [128, 1024], BF16) for i in range(2)]
        self.ffn_halo, self.ffn_halo_r = self.sb("ffn_halo", [128, 2, 44, 2], F32)
        self.ffn_cw, self.ffn_cw_r = self.sb("ffn_cw", [128, 2, 44, 3], F32)
        self.ffn_cb, self.ffn_cb_r = self.sb("ffn_cb", [128, 2, 44], F32)
        self.eps_t, self.eps_r = self.sb("eps_t", [128, 1], F32)
        nc.allow_non_contiguous_dma(reason="small parameter loads")
        SP = cx.SP
        cx.dma(SP, self.ident_f[:], self.consts["ident_f"], w=[self.ident_f_r], sem_res=self.ident_f_r)
        cx.dma(SP, self.ones_f[:], self.consts["ones_f"], w=[self.ones_f_r], sem_res=self.ones_f_r)
        gsrc = [self.mix_norm[0], self.ffn_norm[0], self.mix_norm[1], self.ffn_norm[1], self.final_norm]
        for i, g in enumerate(gsrc):
            cx.dma(SP, self.gains[:, i, :], g.rearrange("(t p) -> p t", p=128), w=[self.gains_r], sem_res=self.gains_r, allow_slow_non_contiguous=True)
        for l in range(2):
            for k in range(3):
                cx.dma(SP, self.ffn_cw[:, l, :, k], self.ffn_dw_w[l, k].rearrange("(t p) -> p t", p=128),
                       w=[self.ffn_cw_r], sem_res=self.ffn_cw_r, allow_slow_non_contiguous=True)
            cx.dma(SP, self.ffn_cb[:, l, :], self.ffn_dw_b[l].rearrange("(t p) -> p t", p=128),
                   w=[self.ffn_cb_r], sem_res=self.ffn_cb_r, allow_slow_non_contiguous=True)
        cx.op(cx.DVE, lambda e: e.memset(self.ffn_halo[:], 0.0), w=[self.ffn_halo_r])
        cx.op(cx.DVE, lambda e: e.memset(self.eps_t[:], EPS), w=[self.eps_r])
        self.ones_b, self.ones_b_r = self.sb("ones_b", [128, 128], BF16)
        cx.op(cx.DVE, lambda e: e.memset(self.ones_b[:], 1.0), w=[self.ones_b_r])
        self.one_t, _ = self.sb("one_t", [128, 1], F32)
        cx.op(cx.DVE, lambda e: e.memset(self.one_t[:], 1.0), w=[self.eps_r])

    def load_block(self, b):
        cx = self.cx
        with ExitStack() as st:
            xin, xin_r = self.sb("xin", [128, NCH, D], F32, st)
            src = self.x[b * NB:(b + 1) * NB, :].rearrange("(c p) d -> p c d", p=128)
            cx.dma(cx.SP, xin[:], src, w=[xin_r], sem_res=self.xT_r)
            for ft in range(8):
                bk, bk_r = self.bank()
                for c in range(NCH):
                    cx.op(cx.PE, lambda e: e.transpose(bk[:, c * 128:(c + 1) * 128], xin[:, c, ft * 128:(ft + 1) * 128], self.ident_f[:]),
                          r=[xin_r, self.ident_f_r], w=[bk_r])
                if ft % 2 == 0:
                    cx.op(cx.ACT, lambda e: e.copy(out=self.xT[:, ft, :], in_=bk[:]), r=[bk_r], w=[self.xT_r])
                else:
                    cx.op(cx.DVE, lambda e: e.tensor_copy(out=self.xT[:, ft, :], in_=bk[:]), r=[bk_r], w=[self.xT_r])
            cx.barrier()

    def store_block(self, b, srcT, srcT_r):
        cx = self.cx
        with ExitStack() as st:
            yo, yo_r = self.sb("yo", [128, NCH, D], F32, st)
            k = 0
            for c in range(NCH):
                for half in range(2):
                    bk, bk_r = self.bank()
                    for f4 in range(4):
                        ft = half * 4 + f4
                        cx.op(cx.PE, lambda e: e.transpose(bk[:, f4 * 128:(f4 + 1) * 128], srcT[:, ft, c * 128:(c + 1) * 128], self.ident_f[:]),
                              r=[srcT_r, self.ident_f_r], w=[bk_r])
                    if k % 2 == 0:
                        cx.op(cx.ACT, lambda e: e.copy(out=yo[:, c, half * 512:(half + 1) * 512], in_=bk[:]), r=[bk_r], w=[yo_r])
                    else:
                        cx.op(cx.DVE, lambda e: e.tensor_copy(out=yo[:, c, half * 512:(half + 1) * 512], in_=bk[:]), r=[bk_r], w=[yo_r])
                    k += 1
            dst = self.y[b * NB:(b + 1) * NB, :].rearrange("(c p) d -> p c d", p=128)
            cx.dma(cx.SP, dst, yo[:], r=[yo_r], w=[self.y_res], sem_res=self.y_res)
            cx.barrier(engines=cx.compute + [cx.SP], with_dma=True)

    def rmsnorm(self, gi, outT, outT_r, out_f32=False):
        cx = self.cx
        with ExitStack() as st:
            sq = [self.sb(f"sq{i}", [128, NB], BF16, st) for i in range(4)]
            rstd, rstd_r = self.sb("rstd", [128, NB], F32, st)
            bk, bk_r = self.bank()
            for ft in range(8):
                s, s_r = sq[ft % 4]
                if ft % 2 == 0:
                    cx.op(cx.ACT, lambda e: e.activation(out=s[:], in_=self.xT[:, ft, :], func=AF.Square), r=[self.xT_r], w=[s_r])
                else:
                    cx.op(cx.DVE, lambda e: e.tensor_tensor(out=s[:], in0=self.xT[:, ft, :], in1=self.xT[:, ft, :], op=ALU.mult), r=[self.xT_r], w=[s_r])
                cx.op(cx.PE, lambda e: e.matmul(bk[:], lhsT=self.ones_b[:], rhs=s[:], start=(ft == 0), stop=(ft == 7)),
                      r=[s_r, self.ones_b_r], w=[bk_r])
            cx.op(cx.ACT, lambda e: e.activation(out=rstd[:], in_=bk[:], func=AF.Sqrt, bias=self.eps_t[:, 0:1], scale=1.0 / D),
                  r=[bk_r, self.eps_r], w=[rstd_r])
            cx.op(cx.DVE, lambda e: e.reciprocal(out=rstd[:], in_=rstd[:]), r=[rstd_r], w=[rstd_r])
            for ft in range(8):
                E = cx.DVE
                cx.op(E, lambda e: e.scalar_tensor_tensor(out=outT[:, ft, :], in0=self.xT[:, ft, :], scalar=self.gains[:, gi, ft:ft + 1],
                                                          in1=rstd[:], op0=ALU.mult, op1=ALU.mult),
                      r=[self.xT_r, self.gains_r, rstd_r], w=[outT_r])
            cx.barrier()

    def ffn(self, l, b):
        cx = self.cx
        self.rmsnorm(1 + 2 * l, self.hnT, self.hnT_r)
        with ExitStack() as st:
            hT, hT_r = self.sb("hT", [128, 22, NB], BF16, st)
            raws = [self.sb(f"raw{i}", [128, NB + 2], F32, st) for i in range(4)]
            accs = [self.sb(f"acc{i}", [128, NB], F32, st) for i in range(4)]
            sg = [self.sb(f"sgate{i}", [128, NB], F32, st) for i in range(2)]
            for j in range(22):
                slot, slot_r, u = self.stream_get()
                assert u is self.U[("up", l, j)]
                res = []
                for gi in range(2):
                    tile = j + 22 * gi
                    wv = slot[:, gi * 1024:(gi + 1) * 1024].rearrange("p (k w) -> p k w", k=8)
                    bk, bk_r = self.bank()
                    for kt in range(8):
                        cx.op(cx.PE, lambda e: e.matmul(bk[:], lhsT=wv[:, kt, :], rhs=self.hnT[:, kt, :], start=(kt == 0), stop=(kt == 7)),
                              r=[slot_r, self.hnT_r], w=[bk_r])
                    raw, raw_r = raws[(2 * j + gi) % 4]
                    acc, acc_r = accs[(2 * j + gi) % 4]
                    cw = self.ffn_cw[:, l, tile, :]
                    cx.op(cx.ACT, lambda e: e.copy(out=raw[:, 2:NB + 2], in_=bk[:]), r=[bk_r], w=[raw_r])
                    cx.op(cx.POOL, lambda e: e.tensor_copy(out=raw[:, 0:2], in_=self.ffn_halo[:, l, tile, :]), r=[self.ffn_halo_r], w=[raw_r])
                    cx.op(cx.ACT, lambda e: e.activation(out=acc[:], in_=bk[:], func=AF.Identity, bias=self.ffn_cb[:, l, tile:tile + 1],
                                                         scale=cw[:, 2:3]),
                          r=[bk_r, self.ffn_cw_r, self.ffn_cb_r], w=[acc_r])
                    E = cx.DVE
                    cx.op(E, lambda e: e.scalar_tensor_tensor(out=acc[:], in0=raw[:, 1:NB + 1], scalar=cw[:, 1:2], in1=acc[:],
                                                              op0=ALU.mult, op1=ALU.add),
                          r=[raw_r, acc_r, self.ffn_cw_r], w=[acc_r])
                    cx.op(E, lambda e: e.scalar_tensor_tensor(out=acc[:], in0=raw[:, 0:NB], scalar=cw[:, 0:1], in1=acc[:],
                                                              op0=ALU.mult, op1=ALU.add),
                          r=[raw_r, acc_r, self.ffn_cw_r], w=[acc_r])
                    cx.op(cx.POOL, lambda e: e.tensor_copy(out=self.ffn_halo[:, l, tile, :], in_=raw[:, NB:NB + 2]), r=[raw_r], w=[self.ffn_halo_r])
                    res.append((acc, acc_r))
                s, s_r = sg[j % 2]
                cx.op(cx.ACT, lambda e: e.activation(out=s[:], in_=res[0][0][:], func=AF.Silu), r=[res[0][1]], w=[s_r])
                cx.op(cx.DVE, lambda e: e.tensor_tensor(out=hT[:, j, :], in0=s[:], in1=res[1][0][:], op=ALU.mult),
                      r=[s_r, res[1][1]], w=[hT_r])
            for ft in range(8):
                slot, slot_r, u = self.stream_get()
                assert u is self.U[("dn", l, ft)]
                wv = slot[:, 0:22 * 128].rearrange("p (k w) -> p k w", k=22)
                bk, bk_r = self.bank()
                for kt in range(22):
                    cx.op(cx.PE, lambda e: e.matmul(bk[:], lhsT=wv[:, kt, :], rhs=hT[:, kt, :], start=(kt == 0), stop=(kt == 21)),
                          r=[slot_r, hT_r], w=[bk_r])
                cx.op(cx.DVE, lambda e: e.tensor_tensor(out=self.xT[:, ft, :], in0=self.xT[:, ft, :], in1=bk[:], op=ALU.add),
                      r=[bk_r, self.xT_r], w=[self.xT_r])
            cx.barrier()

    def bc_mid(self, ap2, n):
        a = ap2.shape[1]
        return ap2.unsqueeze(2).to_broadcast([128, a, n])

    def bc_h(self, ap2, h):
        n = ap2.shape[1]
        return ap2.unsqueeze(1).to_broadcast([128, h, n])

    def setup_even(self):
        cx = self.cx
        SP = cx.SP
        self.retS, self.retS_r = self.sb("retS", [128, 4, 256], F32)
        self.retSb, self.retSb_r = self.sb("retSb", [128, 4, 256], BF16)
        self.ssS, self.ssS_r = self.sb("ssS", [128, 2, 512], F32)
        self.ssSb, self.ssSb_r = self.sb("ssSb", [128, 2, 512], BF16)
        self.exh, self.exh_r = self.sb("exh", [128, 12, 3], F32)
        self.exw, self.exw_r = self.sb("exw", [128, 12, 4], F32)
        self.exb, self.exb_r = self.sb("exb", [128, 12], F32)
        self.dmaskT, self.dmaskT_r = self.sb("dmaskT", [128, 512], F32)
        self.xi, self.xi_r = self.sb("xi", [128, 4], F32)
        self.zeta, self.zeta_r = self.sb("zeta", [128, 4], F32)
        self.tri, self.tri_r = self.sb("tri", [128, 128], F32)
        self.su, self.su_r = self.sb("su", [128, 128], F32)
        self.ident_b, self.ident_b_r = self.sb("ident_b", [128, 128], BF16)
        self.dtb, self.dtb_r = self.sb("dtb", [128, 16], F32)
        self.aneg, self.aneg_r = self.sb("aneg", [128, 16], F32)
        self.dsk, self.dsk_r = self.sb("dsk", [128, 16], F32)
        self.ssmn, self.ssmn_r = self.sb("ssmn", [128, 8], F32)
        for t, r, nm in [(self.dmaskT, self.dmaskT_r, "dmaskT"), (self.xi, self.xi_r, "xi"), (self.zeta, self.zeta_r, "zeta"),
                         (self.tri, self.tri_r, "tri"), (self.su, self.su_r, "su")]:
            cx.dma(SP, t[:], self.consts[nm], w=[r], sem_res=r)
        cx.dma(SP, self.dtb[:], self.e_dt_bias[0].partition_broadcast(128), w=[self.dtb_r], sem_res=self.dtb_r, allow_slow_non_contiguous=True)
        cx.dma(SP, self.aneg[:], self.e_a_log[0].partition_broadcast(128), w=[self.aneg_r], sem_res=self.aneg_r, allow_slow_non_contiguous=True)
        cx.dma(SP, self.dsk[:], self.e_d[0].partition_broadcast(128), w=[self.dsk_r], sem_res=self.dsk_r, allow_slow_non_contiguous=True)
        cx.dma(SP, self.ssmn[:], self.e_ssm_norm[0].rearrange("(t p) -> p t", p=128), w=[self.ssmn_r], sem_res=self.ssmn_r,
               allow_slow_non_contiguous=True)
        for k in range(4):
            cx.dma(SP, self.exw[:, :, k], self.e_conv_w[0, k].rearrange("(t p) -> p t", p=128), w=[self.exw_r], sem_res=self.exw_r,
                   allow_slow_non_contiguous=True)
        cx.dma(SP, self.exb[:], self.e_conv_b[0].rearrange("(t p) -> p t", p=128), w=[self.exb_r], sem_res=self.exb_r,
               allow_slow_non_contiguous=True)
        cx.op(cx.ACT, lambda e: e.activation(out=self.aneg[:], in_=self.aneg[:], func=AF.Exp), r=[self.aneg_r], w=[self.aneg_r])
        cx.op(cx.DVE, lambda e: e.tensor_scalar(out=self.aneg[:], in0=self.aneg[:], scalar1=-1.0, scalar2=None, op0=ALU.mult),
              r=[self.aneg_r], w=[self.aneg_r])
        cx.op(cx.DVE, lambda e: e.tensor_copy(out=self.ident_b[:], in_=self.ident_f[:]), r=[self.ident_f_r], w=[self.ident_b_r])
        cx.op(cx.DVE, lambda e: e.memset(self.retS[:], 0.0), w=[self.retS_r])
        cx.op(cx.DVE, lambda e: e.memset(self.retSb[:], 0.0), w=[self.retSb_r])
        cx.op(cx.DVE, lambda e: e.memset(self.ssS[:], 0.0), w=[self.ssS_r])
        cx.op(cx.DVE, lambda e: e.memset(self.ssSb[:], 0.0), w=[self.ssSb_r])
        cx.op(cx.POOL, lambda e: e.memset(self.exh[:], 0.0), w=[self.exh_r])

    def dump(self, name, src_ap, src_r, dst_ap):
        cx = self.cx
        r = self.dbg_res.setdefault(name, Res("dbg_" + name))
        cx.dma(cx.SP, dst_ap, src_ap, r=[src_r], w=[r], sem_res=r)

    def even(self, b):
        cx = self.cx
        PE, ACT, DVE, POOL, SP = cx.PE, cx.ACT, cx.DVE, cx.POOL, cx.SP
        self.rmsnorm(0, self.hnT, self.hnT_r)
        hnT, hnT_r = self.hnT, self.hnT_r
        with ExitStack() as st:
            B = {}
            for nm, shp, dt in [("qr", [128, NCH, 512], BF16), ("qx", [128, NCH, 512], BF16), ("kr", [128, NCH, 512], BF16),
                                ("kz", [128, NCH, 512], BF16), ("v", [128, NCH, 1024], BF16), ("sg", [128, NCH, 1024], BF16),
                                ("sz", [128, NCH, 1024], BF16), ("xcT", [128, 8, NB], BF16), ("bcT", [128, 4, NB], BF16),
                                ("dt", [128, NCH, 16], F32), ("dta", [128, NCH, 16], F32), ("mixT", [128, 16, NB], BF16),
                                ("cos", [128, NCH, 64], F32), ("sin", [128, NCH, 64], F32)]:
                B[nm] = self.sb(nm, shp, dt, st)
            self.EB = B
            cos, cos_r = B["cos"]
            sin, sin_r = B["sin"]
            cx.dma(SP, cos[:], self.consts["cosT"][:, b * NCH:(b + 1) * NCH, :], w=[cos_r], sem_res=self.xi_r)
            cx.dma(SP, sin[:], self.consts["sinT"][:, b * NCH:(b + 1) * NCH, :], w=[sin_r], sem_res=self.xi_r)
            rot_t = [self.sb(f"rot{i}", [128, 4, 64], F32, st) for i in range(4)]
            qraw = [self.sb(f"qraw{i}", [128, 512], F32, st) for i in range(2)]
            raws = [self.sb(f"xraw{i}", [128, NB + 3], F32, st) for i in range(2)]
            accs = [self.sb(f"xacc{i}", [128, NB], F32, st) for i in range(2)]
            xcT, xcT_r = B["xcT"]
            bcT, bcT_r = B["bcT"]
            for i in range(6):
                slot, slot_r, u = self.stream_get()
                assert u is self.U[("exbc", i)]
                for gi in range(2):
                    tile = 2 * i + gi
                    wv = slot[:, gi * 1024:(gi + 1) * 1024].rearrange("p (k w) -> p k w", k=8)
                    bk, bk_r = self.bank()
                    for kt in range(8):
                        cx.op(PE, lambda e: e.matmul(bk[:], lhsT=wv[:, kt, :], rhs=hnT[:, kt, :], start=(kt == 0), stop=(kt == 7)),
                              r=[slot_r, hnT_r], w=[bk_r])
                    raw, raw_r = raws[tile % 2]
                    acc, acc_r = accs[tile % 2]
                    cw = self.exw[:, tile, :]
                    cx.op(ACT, lambda e: e.copy(out=raw[:, 3:NB + 3], in_=bk[:]), r=[bk_r], w=[raw_r])
                    cx.op(POOL, lambda e: e.tensor_copy(out=raw[:, 0:3], in_=self.exh[:, tile, :]), r=[self.exh_r], w=[raw_r])
                    cx.op(ACT, lambda e: e.activation(out=acc[:], in_=bk[:], func=AF.Identity, bias=self.exb[:, tile:tile + 1], scale=cw[:, 3:4]),
                          r=[bk_r, self.exw_r, self.exb_r], w=[acc_r])
                    for k in (2, 1, 0):
                        cx.op(DVE, lambda e: e.scalar_tensor_tensor(out=acc[:], in0=raw[:, k:k + NB], scalar=cw[:, k:k + 1], in1=acc[:],
                                                                    op0=ALU.mult, op1=ALU.add),
                              r=[raw_r, acc_r, self.exw_r], w=[acc_r])
                    cx.op(POOL, lambda e: e.tensor_copy(out=self.exh[:, tile, :], in_=raw[:, NB:NB + 3]), r=[raw_r], w=[self.exh_r])
                    if tile < 8:
                        cx.op(ACT, lambda e: e.activation(out=xcT[:, tile, :], in_=acc[:], func=AF.Silu), r=[acc_r], w=[xcT_r])
                    else:
                        cx.op(ACT, lambda e: e.activation(out=bcT[:, tile - 8, :], in_=acc[:], func=AF.Silu), r=[acc_r], w=[bcT_r])
            k_ev = 0
            for nm in ["eq", "ek", "ev0", "ev1", "eg0", "eg1", "ez0", "ez1"]:
                slot, slot_r, u = self.stream_get()
                assert u is self.U[nm]
                wv = slot[:, 0:4096].rearrange("p (k w) -> p k w", k=8)
                for c in range(NCH):
                    bk, bk_r = self.bank()
                    for kt in range(8):
                        cx.op(PE, lambda e: e.matmul(bk[:], lhsT=hnT[:, kt, c * 128:(c + 1) * 128], rhs=wv[:, kt, :], start=(kt == 0), stop=(kt == 7)),
                              r=[slot_r, hnT_r], w=[bk_r])
                    if nm in ("eq", "ek"):
                        raw, raw_r = qraw[k_ev % 2]
                        k_ev += 1
                        cx.op(ACT, lambda e: e.copy(out=raw[:], in_=bk[:]), r=[bk_r], w=[raw_r])
                        r4 = raw[:].rearrange("p (h t d) -> p h t d", h=4, t=2)
                        x1, x2 = r4[:, :, 0, :], r4[:, :, 1, :]
                        cb = self.bc_h(cos[:, c, :], 4)
                        sbb = self.bc_h(sin[:, c, :], 4)
                        (t1, t1r), (t2, t2r), (t3, t3r), (t4, t4r) = rot_t
                        cx.op(POOL, lambda e: e.tensor_tensor(out=t1[:], in0=x1, in1=cb, op=ALU.mult), r=[raw_r, cos_r], w=[t1r])
                        cx.op(POOL, lambda e: e.tensor_tensor(out=t2[:], in0=x2, in1=sbb, op=ALU.mult), r=[raw_r, sin_r], w=[t2r])
                        cx.op(POOL, lambda e: e.tensor_tensor(out=t3[:], in0=x1, in1=sbb, op=ALU.mult), r=[raw_r, sin_r], w=[t3r])
                        cx.op(POOL, lambda e: e.tensor_tensor(out=t4[:], in0=x2, in1=cb, op=ALU.mult), r=[raw_r, cos_r], w=[t4r])
                        dst, dst_r = B["qr"] if nm == "eq" else B["kr"]
                        d4 = dst[:, c, :].rearrange("p (h t d) -> p h t d", h=4, t=2)
                        cx.op(DVE, lambda e: e.tensor_tensor(out=d4[:, :, 0, :], in0=t1[:], in1=t2[:], op=ALU.subtract), r=[t1r, t2r], w=[dst_r])
                        cx.op(DVE, lambda e: e.tensor_tensor(out=d4[:, :, 1, :], in0=t3[:], in1=t4[:], op=ALU.add), r=[t3r, t4r], w=[dst_r])
                        d2, d2_r = B["qx"] if nm == "eq" else B["kz"]
                        tab, tab_r = (self.xi, self.xi_r) if nm == "eq" else (self.zeta, self.zeta_r)
                        cx.op(DVE, lambda e: e.tensor_tensor(out=d2[:, c, :].rearrange("p (h d) -> p h d", h=4),
                                                             in0=dst[:, c, :].rearrange("p (h d) -> p h d", h=4),
                                                             in1=self.bc_mid(tab[:], 128), op=ALU.mult),
                              r=[dst_r, tab_r], w=[d2_r])
                    else:
                        half = int(nm[-1])
                        key = {"v": "v", "g": "sg", "z": "sz"}[nm[1]]
                        dst, dst_r = B[key]
                        if key == "v":
                            cx.op(ACT, lambda e: e.copy(out=dst[:, c, half * 512:(half + 1) * 512], in_=bk[:]), r=[bk_r], w=[dst_r])
                        else:
                            cx.op(ACT, lambda e: e.activation(out=dst[:, c, half * 512:(half + 1) * 512], in_=bk[:], func=AF.Silu), r=[bk_r], w=[dst_r])
            slot, slot_r, u = self.stream_get()
            assert u is self.U["edt"]
            wv = slot[:, 0:128].rearrange("p (k w) -> p k w", k=8)
            dtv, dtv_r = self.sb("dtv", [128, 16], F32, st)
            dt, dt_r = B["dt"]
            dta, dta_r = B["dta"]
            for c in range(NCH):
                bk, bk_r = self.bank()
                for kt in range(8):
                    cx.op(PE, lambda e: e.matmul(bk[:, 0:16], lhsT=hnT[:, kt, c * 128:(c + 1) * 128], rhs=wv[:, kt, :], start=(kt == 0), stop=(kt == 7)),
                          r=[slot_r, hnT_r], w=[bk_r])
                cx.op(DVE, lambda e: e.tensor_tensor(out=dtv[:], in0=bk[:, 0:16], in1=self.dtb[:], op=ALU.add), r=[bk_r, self.dtb_r], w=[dtv_r])
                cx.op(ACT, lambda e: e.activation(out=dtv[:], in_=dtv[:], func=AF.Exp), r=[dtv_r], w=[dtv_r])
                cx.op(ACT, lambda e: e.activation(out=dt[:, c, :], in_=dtv[:], func=AF.Ln, bias=self.one_t[:, 0:1]), r=[dtv_r, self.eps_r], w=[dt_r])
                cx.op(DVE, lambda e: e.tensor_tensor(out=dta[:, c, :], in0=dt[:, c, :], in1=self.aneg[:], op=ALU.mult), r=[dt_r, self.aneg_r], w=[dta_r])
            with ExitStack() as st2:
                T = {}
                for nm, shp, dtp in [("qT", [128, 512], BF16), ("qxT", [128, 512], BF16), ("kT", [128, 512], BF16), ("smT", [128, 512], BF16),
                                     ("st6", [128, 4, 6], F32), ("mv", [128, 4, 2], F32), ("rstd4", [128, 4], F32),
                                     ("ytmp", [128, 1024], F32), ("yretb", [128, 1024], BF16),
                                     ("rhsM0", [128, 512], F32), ("rhsM1", [128, 512], F32), ("Lg0", [128, 512], F32), ("Lg1", [128, 512], F32),
                                     ("cbm", [128, 2, 128], F32), ("MT", [128, 16, 128], BF16), ("dec", [128, 16], F32),
                                     ("xs", [128, 1024], BF16), ("Xdt", [128, 1024], BF16), ("Xdec", [128, 1024], BF16), ("Btok", [128, 256], BF16),
                                     ("ea", [128, 32], F32), ("ty", [128, 1024], F32), ("t2", [128, 1024], F32), ("yzb", [128, 1024], BF16),
                                     ("st6g", [128, 2, 6], F32), ("mvg", [128, 2, 2], F32), ("rs2", [128, 2], F32)]:
                    T[nm] = self.sb(nm, shp, dtp, st2)
                for c in range(NCH):
                    if "chunk" not in SKIP:
                        self.even_chunk(b, c, B, T)
                cx.barrier()
            mixT, mixT_r = B["mixT"]
            for ft in range(8):
                slot, slot_r, u = self.stream_get()
                assert u is self.U[("eout", ft)]
                wv = slot[:, 0:2048].rearrange("p (k w) -> p k w", k=16)
                bk, bk_r = self.bank()
                for kt in range(16):
                    cx.op(PE, lambda e: e.matmul(bk[:], lhsT=wv[:, kt, :], rhs=mixT[:, kt, :], start=(kt == 0), stop=(kt == 15)),
                          r=[slot_r, mixT_r], w=[bk_r])
                cx.op(DVE, lambda e: e.tensor_tensor(out=self.xT[:, ft, :], in0=self.xT[:, ft, :], in1=bk[:], op=ALU.add),
                      r=[bk_r, self.xT_r], w=[self.xT_r])
            cx.barrier()

    def even_chunk(self, b, c, B, T):
        gens = []
        if "ret" not in SKIP:
            gens.append(self.even_chunk_ret(b, c, B, T))
        if "ssd" not in SKIP:
            gens.append(self.even_chunk_ssd(b, c, B, T))
        while gens:
            for g in list(gens):
                try:
                    next(g)
                except StopIteration:
                    gens.remove(g)

    def rbank(self):
        self.rbank_i = (getattr(self, "rbank_i", 0) + 1) % 3
        return self.banks[self.rbank_i]

    def sbank(self):
        self.sbank_i = (getattr(self, "sbank_i", 0) + 1) % 3
        return self.banks[3 + self.sbank_i]

    def even_chunk_ret(self, b, c, B, T):
        cx = self.cx
        PE, ACT, DVE, POOL, SP = cx.PE, cx.ACT, cx.DVE, cx.POOL, cx.SP
        cs = slice(c * 128, (c + 1) * 128)
        ib, ib_r = self.ident_b, self.ident_b_r
        for i, (src, dstn) in enumerate([("qr", "qT"), ("qx", "qxT"), ("kr", "kT")]):
            s_t, s_r = B[src]
            d_t, d_r = T[dstn]
            pb, pb_r = self.bankb()
            for h in range(4):
                cx.op(PE, lambda e: e.transpose(pb[:, h * 128:(h + 1) * 128], s_t[:, c, h * 128:(h + 1) * 128], ib[:]), r=[s_r, ib_r], w=[pb_r])
            if i % 2 == 0:
                cx.op(ACT, lambda e: e.copy(out=d_t[:], in_=pb[:, 0:512]), r=[pb_r], w=[d_r])
            else:
                cx.op(DVE, lambda e: e.tensor_copy(out=d_t[:], in_=pb[:, 0:512]), r=[pb_r], w=[d_r])
            yield
        qT, qT_r = T["qT"]
        qxT, qxT_r = T["qxT"]
        kT, kT_r = T["kT"]
        smT, smT_r = T["smT"]
        v, v_r = B["v"]
        bk, bk_r = self.rbank()
        for h in range(4):
            hs = slice(h * 128, (h + 1) * 128)
            cx.op(PE, lambda e: e.matmul(bk[:, hs], lhsT=kT[:, hs], rhs=qT[:, hs], start=True, stop=True), r=[kT_r, qT_r], w=[bk_r])
        yield
        cx.op(DVE, lambda e: e.tensor_tensor(out=smT[:], in0=bk[:], in1=self.dmaskT[:], op=ALU.mult), r=[bk_r, self.dmaskT_r], w=[smT_r])
        yield
        bo = [self.rbank(), self.rbank()]
        for h in range(4):
            hs = slice(h * 128, (h + 1) * 128)
            o_t, o_r = bo[h // 2]
            osl = slice((h % 2) * 256, (h % 2 + 1) * 256)
            cx.op(PE, lambda e: e.matmul(o_t[:, osl], lhsT=smT[:, hs], rhs=v[:, c, h * 256:(h + 1) * 256], start=True, stop=False),
                  r=[smT_r, v_r], w=[o_r])
            cx.op(PE, lambda e: e.matmul(o_t[:, osl], lhsT=qxT[:, hs], rhs=self.retSb[:, h, :], start=False, stop=True),
                  r=[qxT_r, self.retSb_r], w=[o_r])
        yield
        st6, st6_r = T["st6"]
        mv, mv_r = T["mv"]
        rstd4, rstd4_r = T["rstd4"]
        ytmp, ytmp_r = T["ytmp"]
        yretb, yretb_r = T["yretb"]
        sg, sg_r = B["sg"]
        for h in range(4):
            o_t, o_r = bo[h // 2]
            osl = slice((h % 2) * 256, (h % 2 + 1) * 256)
            cx.op(DVE, lambda e: e.bn_stats(out=st6[:, h, :], in_=o_t[:, osl]), r=[o_r], w=[st6_r])
        yield
        for h in range(4):
            cx.op(DVE, lambda e: e.bn_aggr(out=mv[:, h, :], in_=st6[:, h, :]), r=[st6_r], w=[mv_r])
        yield
        cx.op(ACT, lambda e: e.activation(out=rstd4[:], in_=mv[:, :, 1], func=AF.Sqrt, bias=self.eps_t[:, 0:1]), r=[mv_r, self.eps_r], w=[rstd4_r])
        yield
        cx.op(DVE, lambda e: e.reciprocal(out=rstd4[:], in_=rstd4[:]), r=[rstd4_r], w=[rstd4_r])
        yield
        for h in range(4):
            o_t, o_r = bo[h // 2]
            osl = slice((h % 2) * 256, (h % 2 + 1) * 256)
            cx.op(DVE, lambda e: e.tensor_scalar(out=ytmp[:, h * 256:(h + 1) * 256], in0=o_t[:, osl], scalar1=mv[:, h, 0:1], scalar2=rstd4[:, h:h + 1],
                                                 op0=ALU.subtract, op1=ALU.mult),
                  r=[o_r, mv_r, rstd4_r], w=[ytmp_r])
        yield
        cx.op(POOL, lambda e: e.tensor_tensor(out=yretb[:], in0=ytmp[:], in1=sg[:, c, :], op=ALU.mult), r=[ytmp_r, sg_r], w=[yretb_r])
        yield
        mixT, mixT_r = B["mixT"]
        pb, pb_r = self.bankb()
        for t in range(8):
            cx.op(PE, lambda e: e.transpose(pb[:, t * 128:(t + 1) * 128], yretb[:, t * 128:(t + 1) * 128], ib[:]), r=[yretb_r, ib_r], w=[pb_r])
        cx.op(ACT, lambda e: e.copy(out=mixT[:, 0:8, cs], in_=pb[:].rearrange("p (t n) -> p t n", t=8)), r=[pb_r], w=[mixT_r])
        yield
        kz, kz_r = B["kz"]
        bkv = [self.rbank(), self.rbank()]
        for h in range(4):
            k_t, k_r = bkv[h // 2]
            osl = slice((h % 2) * 256, (h % 2 + 1) * 256)
            cx.op(PE, lambda e: e.matmul(k_t[:, osl], lhsT=kz[:, c, h * 128:(h + 1) * 128], rhs=v[:, c, h * 256:(h + 1) * 256], start=True, stop=True),
                  r=[kz_r, v_r], w=[k_r])
        yield
        for h in range(4):
            k_t, k_r = bkv[h // 2]
            osl = slice((h % 2) * 256, (h % 2 + 1) * 256)
            cx.op(DVE, lambda e: e.scalar_tensor_tensor(out=self.retS[:, h, :], in0=self.retS[:, h, :], scalar=RET_GDEC[h], in1=k_t[:, osl],
                                                        op0=ALU.mult, op1=ALU.add),
                  r=[k_r, self.retS_r], w=[self.retS_r])
        yield
        cx.op(ACT, lambda e: e.copy(out=self.retSb[:], in_=self.retS[:]), r=[self.retS_r], w=[self.retSb_r])

    def even_chunk_ssd(self, b, c, B, T):
        cx = self.cx
        PE, ACT, DVE, POOL, SP = cx.PE, cx.ACT, cx.DVE, cx.POOL, cx.SP
        cs = slice(c * 128, (c + 1) * 128)
        ib, ib_r = self.ident_b, self.ident_b_r
        mixT, mixT_r = B["mixT"]
        dt, dt_r = B["dt"]
        dta, dta_r = B["dta"]
        bcT, bcT_r = B["bcT"]
        xcT, xcT_r = B["xcT"]
        sz, sz_r = B["sz"]
        cbm, cbm_r = T["cbm"]
        MT, MT_r = T["MT"]
        dec, dec_r = T["dec"]
        bk, bk_r = self.sbank()
        for g in range(2):
            cx.op(PE, lambda e: e.matmul(bk[:, g * 128:(g + 1) * 128], lhsT=bcT[:, g, cs], rhs=bcT[:, 2 + g, cs], start=True, stop=True),
                  r=[bcT_r], w=[bk_r])
        yield
        cx.op(DVE, lambda e: e.tensor_tensor(out=cbm[:], in0=bk[:, 0:256].rearrange("p (g n) -> p g n", g=2), in1=self.bc_h(self.tri[:], 2), op=ALU.mult),
              r=[bk_r, self.tri_r], w=[cbm_r])
        yield
        for hg in range(4):
            rhsM, rhsM_r = T[f"rhsM{hg % 2}"]
            Lg, Lg_r = T[f"Lg{hg % 2}"]
            g = hg // 2
            cx.op(POOL, lambda e: e.tensor_tensor(out=rhsM[:].rearrange("p (h n) -> p h n", h=4), in0=self.bc_h(self.tri[:], 4),
                                                  in1=self.bc_mid(dta[:, c, hg * 4:(hg + 1) * 4], 128), op=ALU.mult),
                  r=[self.tri_r, dta_r], w=[rhsM_r])
            yield
            bk, bk_r = self.sbank()
            cx.op(PE, lambda e: e.matmul(bk[:], lhsT=self.su[:], rhs=rhsM[:], start=True, stop=True), r=[self.su_r, rhsM_r], w=[bk_r])
            yield
            cx.op(ACT, lambda e: e.activation(out=Lg[:], in_=bk[:], func=AF.Exp), r=[bk_r], w=[Lg_r])
            yield
            L3 = Lg[:].rearrange("p (h n) -> p h n", h=4)
            cx.op(DVE, lambda e: e.tensor_tensor(out=MT[:, hg * 4:(hg + 1) * 4, :], in0=L3, in1=self.bc_h(cbm[:, g, :], 4), op=ALU.mult),
                  r=[Lg_r, cbm_r], w=[MT_r])
            cx.op(POOL, lambda e: e.tensor_copy(out=dec[:, hg * 4:(hg + 1) * 4], in_=L3[:, :, 127]), r=[Lg_r], w=[dec_r])
        yield
        xs, xs_r = T["xs"]
        Xdt, Xdt_r = T["Xdt"]
        Xdec, Xdec_r = T["Xdec"]
        Btok, Btok_r = T["Btok"]
        pb, pb_r = self.bankb()
        for t in range(8):
            cx.op(PE, lambda e: e.transpose(pb[:, t * 128:(t + 1) * 128], xcT[:, t, cs], ib[:]), r=[xcT_r, ib_r], w=[pb_r])
        cx.op(ACT, lambda e: e.copy(out=xs[:], in_=pb[:]), r=[pb_r], w=[xs_r])
        cx.op(DVE, lambda e: e.tensor_tensor(out=Xdt[:].rearrange("p (h d) -> p h d", h=16), in0=pb[:].rearrange("p (h d) -> p h d", h=16),
                                             in1=self.bc_mid(dt[:, c, :], 64), op=ALU.mult),
              r=[pb_r, dt_r], w=[Xdt_r])
        yield
        cx.op(POOL, lambda e: e.tensor_tensor(out=Xdec[:].rearrange("p (h d) -> p h d", h=16), in0=Xdt[:].rearrange("p (h d) -> p h d", h=16),
                                              in1=self.bc_mid(dec[:], 64), op=ALU.mult),
              r=[Xdt_r, dec_r], w=[Xdec_r])
        pb2, pb2_r = self.bankb()
        for g in range(2):
            cx.op(PE, lambda e: e.transpose(pb2[:, g * 128:(g + 1) * 128], bcT[:, g, cs], ib[:]), r=[bcT_r, ib_r], w=[pb2_r])
        cx.op(ACT, lambda e: e.copy(out=Btok[:], in_=pb2[:, 0:256]), r=[pb2_r], w=[Btok_r])
        yield
        ea, ea_r = T["ea"]
        bk, bk_r = self.sbank()
        cx.op(PE, lambda e: e.matmul(bk[:, 0:16], lhsT=self.tri[:], rhs=dta[:, c, :], start=True, stop=True), r=[self.tri_r, dta_r], w=[bk_r])
        cx.op(PE, lambda e: e.matmul(bk[:, 16:32], lhsT=self.ones_f[:], rhs=dta[:, c, :], start=True, stop=True), r=[self.ones_f_r, dta_r], w=[bk_r])
        yield
        cx.op(ACT, lambda e: e.activation(out=ea[:], in_=bk[:, 0:32], func=AF.Exp), r=[bk_r], w=[ea_r])
        yield
        ty, ty_r = T["ty"]
        t2, t2_r = T["t2"]
        for g in range(2):
            gs = slice(g * 512, (g + 1) * 512)
            d_t, d_r = self.sbank()
            for hh in range(8):
                h = g * 8 + hh
                cx.op(PE, lambda e: e.matmul(d_t[:, hh * 64:(hh + 1) * 64], lhsT=MT[:, h, :], rhs=Xdt[:, h * 64:(h + 1) * 64], start=True, stop=True),
                      r=[MT_r, Xdt_r], w=[d_r])
            f_t, f_r = self.sbank()
            cx.op(PE, lambda e: e.matmul(f_t[:], lhsT=bcT[:, 2 + g, cs], rhs=self.ssSb[:, g, :], start=True, stop=True), r=[bcT_r, self.ssSb_r], w=[f_r])
            yield
            cx.op(DVE, lambda e: e.tensor_tensor(out=ty[:, gs].rearrange("p (h d) -> p h d", h=8), in0=f_t[:].rearrange("p (h d) -> p h d", h=8),
                                                 in1=self.bc_mid(ea[:, g * 8:(g + 1) * 8], 64), op=ALU.mult),
                  r=[f_r, ea_r], w=[ty_r])
            cx.op(DVE, lambda e: e.tensor_tensor(out=ty[:, gs], in0=ty[:, gs], in1=d_t[:], op=ALU.add), r=[d_r, ty_r], w=[ty_r])
            yield
        cx.op(POOL, lambda e: e.tensor_tensor(out=t2[:].rearrange("p (h d) -> p h d", h=16), in0=xs[:].rearrange("p (h d) -> p h d", h=16),
                                              in1=self.bc_mid(self.dsk[:], 64), op=ALU.mult),
              r=[xs_r, self.dsk_r], w=[t2_r])
        yield
        cx.op(POOL, lambda e: e.tensor_tensor(out=t2[:], in0=t2[:], in1=ty[:], op=ALU.add), r=[t2_r, ty_r], w=[t2_r])
        yield
        cx.op(POOL, lambda e: e.tensor_tensor(out=t2[:], in0=t2[:], in1=sz[:, c, :], op=ALU.mult), r=[t2_r, sz_r], w=[t2_r])
        yield
        st6g, st6g_r = T["st6g"]
        mvg, mvg_r = T["mvg"]
        rs2, rs2_r = T["rs2"]
        yzb, yzb_r = T["yzb"]
        for g in range(2):
            cx.op(DVE, lambda e: e.bn_stats(out=st6g[:, g, :], in_=t2[:, g * 512:(g + 1) * 512]), r=[t2_r], w=[st6g_r])
        yield
        for g in range(2):
            cx.op(DVE, lambda e: e.bn_aggr(out=mvg[:, g, :], in_=st6g[:, g, :]), r=[st6g_r], w=[mvg_r])
        yield
        cx.op(DVE, lambda e: e.tensor_tensor(out=rs2[:], in0=mvg[:, :, 0], in1=mvg[:, :, 0], op=ALU.mult), r=[mvg_r], w=[rs2_r])
        yield
        cx.op(DVE, lambda e: e.tensor_tensor(out=rs2[:], in0=rs2[:], in1=mvg[:, :, 1], op=ALU.add), r=[mvg_r, rs2_r], w=[rs2_r])
        yield
        cx.op(ACT, lambda e: e.activation(out=rs2[:], in_=rs2[:], func=AF.Sqrt, bias=self.eps_t[:, 0:1]), r=[rs2_r, self.eps_r], w=[rs2_r])
        yield
        cx.op(DVE, lambda e: e.reciprocal(out=rs2[:], in_=rs2[:]), r=[rs2_r], w=[rs2_r])
        yield
        for g in range(2):
            cx.op(ACT, lambda e: e.activation(out=yzb[:, g * 512:(g + 1) * 512], in_=t2[:, g * 512:(g + 1) * 512], func=AF.Copy, scale=rs2[:, g:g + 1]),
                  r=[t2_r, rs2_r], w=[yzb_r])
        yield
        pb, pb_r = self.bankb()
        for t in range(8):
            cx.op(PE, lambda e: e.transpose(pb[:, t * 128:(t + 1) * 128], yzb[:, t * 128:(t + 1) * 128], ib[:]), r=[yzb_r, ib_r], w=[pb_r])
        for t in range(8):
            cx.op(ACT, lambda e: e.activation(out=mixT[:, 8 + t, cs], in_=pb[:, t * 128:(t + 1) * 128], func=AF.Identity, scale=self.ssmn[:, t:t + 1]),
                  r=[pb_r, self.ssmn_r], w=[mixT_r])
        yield
        cdec = ea[:, 16:32]
        for g in range(2):
            s_t, s_r = self.sbank()
            cx.op(PE, lambda e: e.matmul(s_t[:], lhsT=Btok[:, g * 128:(g + 1) * 128], rhs=Xdec[:, g * 512:(g + 1) * 512], start=True, stop=True),
                  r=[Btok_r, Xdec_r], w=[s_r])
            yield
            cx.op(DVE, lambda e: e.tensor_tensor(out=self.ssS[:, g, :].rearrange("p (h d) -> p h d", h=8),
                                                 in0=self.ssS[:, g, :].rearrange("p (h d) -> p h d", h=8),
                                                 in1=self.bc_mid(cdec[:, g * 8:(g + 1) * 8], 64), op=ALU.mult),
                  r=[self.ssS_r, ea_r], w=[self.ssS_r])
            cx.op(DVE, lambda e: e.tensor_tensor(out=self.ssS[:, g, :], in0=self.ssS[:, g, :], in1=s_t[:], op=ALU.add),
                  r=[s_r, self.ssS_r], w=[self.ssS_r])
            yield
        cx.op(ACT, lambda e: e.copy(out=self.ssSb[:], in_=self.ssS[:]), r=[self.ssS_r], w=[self.ssSb_r])


    def setup_odd(self):
        cx = self.cx
        PE, ACT, DVE, POOL, SP = cx.PE, cx.ACT, cx.DVE, cx.POOL, cx.SP
        if "even" not in self.stages:
            self.ident_b, self.ident_b_r = self.sb("ident_b", [128, 128], BF16)
            cx.op(DVE, lambda e: e.tensor_copy(out=self.ident_b[:], in_=self.ident_f[:]), r=[self.ident_f_r], w=[self.ident_b_r])
        self.chalo, self.chalo_r = self.sb("chalo", [128, 4, 30], BF16)
        self.ocw, self.ocw_r = self.sb("ocw", [128, 4, 31], F32)
        self.ocb, self.ocb_r = self.sb("ocb", [128, 4], F32)
        self.olg, self.olg_r = self.sb("olg", [128, 4], F32)
        self.olb, self.olb_r = self.sb("olb", [128, 4], F32)
        self.A8r, self.A8r_r = self.sb("A8r", [128, 2, 16], F32)
        self.A8i, self.A8i_r = self.sb("A8i", [128, 2, 16], F32)
        self.Xs, self.Xs_r = self.sb("Xs", [128, 2, 16], F32)
        self.r8, self.r8_r = self.sb("r8", [128, 2, 16], F32)
        if not hasattr(self, "s5tab"):
            self.s5tab = self.nc.dram_tensor("s5tab", [128, 4096], F32, kind="ExternalOutput" if "dumpw" in SKIP else "Internal").ap()
        cx.op(DVE, lambda e: e.memset(self.chalo[:], 0.0), w=[self.chalo_r])
        cx.op(DVE, lambda e: e.memset(self.Xs[:], 0.0), w=[self.Xs_r])
        for k in range(31):
            cx.dma(SP, self.ocw[:, :, k], self.o_dw_w[0, k].rearrange("(t p) -> p t", p=128), w=[self.ocw_r], sem_res=self.ocw_r,
                   allow_slow_non_contiguous=True)
        for t, r, src in [(self.ocb, self.ocb_r, self.o_dw_b), (self.olg, self.olg_r, self.o_ln_g), (self.olb, self.olb_r, self.o_ln_b)]:
            cx.dma(SP, t[:], src[0].rearrange("(t p) -> p t", p=128), w=[r], sem_res=r, allow_slow_non_contiguous=True)
        with ExitStack() as st0:
            for i in range(4):
                dg, dg_r = self.sb(f"dg{i}", [128, 32, 128], BF16, st0)
                for k in range(31):
                    if k % 2 == 0:
                        cx.op(DVE, lambda e: e.tensor_scalar(out=dg[:, k, :], in0=self.ident_f[:], scalar1=self.ocw[:, i, k:k + 1], scalar2=None,
                                                             op0=ALU.mult), r=[self.ident_f_r, self.ocw_r], w=[dg_r])
                    else:
                        cx.op(ACT, lambda e: e.activation(out=dg[:, k, :], in_=self.ident_f[:], func=AF.Copy, scale=self.ocw[:, i, k:k + 1]),
                              r=[self.ident_f_r, self.ocw_r], w=[dg_r])
                cx.op(DVE, lambda e: e.memset(dg[:, 31, :], 0.0), w=[dg_r])
                u = self.U[("ocv", i)]
                dst = self.wscr[u["off"]: u["off"] + 128 * 4096].rearrange("(p l) -> p l", p=128)
                cx.dma(SP, dst, dg[:].rearrange("q k n -> q (k n)"), r=[dg_r], sem_res=dg_r)
            cx.barrier(engines=cx.compute + [cx.SP], with_dma=True)
        with ExitStack() as st:
            def tl(nm, shp, dt=F32):
                return self.sb("s5_" + nm, shp, dt, st)
            lr, lr_r = tl("lr", [128, 16])
            li, li_r = tl("li", [128, 16])
            stp, stp_r = tl("stp", [128, 16])
            Bre, Bre_r = tl("Bre", [128, 16, 16])
            Bim, Bim_r = tl("Bim", [128, 16, 16])
            Cre, Cre_r = tl("Cre", [128, 16, 16])
            Cim, Cim_r = tl("Cim", [128, 16, 16])
            Dbc, Dbc_r = tl("Dbc", [128, 32, 16])
            tzm, tzm_r = tl("tzm", [128, 128])
            cx.dma(SP, tzm[:], self.consts["tzmask"], w=[tzm_r], sem_res=tzm_r)
            cx.dma(SP, lr[:], self.o_a_re[0].rearrange("g n -> (g n)").rearrange("(p q) -> q p", q=128), w=[lr_r], sem_res=lr_r,
                   allow_slow_non_contiguous=True)
            cx.dma(SP, li[:], self.o_a_im[0].rearrange("g n -> (g n)").rearrange("(p q) -> q p", q=128), w=[li_r], sem_res=li_r,
                   allow_slow_non_contiguous=True)
            ls2 = self.o_log_step[0].rearrange("(p gh) -> gh p", gh=2)
            for gh in range(2):
                cx.dma(SP, stp[gh * 64:(gh + 1) * 64, :], ls2[gh].partition_broadcast(64), w=[stp_r], sem_res=stp_r, allow_slow_non_contiguous=True)
            cx.dma(SP, Bre[:], self.o_b_re[0].rearrange("g n c -> (g n) c").rearrange("(p q) c -> q p c", q=128), w=[Bre_r], sem_res=Bre_r,
                   allow_slow_non_contiguous=True)
            cx.dma(SP, Bim[:], self.o_b_im[0].rearrange("g n c -> (g n) c").rearrange("(p q) c -> q p c", q=128), w=[Bim_r], sem_res=Bim_r,
                   allow_slow_non_contiguous=True)
            for (dst, dst_r, src) in [(Cre, Cre_r, self.o_c_re), (Cim, Cim_r, self.o_c_im)]:
                for g in range(32):
                    p, gh = g // 2, g % 2
                    cx.dma(SP, dst[gh * 64:(gh + 1) * 64, p, :], src[0, g].rearrange("co n -> n co"), w=[dst_r], sem_res=dst_r,
                           allow_slow_non_contiguous=True)
            cx.dma(SP, Dbc[:].rearrange("p g c -> p (g c)"), self.o_d[0].partition_broadcast(128), w=[Dbc_r], sem_res=Dbc_r, allow_slow_non_contiguous=True)
            cx.barrier(engines=cx.compute + [cx.SP], with_dma=True)
            cx.hook = getattr(self, "pro_hook", None)
            cx.op(ACT, lambda e: e.activation(out=stp[:], in_=stp[:], func=AF.Exp), r=[stp_r], w=[stp_r])
            lrs, lrs_r = tl("lrs", [128, 16])
            lis, lis_r = tl("lis", [128, 16])
            cx.op(DVE, lambda e: e.tensor_tensor(out=lrs[:], in0=lr[:], in1=stp[:], op=ALU.mult), r=[lr_r, stp_r], w=[lrs_r])
            cx.op(DVE, lambda e: e.tensor_tensor(out=lis[:], in0=li[:], in1=stp[:], op=ALU.mult), r=[li_r, stp_r], w=[lis_r])
            NM = 17
            mag, mag_r = tl("mag", [128, 16, NM])
            ang, ang_r = tl("ang", [128, 16, NM])
            Pre, Pre_r = tl("Pre", [128, 16, NM])
            Pim, Pim_r = tl("Pim", [128, 16, NM])
            tmpa, tmpa_r = tl("tmpa", [128, 16, NM])
            for mi in range(NM):
                m = float(mi - 8)
                cx.op(ACT, lambda e: e.activation(out=mag[:, :, mi], in_=lrs[:], func=AF.Exp, scale=m), r=[lrs_r], w=[mag_r])
                cx.op(DVE, lambda e: e.tensor_scalar(out=ang[:, :, mi], in0=lis[:], scalar1=m, scalar2=None, op0=ALU.mult), r=[lis_r], w=[ang_r])
            TWO_PI = 2.0 * math.pi
            MAGIC = 12582912.0

            def sin_of(dst, dst_r, shift, ang=ang, ang_r=ang_r, tmpa=tmpa, tmpa_r=tmpa_r):
                cx.op(DVE, lambda e: e.tensor_scalar(out=tmpa[:], in0=ang[:], scalar1=shift, scalar2=1.0 / TWO_PI, op0=ALU.add, op1=ALU.mult),
                      r=[ang_r], w=[tmpa_r])
                cx.op(DVE, lambda e: e.tensor_scalar(out=dst[:], in0=tmpa[:], scalar1=MAGIC, scalar2=None, op0=ALU.add), r=[tmpa_r], w=[dst_r])
                cx.op(DVE, lambda e: e.tensor_scalar(out=dst[:], in0=dst[:], scalar1=-MAGIC, scalar2=None, op0=ALU.add), r=[dst_r], w=[dst_r])
                cx.op(DVE, lambda e: e.tensor_tensor(out=tmpa[:], in0=tmpa[:], in1=dst[:], op=ALU.subtract), r=[tmpa_r, dst_r], w=[tmpa_r])
                cx.op(DVE, lambda e: e.tensor_scalar(out=tmpa[:], in0=tmpa[:], scalar1=TWO_PI, scalar2=math.pi, op0=ALU.mult, op1=ALU.min),
                      r=[tmpa_r], w=[tmpa_r])
                cx.op(DVE, lambda e: e.tensor_scalar(out=tmpa[:], in0=tmpa[:], scalar1=-math.pi, scalar2=None, op0=ALU.max), r=[tmpa_r], w=[tmpa_r])
                cx.op(ACT, lambda e: e.activation(out=dst[:], in_=tmpa[:], func=AF.Sin), r=[tmpa_r], w=[dst_r])

            sin_of(Pim, Pim_r, 0.0)
            sin_of(Pre, Pre_r, math.pi / 2.0)
            cx.op(DVE, lambda e: e.tensor_tensor(out=Pre[:], in0=Pre[:], in1=mag[:], op=ALU.mult), r=[Pre_r, mag_r], w=[Pre_r])
            cx.op(DVE, lambda e: e.tensor_tensor(out=Pim[:], in0=Pim[:], in1=mag[:], op=ALU.mult), r=[Pim_r, mag_r], w=[Pim_r])

            def P(which, m):
                return (Pre if which == "re" else Pim)[:, :, m + 8]
            cx.op(DVE, lambda e: e.tensor_copy(out=self.A8r[:, 0, :], in_=P("re", 8)), r=[Pre_r], w=[self.A8r_r])
            cx.op(DVE, lambda e: e.tensor_copy(out=self.A8r[:, 1, :], in_=P("re", 8)), r=[Pre_r], w=[self.A8r_r])
            cx.op(DVE, lambda e: e.tensor_scalar(out=self.A8i[:, 0, :], in0=P("im", 8), scalar1=-1.0, scalar2=None, op0=ALU.mult), r=[Pim_r], w=[self.A8i_r])
            cx.op(DVE, lambda e: e.tensor_copy(out=self.A8i[:, 1, :], in_=P("im", 8)), r=[Pim_r], w=[self.A8i_r])
            with ExitStack() as st_tab:
                def tl2(nm, shp, dt=F32):
                    return self.sb("s5t_" + nm, shp, dt, st_tab)
                ph8, ph8_r = tl2("ph8", [128, 16])
                k8, k8_r = tl2("k8", [128, 16])
                cx.op(DVE, lambda e: e.tensor_scalar(out=ph8[:], in0=lis[:], scalar1=8.0 / TWO_PI, scalar2=None, op0=ALU.mult), r=[lis_r], w=[ph8_r])
                cx.op(DVE, lambda e: e.tensor_scalar(out=k8[:], in0=ph8[:], scalar1=MAGIC, scalar2=None, op0=ALU.add), r=[ph8_r], w=[k8_r])
                cx.op(DVE, lambda e: e.tensor_scalar(out=k8[:], in0=k8[:], scalar1=-MAGIC, scalar2=None, op0=ALU.add), r=[k8_r], w=[k8_r])
                cx.op(DVE, lambda e: e.tensor_tensor(out=ph8[:], in0=ph8[:], in1=k8[:], op=ALU.subtract), r=[ph8_r, k8_r], w=[ph8_r])
                cx.op(DVE, lambda e: e.tensor_scalar(out=ph8[:], in0=ph8[:], scalar1=TWO_PI, scalar2=None, op0=ALU.mult), r=[ph8_r], w=[ph8_r])
                cidx, cidx_r = tl2("cidx", [128, 64])
                cx.dma(SP, cidx[:], self.consts["cidx"], w=[cidx_r], sem_res=cidx_r)
                angm, angm_r = tl2("angm", [128, 16, 64])
                tmpm, tmpm_r = tl2("tmpm", [128, 16, 64])
                stab, stab_r = tl2("stab", [128, 4096])
                cx.op(DVE, lambda e: e.tensor_tensor(out=angm[:], in0=ph8[:].unsqueeze(2).to_broadcast([128, 16, 64]),
                                                     in1=cidx[:].unsqueeze(1).to_broadcast([128, 16, 64]), op=ALU.mult), r=[ph8_r, cidx_r], w=[angm_r])
                Ec = stab[:, 0:1024].rearrange("q (p c) -> q p c", p=16)
                Es = stab[:, 1024:2048].rearrange("q (p c) -> q p c", p=16)
                rt = stab[:, 2048:4096].rearrange("q (a p c) -> q a p c", a=2, p=16)

                class _V:
                    def __init__(self, ap):
                        self.ap = ap

                    def __getitem__(self, k):
                        return self.ap
                sin_of(_V(Es), stab_r, 0.0, ang=angm, ang_r=angm_r, tmpa=tmpm, tmpa_r=tmpm_r)
                sin_of(_V(Ec), stab_r, math.pi / 2.0, ang=angm, ang_r=angm_r, tmpa=tmpm, tmpa_r=tmpm_r)
                for a_ in range(2):
                    cx.op(DVE, lambda e: e.tensor_copy(out=rt[:, a_, :, :], in_=mag[:, :, 16].unsqueeze(2).to_broadcast([128, 16, 64])), r=[mag_r], w=[stab_r])
                cx.op(DVE, lambda e: e.memset(rt[:, :, :, 0], 0.0), w=[stab_r])
                cx.op(DVE, lambda e: e.tensor_copy(out=self.r8[:, 0, :], in_=mag[:, :, 16]), r=[mag_r], w=[self.r8_r])
                cx.op(DVE, lambda e: e.tensor_copy(out=self.r8[:, 1, :], in_=mag[:, :, 16]), r=[mag_r], w=[self.r8_r])
                cx.dma(SP, self.s5tab, stab[:], r=[stab_r], sem_res=stab_r)
                cx.barrier(engines=cx.compute + [cx.SP], with_dma=True)
            fre, fre_r = tl("fre", [128, 16])
            fim, fim_r = tl("fim", [128, 16])
            den, den_r = tl("den", [128, 16])
            am1, am1_r = tl("am1", [128, 16])
            t16, t16_r = tl("t16", [128, 16])
            cx.op(DVE, lambda e: e.tensor_tensor(out=den[:], in0=lr[:], in1=lr[:], op=ALU.mult), r=[lr_r], w=[den_r])
            cx.op(DVE, lambda e: e.tensor_tensor(out=t16[:], in0=li[:], in1=li[:], op=ALU.mult), r=[li_r], w=[t16_r])
            cx.op(DVE, lambda e: e.tensor_tensor(out=den[:], in0=den[:], in1=t16[:], op=ALU.add), r=[den_r, t16_r], w=[den_r])
            cx.op(DVE, lambda e: e.reciprocal(out=den[:], in_=den[:]), r=[den_r], w=[den_r])
            cx.op(DVE, lambda e: e.tensor_scalar(out=am1[:], in0=P("re", 1), scalar1=-1.0, scalar2=None, op0=ALU.add), r=[Pre_r], w=[am1_r])
            cx.op(DVE, lambda e: e.tensor_tensor(out=fre[:], in0=am1[:], in1=lr[:], op=ALU.mult), r=[am1_r, lr_r], w=[fre_r])
            cx.op(DVE, lambda e: e.tensor_tensor(out=t16[:], in0=P("im", 1), in1=li[:], op=ALU.mult), r=[Pim_r, li_r], w=[t16_r])
            cx.op(DVE, lambda e: e.tensor_tensor(out=fre[:], in0=fre[:], in1=t16[:], op=ALU.add), r=[fre_r, t16_r], w=[fre_r])
            cx.op(DVE, lambda e: e.tensor_tensor(out=fre[:], in0=fre[:], in1=den[:], op=ALU.mult), r=[fre_r, den_r], w=[fre_r])
            cx.op(DVE, lambda e: e.tensor_tensor(out=fim[:], in0=P("im", 1), in1=lr[:], op=ALU.mult), r=[Pim_r, lr_r], w=[fim_r])
            cx.op(DVE, lambda e: e.tensor_tensor(out=t16[:], in0=am1[:], in1=li[:], op=ALU.mult), r=[am1_r, li_r], w=[t16_r])
            cx.op(DVE, lambda e: e.tensor_tensor(out=fim[:], in0=fim[:], in1=t16[:], op=ALU.subtract), r=[fim_r, t16_r], w=[fim_r])
            cx.op(DVE, lambda e: e.tensor_tensor(out=fim[:], in0=fim[:], in1=den[:], op=ALU.mult), r=[fim_r, den_r], w=[fim_r])

            def cmul(dre, dre_r, dim_, dim_r, are, aim, a_rs, bre, bim, b_rs, tA, tA_r, neg_im=False):
                cx.op(DVE, lambda e: e.tensor_tensor(out=dre, in0=are, in1=bre, op=ALU.mult), r=a_rs + b_rs, w=[dre_r])
                cx.op(DVE, lambda e: e.tensor_tensor(out=tA, in0=aim, in1=bim, op=ALU.mult), r=a_rs + b_rs, w=[tA_r])
                cx.op(DVE, lambda e: e.tensor_tensor(out=dre, in0=dre, in1=tA, op=ALU.subtract), r=[dre_r, tA_r], w=[dre_r])
                tB, tB_r = tB_pair
                tBv = tB[:] if len(tA.shape) == 3 else tB[:]
                E2 = POOL if "cmulpool" in SKIP else DVE
                cx.op(E2, lambda e: e.tensor_tensor(out=dim_, in0=are, in1=bim, op=ALU.mult), r=a_rs + b_rs, w=[dim_r])
                cx.op(E2, lambda e: e.tensor_tensor(out=tBv, in0=aim, in1=bre, op=ALU.mult), r=a_rs + b_rs, w=[tB_r])
                cx.op(E2, lambda e: e.tensor_tensor(out=dim_, in0=dim_, in1=tBv, op=ALU.add), r=[dim_r, tB_r], w=[dim_r])

            def bc16(ap2):
                return ap2.unsqueeze(2).to_broadcast([128, 16, 16])
            tB_pair = tl("t3b", [128, 16, 16])
            Bbr, Bbr_r = tl("Bbr", [128, 16, 16])
            Bbi, Bbi_r = tl("Bbi", [128, 16, 16])
            t3, t3_r = tl("t3", [128, 16, 16])
            cmul(Bbr[:], Bbr_r, Bbi[:], Bbi_r, bc16(fre[:]), bc16(fim[:]), [fre_r, fim_r], Bre[:], Bim[:], [Bre_r, Bim_r], t3[:], t3_r)
            CAr, CAr_r = tl("CAr", [128, 16, 8, 16])
            CAi, CAi_r = tl("CAi", [128, 16, 8, 16])
            for s in range(8):
                cmul(CAr[:, :, s, :], CAr_r, CAi[:, :, s, :], CAi_r, bc16(P("re", s + 1)), bc16(P("im", s + 1)), [Pre_r, Pim_r],
                     Cre[:], Cim[:], [Cre_r, Cim_r], t3[:], t3_r)
            with ExitStack() as sc:
                CcU, CcU_r = self.sb("s5_CcU", [128, 16, 2, 2, 128], BF16, sc)
                for ph in range(4):
                    cx.op(DVE, lambda e: e.memset(CcU[:, ph * 4:(ph + 1) * 4], 0.0), w=[CcU_r])
                for gh in range(2):
                    ps_ = slice(gh * 64, (gh + 1) * 64)
                    cx.op(DVE, lambda e: e.tensor_copy(out=CcU[ps_, :, gh, 0, :], in_=CAr[ps_].rearrange("q p j c -> q p (j c)")), r=[CAr_r], w=[CcU_r])
                    cx.op(DVE, lambda e: e.tensor_scalar(out=CcU[ps_, :, gh, 1, :], in0=CAi[ps_].rearrange("q p j c -> q p (j c)"), scalar1=-1.0, scalar2=None,
                                                         op0=ALU.mult), r=[CAi_r], w=[CcU_r])
                ccflat = CcU[:].rearrange("q p a b n -> q (p a b n)")
                for i in range(2):
                    u = self.U[("s5cc", i)]
                    dst = self.wscr[u["off"]: u["off"] + 128 * 4096].rearrange("(p l) -> p l", p=128)
                    cx.dma(SP, dst, ccflat[:, i * 4096:(i + 1) * 4096], r=[CcU_r], sem_res=CcU_r)
                cx.barrier(engines=cx.compute + [cx.SP], with_dma=True)
            with ExitStack() as sc:
                BcTr, BcTr_r = self.sb("s5_BcTr", [128, 16, 8, 16], F32, sc)
                BcTi, BcTi_r = self.sb("s5_BcTi", [128, 16, 8, 16], F32, sc)
                for s in range(8):
                    cmul(BcTr[:, :, s, :], BcTr_r, BcTi[:, :, s, :], BcTi_r, bc16(P("re", 7 - s)), bc16(P("im", 7 - s)), [Pre_r, Pim_r],
                         Bbr[:], Bbi[:], [Bbr_r, Bbi_r], t3[:], t3_r)
                BcU, BcU_r = self.sb("s5_BcU", [128, 8, 2, 2, 128], BF16, sc)
                for i in range(2):
                    for ph in range(2):
                        cx.op(DVE, lambda e: e.memset(BcU[:, ph * 4:(ph + 1) * 4], 0.0), w=[BcU_r])
                    for pp in range(8):
                        p = i * 8 + pp
                        for ri, (src, src_r) in enumerate([(BcTr, BcTr_r), (BcTi, BcTi_r)]):
                            bk, bk_r = self.bank()
                            cx.op(PE, lambda e: e.transpose(bk[:, 0:128], src[:, p, :, :].rearrange("q s c -> q (s c)"), self.ident_f[:]),
                                  r=[src_r, self.ident_f_r], w=[bk_r])
                            for gh in range(2):
                                cx.op(ACT if gh == 0 else DVE, lambda e: (e.copy if gh == 0 else e.tensor_copy)(out=BcU[:, pp, gh, ri, gh * 64:(gh + 1) * 64],
                                                                                                         in_=bk[:, gh * 64:(gh + 1) * 64]),
                                      r=[bk_r], w=[BcU_r])
                    u = self.U[("s5bc", i)]
                    dst = self.wscr[u["off"]: u["off"] + 128 * 4096].rearrange("(p l) -> p l", p=128)
                    cx.dma(SP, dst, BcU[:].rearrange("q p a b n -> q (p a b n)"), r=[BcU_r], sem_res=BcU_r)
                cx.barrier(engines=cx.compute + [cx.SP], with_dma=True)
            with ExitStack() as sc:
                BpTr, BpTr_r = self.sb("s5_BpTr", [128, 16, 8, 16], F32, sc)
                BpTi, BpTi_r = self.sb("s5_BpTi", [128, 16, 8, 16], F32, sc)
                for s in range(8):
                    cmul(BpTr[:, :, s, :], BpTr_r, BpTi[:, :, s, :], BpTi_r, bc16(P("re", -1 - s)), bc16(P("im", -1 - s)), [Pre_r, Pim_r],
                         Bbr[:], Bbi[:], [Bbr_r, Bbi_r], t3[:], t3_r)
                TzU, TzU_r = self.sb("s5_TzU", [128, 32, 128], BF16, sc)
                lm = [self.sb(f"s5_lm{i}", [128, 128], F32, sc) for i in range(4)]
                tzt, tzt_r = self.sb("s5_tzt", [128, 128], F32, sc)
                for (l_, l_r) in lm:
                    cx.op(POOL, lambda e: e.memset(l_[:], 0.0), w=[l_r])
                for g in range(32):
                    p, gh = g // 2, g % 2
                    ps_ = slice(gh * 64, (gh + 1) * 64)
                    po_ = slice((1 - gh) * 64, (2 - gh) * 64)
                    (l0, l0_r), (l1, l1_r) = lm[(g % 2) * 2], lm[(g % 2) * 2 + 1]
                    cx.op(POOL, lambda e: e.tensor_copy(out=l0[ps_, :], in_=BpTr[ps_, p, :, :].rearrange("q s c -> q (s c)")), r=[BpTr_r], w=[l0_r])
                    cx.op(ACT, lambda e: e.activation(out=l1[ps_, :], in_=BpTi[ps_, p, :, :].rearrange("q s c -> q (s c)"), func=AF.Copy, scale=-1.0),
                          r=[BpTi_r], w=[l1_r])
                    bk, bk_r = self.bank()
                    cx.op(PE, lambda e: e.matmul(bk[:, 0:128], lhsT=l0[:], rhs=CAr[:, p, :, :].rearrange("q j c -> q (j c)"), start=True, stop=False),
                          r=[l0_r, CAr_r], w=[bk_r])
                    cx.op(PE, lambda e: e.matmul(bk[:, 0:128], lhsT=l1[:], rhs=CAi[:, p, :, :].rearrange("q j c -> q (j c)"), start=False, stop=True),
                          r=[l1_r, CAi_r], w=[bk_r])
                    cx.op(DVE, lambda e: e.tensor_tensor(out=tzt[:], in0=bk[:, 0:128], in1=tzm[:], op=ALU.mult), r=[bk_r, tzm_r], w=[tzt_r])
                    cx.op(DVE, lambda e: e.tensor_tensor(out=TzU[:, g, :].rearrange("q (j c) -> q j c", j=8),
                                                         in0=self.ident_f[:].rearrange("q (j c) -> q j c", j=8),
                                                         in1=Dbc[:, g, :].unsqueeze(1).to_broadcast([128, 8, 16]), op=ALU.mult),
                          r=[self.ident_f_r, Dbc_r], w=[TzU_r])
                    cx.op(DVE, lambda e: e.tensor_tensor(out=TzU[:, g, :], in0=TzU[:, g, :], in1=tzt[:], op=ALU.add), r=[tzt_r, TzU_r], w=[TzU_r])
                u = self.U["s5tz"]
                dst = self.wscr[u["off"]: u["off"] + 128 * 4096].rearrange("(p l) -> p l", p=128)
                cx.dma(SP, dst, TzU[:].rearrange("q g n -> q (g n)"), r=[TzU_r], sem_res=TzU_r)
                cx.barrier(engines=cx.compute + [cx.SP], with_dma=True)
            cx.barrier(engines=cx.compute + [cx.SP], with_dma=True)

    def odd(self, b):
        cx = self.cx
        PE, ACT, DVE, POOL, SP = cx.PE, cx.ACT, cx.DVE, cx.POOL, cx.SP
        self.rmsnorm(2, self.hnT, self.hnT_r)
        hnT, hnT_r = self.hnT, self.hnT_r
        NC8 = NB // 8
        with ExitStack() as st:
            def tl(nm, shp, dt=F32):
                return self.sb("o_" + nm, shp, dt, st)
            mixT, mixT_r = tl("mixT", [128, 8, NB], BF16)
            cbuf, cbuf_r = tl("cbuf", [128, 4, NB + 30], BF16)
            cacc, cacc_r = tl("cacc", [128, 4, NB])
            sig = [tl(f"sig{i}", [128, NB]) for i in range(2)]
            uS, uS_r = tl("uS", [64, 32, 128], BF16)
            U, U_r = tl("U", [128, 32, NC8], BF16)
            V, V_r = tl("V", [128, 2, 16, NC8])
            Xst, Xst_r = tl("Xst", [128, 2, 16, NC8], BF16)
            slot, slot_r, u = self.stream_get()
            assert u is self.U["ou"]
            wv = slot[:, 0:4096].rearrange("p (k w) -> p k w", k=8)
            for s in range(8):
                bk, bk_r = self.bank()
                for kt in range(8):
                    lh = hnT[:, kt, :].rearrange("p (c s) -> p s c", s=8)[:, s, :]
                    cx.op(PE, lambda e: e.matmul(bk[0:64, :], lhsT=lh, rhs=wv[:, kt, :], start=(kt == 0), stop=(kt == 7)), r=[slot_r, hnT_r], w=[bk_r])
                src = bk[0:64, :].rearrange("p (g c) -> p g c", g=32)
                if s % 2 == 0:
                    cx.op(ACT, lambda e: e.copy(out=uS[:, :, s * 16:(s + 1) * 16], in_=src), r=[bk_r], w=[uS_r])
                else:
                    cx.op(DVE, lambda e: e.tensor_copy(out=uS[:, :, s * 16:(s + 1) * 16], in_=src), r=[bk_r], w=[uS_r])
            for g8 in range(4):
                pb, pb_r = self.bankb()
                for gi in range(8):
                    g = g8 * 8 + gi
                    cx.op(PE, lambda e: e.transpose(pb[:, gi * 64:(gi + 1) * 64], uS[:, g, :], self.ident_b[0:64, 0:64]), r=[uS_r, self.ident_b_r], w=[pb_r])
                if g8 % 2 == 0:
                    cx.op(ACT, lambda e: e.copy(out=U[:, g8 * 8:(g8 + 1) * 8, :].rearrange("p g c -> p (g c)"), in_=pb[:, 0:512]), r=[pb_r], w=[U_r])
                else:
                    cx.op(DVE, lambda e: e.tensor_copy(out=U[:, g8 * 8:(g8 + 1) * 8, :].rearrange("p g c -> p (g c)"), in_=pb[:, 0:512]), r=[pb_r], w=[U_r])
            bcs = []
            for i in range(2):
                slot, slot_r, u = self.stream_get(hold_prev=i)
                assert u is self.U[("s5bc", i)]
                bcs.append((slot[:, 0:4096].rearrange("q (p a b n) -> q p a b n", p=8, a=2, b=2), slot_r))
            for ri in range(2):
                for ph in range(2):
                    bk, bk_r = self.bank()
                    for pi in range(8):
                        p = ph * 8 + pi
                        bcv, bc_r = bcs[p // 8]
                        for gh in range(2):
                            cx.op(PE, lambda e: e.matmul(bk[:, pi * 64:(pi + 1) * 64], lhsT=bcv[:, p % 8, gh, ri, :], rhs=U[:, 2 * p + gh, :],
                                                         start=(gh == 0), stop=(gh == 1)), r=[bc_r, U_r], w=[bk_r])
                    cx.op(ACT, lambda e: e.copy(out=V[:, ri, ph * 8:(ph + 1) * 8, :].rearrange("q p c -> q (p c)"), in_=bk[:]), r=[bk_r], w=[V_r])
            with ExitStack() as st_scan:
                def tls(nm, shp, dt=F32):
                    return self.sb("os_" + nm, shp, dt, st_scan if "noscanscope" not in SKIP else st)
                tab, tab_r = tls("tab", [128, 4096])
                cx.dma(SP, tab[:], self.s5tab, w=[tab_r], sem_res=self.r8_r)
                Ec = tab[:, 0:1024].rearrange("q (p c) -> q p c", p=16)
                Es = tab[:, 1024:2048].rearrange("q (p c) -> q p c", p=16)
                rtf = tab[:, 2048:4096]
                Vt, Vt_r = tls("Vt", [128, 2, 16, NC8])
                Yt, Yt_r = tls("Yt", [128, 2, 16, NC8])
                mA, mA_r = tls("mA", [128, 16, NC8])
                mB, mB_r = tls("mB", [128, 16, NC8])
                Xs, Xs_r = self.Xs, self.Xs_r
                Vr, Vi = V[:, 0, :, :], V[:, 1, :, :]
                cx.op(POOL, lambda e: e.tensor_tensor(out=mA[:], in0=Vr, in1=Ec, op=ALU.mult), r=[V_r, tab_r], w=[mA_r])
                cx.op(DVE, lambda e: e.tensor_tensor(out=mB[:], in0=Vi, in1=Es, op=ALU.mult), r=[V_r, tab_r], w=[mB_r])
                cx.op(DVE, lambda e: e.tensor_tensor(out=Vt[:, 0, :, :], in0=mA[:], in1=mB[:], op=ALU.add), r=[mA_r, mB_r], w=[Vt_r])
                cx.op(POOL, lambda e: e.tensor_tensor(out=mA[:], in0=Vi, in1=Ec, op=ALU.mult), r=[V_r, tab_r], w=[mA_r])
                cx.op(DVE, lambda e: e.tensor_tensor(out=mB[:], in0=Vr, in1=Es, op=ALU.mult), r=[V_r, tab_r], w=[mB_r])
                cx.op(DVE, lambda e: e.tensor_tensor(out=Vt[:, 1, :, :], in0=mA[:], in1=mB[:], op=ALU.subtract), r=[mA_r, mB_r], w=[Vt_r])
                T1, T1_r = tls("T1", [128, 2, 16])
                cx.op(DVE, lambda e: e.tensor_tensor(out=T1[:], in0=Xs[:], in1=self.r8[:], op=ALU.mult), r=[Xs_r, self.r8_r], w=[T1_r])
                cx.op(DVE, lambda e: e.tensor_tensor(out=Vt[:, :, :, 0], in0=Vt[:, :, :, 0], in1=T1[:], op=ALU.add), r=[Vt_r, T1_r], w=[Vt_r])
                cx.op(POOL, lambda e: e.tensor_copy(out=Xst[:, :, :, 0], in_=Xs[:]), r=[Xs_r], w=[Xst_r])
                cx.op(DVE, lambda e: e.tensor_tensor_scan(out=Yt[:].rearrange("q a p c -> q (a p c)"), data0=rtf,
                                                          data1=Vt[:].rearrange("q a p c -> q (a p c)"), initial=0.0, op0=ALU.mult, op1=ALU.add),
                      r=[tab_r, Vt_r], w=[Yt_r])
                Ytr, Yti = Yt[:, 0, :, :], Yt[:, 1, :, :]
                cx.op(POOL, lambda e: e.tensor_tensor(out=mA[:], in0=Ytr, in1=Ec, op=ALU.mult), r=[Yt_r, tab_r], w=[mA_r])
                cx.op(DVE, lambda e: e.tensor_tensor(out=mB[:], in0=Yti, in1=Es, op=ALU.mult), r=[Yt_r, tab_r], w=[mB_r])
                cx.op(DVE, lambda e: e.tensor_tensor(out=V[:, 0, :, :], in0=mA[:], in1=mB[:], op=ALU.subtract), r=[mA_r, mB_r], w=[V_r])
                cx.op(POOL, lambda e: e.tensor_tensor(out=mA[:], in0=Yti, in1=Ec, op=ALU.mult), r=[Yt_r, tab_r], w=[mA_r])
                cx.op(DVE, lambda e: e.tensor_tensor(out=mB[:], in0=Ytr, in1=Es, op=ALU.mult), r=[Yt_r, tab_r], w=[mB_r])
                cx.op(DVE, lambda e: e.tensor_tensor(out=V[:, 1, :, :], in0=mA[:], in1=mB[:], op=ALU.add), r=[mA_r, mB_r], w=[V_r])
                cx.op(ACT, lambda e: e.copy(out=Xst[:, :, :, 1:NC8], in_=V[:, :, :, 0:NC8 - 1]), r=[V_r], w=[Xst_r])
                cx.op(POOL, lambda e: e.tensor_copy(out=Xs[:], in_=V[:, :, :, NC8 - 1]), r=[V_r], w=[Xs_r])
                for i in range(4):
                    slot, slot_r, u = self.stream_get()
                    assert u is self.U[("oin", i)]
                    bks = []
                    for gi in range(2):
                        wv = slot[:, gi * 1024:(gi + 1) * 1024].rearrange("p (k w) -> p k w", k=8)
                        bk, bk_r = self.bank()
                        for kt in range(8):
                            cx.op(PE, lambda e: e.matmul(bk[:], lhsT=wv[:, kt, :], rhs=hnT[:, kt, :], start=(kt == 0), stop=(kt == 7)), r=[slot_r, hnT_r], w=[bk_r])
                        bks.append((bk, bk_r))
                    sg_t, sg_r = sig[i % 2]
                    cx.op(ACT, lambda e: e.activation(out=sg_t[:], in_=bks[1][0][:], func=AF.Sigmoid), r=[bks[1][1]], w=[sg_r])
                    cx.op(DVE, lambda e: e.tensor_tensor(out=cbuf[:, i, 30:NB + 30], in0=bks[0][0][:], in1=sg_t[:], op=ALU.mult), r=[bks[0][1], sg_r], w=[cbuf_r])
                    cx.op(POOL, lambda e: e.tensor_copy(out=cbuf[:, i, 0:30], in_=self.chalo[:, i, :]), r=[self.chalo_r], w=[cbuf_r])
                    cx.op(POOL, lambda e: e.tensor_copy(out=self.chalo[:, i, :], in_=cbuf[:, i, NB:NB + 30]), r=[cbuf_r], w=[self.chalo_r])
                    dslot, dslot_r, u = self.stream_get(hold_prev=0)
                    assert u is self.U[("ocv", i)]
                    dv = dslot[:, 0:4096].rearrange("q (k n) -> q k n", k=32)
                    bk, bk_r = self.bank()
                    for k in range(31):
                        cx.op(PE, lambda e: e.matmul(bk[:], lhsT=dv[:, k, :], rhs=cbuf[:, i, k:k + NB], start=(k == 0), stop=(k == 30)), r=[dslot_r, cbuf_r], w=[bk_r])
                    cx.op(ACT, lambda e: e.activation(out=cacc[:, i, :], in_=bk[:], func=AF.Identity, bias=self.ocb[:, i:i + 1]), r=[bk_r, self.ocb_r], w=[cacc_r])
                cx.barrier()
            sq = [tl(f"sq{i}", [128, NB]) for i in range(2)]
            b1, b1_r = self.bank()
            b2, b2_r = self.bank()
            for i in range(4):
                cx.op(PE, lambda e: e.matmul(b1[:], lhsT=self.ones_f[:], rhs=cacc[:, i, :], start=(i == 0), stop=(i == 3)), r=[self.ones_f_r, cacc_r], w=[b1_r])
            for i in range(4):
                s_t, s_r = sq[i % 2]
                cx.op(ACT, lambda e: e.activation(out=s_t[:], in_=cacc[:, i, :], func=AF.Square), r=[cacc_r], w=[s_r])
                cx.op(PE, lambda e: e.matmul(b2[:], lhsT=self.ones_f[:], rhs=s_t[:], start=(i == 0), stop=(i == 3)), r=[self.ones_f_r, s_r], w=[b2_r])
            mean, mean_r = tl("mean", [128, NB])
            var, var_r = tl("var", [128, NB])
            cx.op(DVE, lambda e: e.tensor_scalar(out=mean[:], in0=b1[:], scalar1=1.0 / 512, scalar2=None, op0=ALU.mult), r=[b1_r], w=[mean_r])
            cx.op(DVE, lambda e: e.tensor_tensor(out=var[:], in0=mean[:], in1=mean[:], op=ALU.mult), r=[mean_r], w=[var_r])
            cx.op(DVE, lambda e: e.scalar_tensor_tensor(out=var[:], in0=b2[:], scalar=1.0 / 512, in1=var[:], op0=ALU.mult, op1=ALU.subtract),
                  r=[b2_r, var_r], w=[var_r])
            cx.op(ACT, lambda e: e.activation(out=var[:], in_=var[:], func=AF.Sqrt, bias=self.eps_t[:, 0:1]), r=[var_r, self.eps_r], w=[var_r])
            cx.op(DVE, lambda e: e.reciprocal(out=var[:], in_=var[:]), r=[var_r], w=[var_r])
            for i in range(4):
                cx.op(DVE, lambda e: e.tensor_tensor(out=cacc[:, i, :], in0=cacc[:, i, :], in1=mean[:], op=ALU.subtract), r=[cacc_r, mean_r], w=[cacc_r])
                cx.op(DVE, lambda e: e.tensor_tensor(out=cacc[:, i, :], in0=cacc[:, i, :], in1=var[:], op=ALU.mult), r=[cacc_r, var_r], w=[cacc_r])
                cx.op(ACT, lambda e: e.activation(out=mixT[:, i, :], in_=cacc[:, i, :], func=AF.Silu, bias=self.olb[:, i:i + 1], scale=self.olg[:, i:i + 1]),
                      r=[cacc_r, self.olg_r, self.olb_r], w=[mixT_r])
            slot, tz_r, u = self.stream_get()
            assert u is self.U["s5tz"]
            tzv = slot[:, 0:4096].rearrange("q (g n) -> q g n", g=32)
            ccs = []
            for i in range(2):
                slot, slot_r, u = self.stream_get(hold_prev=i + 1)
                assert u is self.U[("s5cc", i)]
                ccs.append((slot[:, 0:4096].rearrange("q (p a b n) -> q p a b n", p=8, a=2, b=2), slot_r))
            stok, stok_r = tl("stok", [64, 8, 512])
            for g4 in range(8):
                bk, bk_r = self.bank()
                for gi in range(4):
                    g = g4 * 4 + gi
                    p, gh = g // 2, g % 2
                    ccv, cc_r = ccs[p // 8]
                    reg = bk[0:64, gi * 128:(gi + 1) * 128]
                    cx.op(PE, lambda e: e.matmul(reg, lhsT=U[:, g, :], rhs=tzv[:, g, :], start=True, stop=False), r=[U_r, tz_r], w=[bk_r])
                    cx.op(PE, lambda e: e.matmul(reg, lhsT=Xst[:, 0, p, :], rhs=ccv[:, p % 8, gh, 0, :], start=False, stop=False), r=[Xst_r, cc_r], w=[bk_r])
                    cx.op(PE, lambda e: e.matmul(reg, lhsT=Xst[:, 1, p, :], rhs=ccv[:, p % 8, gh, 1, :], start=False, stop=True), r=[Xst_r, cc_r], w=[bk_r])
                src = bk[0:64, :].rearrange("p (g j c) -> p j g c", g=4, j=8)
                dst = stok[:, :, g4 * 64:(g4 + 1) * 64].rearrange("p j (g c) -> p j g c", g=4)
                if g4 % 2 == 0:
                    cx.op(ACT, lambda e: e.copy(out=dst, in_=src), r=[bk_r], w=[stok_r])
                else:
                    cx.op(DVE, lambda e: e.tensor_copy(out=dst, in_=src), r=[bk_r], w=[stok_r])
            syT, syT_r = tl("syT", [128, 4, NB])
            for t in range(4):
                bk, bk_r = self.bank()
                for j in range(8):
                    cx.op(PE, lambda e: e.transpose(bk[:, j * 64:(j + 1) * 64], stok[:, j, t * 128:(t + 1) * 128], self.ident_f[0:64, 0:64]),
                          r=[stok_r, self.ident_f_r], w=[bk_r])
                src = bk[:].rearrange("p (j c) -> p j c", j=8)
                dst = syT[:, t, :].rearrange("p (c j) -> p j c", j=8)
                if t % 2 == 0:
                    cx.op(ACT, lambda e: e.copy(out=dst, in_=src), r=[bk_r], w=[syT_r])
                else:
                    cx.op(DVE, lambda e: e.tensor_copy(out=dst, in_=src), r=[bk_r], w=[syT_r])
            if "s5y" in self.dbg:
                for t in range(4):
                    self.dump("s5y", syT[:, t, :], syT_r, self.dbg_out["s5y"][t * 128:(t + 1) * 128, b * NB:(b + 1) * NB])
            gl, gl_r = tl("gl", [128, 4, NB])
            glb, glb_r = tl("glb", [128, 4, NB], BF16)
            gt = [tl(f"gt{i}", [128, NB]) for i in range(2)]
            K2 = 2.0 * math.sqrt(2.0 / math.pi)
            for t in range(4):
                g_t, g_r = gt[t % 2]
                cx.op(DVE, lambda e: e.tensor_tensor(out=g_t[:], in0=syT[:, t, :], in1=syT[:, t, :], op=ALU.mult), r=[syT_r], w=[g_r])
                cx.op(DVE, lambda e: e.tensor_scalar(out=g_t[:], in0=g_t[:], scalar1=0.044715, scalar2=1.0, op0=ALU.mult, op1=ALU.add), r=[g_r], w=[g_r])
                cx.op(DVE, lambda e: e.tensor_tensor(out=g_t[:], in0=g_t[:], in1=syT[:, t, :], op=ALU.mult), r=[g_r, syT_r], w=[g_r])
                cx.op(ACT, lambda e: e.activation(out=g_t[:], in_=g_t[:], func=AF.Sigmoid, scale=K2), r=[g_r], w=[g_r])
                cx.op(DVE, lambda e: e.tensor_tensor(out=gl[:, t, :], in0=g_t[:], in1=syT[:, t, :], op=ALU.mult), r=[g_r, syT_r], w=[gl_r])
                cx.op(POOL, lambda e: e.tensor_copy(out=glb[:, t, :], in_=gl[:, t, :]), r=[gl_r], w=[glb_r])
            slot, slot_r, u = self.stream_get()
            assert u is self.U["oglu"]
            gv = slot[:, 0:2048].rearrange("p (o k w) -> p o k w", o=4, k=4)
            for ot in range(4):
                bk, bk_r = self.bank()
                for kt in range(4):
                    cx.op(PE, lambda e: e.matmul(bk[:], lhsT=gv[:, ot, kt, :], rhs=glb[:, kt, :], start=(kt == 0), stop=(kt == 3)), r=[slot_r, glb_r], w=[bk_r])
                g_t, g_r = gt[ot % 2]
                cx.op(ACT, lambda e: e.activation(out=g_t[:], in_=bk[:], func=AF.Sigmoid), r=[bk_r], w=[g_r])
                cx.op(DVE, lambda e: e.tensor_tensor(out=mixT[:, 4 + ot, :], in0=g_t[:], in1=gl[:, ot, :], op=ALU.mult), r=[g_r, gl_r], w=[mixT_r])
            for i in range(4):
                slot, slot_r, u = self.stream_get()
                assert u is self.U[("oout", i)]
                for gi in range(2):
                    ft = 2 * i + gi
                    wv = slot[:, gi * 1024:(gi + 1) * 1024].rearrange("p (k w) -> p k w", k=8)
                    bk, bk_r = self.bank()
                    for kt in range(8):
                        cx.op(PE, lambda e: e.matmul(bk[:], lhsT=wv[:, kt, :], rhs=mixT[:, kt, :], start=(kt == 0), stop=(kt == 7)), r=[slot_r, mixT_r], w=[bk_r])
                    cx.op(DVE, lambda e: e.tensor_tensor(out=self.xT[:, ft, :], in0=self.xT[:, ft, :], in1=bk[:], op=ALU.add), r=[bk_r, self.xT_r], w=[self.xT_r])
            cx.barrier()


    def final(self, b):
        cx = self.cx
        with ExitStack() as st:
            oT, oT_r = self.sb("oT", [128, 8, NB], F32, st)
            self.rmsnorm(4, oT, oT_r)
            self.store_block(b, oT, oT_r)

    def build(self):
        nc, cx = self.nc, self.cx
        self.declare_io()
        self.declare_units()
        self.wscr = nc.dram_tensor("wscr", [self.unit_off], BF16, kind="ExternalOutput" if "dumpw" in SKIP else "Internal").ap()
        self.wscr_res = Res("wscr")
        self.dbg_res = {}
        self.dbg_out = {}
        for nm in self.dbg:
            if nm == "s5setup":
                continue
            self.dbg_out[nm] = nc.dram_tensor("dbg_" + nm, [self.ntok, 1024], F32, kind="ExternalOutput").ap()
        self.y_res = Res("y")
        if "nodry" in SKIP:
            self.pro_finish = True
            for _ in self.prologue():
                pass
        if "nodry" not in SKIP:
            cx.dry = True
            self.setup()
            if "even" in self.stages:
                self.setup_even()
            if "odd" in self.stages:
                self.setup_odd()
            cx.dry = False
        pro = self.prologue() if "nodry" not in SKIP else iter(())
        next(pro, None)
        self.pro_hook = (lambda: next(pro, None)) if "hook" in SKIP else None
        self.setup()
        if "even" in self.stages:
            self.setup_even()
        if "odd" in self.stages:
            self.setup_odd()
        cx.hook = None
        self.pro_finish = True
        for _ in pro:
            pass
        cx.barrier(engines=cx.compute + [cx.SP], with_dma=True)
        self.plan_stream()
        for b in range(self.nblk):
            self.load_block(b)
            if "even" in self.stages:
                self.even(b)
            if "ffn0" in self.stages:
                self.ffn(0, b)
            if "odd" in self.stages:
                self.odd(b)
            if "ffn1" in self.stages:
                self.ffn(1, b)
            if "final" in self.stages:
                self.final(b)
            else:
                self.store_block(b, self.xT, self.xT_r)
        cx.barrier(engines=cx.compute + [cx.SP], with_dma=True)
        return nc


INPUT_NAMES = ["mix_norm", "ffn_norm", "final_norm", "ffn_w_up", "ffn_dw_w", "ffn_dw_b", "ffn_w_down",
               "e_w_in", "e_conv_w", "e_conv_b", "e_dt_bias", "e_a_log", "e_d", "e_ssm_norm", "e_w_out",
               "o_w_in", "o_dw_w", "o_dw_b", "o_ln_g", "o_ln_b", "o_a_re", "o_a_im", "o_b_re", "o_b_im", "o_c_re", "o_c_im",
               "o_d", "o_log_step", "o_glu_w", "o_w_out"]


def kernel(**inputs):
    bld = Builder()
    nc = bld.build()
    consts = make_consts()
    in_maps = []
    NCORE = 4
    for core in range(NCORE):
        bidx = core % 4
        m = {"x": np.ascontiguousarray(inputs["x"][bidx])}
        for k in INPUT_NAMES:
            m[k] = np.ascontiguousarray(inputs[k])
        for k, v in consts.items():
            m["c_" + k] = v
        in_maps.append(m)
    res = run_bass_kernel_spmd(nc, in_maps, core_ids=list(range(NCORE)))
    out = np.stack([res.results[i]["y"] for i in range(4)], axis=0)
    return out.astype(np.float32)
```

```python
import math
from contextlib import ExitStack
import numpy as np
import concourse.bass as bass
import concourse.mybir as mybir
from concourse.bass_utils import run_bass_kernel_spmd

F32 = mybir.dt.float32
BF16 = mybir.dt.bfloat16
ALU = mybir.AluOpType
AF = mybir.ActivationFunctionType

D = 1024
SEQ = 4096
NB = 512
NCH = NB // 128
DFF = 2816
EPS = 1e-6
SLOT = 4096
NSLOT = 4
import os
SKIP = set(os.environ.get('MK_SKIP', '').split(','))
SSD_STOP = int(os.environ.get('MK_SSD_STOP', '99'))


class Sem:
    def __init__(self, h, name):
        self.h = h
        self.name = name
        self.count = 0


class Res:
    __slots__ = ("name", "w", "r", "dsem", "excl")

    def __init__(self, name, excl=False):
        self.name = name
        self.excl = excl
        self.w = {}
        self.r = {}
        self.dsem = None


class Eng:
    def __init__(self, name, eng, sem):
        self.name = name
        self.eng = eng
        self.sem = sem
        self.known = {}


class Ctx:
    def __init__(self, nc, stack):
        self.nc = nc
        self.stack = stack
        self.nsem = 0
        self.PE = self._eng("pe", nc.tensor)
        self.ACT = self._eng("act", nc.scalar)
        self.DVE = self._eng("dve", nc.vector)
        self.POOL = self._eng("pool", nc.gpsimd)
        self.SP = self._eng("sp", nc.sync)
        self.compute = [self.PE, self.ACT, self.DVE, self.POOL]
        self.dsems = []

    def new_sem(self, name):
        self.nsem += 1
        h = self.stack.enter_context(self.nc.semaphore(name))
        return Sem(h, name)

    def _eng(self, name, eng):
        return Eng(name, eng, self.new_sem("s_" + name))

    def _wait(self, E, need):
        for s, (v, snap) in need.items():
            if E.known.get(s, 0) >= v:
                continue
            E.eng.wait_ge(s.h, v)
            E.known[s] = v
            for s2, v2 in snap.items():
                if E.known.get(s2, 0) < v2:
                    E.known[s2] = v2

    def _collect(self, E, reads, writes):
        need = {}

        def req(s, ev):
            if s not in need or need[s][0] < ev[0]:
                need[s] = ev

        for r in reads:
            for s, ev in r.w.items():
                req(s, ev)
            if r.excl:
                for s, ev in r.r.items():
                    if s is not E.sem:
                        req(s, ev)
        for w in writes:
            for s, ev in w.w.items():
                if s is E.sem:
                    continue
                req(s, ev)
            for s, ev in w.r.items():
                if s is E.sem:
                    continue
                req(s, ev)
        return need

    hook = None
    hook_n = 0

    def _tick(self):
        if self.hook is not None:
            self.hook_n += 1
            if self.hook_n % 6 == 0:
                h, self.hook = self.hook, None
                h()
                if self.hook is None:
                    self.hook = h

    dry = False

    def op(self, E, fn, r=(), w=()):
        if self.dry:
            return None
        self._tick()
        self._wait(E, self._collect(E, r, w))
        ins = fn(E.eng)
        E.sem.count += 1
        ins.then_inc(E.sem.h, 1)
        ev = (E.sem.count, dict(E.known))
        for x in r:
            x.r[E.sem] = ev
        for x in w:
            x.w[E.sem] = ev
            x.r = {}
        return ins

    def dma(self, E, out, in_, r=(), w=(), sem_res=None, **kw):
        if self.dry:
            return None
        self._tick()
        self._wait(E, self._collect(E, r, w))
        if sem_res.dsem is None:
            sem_res.dsem = self.new_sem("d_" + sem_res.name)
            self.dsems.append(sem_res.dsem)
        ds = sem_res.dsem
        ins = E.eng.dma_start(out=out, in_=in_, **kw)
        ds.count += 16
        ins.then_inc(ds.h, 16)
        ev = (ds.count, dict(E.known))
        for x in r:
            x.r[ds] = ev
        for x in w:
            x.w[ds] = ev
            x.r = {}
        return ins

    def barrier(self, engines=None, with_dma=False, dma_res=None):
        if self.dry:
            return
        engines = engines or self.compute
        for E in engines:
            need = {}
            for E2 in engines:
                if E2 is not E and E2.sem.count > 0:
                    need[E2.sem] = (E2.sem.count, dict(E2.known))
            if dma_res is not None:
                for rr in dma_res:
                    if rr.dsem is not None and rr.dsem.count > 0:
                        need[rr.dsem] = (rr.dsem.count, {})
            elif with_dma:
                for ds in self.dsems:
                    if ds.count > 0:
                        need[ds] = (ds.count, {})
            self._wait(E, need)


def make_consts():
    c = {}
    c["ident_f"] = np.eye(128, dtype=np.float32)
    c["ones_f"] = np.ones((128, 128), dtype=np.float32)
    inv = (10000.0 ** (-np.arange(0, 128, 2, dtype=np.float32) / np.float32(128))).astype(np.float32)
    pos = np.arange(SEQ, dtype=np.float32)
    ang = (pos[:, None] * inv[None, :]).astype(np.float32).astype(np.float64)
    c["cosT"] = np.ascontiguousarray(np.cos(ang).reshape(SEQ // 128, 128, 64).transpose(1, 0, 2)).astype(np.float32)
    c["sinT"] = np.ascontiguousarray(np.sin(ang).reshape(SEQ // 128, 128, 64).transpose(1, 0, 2)).astype(np.float32)
    gam = 1.0 - 2.0 ** (-5.0 - np.arange(4, dtype=np.float64))
    idx = np.arange(128, dtype=np.float64)
    diff = idx[None, :] - idx[:, None]
    dm = np.where(diff[None] >= 0, gam[:, None, None] ** np.maximum(diff, 0)[None], 0.0) * (128 ** -0.5)
    c["dmaskT"] = np.ascontiguousarray(dm.transpose(1, 0, 2)).reshape(128, 512).astype(np.float32)
    c["xi"] = (gam[None, :] ** (idx[:, None] + 1)).astype(np.float32)
    c["zeta"] = ((gam[None, :] ** (127 - idx[:, None])) * (128 ** -0.5)).astype(np.float32)
    tri = (idx[:, None] <= idx[None, :]).astype(np.float32)
    c["tri"] = tri
    c["su"] = (1.0 - tri).astype(np.float32)
    sj = np.arange(128) // 16
    c["tzmask"] = (sj[None, :] >= sj[:, None]).astype(np.float32)
    c["cidx"] = np.tile(np.arange(1, 65, dtype=np.float32)[None, :], (128, 1))
    return c


RET_GDEC = [float((1.0 - 2.0 ** (-5.0 - h)) ** 128) for h in range(4)]


class Builder:
    def __init__(self, nblk=SEQ // NB, stages=("even", "ffn0", "odd", "ffn1", "final"), dbg=()):
        self.nblk = nblk
        self.stages = stages
        self.dbg = dbg
        self.ntok = nblk * NB
        self.nc = bass.Bass("TRN2", target_bir_lowering=False)
        self.stack = ExitStack()
        self.cx = Ctx(self.nc, self.stack)
        self.units = []
        self.unit_off = 0
        self.stream_order = []
        self.bank_i = 0
        self.bankb_i = 0

    def din(self, name, shape, dt=F32):
        return self.nc.dram_tensor(name, list(shape), dt, kind="ExternalInput").ap()

    def declare_io(self):
        nc = self.nc
        self.x = self.din("x", [self.ntok, D])
        self.y = nc.dram_tensor("y", [self.ntok, D], F32, kind="ExternalOutput").ap()
        self.mix_norm = self.din("mix_norm", [2, D])
        self.ffn_norm = self.din("ffn_norm", [2, D])
        self.final_norm = self.din("final_norm", [D])
        self.ffn_w_up = self.din("ffn_w_up", [2, D, 2 * DFF])
        self.ffn_dw_w = self.din("ffn_dw_w", [2, 3, 2 * DFF])
        self.ffn_dw_b = self.din("ffn_dw_b", [2, 2 * DFF])
        self.ffn_w_down = self.din("ffn_w_down", [2, DFF, D])
        self.e_w_in = self.din("e_w_in", [1, D, 5648])
        self.e_conv_w = self.din("e_conv_w", [1, 4, 1536])
        self.e_conv_b = self.din("e_conv_b", [1, 1536])
        self.e_dt_bias = self.din("e_dt_bias", [1, 16])
        self.e_a_log = self.din("e_a_log", [1, 16])
        self.e_d = self.din("e_d", [1, 16])
        self.e_ssm_norm = self.din("e_ssm_norm", [1, 1024])
        self.e_w_out = self.din("e_w_out", [1, 2048, 1024])
        self.o_w_in = self.din("o_w_in", [1, D, 1536])
        self.o_dw_w = self.din("o_dw_w", [1, 31, 512])
        self.o_dw_b = self.din("o_dw_b", [1, 512])
        self.o_ln_g = self.din("o_ln_g", [1, 512])
        self.o_ln_b = self.din("o_ln_b", [1, 512])
        self.o_a_re = self.din("o_a_re", [1, 32, 64])
        self.o_a_im = self.din("o_a_im", [1, 32, 64])
        self.o_b_re = self.din("o_b_re", [1, 32, 64, 16])
        self.o_b_im = self.din("o_b_im", [1, 32, 64, 16])
        self.o_c_re = self.din("o_c_re", [1, 32, 16, 64])
        self.o_c_im = self.din("o_c_im", [1, 32, 16, 64])
        self.o_d = self.din("o_d", [1, 512])
        self.o_log_step = self.din("o_log_step", [1, 32])
        self.o_glu_w = self.din("o_glu_w", [1, 512, 512])
        self.o_w_out = self.din("o_w_out", [1, D, D])
        self.consts = {}
        for k, v in make_consts().items():
            self.consts[k] = self.din("c_" + k, v.shape)

    def sb(self, name, shape, dt=F32, stack=None):
        if stack is None:
            if not hasattr(self, "persist"):
                self.persist = {}
            if name not in self.persist:
                t = self.stack.enter_context(self.nc.sbuf_tensor(f"{name}_p", list(shape), dt))
                self.persist[name] = (t, Res(name))
            return self.persist[name]
        self.name_i = getattr(self, "name_i", 0) + 1
        t = stack.enter_context(self.nc.sbuf_tensor(f"{name}_{self.name_i}", list(shape), dt))
        return t, Res(name)

    def ps(self, name, shape, dt=F32):
        if not hasattr(self, "persist_ps"):
            self.persist_ps = {}
        if name not in self.persist_ps:
            t = self.stack.enter_context(self.nc.psum_tensor(name, list(shape), dt))
            self.persist_ps[name] = (t, Res(name, excl=True))
        return self.persist_ps[name]

    def bank(self):
        i = self.bank_i
        self.bank_i = (i + 1) % len(self.banks)
        return self.banks[i]

    def bankb(self):
        i = self.bankb_i
        self.bankb_i = (i + 1) % len(self.banksb)
        return self.banksb[i]

    def add_unit(self, name, pieces):
        L = sum(kt * w for (_, kt, _, w) in pieces)
        assert L <= SLOT, (name, L)
        u = dict(name=name, L=L, off=self.unit_off, pieces=pieces)
        self.unit_off += 128 * L
        self.units.append(u)
        return u

    def declare_units(self):
        self.U = {}
        for l in range(2):
            wup = self.ffn_w_up[l]
            for j in range(22):
                self.U[("up", l, j)] = self.add_unit(f"up{l}_{j}", [(wup, 8, j * 128, 128), (wup, 8, (22 + j) * 128, 128)])
            wdn = self.ffn_w_down[l]
            for ft in range(8):
                self.U[("dn", l, ft)] = self.add_unit(f"dn{l}_{ft}", [(wdn, 22, ft * 128, 128)])

        w = self.e_w_in[0]
        for i, nm in enumerate(["eq", "ek", "ev0", "ev1", "eg0", "eg1", "ez0", "ez1"]):
            self.U[nm] = self.add_unit(nm, [(w, 8, 512 * i, 512)])
        self.U["edt"] = self.add_unit("edt", [(w, 8, 5632, 16)])
        for i in range(6):
            self.U[("exbc", i)] = self.add_unit(f"exbc{i}", [(w, 8, 4096 + (2 * i) * 128, 128), (w, 8, 4096 + (2 * i + 1) * 128, 128)])
        for ft in range(8):
            self.U[("eout", ft)] = self.add_unit(f"eout{ft}", [(self.e_w_out[0], 16, ft * 128, 128)])

        w = self.o_w_in[0]
        for i in range(4):
            self.U[("oin", i)] = self.add_unit(f"oin{i}", [(w, 8, i * 128, 128), (w, 8, 512 + i * 128, 128)])
        self.U["ou"] = self.add_unit("ou", [(w, 8, 1024, 512)])
        self.U["oglu"] = self.add_unit("oglu", [(self.o_glu_w[0], 4, ot * 128, 128) for ot in range(4)])
        for i in range(4):
            self.U[("oout", i)] = self.add_unit(f"oout{i}", [(self.o_w_out[0], 8, (2 * i) * 128, 128), (self.o_w_out[0], 8, (2 * i + 1) * 128, 128)])
        self.n_cast_units = len(self.units)
        for nm in [("s5bc", 0), ("s5bc", 1), ("s5cc", 0), ("s5cc", 1), "s5tz"] + [("ocv", i) for i in range(4)]:
            self.U[nm] = dict(name=str(nm), L=4096, off=self.unit_off, pieces=[])
            self.unit_off += 128 * 4096

    def plan_stream(self):
        order = []
        for b in range(self.nblk):
            if "even" in self.stages:
                order += [self.U[("exbc", i)] for i in range(6)]
                order += [self.U[nm] for nm in ["eq", "ek", "ev0", "ev1", "eg0", "eg1", "ez0", "ez1", "edt"]]
                order += [self.U[("eout", ft)] for ft in range(8)]
            for l in range(2):
                if l == 1 and "odd" in self.stages:
                    order += [self.U["ou"], self.U[("s5bc", 0)], self.U[("s5bc", 1)]]
                    for i in range(4):
                        order += [self.U[("oin", i)], self.U[("ocv", i)]]
                    order += [self.U["s5tz"], self.U[("s5cc", 0)], self.U[("s5cc", 1)], self.U["oglu"]]
                    order += [self.U[("oout", i)] for i in range(4)]
                if f"ffn{l}" in self.stages:
                    order += [self.U[("up", l, j)] for j in range(22)]
                    order += [self.U[("dn", l, ft)] for ft in range(8)]
        self.stream_order = order
        self.stream_next_load = 0
        self.stream_next_use = 0

    def stream_get(self, hold_prev=0):
        cx = self.cx
        i = self.stream_next_use
        self.stream_next_use += 1
        lim = min(i - hold_prev + NSLOT - 1, len(self.stream_order) - 1)
        while self.stream_next_load <= lim:
            k = self.stream_next_load
            u = self.stream_order[k]
            st, sr = self.slots[k % NSLOT]
            src = self.wscr[u["off"]: u["off"] + 128 * u["L"]].rearrange("(p l) -> p l", p=128)
            cx.dma(cx.SP, st[:, 0:u["L"]], src, w=[sr], sem_res=sr)
            self.stream_next_load += 1
        st, sr = self.slots[i % NSLOT]
        assert self.stream_order[i] is not None
        return st, sr, self.stream_order[i]

    def prologue_swdge(self):
        cx = self.cx
        units = self.units[:self.n_cast_units]
        for u in units:
            off = 0
            for (src, kt, c0, w) in u["pieces"]:
                dst = self.wscr[u["off"]: u["off"] + 128 * u["L"]].rearrange("(p l) -> p l", p=128)[:, off: off + kt * w].rearrange("p (k w) -> p k w", k=kt)
                s_ = src.rearrange("(k p) n -> p k n", p=128)[:, :, c0:c0 + w]
                cx.dma(cx.POOL, dst, s_, sem_res=self.wscr_res)
                off += kt * w
        return
        yield

    def prologue(self):
        if "noswdge" not in SKIP:
            yield from self.prologue_swdge()
            return
        cx, nc = self.cx, self.nc
        with ExitStack() as st:
            NS = 3
            stg32 = [self.sb(f"stg32_{i}", [128, SLOT], F32, st) for i in range(NS)]
            stg16 = [self.sb(f"stg16_{i}", [128, SLOT], BF16, st) for i in range(NS)]
            cast_engs = [cx.ACT, cx.DVE]
            units = self.units[:self.n_cast_units]

            def load(i):
                u = units[i]
                t32, r32 = stg32[i % NS]
                off = 0
                for (src, kt, c0, w) in u["pieces"]:
                    dst = t32[:, off: off + kt * w].rearrange("p (k w) -> p k w", k=kt)
                    s_ = src.rearrange("(k p) n -> p k n", p=128)[:, :, c0:c0 + w]
                    cx.dma(cx.SP, dst, s_, w=[r32], sem_res=r32)
                    off += kt * w

            def cast_store(i):
                u = units[i]
                t32, r32 = stg32[i % NS]
                t16, r16 = stg16[i % NS]
                L = u["L"]
                E = cast_engs[i % len(cast_engs)]
                if E is cx.ACT:
                    cx.op(E, lambda e: e.copy(out=t16[:, 0:L], in_=t32[:, 0:L]), r=[r32], w=[r16])
                else:
                    cx.op(E, lambda e: e.tensor_copy(out=t16[:, 0:L], in_=t32[:, 0:L]), r=[r32], w=[r16])
                dst = self.wscr[u["off"]: u["off"] + 128 * L].rearrange("(p l) -> p l", p=128)
                cx.dma(cx.SP, dst, t16[:, 0:L], r=[r16], sem_res=r16)

            n = len(units)
            for i in range(min(NS - 1, n)):
                load(i)
            for i in range(n):
                if i + NS - 1 < n:
                    load(i + NS - 1)
                cast_store(i)
                yield
            while not getattr(self, "pro_finish", False):
                yield
            cx.barrier(engines=cx.compute + [cx.SP], with_dma=True)

    def setup(self):
        cx, nc = self.cx, self.nc
        self.xT, self.xT_r = self.sb("xT", [128, 8, NB], F32)
        self.hnT, self.hnT_r = self.sb("hnT", [128, 8, NB], BF16)
        self.ident_f, self.ident_f_r = self.sb("ident_f", [128, 128], F32)
        self.ones_f, self.ones_f_r = self.sb("ones_f", [128, 128], F32)
        self.gains, self.gains_r = self.sb("gains", [128, 5, 8], F32)
        self.slots = [self.sb(f"wslot{i}", [128, SLOT], BF16) for i in range(NSLOT)]
        self.banks = [self.ps(f"bank{i}", [128, 512], F32) for i in range(6)]
        self.banksb = [self.ps(f"bankb{i}", [128, 1024], BF16) for i in range(2)]
        self.ffn_halo, self.ffn_halo_r = self.sb("ffn_halo", [128, 2, 44, 2], F32)
        self.ffn_cw, self.ffn_cw_r = self.sb("ffn_cw", [128, 2, 44, 3], F32)
        self.ffn_cb, self.ffn_cb_r = self.sb("ffn_cb", [128, 2, 44], F32)
        self.eps_t, self.eps_r = self.sb("eps_t", [128, 1], F32)
        nc.allow_non_contiguous_dma(reason="small parameter loads")
        SP = cx.SP
        cx.dma(SP, self.ident_f[:], self.consts["ident_f"], w=[self.ident_f_r], sem_res=self.ident_f_r)
        cx.dma(SP, self.ones_f[:], self.consts["ones_f"], w=[self.ones_f_r], sem_res=self.ones_f_r)
        gsrc = [self.mix_norm[0], self.ffn_norm[0], self.mix_norm[1], self.ffn_norm[1], self.final_norm]
        for i, g in enumerate(gsrc):
            cx.dma(SP, self.gains[:, i, :], g.rearrange("(t p) -> p t", p=128), w=[self.gains_r], sem_res=self.gains_r, allow_slow_non_contiguous=True)
        for l in range(2):
            for k in range(3):
                cx.dma(SP, self.ffn_cw[:, l, :, k], self.ffn_dw_w[l, k].rearrange("(t p) -> p t", p=128),
                       w=[self.ffn_cw_r], sem_res=self.ffn_cw_r, allow_slow_non_contiguous=True)
            cx.dma(SP, self.ffn_cb[:, l, :], self.ffn_dw_b[l].rearrange("(t p) -> p t", p=128),
                   w=[self.ffn_cb_r], sem_res=self.ffn_cb_r, allow_slow_non_contiguous=True)
        cx.op(cx.DVE, lambda e: e.memset(self.ffn_halo[:], 0.0), w=[self.ffn_halo_r])
        cx.op(cx.DVE, lambda e: e.memset(self.eps_t[:], EPS), w=[self.eps_r])
        self.ones_b, self.ones_b_r = self.sb("ones_b", [128, 128], BF16)
        cx.op(cx.DVE, lambda e: e.memset(self.ones_b[:], 1.0), w=[self.ones_b_r])
        self.one_t, _ = self.sb("one_t", [128, 1], F32)
        cx.op(cx.DVE, lambda e: e.memset(self.one_t[:], 1.0), w=[self.eps_r])

    def load_block(self, b):
        cx = self.cx
        with ExitStack() as st:
            xin, xin_r = self.sb("xin", [128, NCH, D], F32, st)
            src = self.x[b * NB:(b + 1) * NB, :].rearrange("(c p) d -> p c d", p=128)
            cx.dma(cx.SP, xin[:], src, w=[xin_r], sem_res=self.xT_r)
            for ft in range(8):
                bk, bk_r = self.bank()
                for c in range(NCH):
                    cx.op(cx.PE, lambda e: e.transpose(bk[:, c * 128:(c + 1) * 128], xin[:, c, ft * 128:(ft + 1) * 128], self.ident_f[:]),
                          r=[xin_r, self.ident_f_r], w=[bk_r])
                if ft % 2 == 0:
                    cx.op(cx.ACT, lambda e: e.copy(out=self.xT[:, ft, :], in_=bk[:]), r=[bk_r], w=[self.xT_r])
                else:
                    cx.op(cx.DVE, lambda e: e.tensor_copy(out=self.xT[:, ft, :], in_=bk[:]), r=[bk_r], w=[self.xT_r])
            cx.barrier()

    def store_block(self, b, srcT, srcT_r):
        cx = self.cx
        with ExitStack() as st:
            yo, yo_r = self.sb("yo", [128, NCH, D], F32, st)
            k = 0
            for c in range(NCH):
                for half in range(2):
                    bk, bk_r = self.bank()
                    for f4 in range(4):
                        ft = half * 4 + f4
                        cx.op(cx.PE, lambda e: e.transpose(bk[:, f4 * 128:(f4 + 1) * 128], srcT[:, ft, c * 128:(c + 1) * 128], self.ident_f[:]),
                              r=[srcT_r, self.ident_f_r], w=[bk_r])
                    if k % 2 == 0:
                        cx.op(cx.ACT, lambda e: e.copy(out=yo[:, c, half * 512:(half + 1) * 512], in_=bk[:]), r=[bk_r], w=[yo_r])
                    else:
                        cx.op(cx.DVE, lambda e: e.tensor_copy(out=yo[:, c, half * 512:(half + 1) * 512], in_=bk[:]), r=[bk_r], w=[yo_r])
                    k += 1
            dst = self.y[b * NB:(b + 1) * NB, :].rearrange("(c p) d -> p c d", p=128)
            cx.dma(cx.SP, dst, yo[:], r=[yo_r], w=[self.y_res], sem_res=self.y_res)
            cx.barrier(engines=cx.compute + [cx.SP], with_dma=True)

    def rmsnorm(self, gi, outT, outT_r, out_f32=False):
        cx = self.cx
        with ExitStack() as st:
            sq = [self.sb(f"sq{i}", [128, NB], BF16, st) for i in range(4)]
            rstd, rstd_r = self.sb("rstd", [128, NB], F32, st)
            bk, bk_r = self.bank()
            for ft in range(8):
                s, s_r = sq[ft % 4]
                if ft % 2 == 0:
                    cx.op(cx.ACT, lambda e: e.activation(out=s[:], in_=self.xT[:, ft, :], func=AF.Square), r=[self.xT_r], w=[s_r])
                else:
                    cx.op(cx.DVE, lambda e: e.tensor_tensor(out=s[:], in0=self.xT[:, ft, :], in1=self.xT[:, ft, :], op=ALU.mult), r=[self.xT_r], w=[s_r])
                cx.op(cx.PE, lambda e: e.matmul(bk[:], lhsT=self.ones_b[:], rhs=s[:], start=(ft == 0), stop=(ft == 7)),
                      r=[s_r, self.ones_b_r], w=[bk_r])
            cx.op(cx.ACT, lambda e: e.activation(out=rstd[:], in_=bk[:], func=AF.Sqrt, bias=self.eps_t[:, 0:1], scale=1.0 / D),
                  r=[bk_r, self.eps_r], w=[rstd_r])
            cx.op(cx.DVE, lambda e: e.reciprocal(out=rstd[:], in_=rstd[:]), r=[rstd_r], w=[rstd_r])
            for ft in range(8):
                E = cx.DVE
                cx.op(E, lambda e: e.scalar_tensor_tensor(out=outT[:, ft, :], in0=self.xT[:, ft, :], scalar=self.gains[:, gi, ft:ft + 1],
                                                          in1=rstd[:], op0=ALU.mult, op1=ALU.mult),
                      r=[self.xT_r, self.gains_r, rstd_r], w=[outT_r])
            cx.barrier()

    def ffn(self, l, b):
        cx = self.cx
        self.rmsnorm(1 + 2 * l, self.hnT, self.hnT_r)
        with ExitStack() as st:
            hT, hT_r = self.sb("hT", [128, 22, NB], BF16, st)
            raws = [self.sb(f"raw{i}", [128, NB + 2], F32, st) for i in range(4)]
            accs = [self.sb(f"acc{i}", [128, NB], F32, st) for i in range(4)]
            sg = [self.sb(f"sgate{i}", [128, NB], F32, st) for i in range(2)]
            for j in range(22):
                slot, slot_r, u = self.stream_get()
                assert u is self.U[("up", l, j)]
                res = []
                for gi in range(2):
                    tile = j + 22 * gi
                    wv = slot[:, gi * 1024:(gi + 1) * 1024].rearrange("p (k w) -> p k w", k=8)
                    bk, bk_r = self.bank()
                    for kt in range(8):
                        cx.op(cx.PE, lambda e: e.matmul(bk[:], lhsT=wv[:, kt, :], rhs=self.hnT[:, kt, :], start=(kt == 0), stop=(kt == 7)),
                              r=[slot_r, self.hnT_r], w=[bk_r])
                    raw, raw_r = raws[(2 * j + gi) % 4]
                    acc, acc_r = accs[(2 * j + gi) % 4]
                    cw = self.ffn_cw[:, l, tile, :]
                    cx.op(cx.ACT, lambda e: e.copy(out=raw[:, 2:NB + 2], in_=bk[:]), r=[bk_r], w=[raw_r])
                    cx.op(cx.POOL, lambda e: e.tensor_copy(out=raw[:, 0:2], in_=self.ffn_halo[:, l, tile, :]), r=[self.ffn_halo_r], w=[raw_r])
                    cx.op(cx.ACT, lambda e: e.activation(out=acc[:], in_=bk[:], func=AF.Identity, bias=self.ffn_cb[:, l, tile:tile + 1],
                                                         scale=cw[:, 2:3]),
                          r=[bk_r, self.ffn_cw_r, self.ffn_cb_r], w=[acc_r])
                    E = cx.DVE
                    cx.op(E, lambda e: e.scalar_tensor_tensor(out=acc[:], in0=raw[:, 1:NB + 1], scalar=cw[:, 1:2], in1=acc[:],
                                                              op0=ALU.mult, op1=ALU.add),
                          r=[raw_r, acc_r, self.ffn_cw_r], w=[acc_r])
                    cx.op(E, lambda e: e.scalar_tensor_tensor(out=acc[:], in0=raw[:, 0:NB], scalar=cw[:, 0:1], in1=acc[:],
                                                              op0=ALU.mult, op1=ALU.add),
                          r=[raw_r, acc_r, self.ffn_cw_r], w=[acc_r])
                    cx.op(cx.POOL, lambda e: e.tensor_copy(out=self.ffn_halo[:, l, tile, :], in_=raw[:, NB:NB + 2]), r=[raw_r], w=[self.ffn_halo_r])
                    res.append((acc, acc_r))
                s, s_r = sg[j % 2]
                cx.op(cx.ACT, lambda e: e.activation(out=s[:], in_=res[0][0][:], func=AF.Silu), r=[res[0][1]], w=[s_r])
                cx.op(cx.DVE, lambda e: e.tensor_tensor(out=hT[:, j, :], in0=s[:], in1=res[1][0][:], op=ALU.mult),
                      r=[s_r, res[1][1]], w=[hT_r])
            for ft in range(8):
                slot, slot_r, u = self.stream_get()
                assert u is self.U[("dn", l, ft)]
                wv = slot[:, 0:22 * 128].rearrange("p (k w) -> p k w", k=22)
                bk, bk_r = self.bank()
                for kt in range(22):
                    cx.op(cx.PE, lambda e: e.matmul(bk[:], lhsT=wv[:, kt, :], rhs=hT[:, kt, :], start=(kt == 0), stop=(kt == 21)),
                          r=[slot_r, hT_r], w=[bk_r])
                cx.op(cx.DVE, lambda e: e.tensor_tensor(out=self.xT[:, ft, :], in0=self.xT[:, ft, :], in1=bk[:], op=ALU.add),
                      r=[bk_r, self.xT_r], w=[self.xT_r])
            cx.barrier()

    def bc_mid(self, ap2, n):
        a = ap2.shape[1]
        return ap2.unsqueeze(2).to_broadcast([128, a, n])

    def bc_h(self, ap2, h):
        n = ap2.shape[1]
        return ap2.unsqueeze(1).to_broadcast([128, h, n])

    def setup_even(self):
        cx = self.cx
        SP = cx.SP
        self.retS, self.retS_r = self.sb("retS", [128, 4, 256], F32)
        self.retSb, self.retSb_r = self.sb("retSb", [128, 4, 256], BF16)
        self.ssS, self.ssS_r = self.sb("ssS", [128, 2, 512], F32)
        self.ssSb, self.ssSb_r = self.sb("ssSb", [128, 2, 512], BF16)
        self.exh, self.exh_r = self.sb("exh", [128, 12, 3], F32)
        self.exw, self.exw_r = self.sb("exw", [128, 12, 4], F32)
        self.exb, self.exb_r = self.sb("exb", [128, 12], F32)
        self.dmaskT, self.dmaskT_r = self.sb("dmaskT", [128, 512], F32)
        self.xi, self.xi_r = self.sb("xi", [128, 4], F32)
        self.zeta, self.zeta_r = self.sb("zeta", [128, 4], F32)
        self.tri, self.tri_r = self.sb("tri", [128, 128], F32)
        self.su, self.su_r = self.sb("su", [128, 128], F32)
        self.ident_b, self.ident_b_r = self.sb("ident_b", [128, 128], BF16)
        self.dtb, self.dtb_r = self.sb("dtb", [128, 16], F32)
        self.aneg, self.aneg_r = self.sb("aneg", [128, 16], F32)
        self.dsk, self.dsk_r = self.sb("dsk", [128, 16], F32)
        self.ssmn, self.ssmn_r = self.sb("ssmn", [128, 8], F32)
        for t, r, nm in [(self.dmaskT, self.dmaskT_r, "dmaskT"), (self.xi, self.xi_r, "xi"), (self.zeta, self.zeta_r, "zeta"),
                         (self.tri, self.tri_r, "tri"), (self.su, self.su_r, "su")]:
            cx.dma(SP, t[:], self.consts[nm], w=[r], sem_res=r)
        cx.dma(SP, self.dtb[:], self.e_dt_bias[0].partition_broadcast(128), w=[self.dtb_r], sem_res=self.dtb_r, allow_slow_non_contiguous=True)
        cx.dma(SP, self.aneg[:], self.e_a_log[0].partition_broadcast(128), w=[self.aneg_r], sem_res=self.aneg_r, allow_slow_non_contiguous=True)
        cx.dma(SP, self.dsk[:], self.e_d[0].partition_broadcast(128), w=[self.dsk_r], sem_res=self.dsk_r, allow_slow_non_contiguous=True)
        cx.dma(SP, self.ssmn[:], self.e_ssm_norm[0].rearrange("(t p) -> p t", p=128), w=[self.ssmn_r], sem_res=self.ssmn_r,
               allow_slow_non_contiguous=True)
        for k in range(4):
            cx.dma(SP, self.exw[:, :, k], self.e_conv_w[0, k].rearrange("(t p) -> p t", p=128), w=[self.exw_r], sem_res=self.exw_r,
                   allow_slow_non_contiguous=True)
        cx.dma(SP, self.exb[:], self.e_conv_b[0].rearrange("(t p) -> p t", p=128), w=[self.exb_r], sem_res=self.exb_r,
               allow_slow_non_contiguous=True)
        cx.op(cx.ACT, lambda e: e.activation(out=self.aneg[:], in_=self.aneg[:], func=AF.Exp), r=[self.aneg_r], w=[self.aneg_r])
        cx.op(cx.DVE, lambda e: e.tensor_scalar(out=self.aneg[:], in0=self.aneg[:], scalar1=-1.0, scalar2=None, op0=ALU.mult),
              r=[self.aneg_r], w=[self.aneg_r])
        cx.op(cx.DVE, lambda e: e.tensor_copy(out=self.ident_b[:], in_=self.ident_f[:]), r=[self.ident_f_r], w=[self.ident_b_r])
        cx.op(cx.DVE, lambda e: e.memset(self.retS[:], 0.0), w=[self.retS_r])
        cx.op(cx.DVE, lambda e: e.memset(self.retSb[:], 0.0), w=[self.retSb_r])
        cx.op(cx.DVE, lambda e: e.memset(self.ssS[:], 0.0), w=[self.ssS_r])
        cx.op(cx.DVE, lambda e: e.memset(self.ssSb[:], 0.0), w=[self.ssSb_r])
        cx.op(cx.DVE, lambda e: e.memset(self.exh[:], 0.0), w=[self.exh_r])

    def dump(self, name, src_ap, src_r, dst_ap):
        cx = self.cx
        r = self.dbg_res.setdefault(name, Res("dbg_" + name))
        cx.dma(cx.SP, dst_ap, src_ap, r=[src_r], w=[r], sem_res=r)

    def even(self, b):
        cx = self.cx
        PE, ACT, DVE, POOL, SP = cx.PE, cx.ACT, cx.DVE, cx.POOL, cx.SP
        self.rmsnorm(0, self.hnT, self.hnT_r)
        hnT, hnT_r = self.hnT, self.hnT_r
        with ExitStack() as st:
            B = {}
            for nm, shp, dt in [("qr", [128, NCH, 512], BF16), ("qx", [128, NCH, 512], BF16), ("kr", [128, NCH, 512], BF16),
                                ("kz", [128, NCH, 512], BF16), ("v", [128, NCH, 1024], BF16), ("sg", [128, NCH, 1024], BF16),
                                ("sz", [128, NCH, 1024], BF16), ("xcT", [128, 8, NB], BF16), ("bcT", [128, 4, NB], BF16),
                                ("dt", [128, NCH, 16], F32), ("dta", [128, NCH, 16], F32), ("mixT", [128, 16, NB], BF16),
                                ("cos", [128, NCH, 64], F32), ("sin", [128, NCH, 64], F32)]:
                B[nm] = self.sb(nm, shp, dt, st)
            self.EB = B
            cos, cos_r = B["cos"]
            sin, sin_r = B["sin"]
            cx.dma(SP, cos[:], self.consts["cosT"][:, b * NCH:(b + 1) * NCH, :], w=[cos_r], sem_res=self.xi_r)
            cx.dma(SP, sin[:], self.consts["sinT"][:, b * NCH:(b + 1) * NCH, :], w=[sin_r], sem_res=self.xi_r)
            rot_t = [self.sb(f"rot{i}", [128, 4, 64], F32, st) for i in range(4)]
            qraw = [self.sb(f"qraw{i}", [128, 512], F32, st) for i in range(2)]
            raws = [self.sb(f"xraw{i}", [128, NB + 3], F32, st) for i in range(2)]
            accs = [self.sb(f"xacc{i}", [128, NB], F32, st) for i in range(2)]
            xcT, xcT_r = B["xcT"]
            bcT, bcT_r = B["bcT"]
            for i in range(6):
                slot, slot_r, u = self.stream_get()
                assert u is self.U[("exbc", i)]
                for gi in range(2):
                    tile = 2 * i + gi
                    wv = slot[:, gi * 1024:(gi + 1) * 1024].rearrange("p (k w) -> p k w", k=8)
                    bk, bk_r = self.bank()
                    for kt in range(8):
                        cx.op(PE, lambda e: e.matmul(bk[:], lhsT=wv[:, kt, :], rhs=hnT[:, kt, :], start=(kt == 0), stop=(kt == 7)),
                              r=[slot_r, hnT_r], w=[bk_r])
                    raw, raw_r = raws[tile % 2]
                    acc, acc_r = accs[tile % 2]
                    cw = self.exw[:, tile, :]
                    cx.op(ACT, lambda e: e.copy(out=raw[:, 3:NB + 3], in_=bk[:]), r=[bk_r], w=[raw_r])
                    cx.op(POOL, lambda e: e.tensor_copy(out=raw[:, 0:3], in_=self.exh[:, tile, :]), r=[self.exh_r], w=[raw_r])
                    cx.op(ACT, lambda e: e.activation(out=acc[:], in_=bk[:], func=AF.Identity, bias=self.exb[:, tile:tile + 1], scale=cw[:, 3:4]),
                          r=[bk_r, self.exw_r, self.exb_r], w=[acc_r])
                    for k in (2, 1, 0):
                        cx.op(DVE, lambda e: e.scalar_tensor_tensor(out=acc[:], in0=raw[:, k:k + NB], scalar=cw[:, k:k + 1], in1=acc[:],
                                                                    op0=ALU.mult, op1=ALU.add),
                              r=[raw_r, acc_r, self.exw_r], w=[acc_r])
                    cx.op(POOL, lambda e: e.tensor_copy(out=self.exh[:, tile, :], in_=raw[:, NB:NB + 3]), r=[raw_r], w=[self.exh_r])
                    if tile < 8:
                        cx.op(ACT, lambda e: e.activation(out=xcT[:, tile, :], in_=acc[:], func=AF.Silu), r=[acc_r], w=[xcT_r])
                    else:
                        cx.op(ACT, lambda e: e.activation(out=bcT[:, tile - 8, :], in_=acc[:], func=AF.Silu), r=[acc_r], w=[bcT_r])
            k_ev = 0
            for nm in ["eq", "ek", "ev0", "ev1", "eg0", "eg1", "ez0", "ez1"]:
                slot, slot_r, u = self.stream_get()
                assert u is self.U[nm]
                wv = slot[:, 0:4096].rearrange("p (k w) -> p k w", k=8)
                for c in range(NCH):
                    bk, bk_r = self.bank()
                    for kt in range(8):
                        cx.op(PE, lambda e: e.matmul(bk[:], lhsT=hnT[:, kt, c * 128:(c + 1) * 128], rhs=wv[:, kt, :], start=(kt == 0), stop=(kt == 7)),
                              r=[slot_r, hnT_r], w=[bk_r])
                    if nm in ("eq", "ek"):
                        raw, raw_r = qraw[k_ev % 2]
                        k_ev += 1
                        cx.op(ACT, lambda e: e.copy(out=raw[:], in_=bk[:]), r=[bk_r], w=[raw_r])
                        r4 = raw[:].rearrange("p (h t d) -> p h t d", h=4, t=2)
                        x1, x2 = r4[:, :, 0, :], r4[:, :, 1, :]
                        cb = self.bc_h(cos[:, c, :], 4)
                        sbb = self.bc_h(sin[:, c, :], 4)
                        (t1, t1r), (t2, t2r), (t3, t3r), (t4, t4r) = rot_t
                        cx.op(POOL, lambda e: e.tensor_tensor(out=t1[:], in0=x1, in1=cb, op=ALU.mult), r=[raw_r, cos_r], w=[t1r])
                        cx.op(POOL, lambda e: e.tensor_tensor(out=t2[:], in0=x2, in1=sbb, op=ALU.mult), r=[raw_r, sin_r], w=[t2r])
                        cx.op(POOL, lambda e: e.tensor_tensor(out=t3[:], in0=x1, in1=sbb, op=ALU.mult), r=[raw_r, sin_r], w=[t3r])
                        cx.op(POOL, lambda e: e.tensor_tensor(out=t4[:], in0=x2, in1=cb, op=ALU.mult), r=[raw_r, cos_r], w=[t4r])
                        dst, dst_r = B["qr"] if nm == "eq" else B["kr"]
                        d4 = dst[:, c, :].rearrange("p (h t d) -> p h t d", h=4, t=2)
                        cx.op(DVE, lambda e: e.tensor_tensor(out=d4[:, :, 0, :], in0=t1[:], in1=t2[:], op=ALU.subtract), r=[t1r, t2r], w=[dst_r])
                        cx.op(DVE, lambda e: e.tensor_tensor(out=d4[:, :, 1, :], in0=t3[:], in1=t4[:], op=ALU.add), r=[t3r, t4r], w=[dst_r])
                        d2, d2_r = B["qx"] if nm == "eq" else B["kz"]
                        tab, tab_r = (self.xi, self.xi_r) if nm == "eq" else (self.zeta, self.zeta_r)
                        cx.op(DVE, lambda e: e.tensor_tensor(out=d2[:, c, :].rearrange("p (h d) -> p h d", h=4),
                                                             in0=dst[:, c, :].rearrange("p (h d) -> p h d", h=4),
                                                             in1=self.bc_mid(tab[:], 128), op=ALU.mult),
                              r=[dst_r, tab_r], w=[d2_r])
                    else:
                        half = int(nm[-1])
                        key = {"v": "v", "g": "sg", "z": "sz"}[nm[1]]
                        dst, dst_r = B[key]
                        if key == "v":
                            cx.op(ACT, lambda e: e.copy(out=dst[:, c, half * 512:(half + 1) * 512], in_=bk[:]), r=[bk_r], w=[dst_r])
                        else:
                            cx.op(ACT, lambda e: e.activation(out=dst[:, c, half * 512:(half + 1) * 512], in_=bk[:], func=AF.Silu), r=[bk_r], w=[dst_r])
            slot, slot_r, u = self.stream_get()
            assert u is self.U["edt"]
            wv = slot[:, 0:128].rearrange("p (k w) -> p k w", k=8)
            dtv, dtv_r = self.sb("dtv", [128, 16], F32, st)
            dt, dt_r = B["dt"]
            dta, dta_r = B["dta"]
            for c in range(NCH):
                bk, bk_r = self.bank()
                for kt in range(8):
                    cx.op(PE, lambda e: e.matmul(bk[:, 0:16], lhsT=hnT[:, kt, c * 128:(c + 1) * 128], rhs=wv[:, kt, :], start=(kt == 0), stop=(kt == 7)),
                          r=[slot_r, hnT_r], w=[bk_r])
                cx.op(DVE, lambda e: e.tensor_tensor(out=dtv[:], in0=bk[:, 0:16], in1=self.dtb[:], op=ALU.add), r=[bk_r, self.dtb_r], w=[dtv_r])
                cx.op(ACT, lambda e: e.activation(out=dtv[:], in_=dtv[:], func=AF.Exp), r=[dtv_r], w=[dtv_r])
                cx.op(ACT, lambda e: e.activation(out=dt[:, c, :], in_=dtv[:], func=AF.Ln, bias=self.one_t[:, 0:1]), r=[dtv_r, self.eps_r], w=[dt_r])
                cx.op(DVE, lambda e: e.tensor_tensor(out=dta[:, c, :], in0=dt[:, c, :], in1=self.aneg[:], op=ALU.mult), r=[dt_r, self.aneg_r], w=[dta_r])
            with ExitStack() as st2:
                T = {}
                for nm, shp, dtp in [("qT", [128, 512], BF16), ("qxT", [128, 512], BF16), ("kT", [128, 512], BF16), ("smT", [128, 512], BF16),
                                     ("st6", [128, 4, 6], F32), ("mv", [128, 4, 2], F32), ("rstd4", [128, 4], F32),
                                     ("ytmp", [128, 1024], F32), ("yretb", [128, 1024], BF16),
                                     ("rhsM0", [128, 512], F32), ("rhsM1", [128, 512], F32), ("Lg0", [128, 512], F32), ("Lg1", [128, 512], F32),
                                     ("cbm", [128, 2, 128], F32), ("MT", [128, 16, 128], BF16), ("dec", [128, 16], F32),
                                     ("xs", [128, 1024], BF16), ("Xdt", [128, 1024], BF16), ("Xdec", [128, 1024], BF16), ("Btok", [128, 256], BF16),
                                     ("ea", [128, 32], F32), ("ty", [128, 1024], F32), ("t2", [128, 1024], F32), ("yzb", [128, 1024], BF16),
                                     ("st6g", [128, 2, 6], F32), ("mvg", [128, 2, 2], F32), ("rs2", [128, 2], F32)]:
                    T[nm] = self.sb(nm, shp, dtp, st2)
                for c in range(NCH):
                    if "chunk" not in SKIP:
                        self.even_chunk(b, c, B, T)
                cx.barrier()
            mixT, mixT_r = B["mixT"]
            for ft in range(8):
                slot, slot_r, u = self.stream_get()
                assert u is self.U[("eout", ft)]
                wv = slot[:, 0:2048].rearrange("p (k w) -> p k w", k=16)
                bk, bk_r = self.bank()
                for kt in range(16):
                    cx.op(PE, lambda e: e.matmul(bk[:], lhsT=wv[:, kt, :], rhs=mixT[:, kt, :], start=(kt == 0), stop=(kt == 15)),
                          r=[slot_r, mixT_r], w=[bk_r])
                cx.op(DVE, lambda e: e.tensor_tensor(out=self.xT[:, ft, :], in0=self.xT[:, ft, :], in1=bk[:], op=ALU.add),
                      r=[bk_r, self.xT_r], w=[self.xT_r])
            cx.barrier()

    def even_chunk(self, b, c, B, T):
        gens = []
        if "ret" not in SKIP:
            gens.append(self.even_chunk_ret(b, c, B, T))
        if "ssd" not in SKIP:
            gens.append(self.even_chunk_ssd(b, c, B, T))
        while gens:
            for g in list(gens):
                try:
                    next(g)
                except StopIteration:
                    gens.remove(g)

    def rbank(self):
        self.rbank_i = (getattr(self, "rbank_i", 0) + 1) % 3
        return self.banks[self.rbank_i]

    def sbank(self):
        self.sbank_i = (getattr(self, "sbank_i", 0) + 1) % 3
        return self.banks[3 + self.sbank_i]

    def even_chunk_ret(self, b, c, B, T):
        cx = self.cx
        PE, ACT, DVE, POOL, SP = cx.PE, cx.ACT, cx.DVE, cx.POOL, cx.SP
        cs = slice(c * 128, (c + 1) * 128)
        ib, ib_r = self.ident_b, self.ident_b_r
        for i, (src, dstn) in enumerate([("qr", "qT"), ("qx", "qxT"), ("kr", "kT")]):
            s_t, s_r = B[src]
            d_t, d_r = T[dstn]
            pb, pb_r = self.bankb()
            for h in range(4):
                cx.op(PE, lambda e: e.transpose(pb[:, h * 128:(h + 1) * 128], s_t[:, c, h * 128:(h + 1) * 128], ib[:]), r=[s_r, ib_r], w=[pb_r])
            if i % 2 == 0:
                cx.op(ACT, lambda e: e.copy(out=d_t[:], in_=pb[:, 0:512]), r=[pb_r], w=[d_r])
            else:
                cx.op(DVE, lambda e: e.tensor_copy(out=d_t[:], in_=pb[:, 0:512]), r=[pb_r], w=[d_r])
            yield
        qT, qT_r = T["qT"]
        qxT, qxT_r = T["qxT"]
        kT, kT_r = T["kT"]
        smT, smT_r = T["smT"]
        v, v_r = B["v"]
        bk, bk_r = self.rbank()
        for h in range(4):
            hs = slice(h * 128, (h + 1) * 128)
            cx.op(PE, lambda e: e.matmul(bk[:, hs], lhsT=kT[:, hs], rhs=qT[:, hs], start=True, stop=True), r=[kT_r, qT_r], w=[bk_r])
        yield
        cx.op(DVE, lambda e: e.tensor_tensor(out=smT[:], in0=bk[:], in1=self.dmaskT[:], op=ALU.mult), r=[bk_r, self.dmaskT_r], w=[smT_r])
        yield
        bo = [self.rbank(), self.rbank()]
        for h in range(4):
            hs = slice(h * 128, (h + 1) * 128)
            o_t, o_r = bo[h // 2]
            osl = slice((h % 2) * 256, (h % 2 + 1) * 256)
            cx.op(PE, lambda e: e.matmul(o_t[:, osl], lhsT=smT[:, hs], rhs=v[:, c, h * 256:(h + 1) * 256], start=True, stop=False),
                  r=[smT_r, v_r], w=[o_r])
            cx.op(PE, lambda e: e.matmul(o_t[:, osl], lhsT=qxT[:, hs], rhs=self.retSb[:, h, :], start=False, stop=True),
                  r=[qxT_r, self.retSb_r], w=[o_r])
        yield
        st6, st6_r = T["st6"]
        mv, mv_r = T["mv"]
        rstd4, rstd4_r = T["rstd4"]
        ytmp, ytmp_r = T["ytmp"]
        yretb, yretb_r = T["yretb"]
        sg, sg_r = B["sg"]
        for h in range(4):
            o_t, o_r = bo[h // 2]
            osl = slice((h % 2) * 256, (h % 2 + 1) * 256)
            cx.op(DVE, lambda e: e.bn_stats(out=st6[:, h, :], in_=o_t[:, osl]), r=[o_r], w=[st6_r])
        yield
        for h in range(4):
            cx.op(DVE, lambda e: e.bn_aggr(out=mv[:, h, :], in_=st6[:, h, :]), r=[st6_r], w=[mv_r])
        yield
        cx.op(ACT, lambda e: e.activation(out=rstd4[:], in_=mv[:, :, 1], func=AF.Sqrt, bias=self.eps_t[:, 0:1]), r=[mv_r, self.eps_r], w=[rstd4_r])
        yield
        cx.op(DVE, lambda e: e.reciprocal(out=rstd4[:], in_=rstd4[:]), r=[rstd4_r], w=[rstd4_r])
        yield
        for h in range(4):
            o_t, o_r = bo[h // 2]
            osl = slice((h % 2) * 256, (h % 2 + 1) * 256)
            cx.op(DVE, lambda e: e.tensor_scalar(out=ytmp[:, h * 256:(h + 1) * 256], in0=o_t[:, osl], scalar1=mv[:, h, 0:1], scalar2=rstd4[:, h:h + 1],
                                                 op0=ALU.subtract, op1=ALU.mult),
                  r=[o_r, mv_r, rstd4_r], w=[ytmp_r])
        yield
        cx.op(POOL, lambda e: e.tensor_tensor(out=yretb[:], in0=ytmp[:], in1=sg[:, c, :], op=ALU.mult), r=[ytmp_r, sg_r], w=[yretb_r])
        yield
        mixT, mixT_r = B["mixT"]
        pb, pb_r = self.bankb()
        for t in range(8):
            cx.op(PE, lambda e: e.transpose(pb[:, t * 128:(t + 1) * 128], yretb[:, t * 128:(t + 1) * 128], ib[:]), r=[yretb_r, ib_r], w=[pb_r])
        cx.op(ACT, lambda e: e.copy(out=mixT[:, 0:8, cs], in_=pb[:].rearrange("p (t n) -> p t n", t=8)), r=[pb_r], w=[mixT_r])
        yield
        kz, kz_r = B["kz"]
        bkv = [self.rbank(), self.rbank()]
        for h in range(4):
            k_t, k_r = bkv[h // 2]
            osl = slice((h % 2) * 256, (h % 2 + 1) * 256)
            cx.op(PE, lambda e: e.matmul(k_t[:, osl], lhsT=kz[:, c, h * 128:(h + 1) * 128], rhs=v[:, c, h * 256:(h + 1) * 256], start=True, stop=True),
                  r=[kz_r, v_r], w=[k_r])
        yield
        for h in range(4):
            k_t, k_r = bkv[h // 2]
            osl = slice((h % 2) * 256, (h % 2 + 1) * 256)
            cx.op(DVE, lambda e: e.scalar_tensor_tensor(out=self.retS[:, h, :], in0=self.retS[:, h, :], scalar=RET_GDEC[h], in1=k_t[:, osl],
                                                        op0=ALU.mult, op1=ALU.add),
                  r=[k_r, self.retS_r], w=[self.retS_r])
        yield
        cx.op(ACT, lambda e: e.copy(out=self.retSb[:], in_=self.retS[:]), r=[self.retS_r], w=[self.retSb_r])

    def even_chunk_ssd(self, b, c, B, T):
        cx = self.cx
        PE, ACT, DVE, POOL, SP = cx.PE, cx.ACT, cx.DVE, cx.POOL, cx.SP
        cs = slice(c * 128, (c + 1) * 128)
        ib, ib_r = self.ident_b, self.ident_b_r
        mixT, mixT_r = B["mixT"]
        dt, dt_r = B["dt"]
        dta, dta_r = B["dta"]
        bcT, bcT_r = B["bcT"]
        xcT, xcT_r = B["xcT"]
        sz, sz_r = B["sz"]
        cbm, cbm_r = T["cbm"]
        MT, MT_r = T["MT"]
        dec, dec_r = T["dec"]
        bk, bk_r = self.sbank()
        for g in range(2):
            cx.op(PE, lambda e: e.matmul(bk[:, g * 128:(g + 1) * 128], lhsT=bcT[:, g, cs], rhs=bcT[:, 2 + g, cs], start=True, stop=True),
                  r=[bcT_r], w=[bk_r])
        yield
        cx.op(DVE, lambda e: e.tensor_tensor(out=cbm[:], in0=bk[:, 0:256].rearrange("p (g n) -> p g n", g=2), in1=self.bc_h(self.tri[:], 2), op=ALU.mult),
              r=[bk_r, self.tri_r], w=[cbm_r])
        yield
        for hg in range(4):
            rhsM, rhsM_r = T[f"rhsM{hg % 2}"]
            Lg, Lg_r = T[f"Lg{hg % 2}"]
            g = hg // 2
            cx.op(POOL, lambda e: e.tensor_tensor(out=rhsM[:].rearrange("p (h n) -> p h n", h=4), in0=self.bc_h(self.tri[:], 4),
                                                  in1=self.bc_mid(dta[:, c, hg * 4:(hg + 1) * 4], 128), op=ALU.mult),
                  r=[self.tri_r, dta_r], w=[rhsM_r])
            yield
            bk, bk_r = self.sbank()
            cx.op(PE, lambda e: e.matmul(bk[:], lhsT=self.su[:], rhs=rhsM[:], start=True, stop=True), r=[self.su_r, rhsM_r], w=[bk_r])
            yield
            cx.op(ACT, lambda e: e.activation(out=Lg[:], in_=bk[:], func=AF.Exp), r=[bk_r], w=[Lg_r])
            yield
            L3 = Lg[:].rearrange("p (h n) -> p h n", h=4)
            cx.op(DVE, lambda e: e.tensor_tensor(out=MT[:, hg * 4:(hg + 1) * 4, :], in0=L3, in1=self.bc_h(cbm[:, g, :], 4), op=ALU.mult),
                  r=[Lg_r, cbm_r], w=[MT_r])
            cx.op(POOL, lambda e: e.tensor_copy(out=dec[:, hg * 4:(hg + 1) * 4], in_=L3[:, :, 127]), r=[Lg_r], w=[dec_r])
        yield
        xs, xs_r = T["xs"]
        Xdt, Xdt_r = T["Xdt"]
        Xdec, Xdec_r = T["Xdec"]
        Btok, Btok_r = T["Btok"]
        pb, pb_r = self.bankb()
        for t in range(8):
            cx.op(PE, lambda e: e.transpose(pb[:, t * 128:(t + 1) * 128], xcT[:, t, cs], ib[:]), r=[xcT_r, ib_r], w=[pb_r])
        cx.op(ACT, lambda e: e.copy(out=xs[:], in_=pb[:]), r=[pb_r], w=[xs_r])
        cx.op(DVE, lambda e: e.tensor_tensor(out=Xdt[:].rearrange("p (h d) -> p h d", h=16), in0=pb[:].rearrange("p (h d) -> p h d", h=16),
                                             in1=self.bc_mid(dt[:, c, :], 64), op=ALU.mult),
              r=[pb_r, dt_r], w=[Xdt_r])
        yield
        cx.op(POOL, lambda e: e.tensor_tensor(out=Xdec[:].rearrange("p (h d) -> p h d", h=16), in0=Xdt[:].rearrange("p (h d) -> p h d", h=16),
                                              in1=self.bc_mid(dec[:], 64), op=ALU.mult),
              r=[Xdt_r, dec_r], w=[Xdec_r])
        pb2, pb2_r = self.bankb()
        for g in range(2):
            cx.op(PE, lambda e: e.transpose(pb2[:, g * 128:(g + 1) * 128], bcT[:, g, cs], ib[:]), r=[bcT_r, ib_r], w=[pb2_r])
        cx.op(ACT, lambda e: e.copy(out=Btok[:], in_=pb2[:, 0:256]), r=[pb2_r], w=[Btok_r])
        yield
        ea, ea_r = T["ea"]
        bk, bk_r = self.sbank()
        cx.op(PE, lambda e: e.matmul(bk[:, 0:16], lhsT=self.tri[:], rhs=dta[:, c, :], start=True, stop=True), r=[self.tri_r, dta_r], w=[bk_r])
        cx.op(PE, lambda e: e.matmul(bk[:, 16:32], lhsT=self.ones_f[:], rhs=dta[:, c, :], start=True, stop=True), r=[self.ones_f_r, dta_r], w=[bk_r])
        yield
        cx.op(ACT, lambda e: e.activation(out=ea[:], in_=bk[:, 0:32], func=AF.Exp), r=[bk_r], w=[ea_r])
        yield
        ty, ty_r = T["ty"]
        t2, t2_r = T["t2"]
        for g in range(2):
            gs = slice(g * 512, (g + 1) * 512)
            d_t, d_r = self.sbank()
            for hh in range(8):
                h = g * 8 + hh
                cx.op(PE, lambda e: e.matmul(d_t[:, hh * 64:(hh + 1) * 64], lhsT=MT[:, h, :], rhs=Xdt[:, h * 64:(h + 1) * 64], start=True, stop=True),
                      r=[MT_r, Xdt_r], w=[d_r])
            f_t, f_r = self.sbank()
            cx.op(PE, lambda e: e.matmul(f_t[:], lhsT=bcT[:, 2 + g, cs], rhs=self.ssSb[:, g, :], start=True, stop=True), r=[bcT_r, self.ssSb_r], w=[f_r])
            yield
            cx.op(DVE, lambda e: e.tensor_tensor(out=ty[:, gs].rearrange("p (h d) -> p h d", h=8), in0=f_t[:].rearrange("p (h d) -> p h d", h=8),
                                                 in1=self.bc_mid(ea[:, g * 8:(g + 1) * 8], 64), op=ALU.mult),
                  r=[f_r, ea_r], w=[ty_r])
            cx.op(DVE, lambda e: e.tensor_tensor(out=ty[:, gs], in0=ty[:, gs], in1=d_t[:], op=ALU.add), r=[d_r, ty_r], w=[ty_r])
            yield
        cx.op(POOL, lambda e: e.tensor_tensor(out=t2[:].rearrange("p (h d) -> p h d", h=16), in0=xs[:].rearrange("p (h d) -> p h d", h=16),
                                              in1=self.bc_mid(self.dsk[:], 64), op=ALU.mult),
              r=[xs_r, self.dsk_r], w=[t2_r])
        yield
        cx.op(POOL, lambda e: e.tensor_tensor(out=t2[:], in0=t2[:], in1=ty[:], op=ALU.add), r=[t2_r, ty_r], w=[t2_r])
        yield
        cx.op(POOL, lambda e: e.tensor_tensor(out=t2[:], in0=t2[:], in1=sz[:, c, :], op=ALU.mult), r=[t2_r, sz_r], w=[t2_r])
        yield
        st6g, st6g_r = T["st6g"]
        mvg, mvg_r = T["mvg"]
        rs2, rs2_r = T["rs2"]
        yzb, yzb_r = T["yzb"]
        for g in range(2):
            cx.op(DVE, lambda e: e.bn_stats(out=st6g[:, g, :], in_=t2[:, g * 512:(g + 1) * 512]), r=[t2_r], w=[st6g_r])
        yield
        for g in range(2):
            cx.op(DVE, lambda e: e.bn_aggr(out=mvg[:, g, :], in_=st6g[:, g, :]), r=[st6g_r], w=[mvg_r])
        yield
        cx.op(DVE, lambda e: e.tensor_tensor(out=rs2[:], in0=mvg[:, :, 0], in1=mvg[:, :, 0], op=ALU.mult), r=[mvg_r], w=[rs2_r])
        yield
        cx.op(DVE, lambda e: e.tensor_tensor(out=rs2[:], in0=rs2[:], in1=mvg[:, :, 1], op=ALU.add), r=[mvg_r, rs2_r], w=[rs2_r])
        yield
        cx.op(ACT, lambda e: e.activation(out=rs2[:], in_=rs2[:], func=AF.Sqrt, bias=self.eps_t[:, 0:1]), r=[rs2_r, self.eps_r], w=[rs2_r])
        yield
        cx.op(DVE, lambda e: e.reciprocal(out=rs2[:], in_=rs2[:]), r=[rs2_r], w=[rs2_r])
        yield
        for g in range(2):
            cx.op(ACT, lambda e: e.activation(out=yzb[:, g * 512:(g + 1) * 512], in_=t2[:, g * 512:(g + 1) * 512], func=AF.Copy, scale=rs2[:, g:g + 1]),
                  r=[t2_r, rs2_r], w=[yzb_r])
        yield
        pb, pb_r = self.bankb()
        for t in range(8):
            cx.op(PE, lambda e: e.transpose(pb[:, t * 128:(t + 1) * 128], yzb[:, t * 128:(t + 1) * 128], ib[:]), r=[yzb_r, ib_r], w=[pb_r])
        for t in range(8):
            cx.op(ACT, lambda e: e.activation(out=mixT[:, 8 + t, cs], in_=pb[:, t * 128:(t + 1) * 128], func=AF.Identity, scale=self.ssmn[:, t:t + 1]),
                  r=[pb_r, self.ssmn_r], w=[mixT_r])
        yield
        cdec = ea[:, 16:32]
        for g in range(2):
            s_t, s_r = self.sbank()
            cx.op(PE, lambda e: e.matmul(s_t[:], lhsT=Btok[:, g * 128:(g + 1) * 128], rhs=Xdec[:, g * 512:(g + 1) * 512], start=True, stop=True),
                  r=[Btok_r, Xdec_r], w=[s_r])
            yield
            cx.op(DVE, lambda e: e.tensor_tensor(out=self.ssS[:, g, :].rearrange("p (h d) -> p h d", h=8),
                                                 in0=self.ssS[:, g, :].rearrange("p (h d) -> p h d", h=8),
                                                 in1=self.bc_mid(cdec[:, g * 8:(g + 1) * 8], 64), op=ALU.mult),
                  r=[self.ssS_r, ea_r], w=[self.ssS_r])
            cx.op(DVE, lambda e: e.tensor_tensor(out=self.ssS[:, g, :], in0=self.ssS[:, g, :], in1=s_t[:], op=ALU.add),
                  r=[s_r, self.ssS_r], w=[self.ssS_r])
            yield
        cx.op(ACT, lambda e: e.copy(out=self.ssSb[:], in_=self.ssS[:]), r=[self.ssS_r], w=[self.ssSb_r])


    def setup_odd(self):
        cx = self.cx
        PE, ACT, DVE, POOL, SP = cx.PE, cx.ACT, cx.DVE, cx.POOL, cx.SP
        if "even" not in self.stages:
            self.ident_b, self.ident_b_r = self.sb("ident_b", [128, 128], BF16)
            cx.op(DVE, lambda e: e.tensor_copy(out=self.ident_b[:], in_=self.ident_f[:]), r=[self.ident_f_r], w=[self.ident_b_r])
        self.chalo, self.chalo_r = self.sb("chalo", [128, 4, 30], BF16)
        self.ocw, self.ocw_r = self.sb("ocw", [128, 4, 31], F32)
        self.ocb, self.ocb_r = self.sb("ocb", [128, 4], F32)
        self.olg, self.olg_r = self.sb("olg", [128, 4], F32)
        self.olb, self.olb_r = self.sb("olb", [128, 4], F32)
        self.A8r, self.A8r_r = self.sb("A8r", [128, 2, 16], F32)
        self.A8i, self.A8i_r = self.sb("A8i", [128, 2, 16], F32)
        self.Xs, self.Xs_r = self.sb("Xs", [128, 2, 16], F32)
        self.r8, self.r8_r = self.sb("r8", [128, 2, 16], F32)
        if not hasattr(self, "s5tab"):
            self.s5tab = self.nc.dram_tensor("s5tab", [128, 4096], F32, kind="ExternalOutput" if "dumpw" in SKIP else "Internal").ap()
        cx.op(DVE, lambda e: e.memset(self.chalo[:], 0.0), w=[self.chalo_r])
        cx.op(DVE, lambda e: e.memset(self.Xs[:], 0.0), w=[self.Xs_r])
        for k in range(31):
            cx.dma(SP, self.ocw[:, :, k], self.o_dw_w[0, k].rearrange("(t p) -> p t", p=128), w=[self.ocw_r], sem_res=self.ocw_r,
                   allow_slow_non_contiguous=True)
        for t, r, src in [(self.ocb, self.ocb_r, self.o_dw_b), (self.olg, self.olg_r, self.o_ln_g), (self.olb, self.olb_r, self.o_ln_b)]:
            cx.dma(SP, t[:], src[0].rearrange("(t p) -> p t", p=128), w=[r], sem_res=r, allow_slow_non_contiguous=True)
        dg_rs = []
        with ExitStack() as st0:
            for i in range(4):
                dg, dg_r = self.sb(f"dg{i}", [128, 32, 128], BF16, st0)
                for k in range(31):
                    if k % 2 == 0:
                        cx.op(DVE, lambda e: e.tensor_scalar(out=dg[:, k, :], in0=self.ident_f[:], scalar1=self.ocw[:, i, k:k + 1], scalar2=None,
                                                             op0=ALU.mult), r=[self.ident_f_r, self.ocw_r], w=[dg_r])
                    else:
                        cx.op(ACT, lambda e: e.activation(out=dg[:, k, :], in_=self.ident_f[:], func=AF.Copy, scale=self.ocw[:, i, k:k + 1]),
                              r=[self.ident_f_r, self.ocw_r], w=[dg_r])
                cx.op(DVE, lambda e: e.memset(dg[:, 31, :], 0.0), w=[dg_r])
                u = self.U[("ocv", i)]
                dst = self.wscr[u["off"]: u["off"] + 128 * 4096].rearrange("(p l) -> p l", p=128)
                cx.dma(SP, dst, dg[:].rearrange("q k n -> q (k n)"), r=[dg_r], sem_res=dg_r)
                dg_rs.append(dg_r)
            cx.barrier(engines=cx.compute + [cx.SP], dma_res=dg_rs)
        with ExitStack() as st:
            def tl(nm, shp, dt=F32):
                return self.sb("s5_" + nm, shp, dt, st)
            lr, lr_r = tl("lr", [128, 16])
            li, li_r = tl("li", [128, 16])
            stp, stp_r = tl("stp", [128, 16])
            Bre, Bre_r = tl("Bre", [128, 16, 16])
            Bim, Bim_r = tl("Bim", [128, 16, 16])
            Cre, Cre_r = tl("Cre", [128, 16, 16])
            Cim, Cim_r = tl("Cim", [128, 16, 16])
            Dbc, Dbc_r = tl("Dbc", [128, 32, 16])
            tzm, tzm_r = tl("tzm", [128, 128])
            cx.dma(SP, tzm[:], self.consts["tzmask"], w=[tzm_r], sem_res=tzm_r)
            cx.dma(SP, lr[:], self.o_a_re[0].rearrange("g n -> (g n)").rearrange("(p q) -> q p", q=128), w=[lr_r], sem_res=lr_r,
                   allow_slow_non_contiguous=True)
            cx.dma(SP, li[:], self.o_a_im[0].rearrange("g n -> (g n)").rearrange("(p q) -> q p", q=128), w=[li_r], sem_res=li_r,
                   allow_slow_non_contiguous=True)
            ls2 = self.o_log_step[0].rearrange("(p gh) -> gh p", gh=2)
            for gh in range(2):
                cx.dma(SP, stp[gh * 64:(gh + 1) * 64, :], ls2[gh].partition_broadcast(64), w=[stp_r], sem_res=stp_r, allow_slow_non_contiguous=True)
            cx.dma(SP, Bre[:], self.o_b_re[0].rearrange("g n c -> (g n) c").rearrange("(p q) c -> q p c", q=128), w=[Bre_r], sem_res=Bre_r,
                   allow_slow_non_contiguous=True)
            cx.dma(SP, Bim[:], self.o_b_im[0].rearrange("g n c -> (g n) c").rearrange("(p q) c -> q p c", q=128), w=[Bim_r], sem_res=Bim_r,
                   allow_slow_non_contiguous=True)
            for (dst, dst_r, src) in [(Cre, Cre_r, self.o_c_re), (Cim, Cim_r, self.o_c_im)]:
                for g in range(32):
                    p, gh = g // 2, g % 2
                    cx.dma(SP, dst[gh * 64:(gh + 1) * 64, p, :], src[0, g].rearrange("co n -> n co"), w=[dst_r], sem_res=dst_r,
                           allow_slow_non_contiguous=True)
            cx.dma(SP, Dbc[:].rearrange("p g c -> p (g c)"), self.o_d[0].partition_broadcast(128), w=[Dbc_r], sem_res=Dbc_r, allow_slow_non_contiguous=True)
            cx.op(ACT, lambda e: e.activation(out=stp[:], in_=stp[:], func=AF.Exp), r=[stp_r], w=[stp_r])
            lrs, lrs_r = tl("lrs", [128, 16])
            lis, lis_r = tl("lis", [128, 16])
            cx.op(DVE, lambda e: e.tensor_tensor(out=lrs[:], in0=lr[:], in1=stp[:], op=ALU.mult), r=[lr_r, stp_r], w=[lrs_r])
            cx.op(DVE, lambda e: e.tensor_tensor(out=lis[:], in0=li[:], in1=stp[:], op=ALU.mult), r=[li_r, stp_r], w=[lis_r])
            NM = 17
            mag, mag_r = tl("mag", [128, 16, NM])
            ang, ang_r = tl("ang", [128, 16, NM])
            Pre, Pre_r = tl("Pre", [128, 16, NM])
            Pim, Pim_r = tl("Pim", [128, 16, NM])
            tmpa, tmpa_r = tl("tmpa", [128, 16, NM])
            for mi in range(NM):
                m = float(mi - 8)
                cx.op(ACT, lambda e: e.activation(out=mag[:, :, mi], in_=lrs[:], func=AF.Exp, scale=m), r=[lrs_r], w=[mag_r])
                cx.op(DVE, lambda e: e.tensor_scalar(out=ang[:, :, mi], in0=lis[:], scalar1=m, scalar2=None, op0=ALU.mult), r=[lis_r], w=[ang_r])
            TWO_PI = 2.0 * math.pi
            MAGIC = 12582912.0

            def sin_of(dst, dst_r, shift, ang=ang, ang_r=ang_r, tmpa=tmpa, tmpa_r=tmpa_r):
                cx.op(DVE, lambda e: e.tensor_scalar(out=tmpa[:], in0=ang[:], scalar1=shift, scalar2=1.0 / TWO_PI, op0=ALU.add, op1=ALU.mult),
                      r=[ang_r], w=[tmpa_r])
                cx.op(DVE, lambda e: e.tensor_scalar(out=dst[:], in0=tmpa[:], scalar1=MAGIC, scalar2=None, op0=ALU.add), r=[tmpa_r], w=[dst_r])
                cx.op(DVE, lambda e: e.tensor_scalar(out=dst[:], in0=dst[:], scalar1=-MAGIC, scalar2=None, op0=ALU.add), r=[dst_r], w=[dst_r])
                cx.op(DVE, lambda e: e.tensor_tensor(out=tmpa[:], in0=tmpa[:], in1=dst[:], op=ALU.subtract), r=[tmpa_r, dst_r], w=[tmpa_r])
                cx.op(DVE, lambda e: e.tensor_scalar(out=tmpa[:], in0=tmpa[:], scalar1=TWO_PI, scalar2=math.pi, op0=ALU.mult, op1=ALU.min),
                      r=[tmpa_r], w=[tmpa_r])
                cx.op(DVE, lambda e: e.tensor_scalar(out=tmpa[:], in0=tmpa[:], scalar1=-math.pi, scalar2=None, op0=ALU.max), r=[tmpa_r], w=[tmpa_r])
                cx.op(ACT, lambda e: e.activation(out=dst[:], in_=tmpa[:], func=AF.Sin), r=[tmpa_r], w=[dst_r])

            sin_of(Pim, Pim_r, 0.0)
            sin_of(Pre, Pre_r, math.pi / 2.0)
            cx.op(DVE, lambda e: e.tensor_tensor(out=Pre[:], in0=Pre[:], in1=mag[:], op=ALU.mult), r=[Pre_r, mag_r], w=[Pre_r])
            cx.op(DVE, lambda e: e.tensor_tensor(out=Pim[:], in0=Pim[:], in1=mag[:], op=ALU.mult), r=[Pim_r, mag_r], w=[Pim_r])

            def P(which, m):
                return (Pre if which == "re" else Pim)[:, :, m + 8]
            cx.op(DVE, lambda e: e.tensor_copy(out=self.A8r[:, 0, :], in_=P("re", 8)), r=[Pre_r], w=[self.A8r_r])
            cx.op(DVE, lambda e: e.tensor_copy(out=self.A8r[:, 1, :], in_=P("re", 8)), r=[Pre_r], w=[self.A8r_r])
            cx.op(DVE, lambda e: e.tensor_scalar(out=self.A8i[:, 0, :], in0=P("im", 8), scalar1=-1.0, scalar2=None, op0=ALU.mult), r=[Pim_r], w=[self.A8i_r])
            cx.op(DVE, lambda e: e.tensor_copy(out=self.A8i[:, 1, :], in_=P("im", 8)), r=[Pim_r], w=[self.A8i_r])
            with ExitStack() as st_tab:
                def tl2(nm, shp, dt=F32):
                    return self.sb("s5t_" + nm, shp, dt, st_tab)
                ph8, ph8_r = tl2("ph8", [128, 16])
                k8, k8_r = tl2("k8", [128, 16])
                cx.op(DVE, lambda e: e.tensor_scalar(out=ph8[:], in0=lis[:], scalar1=8.0 / TWO_PI, scalar2=None, op0=ALU.mult), r=[lis_r], w=[ph8_r])
                cx.op(DVE, lambda e: e.tensor_scalar(out=k8[:], in0=ph8[:], scalar1=MAGIC, scalar2=None, op0=ALU.add), r=[ph8_r], w=[k8_r])
                cx.op(DVE, lambda e: e.tensor_scalar(out=k8[:], in0=k8[:], scalar1=-MAGIC, scalar2=None, op0=ALU.add), r=[k8_r], w=[k8_r])
                cx.op(DVE, lambda e: e.tensor_tensor(out=ph8[:], in0=ph8[:], in1=k8[:], op=ALU.subtract), r=[ph8_r, k8_r], w=[ph8_r])
                cx.op(DVE, lambda e: e.tensor_scalar(out=ph8[:], in0=ph8[:], scalar1=TWO_PI, scalar2=None, op0=ALU.mult), r=[ph8_r], w=[ph8_r])
                cidx, cidx_r = tl2("cidx", [128, 64])
                cx.dma(SP, cidx[:], self.consts["cidx"], w=[cidx_r], sem_res=cidx_r)
                angm, angm_r = tl2("angm", [128, 16, 64])
                tmpm, tmpm_r = tl2("tmpm", [128, 16, 64])
                stab, stab_r = tl2("stab", [128, 4096])
                cx.op(DVE, lambda e: e.tensor_tensor(out=angm[:], in0=ph8[:].unsqueeze(2).to_broadcast([128, 16, 64]),
                                                     in1=cidx[:].unsqueeze(1).to_broadcast([128, 16, 64]), op=ALU.mult), r=[ph8_r, cidx_r], w=[angm_r])
                Ec = stab[:, 0:1024].rearrange("q (p c) -> q p c", p=16)
                Es = stab[:, 1024:2048].rearrange("q (p c) -> q p c", p=16)
                rt = stab[:, 2048:4096].rearrange("q (a p c) -> q a p c", a=2, p=16)

                class _V:
                    def __init__(self, ap):
                        self.ap = ap

                    def __getitem__(self, k):
                        return self.ap
                sin_of(_V(Es), stab_r, 0.0, ang=angm, ang_r=angm_r, tmpa=tmpm, tmpa_r=tmpm_r)
                sin_of(_V(Ec), stab_r, math.pi / 2.0, ang=angm, ang_r=angm_r, tmpa=tmpm, tmpa_r=tmpm_r)
                for a_ in range(2):
                    cx.op(DVE, lambda e: e.tensor_copy(out=rt[:, a_, :, :], in_=mag[:, :, 16].unsqueeze(2).to_broadcast([128, 16, 64])), r=[mag_r], w=[stab_r])
                cx.op(DVE, lambda e: e.memset(rt[:, :, :, 0], 0.0), w=[stab_r])
                cx.op(DVE, lambda e: e.tensor_copy(out=self.r8[:, 0, :], in_=mag[:, :, 16]), r=[mag_r], w=[self.r8_r])
                cx.op(DVE, lambda e: e.tensor_copy(out=self.r8[:, 1, :], in_=mag[:, :, 16]), r=[mag_r], w=[self.r8_r])
                cx.dma(SP, self.s5tab, stab[:], r=[stab_r], sem_res=stab_r)
                cx.barrier(engines=cx.compute + [cx.SP], dma_res=[stab_r, cidx_r])
            fre, fre_r = tl("fre", [128, 16])
            fim, fim_r = tl("fim", [128, 16])
            den, den_r = tl("den", [128, 16])
            am1, am1_r = tl("am1", [128, 16])
            t16, t16_r = tl("t16", [128, 16])
            cx.op(DVE, lambda e: e.tensor_tensor(out=den[:], in0=lr[:], in1=lr[:], op=ALU.mult), r=[lr_r], w=[den_r])
            cx.op(DVE, lambda e: e.tensor_tensor(out=t16[:], in0=li[:], in1=li[:], op=ALU.mult), r=[li_r], w=[t16_r])
            cx.op(DVE, lambda e: e.tensor_tensor(out=den[:], in0=den[:], in1=t16[:], op=ALU.add), r=[den_r, t16_r], w=[den_r])
            cx.op(DVE, lambda e: e.reciprocal(out=den[:], in_=den[:]), r=[den_r], w=[den_r])
            cx.op(DVE, lambda e: e.tensor_scalar(out=am1[:], in0=P("re", 1), scalar1=-1.0, scalar2=None, op0=ALU.add), r=[Pre_r], w=[am1_r])
            cx.op(DVE, lambda e: e.tensor_tensor(out=fre[:], in0=am1[:], in1=lr[:], op=ALU.mult), r=[am1_r, lr_r], w=[fre_r])
            cx.op(DVE, lambda e: e.tensor_tensor(out=t16[:], in0=P("im", 1), in1=li[:], op=ALU.mult), r=[Pim_r, li_r], w=[t16_r])
            cx.op(DVE, lambda e: e.tensor_tensor(out=fre[:], in0=fre[:], in1=t16[:], op=ALU.add), r=[fre_r, t16_r], w=[fre_r])
            cx.op(DVE, lambda e: e.tensor_tensor(out=fre[:], in0=fre[:], in1=den[:], op=ALU.mult), r=[fre_r, den_r], w=[fre_r])
            cx.op(DVE, lambda e: e.tensor_tensor(out=fim[:], in0=P("im", 1), in1=lr[:], op=ALU.mult), r=[Pim_r, lr_r], w=[fim_r])
            cx.op(DVE, lambda e: e.tensor_tensor(out=t16[:], in0=am1[:], in1=li[:], op=ALU.mult), r=[am1_r, li_r], w=[t16_r])
            cx.op(DVE, lambda e: e.tensor_tensor(out=fim[:], in0=fim[:], in1=t16[:], op=ALU.subtract), r=[fim_r, t16_r], w=[fim_r])
            cx.op(DVE, lambda e: e.tensor_tensor(out=fim[:], in0=fim[:], in1=den[:], op=ALU.mult), r=[fim_r, den_r], w=[fim_r])

            def cmul(dre, dre_r, dim_, dim_r, are, aim, a_rs, bre, bim, b_rs, tA, tA_r, neg_im=False):
                cx.op(DVE, lambda e: e.tensor_tensor(out=dre, in0=are, in1=bre, op=ALU.mult), r=a_rs + b_rs, w=[dre_r])
                cx.op(DVE, lambda e: e.tensor_tensor(out=tA, in0=aim, in1=bim, op=ALU.mult), r=a_rs + b_rs, w=[tA_r])
                cx.op(DVE, lambda e: e.tensor_tensor(out=dre, in0=dre, in1=tA, op=ALU.subtract), r=[dre_r, tA_r], w=[dre_r])
                tB, tB_r = tB_pair
                tBv = tB[:] if len(tA.shape) == 3 else tB[:]
                E2 = POOL if "cmulpool" in SKIP else DVE
                cx.op(E2, lambda e: e.tensor_tensor(out=dim_, in0=are, in1=bim, op=ALU.mult), r=a_rs + b_rs, w=[dim_r])
                cx.op(E2, lambda e: e.tensor_tensor(out=tBv, in0=aim, in1=bre, op=ALU.mult), r=a_rs + b_rs, w=[tB_r])
                cx.op(E2, lambda e: e.tensor_tensor(out=dim_, in0=dim_, in1=tBv, op=ALU.add), r=[dim_r, tB_r], w=[dim_r])

            def bc16(ap2):
                return ap2.unsqueeze(2).to_broadcast([128, 16, 16])
            tB_pair = tl("t3b", [128, 16, 16])
            Bbr, Bbr_r = tl("Bbr", [128, 16, 16])
            Bbi, Bbi_r = tl("Bbi", [128, 16, 16])
            t3, t3_r = tl("t3", [128, 16, 16])
            cmul(Bbr[:], Bbr_r, Bbi[:], Bbi_r, bc16(fre[:]), bc16(fim[:]), [fre_r, fim_r], Bre[:], Bim[:], [Bre_r, Bim_r], t3[:], t3_r)
            CAr, CAr_r = tl("CAr", [128, 16, 8, 16])
            CAi, CAi_r = tl("CAi", [128, 16, 8, 16])
            for s in range(8):
                cmul(CAr[:, :, s, :], CAr_r, CAi[:, :, s, :], CAi_r, bc16(P("re", s + 1)), bc16(P("im", s + 1)), [Pre_r, Pim_r],
                     Cre[:], Cim[:], [Cre_r, Cim_r], t3[:], t3_r)
            with ExitStack() as sc:
                CcU, CcU_r = self.sb("s5_CcU", [128, 16, 2, 2, 128], BF16, sc)
                for ph in range(4):
                    cx.op(DVE, lambda e: e.memset(CcU[:, ph * 4:(ph + 1) * 4], 0.0), w=[CcU_r])
                for gh in range(2):
                    ps_ = slice(gh * 64, (gh + 1) * 64)
                    cx.op(DVE, lambda e: e.tensor_copy(out=CcU[ps_, :, gh, 0, :], in_=CAr[ps_].rearrange("q p j c -> q p (j c)")), r=[CAr_r], w=[CcU_r])
                    cx.op(DVE, lambda e: e.tensor_scalar(out=CcU[ps_, :, gh, 1, :], in0=CAi[ps_].rearrange("q p j c -> q p (j c)"), scalar1=-1.0, scalar2=None,
                                                         op0=ALU.mult), r=[CAi_r], w=[CcU_r])
                ccflat = CcU[:].rearrange("q p a b n -> q (p a b n)")
                for i in range(2):
                    u = self.U[("s5cc", i)]
                    dst = self.wscr[u["off"]: u["off"] + 128 * 4096].rearrange("(p l) -> p l", p=128)
                    cx.dma(SP, dst, ccflat[:, i * 4096:(i + 1) * 4096], r=[CcU_r], sem_res=CcU_r)
                cx.barrier(engines=cx.compute + [cx.SP], dma_res=[CcU_r])
            with ExitStack() as sc:
                BcTr, BcTr_r = self.sb("s5_BcTr", [128, 16, 8, 16], F32, sc)
                BcTi, BcTi_r = self.sb("s5_BcTi", [128, 16, 8, 16], F32, sc)
                for s in range(8):
                    cmul(BcTr[:, :, s, :], BcTr_r, BcTi[:, :, s, :], BcTi_r, bc16(P("re", 7 - s)), bc16(P("im", 7 - s)), [Pre_r, Pim_r],
                         Bbr[:], Bbi[:], [Bbr_r, Bbi_r], t3[:], t3_r)
                BcU, BcU_r = self.sb("s5_BcU", [128, 8, 2, 2, 128], BF16, sc)
                for i in range(2):
                    for ph in range(2):
                        cx.op(DVE, lambda e: e.memset(BcU[:, ph * 4:(ph + 1) * 4], 0.0), w=[BcU_r])
                    for pp in range(8):
                        p = i * 8 + pp
                        for ri, (src, src_r) in enumerate([(BcTr, BcTr_r), (BcTi, BcTi_r)]):
                            bk, bk_r = self.bank()
                            cx.op(PE, lambda e: e.transpose(bk[:, 0:128], src[:, p, :, :].rearrange("q s c -> q (s c)"), self.ident_f[:]),
                                  r=[src_r, self.ident_f_r], w=[bk_r])
                            for gh in range(2):
                                cx.op(ACT if gh == 0 else DVE, lambda e: (e.copy if gh == 0 else e.tensor_copy)(out=BcU[:, pp, gh, ri, gh * 64:(gh + 1) * 64],
                                                                                                         in_=bk[:, gh * 64:(gh + 1) * 64]),
                                      r=[bk_r], w=[BcU_r])
                    u = self.U[("s5bc", i)]
                    dst = self.wscr[u["off"]: u["off"] + 128 * 4096].rearrange("(p l) -> p l", p=128)
                    cx.dma(SP, dst, BcU[:].rearrange("q p a b n -> q (p a b n)"), r=[BcU_r], sem_res=BcU_r)
                cx.barrier(engines=cx.compute + [cx.SP], dma_res=[BcU_r])
            with ExitStack() as sc:
                BpTr, BpTr_r = self.sb("s5_BpTr", [128, 16, 8, 16], F32, sc)
                BpTi, BpTi_r = self.sb("s5_BpTi", [128, 16, 8, 16], F32, sc)
                for s in range(8):
                    cmul(BpTr[:, :, s, :], BpTr_r, BpTi[:, :, s, :], BpTi_r, bc16(P("re", -1 - s)), bc16(P("im", -1 - s)), [Pre_r, Pim_r],
                         Bbr[:], Bbi[:], [Bbr_r, Bbi_r], t3[:], t3_r)
                TzU, TzU_r = self.sb("s5_TzU", [128, 32, 128], BF16, sc)
                lm = [self.sb(f"s5_lm{i}", [128, 128], F32, sc) for i in range(4)]
                tzt, tzt_r = self.sb("s5_tzt", [128, 128], F32, sc)
                for (l_, l_r) in lm:
                    cx.op(DVE, lambda e: e.memset(l_[:], 0.0), w=[l_r])
                for g in range(32):
                    p, gh = g // 2, g % 2
                    ps_ = slice(gh * 64, (gh + 1) * 64)
                    po_ = slice((1 - gh) * 64, (2 - gh) * 64)
                    (l0, l0_r), (l1, l1_r) = lm[(g % 2) * 2], lm[(g % 2) * 2 + 1]
                    cx.op(DVE, lambda e: e.tensor_copy(out=l0[ps_, :], in_=BpTr[ps_, p, :, :].rearrange("q s c -> q (s c)")), r=[BpTr_r], w=[l0_r])
                    cx.op(ACT, lambda e: e.activation(out=l1[ps_, :], in_=BpTi[ps_, p, :, :].rearrange("q s c -> q (s c)"), func=AF.Copy, scale=-1.0),
                          r=[BpTi_r], w=[l1_r])
                    bk, bk_r = self.bank()
                    cx.op(PE, lambda e: e.matmul(bk[:, 0:128], lhsT=l0[:], rhs=CAr[:, p, :, :].rearrange("q j c -> q (j c)"), start=True, stop=False),
                          r=[l0_r, CAr_r], w=[bk_r])
                    cx.op(PE, lambda e: e.matmul(bk[:, 0:128], lhsT=l1[:], rhs=CAi[:, p, :, :].rearrange("q j c -> q (j c)"), start=False, stop=True),
                          r=[l1_r, CAi_r], w=[bk_r])
                    cx.op(DVE, lambda e: e.tensor_tensor(out=tzt[:], in0=bk[:, 0:128], in1=tzm[:], op=ALU.mult), r=[bk_r, tzm_r], w=[tzt_r])
                    cx.op(DVE, lambda e: e.tensor_tensor(out=TzU[:, g, :].rearrange("q (j c) -> q j c", j=8),
                                                         in0=self.ident_f[:].rearrange("q (j c) -> q j c", j=8),
                                                         in1=Dbc[:, g, :].unsqueeze(1).to_broadcast([128, 8, 16]), op=ALU.mult),
                          r=[self.ident_f_r, Dbc_r], w=[TzU_r])
                    cx.op(DVE, lambda e: e.tensor_tensor(out=TzU[:, g, :], in0=TzU[:, g, :], in1=tzt[:], op=ALU.add), r=[tzt_r, TzU_r], w=[TzU_r])
                u = self.U["s5tz"]
                dst = self.wscr[u["off"]: u["off"] + 128 * 4096].rearrange("(p l) -> p l", p=128)
                cx.dma(SP, dst, TzU[:].rearrange("q g n -> q (g n)"), r=[TzU_r], sem_res=TzU_r)
                cx.barrier(engines=cx.compute + [cx.SP], dma_res=[TzU_r])
            cx.barrier(engines=cx.compute + [cx.SP], with_dma=True)

    def odd(self, b):
        cx = self.cx
        PE, ACT, DVE, POOL, SP = cx.PE, cx.ACT, cx.DVE, cx.POOL, cx.SP
        self.rmsnorm(2, self.hnT, self.hnT_r)
        hnT, hnT_r = self.hnT, self.hnT_r
        NC8 = NB // 8
        with ExitStack() as st:
            def tl(nm, shp, dt=F32):
                return self.sb("o_" + nm, shp, dt, st)
            mixT, mixT_r = tl("mixT", [128, 8, NB], BF16)
            cbuf, cbuf_r = tl("cbuf", [128, 4, NB + 30], BF16)
            cacc, cacc_r = tl("cacc", [128, 4, NB])
            sig = [tl(f"sig{i}", [128, NB]) for i in range(2)]
            uS, uS_r = tl("uS", [64, 32, 128], BF16)
            U, U_r = tl("U", [128, 32, NC8], BF16)
            V, V_r = tl("V", [128, 2, 16, NC8])
            Xst, Xst_r = tl("Xst", [128, 2, 16, NC8], BF16)
            slot, slot_r, u = self.stream_get()
            assert u is self.U["ou"]
            wv = slot[:, 0:4096].rearrange("p (k w) -> p k w", k=8)
            for s in range(8):
                bk, bk_r = self.bank()
                for kt in range(8):
                    lh = hnT[:, kt, :].rearrange("p (c s) -> p s c", s=8)[:, s, :]
                    cx.op(PE, lambda e: e.matmul(bk[0:64, :], lhsT=lh, rhs=wv[:, kt, :], start=(kt == 0), stop=(kt == 7)), r=[slot_r, hnT_r], w=[bk_r])
                src = bk[0:64, :].rearrange("p (g c) -> p g c", g=32)
                if s % 2 == 0:
                    cx.op(ACT, lambda e: e.copy(out=uS[:, :, s * 16:(s + 1) * 16], in_=src), r=[bk_r], w=[uS_r])
                else:
                    cx.op(DVE, lambda e: e.tensor_copy(out=uS[:, :, s * 16:(s + 1) * 16], in_=src), r=[bk_r], w=[uS_r])
            for g8 in range(4):
                pb, pb_r = self.bankb()
                for gi in range(8):
                    g = g8 * 8 + gi
                    cx.op(PE, lambda e: e.transpose(pb[:, gi * 64:(gi + 1) * 64], uS[:, g, :], self.ident_b[0:64, 0:64]), r=[uS_r, self.ident_b_r], w=[pb_r])
                if g8 % 2 == 0:
                    cx.op(ACT, lambda e: e.copy(out=U[:, g8 * 8:(g8 + 1) * 8, :].rearrange("p g c -> p (g c)"), in_=pb[:, 0:512]), r=[pb_r], w=[U_r])
                else:
                    cx.op(DVE, lambda e: e.tensor_copy(out=U[:, g8 * 8:(g8 + 1) * 8, :].rearrange("p g c -> p (g c)"), in_=pb[:, 0:512]), r=[pb_r], w=[U_r])
            bcs = []
            for i in range(2):
                slot, slot_r, u = self.stream_get(hold_prev=i)
                assert u is self.U[("s5bc", i)]
                bcs.append((slot[:, 0:4096].rearrange("q (p a b n) -> q p a b n", p=8, a=2, b=2), slot_r))
            for ri in range(2):
                for ph in range(2):
                    bk, bk_r = self.bank()
                    for pi in range(8):
                        p = ph * 8 + pi
                        bcv, bc_r = bcs[p // 8]
                        for gh in range(2):
                            cx.op(PE, lambda e: e.matmul(bk[:, pi * 64:(pi + 1) * 64], lhsT=bcv[:, p % 8, gh, ri, :], rhs=U[:, 2 * p + gh, :],
                                                         start=(gh == 0), stop=(gh == 1)), r=[bc_r, U_r], w=[bk_r])
                    cx.op(ACT, lambda e: e.copy(out=V[:, ri, ph * 8:(ph + 1) * 8, :].rearrange("q p c -> q (p c)"), in_=bk[:]), r=[bk_r], w=[V_r])
            with ExitStack() as st_scan:
                def tls(nm, shp, dt=F32):
                    return self.sb("os_" + nm, shp, dt, st_scan if "noscanscope" not in SKIP else st)
                tab, tab_r = tls("tab", [128, 4096])
                cx.dma(SP, tab[:], self.s5tab, w=[tab_r], sem_res=self.r8_r)
                Ec = tab[:, 0:1024].rearrange("q (p c) -> q p c", p=16)
                Es = tab[:, 1024:2048].rearrange("q (p c) -> q p c", p=16)
                rtf = tab[:, 2048:4096]
                Vt, Vt_r = tls("Vt", [128, 2, 16, NC8])
                Yt, Yt_r = tls("Yt", [128, 2, 16, NC8])
                mA, mA_r = tls("mA", [128, 16, NC8])
                mB, mB_r = tls("mB", [128, 16, NC8])
                Xs, Xs_r = self.Xs, self.Xs_r
                Vr, Vi = V[:, 0, :, :], V[:, 1, :, :]
                cx.op(POOL, lambda e: e.tensor_tensor(out=mA[:], in0=Vr, in1=Ec, op=ALU.mult), r=[V_r, tab_r], w=[mA_r])
                cx.op(DVE, lambda e: e.tensor_tensor(out=mB[:], in0=Vi, in1=Es, op=ALU.mult), r=[V_r, tab_r], w=[mB_r])
                cx.op(DVE, lambda e: e.tensor_tensor(out=Vt[:, 0, :, :], in0=mA[:], in1=mB[:], op=ALU.add), r=[mA_r, mB_r], w=[Vt_r])
                cx.op(POOL, lambda e: e.tensor_tensor(out=mA[:], in0=Vi, in1=Ec, op=ALU.mult), r=[V_r, tab_r], w=[mA_r])
                cx.op(DVE, lambda e: e.tensor_tensor(out=mB[:], in0=Vr, in1=Es, op=ALU.mult), r=[V_r, tab_r], w=[mB_r])
                cx.op(DVE, lambda e: e.tensor_tensor(out=Vt[:, 1, :, :], in0=mA[:], in1=mB[:], op=ALU.subtract), r=[mA_r, mB_r], w=[Vt_r])
                T1, T1_r = tls("T1", [128, 2, 16])
                cx.op(DVE, lambda e: e.tensor_tensor(out=T1[:], in0=Xs[:], in1=self.r8[:], op=ALU.mult), r=[Xs_r, self.r8_r], w=[T1_r])
                cx.op(DVE, lambda e: e.tensor_tensor(out=Vt[:, :, :, 0], in0=Vt[:, :, :, 0], in1=T1[:], op=ALU.add), r=[Vt_r, T1_r], w=[Vt_r])
                cx.op(POOL, lambda e: e.tensor_copy(out=Xst[:, :, :, 0], in_=Xs[:]), r=[Xs_r], w=[Xst_r])
                cx.op(DVE, lambda e: e.tensor_tensor_scan(out=Yt[:].rearrange("q a p c -> q (a p c)"), data0=rtf,
                                                          data1=Vt[:].rearrange("q a p c -> q (a p c)"), initial=0.0, op0=ALU.mult, op1=ALU.add),
                      r=[tab_r, Vt_r], w=[Yt_r])
                Ytr, Yti = Yt[:, 0, :, :], Yt[:, 1, :, :]
                cx.op(POOL, lambda e: e.tensor_tensor(out=mA[:], in0=Ytr, in1=Ec, op=ALU.mult), r=[Yt_r, tab_r], w=[mA_r])
                cx.op(DVE, lambda e: e.tensor_tensor(out=mB[:], in0=Yti, in1=Es, op=ALU.mult), r=[Yt_r, tab_r], w=[mB_r])
                cx.op(DVE, lambda e: e.tensor_tensor(out=V[:, 0, :, :], in0=mA[:], in1=mB[:], op=ALU.subtract), r=[mA_r, mB_r], w=[V_r])
                cx.op(POOL, lambda e: e.tensor_tensor(out=mA[:], in0=Yti, in1=Ec, op=ALU.mult), r=[Yt_r, tab_r], w=[mA_r])
                cx.op(DVE, lambda e: e.tensor_tensor(out=mB[:], in0=Ytr, in1=Es, op=ALU.mult), r=[Yt_r, tab_r], w=[mB_r])
                cx.op(DVE, lambda e: e.tensor_tensor(out=V[:, 1, :, :], in0=mA[:], in1=mB[:], op=ALU.add), r=[mA_r, mB_r], w=[V_r])
                cx.op(ACT, lambda e: e.copy(out=Xst[:, :, :, 1:NC8], in_=V[:, :, :, 0:NC8 - 1]), r=[V_r], w=[Xst_r])
                cx.op(POOL, lambda e: e.tensor_copy(out=Xs[:], in_=V[:, :, :, NC8 - 1]), r=[V_r], w=[Xs_r])
                for i in range(4):
                    slot, slot_r, u = self.stream_get()
                    assert u is self.U[("oin", i)]
                    bks = []
                    for gi in range(2):
                        wv = slot[:, gi * 1024:(gi + 1) * 1024].rearrange("p (k w) -> p k w", k=8)
                        bk, bk_r = self.bank()
                        for kt in range(8):
                            cx.op(PE, lambda e: e.matmul(bk[:], lhsT=wv[:, kt, :], rhs=hnT[:, kt, :], start=(kt == 0), stop=(kt == 7)), r=[slot_r, hnT_r], w=[bk_r])
                        bks.append((bk, bk_r))
                    sg_t, sg_r = sig[i % 2]
                    cx.op(ACT, lambda e: e.activation(out=sg_t[:], in_=bks[1][0][:], func=AF.Sigmoid), r=[bks[1][1]], w=[sg_r])
                    cx.op(DVE, lambda e: e.tensor_tensor(out=cbuf[:, i, 30:NB + 30], in0=bks[0][0][:], in1=sg_t[:], op=ALU.mult), r=[bks[0][1], sg_r], w=[cbuf_r])
                    cx.op(POOL, lambda e: e.tensor_copy(out=cbuf[:, i, 0:30], in_=self.chalo[:, i, :]), r=[self.chalo_r], w=[cbuf_r])
                    cx.op(POOL, lambda e: e.tensor_copy(out=self.chalo[:, i, :], in_=cbuf[:, i, NB:NB + 30]), r=[cbuf_r], w=[self.chalo_r])
                    dslot, dslot_r, u = self.stream_get(hold_prev=0)
                    assert u is self.U[("ocv", i)]
                    dv = dslot[:, 0:4096].rearrange("q (k n) -> q k n", k=32)
                    bk, bk_r = self.bank()
                    for k in range(31):
                        cx.op(PE, lambda e: e.matmul(bk[:], lhsT=dv[:, k, :], rhs=cbuf[:, i, k:k + NB], start=(k == 0), stop=(k == 30)), r=[dslot_r, cbuf_r], w=[bk_r])
                    cx.op(ACT, lambda e: e.activation(out=cacc[:, i, :], in_=bk[:], func=AF.Identity, bias=self.ocb[:, i:i + 1]), r=[bk_r, self.ocb_r], w=[cacc_r])
                cx.barrier()
            sq = [tl(f"sq{i}", [128, NB]) for i in range(2)]
            b1, b1_r = self.bank()
            b2, b2_r = self.bank()
            for i in range(4):
                cx.op(PE, lambda e: e.matmul(b1[:], lhsT=self.ones_f[:], rhs=cacc[:, i, :], start=(i == 0), stop=(i == 3)), r=[self.ones_f_r, cacc_r], w=[b1_r])
            for i in range(4):
                s_t, s_r = sq[i % 2]
                cx.op(ACT, lambda e: e.activation(out=s_t[:], in_=cacc[:, i, :], func=AF.Square), r=[cacc_r], w=[s_r])
                cx.op(PE, lambda e: e.matmul(b2[:], lhsT=self.ones_f[:], rhs=s_t[:], start=(i == 0), stop=(i == 3)), r=[self.ones_f_r, s_r], w=[b2_r])
            mean, mean_r = tl("mean", [128, NB])
            var, var_r = tl("var", [128, NB])
            cx.op(DVE, lambda e: e.tensor_scalar(out=mean[:], in0=b1[:], scalar1=1.0 / 512, scalar2=None, op0=ALU.mult), r=[b1_r], w=[mean_r])
            cx.op(DVE, lambda e: e.tensor_tensor(out=var[:], in0=mean[:], in1=mean[:], op=ALU.mult), r=[mean_r], w=[var_r])
            cx.op(DVE, lambda e: e.scalar_tensor_tensor(out=var[:], in0=b2[:], scalar=1.0 / 512, in1=var[:], op0=ALU.mult, op1=ALU.subtract),
                  r=[b2_r, var_r], w=[var_r])
            cx.op(ACT, lambda e: e.activation(out=var[:], in_=var[:], func=AF.Sqrt, bias=self.eps_t[:, 0:1]), r=[var_r, self.eps_r], w=[var_r])
            cx.op(DVE, lambda e: e.reciprocal(out=var[:], in_=var[:]), r=[var_r], w=[var_r])
            for i in range(4):
                cx.op(DVE, lambda e: e.tensor_tensor(out=cacc[:, i, :], in0=cacc[:, i, :], in1=mean[:], op=ALU.subtract), r=[cacc_r, mean_r], w=[cacc_r])
                cx.op(DVE, lambda e: e.tensor_tensor(out=cacc[:, i, :], in0=cacc[:, i, :], in1=var[:], op=ALU.mult), r=[cacc_r, var_r], w=[cacc_r])
                cx.op(ACT, lambda e: e.activation(out=mixT[:, i, :], in_=cacc[:, i, :], func=AF.Silu, bias=self.olb[:, i:i + 1], scale=self.olg[:, i:i + 1]),
                      r=[cacc_r, self.olg_r, self.olb_r], w=[mixT_r])
            slot, tz_r, u = self.stream_get()
            assert u is self.U["s5tz"]
            tzv = slot[:, 0:4096].rearrange("q (g n) -> q g n", g=32)
            ccs = []
            for i in range(2):
                slot, slot_r, u = self.stream_get(hold_prev=i + 1)
                assert u is self.U[("s5cc", i)]
                ccs.append((slot[:, 0:4096].rearrange("q (p a b n) -> q p a b n", p=8, a=2, b=2), slot_r))
            stok, stok_r = tl("stok", [64, 8, 512])
            for g4 in range(8):
                bk, bk_r = self.bank()
                for gi in range(4):
                    g = g4 * 4 + gi
                    p, gh = g // 2, g % 2
                    ccv, cc_r = ccs[p // 8]
                    reg = bk[0:64, gi * 128:(gi + 1) * 128]
                    cx.op(PE, lambda e: e.matmul(reg, lhsT=U[:, g, :], rhs=tzv[:, g, :], start=True, stop=False), r=[U_r, tz_r], w=[bk_r])
                    cx.op(PE, lambda e: e.matmul(reg, lhsT=Xst[:, 0, p, :], rhs=ccv[:, p % 8, gh, 0, :], start=False, stop=False), r=[Xst_r, cc_r], w=[bk_r])
                    cx.op(PE, lambda e: e.matmul(reg, lhsT=Xst[:, 1, p, :], rhs=ccv[:, p % 8, gh, 1, :], start=False, stop=True), r=[Xst_r, cc_r], w=[bk_r])
                src = bk[0:64, :].rearrange("p (g j c) -> p j g c", g=4, j=8)
                dst = stok[:, :, g4 * 64:(g4 + 1) * 64].rearrange("p j (g c) -> p j g c", g=4)
                if g4 % 2 == 0:
                    cx.op(ACT, lambda e: e.copy(out=dst, in_=src), r=[bk_r], w=[stok_r])
                else:
                    cx.op(DVE, lambda e: e.tensor_copy(out=dst, in_=src), r=[bk_r], w=[stok_r])
            syT, syT_r = tl("syT", [128, 4, NB])
            for t in range(4):
                bk, bk_r = self.bank()
                for j in range(8):
                    cx.op(PE, lambda e: e.transpose(bk[:, j * 64:(j + 1) * 64], stok[:, j, t * 128:(t + 1) * 128], self.ident_f[0:64, 0:64]),
                          r=[stok_r, self.ident_f_r], w=[bk_r])
                src = bk[:].rearrange("p (j c) -> p j c", j=8)
                dst = syT[:, t, :].rearrange("p (c j) -> p j c", j=8)
                if t % 2 == 0:
                    cx.op(ACT, lambda e: e.copy(out=dst, in_=src), r=[bk_r], w=[syT_r])
                else:
                    cx.op(DVE, lambda e: e.tensor_copy(out=dst, in_=src), r=[bk_r], w=[syT_r])
            if "s5y" in self.dbg:
                for t in range(4):
                    self.dump("s5y", syT[:, t, :], syT_r, self.dbg_out["s5y"][t * 128:(t + 1) * 128, b * NB:(b + 1) * NB])
            gl, gl_r = tl("gl", [128, 4, NB])
            glb, glb_r = tl("glb", [128, 4, NB], BF16)
            gt = [tl(f"gt{i}", [128, NB]) for i in range(2)]
            K2 = 2.0 * math.sqrt(2.0 / math.pi)
            for t in range(4):
                g_t, g_r = gt[t % 2]
                cx.op(DVE, lambda e: e.tensor_tensor(out=g_t[:], in0=syT[:, t, :], in1=syT[:, t, :], op=ALU.mult), r=[syT_r], w=[g_r])
                cx.op(DVE, lambda e: e.tensor_scalar(out=g_t[:], in0=g_t[:], scalar1=0.044715, scalar2=1.0, op0=ALU.mult, op1=ALU.add), r=[g_r], w=[g_r])
                cx.op(DVE, lambda e: e.tensor_tensor(out=g_t[:], in0=g_t[:], in1=syT[:, t, :], op=ALU.mult), r=[g_r, syT_r], w=[g_r])
                cx.op(ACT, lambda e: e.activation(out=g_t[:], in_=g_t[:], func=AF.Sigmoid, scale=K2), r=[g_r], w=[g_r])
                cx.op(DVE, lambda e: e.tensor_tensor(out=gl[:, t, :], in0=g_t[:], in1=syT[:, t, :], op=ALU.mult), r=[g_r, syT_r], w=[gl_r])
                cx.op(POOL, lambda e: e.tensor_copy(out=glb[:, t, :], in_=gl[:, t, :]), r=[gl_r], w=[glb_r])
            slot, slot_r, u = self.stream_get()
            assert u is self.U["oglu"]
            gv = slot[:, 0:2048].rearrange("p (o k w) -> p o k w", o=4, k=4)
            for ot in range(4):
                bk, bk_r = self.bank()
                for kt in range(4):
                    cx.op(PE, lambda e: e.matmul(bk[:], lhsT=gv[:, ot, kt, :], rhs=glb[:, kt, :], start=(kt == 0), stop=(kt == 3)), r=[slot_r, glb_r], w=[bk_r])
                g_t, g_r = gt[ot % 2]
                cx.op(ACT, lambda e: e.activation(out=g_t[:], in_=bk[:], func=AF.Sigmoid), r=[bk_r], w=[g_r])
                cx.op(DVE, lambda e: e.tensor_tensor(out=mixT[:, 4 + ot, :], in0=g_t[:], in1=gl[:, ot, :], op=ALU.mult), r=[g_r, gl_r], w=[mixT_r])
            for i in range(4):
                slot, slot_r, u = self.stream_get()
                assert u is self.U[("oout", i)]
                for gi in range(2):
                    ft = 2 * i + gi
                    wv = slot[:, gi * 1024:(gi + 1) * 1024].rearrange("p (k w) -> p k w", k=8)
                    bk, bk_r = self.bank()
                    for kt in range(8):
                        cx.op(PE, lambda e: e.matmul(bk[:], lhsT=wv[:, kt, :], rhs=mixT[:, kt, :], start=(kt == 0), stop=(kt == 7)), r=[slot_r, mixT_r], w=[bk_r])
                    cx.op(DVE, lambda e: e.tensor_tensor(out=self.xT[:, ft, :], in0=self.xT[:, ft, :], in1=bk[:], op=ALU.add), r=[bk_r, self.xT_r], w=[self.xT_r])
            cx.barrier()


    def final(self, b):
        cx = self.cx
        with ExitStack() as st:
            oT, oT_r = self.sb("oT", [128, 8, NB], F32, st)
            self.rmsnorm(4, oT, oT_r)
            self.store_block(b, oT, oT_r)

    def build(self):
        nc, cx = self.nc, self.cx
        self.declare_io()
        self.declare_units()
        self.wscr = nc.dram_tensor("wscr", [self.unit_off], BF16, kind="ExternalOutput" if "dumpw" in SKIP else "Internal").ap()
        self.wscr_res = Res("wscr")
        self.dbg_res = {}
        self.dbg_out = {}
        for nm in self.dbg:
            if nm == "s5setup":
                continue
            self.dbg_out[nm] = nc.dram_tensor("dbg_" + nm, [self.ntok, 1024], F32, kind="ExternalOutput").ap()
        self.y_res = Res("y")
        if "nodry" in SKIP:
            self.pro_finish = True
            for _ in self.prologue():
                pass
        if "nodry" not in SKIP:
            cx.dry = True
            self.setup()
            if "even" in self.stages:
                self.setup_even()
            if "odd" in self.stages:
                self.setup_odd()
            cx.dry = False
        pro = self.prologue() if "nodry" not in SKIP else iter(())
        next(pro, None)
        self.pro_hook = (lambda: next(pro, None)) if "hook" in SKIP else None
        self.setup()
        if "even" in self.stages:
            self.setup_even()
        if "odd" in self.stages:
            self.setup_odd()
        cx.hook = None
        self.pro_finish = True
        for _ in pro:
            pass
        cx.barrier(engines=cx.compute + [cx.SP], with_dma=True)
        self.plan_stream()
        for b in range(self.nblk):
            self.load_block(b)
            if "even" in self.stages:
                self.even(b)
            if "ffn0" in self.stages:
                self.ffn(0, b)
            if "odd" in self.stages:
                self.odd(b)
            if "ffn1" in self.stages:
                self.ffn(1, b)
            if "final" in self.stages:
                self.final(b)
            else:
                self.store_block(b, self.xT, self.xT_r)
        cx.barrier(engines=cx.compute + [cx.SP], with_dma=True)
        return nc


INPUT_NAMES = ["mix_norm", "ffn_norm", "final_norm", "ffn_w_up", "ffn_dw_w", "ffn_dw_b", "ffn_w_down",
               "e_w_in", "e_conv_w", "e_conv_b", "e_dt_bias", "e_a_log", "e_d", "e_ssm_norm", "e_w_out",
               "o_w_in", "o_dw_w", "o_dw_b", "o_ln_g", "o_ln_b", "o_a_re", "o_a_im", "o_b_re", "o_b_im", "o_c_re", "o_c_im",
               "o_d", "o_log_step", "o_glu_w", "o_w_out"]


def kernel(**inputs):
    bld = Builder()
    nc = bld.build()
    consts = make_consts()
    in_maps = []
    NCORE = 4
    for core in range(NCORE):
        bidx = core % 4
        m = {"x": np.ascontiguousarray(inputs["x"][bidx])}
        for k in INPUT_NAMES:
            m[k] = np.ascontiguousarray(inputs[k])
        for k, v in consts.items():
            m["c_" + k] = v
        in_maps.append(m)
    res = run_bass_kernel_spmd(nc, in_maps, core_ids=list(range(NCORE)))
    out = np.stack([res.results[i]["y"] for i in range(4)], axis=0)
    return out.astype(np.float32)
```
